# Optimizing a Trainium2 kernel written in Bass

```python
import math
import jax, jax.numpy as jnp
from jax import lax
import numpy as np

D_MODEL = 1024
BATCH = 4
SEQ = 8192
DEPTH = 2
DEC_BATCH = 8
DEC_SEQ = 64
PAST_LEN = 2048

CHUNK = 64
Q_BLOCK = 128
SB_HEADS = 8
SB_HEAD_DIM = D_MODEL // 16
SB_WIDTH = SB_HEADS * SB_HEAD_DIM
GDN_HEADS = 4
GDN_HEAD_DIM = D_MODEL // 8
GDN_WIDTH = GDN_HEADS * GDN_HEAD_DIM
CONV_WIDTH = 4
CONV_DIM = 3 * GDN_WIDTH
N_BRANCH = 2
D_FF = 11 * D_MODEL // 4
NORM_EPS = 1e-6
IN_SIZES = (SB_WIDTH, SB_WIDTH, SB_WIDTH, CONV_DIM, GDN_WIDTH, GDN_HEADS, GDN_HEADS, N_BRANCH * D_MODEL)
IN_DIM = sum(IN_SIZES)

kernel_name = 'stickbreak_gdn_macaron_stream_step'


def rms_norm(x, gain):
    xf = x.astype(jnp.float32)
    y = xf * lax.rsqrt(jnp.mean(xf * xf, axis=-1, keepdims=True) + NORM_EPS)
    return (y * gain.astype(jnp.float32)).astype(x.dtype)


def l2_norm(x):
    xf = x.astype(jnp.float32)
    return xf * lax.rsqrt(jnp.sum(xf * xf, axis=-1, keepdims=True) + NORM_EPS)


def swiglu_ffn(h, w_up, w_down):
    gate, up = jnp.split(h @ w_up, 2, axis=-1)
    return (jax.nn.silu(gate) * up) @ w_down


def causal_conv(u, hist, w):
    L = u.shape[1]
    up = jnp.concatenate([hist.astype(u.dtype), u], axis=1)
    y = up[:, 0:L] * w[0]
    for i in range(1, CONV_WIDTH):
        y = y + up[:, i:i + L] * w[i]
    return y, up[:, L:]


def stick_breaking_block(q, k, v, q_pos, k_pos):
    z = jnp.einsum('bqhd,bkhd->bhqk', q, k, preferred_element_type=jnp.float32) * (SB_HEAD_DIM ** -0.5)
    visible = k_pos[None, :] < q_pos[:, None]
    neg_log_keep = jnp.where(visible, jax.nn.softplus(z), 0.0)
    between = lax.cumsum(neg_log_keep, axis=3, reverse=True) - neg_log_keep
    log_a = jax.nn.log_sigmoid(z) - between
    a = jnp.where(visible, jnp.exp(log_a), 0.0)
    return jnp.einsum('bhqk,bkhd->bqhd', a.astype(v.dtype), v)


def stick_breaking_prompt(q, k, v):
    B, L, H, D = q.shape
    nb = L // Q_BLOCK
    qb = q.reshape(B, nb, Q_BLOCK, H, D).swapaxes(0, 1)
    k_pos = jnp.arange(L, dtype=jnp.int32)

    def one_block(args):
        q_blk, blk = args
        q_pos = blk * Q_BLOCK + jnp.arange(Q_BLOCK, dtype=jnp.int32)
        return stick_breaking_block(q_blk, k, v, q_pos, k_pos)

    o = lax.map(one_block, (qb, jnp.arange(nb, dtype=jnp.int32)))
    return o.swapaxes(0, 1).reshape(B, L, H, D)


def gated_delta_rule(q, k, v, g, beta, s0, chunk):
    B, L, H, _ = q.shape
    DV = v.shape[-1]
    n = L // chunk
    f32 = jnp.float32

    def blocks(t):
        t = t.astype(f32).reshape((B, n, chunk, H) + t.shape[3:])
        return jnp.moveaxis(t, (1, 3), (0, 2))

    qc, kc, vc, gc, bc = blocks(q), blocks(k), blocks(v), blocks(g), blocks(beta)
    gcum = jnp.cumsum(gc, axis=-1)
    idx = jnp.arange(chunk)
    incl = idx[:, None] >= idx[None, :]
    strict = idx[:, None] > idx[None, :]
    gamma = jnp.exp(jnp.where(incl, gcum[..., :, None] - gcum[..., None, :], -jnp.inf))
    kb = kc * bc[..., None]
    a_strict = jnp.where(strict, jnp.einsum('nbhid,nbhjd->nbhij', kb, kc) * gamma, 0.0)
    t_mat = a_strict + jnp.eye(chunk, dtype=f32)
    rhs = jnp.concatenate([vc * bc[..., None], kb * jnp.exp(gcum)[..., None]], axis=-1)
    sol = lax.linalg.triangular_solve(t_mat, rhs, left_side=True, lower=True, unit_diagonal=True)
    u, w = sol[..., :DV], sol[..., DV:]
    qk = jnp.einsum('nbhid,nbhjd->nbhij', qc, kc) * gamma
    q_dec = qc * jnp.exp(gcum)[..., None]
    k_end = kc * jnp.exp(gcum[..., -1:] - gcum)[..., None]
    chunk_decay = jnp.exp(gcum[..., -1])

    def step(s, xs):
        u_c, w_c, qk_c, qd_c, ke_c, cd_c = xs
        v_new = u_c - jnp.einsum('bhck,bhkv->bhcv', w_c, s)
        o = jnp.einsum('bhck,bhkv->bhcv', qd_c, s) + jnp.einsum('bhij,bhjv->bhiv', qk_c, v_new)
        s = s * cd_c[..., None, None] + jnp.einsum('bhck,bhcv->bhkv', ke_c, v_new)
        return s, o

    s_fin, o = lax.scan(step, s0.astype(f32), (u, w, qk, q_dec, k_end, chunk_decay))
    o = jnp.moveaxis(o, (0, 2), (1, 3)).reshape(B, L, H, DV)
    return o, s_fin


def gdn_branch(qkv_raw, z, b, a, conv_hist, s0, conv_w, a_log, dt_bias, head_gain, chunk):
    B, L, _ = qkv_raw.shape
    qkv, conv_new = causal_conv(qkv_raw, conv_hist, conv_w)
    qkv = jax.nn.silu(qkv)
    q, k, v = jnp.split(qkv, 3, axis=-1)
    hs = (B, L, GDN_HEADS, GDN_HEAD_DIM)
    q = l2_norm(q.reshape(hs)) * (GDN_HEAD_DIM ** -0.5)
    k = l2_norm(k.reshape(hs))
    v = v.reshape(hs)
    beta = jax.nn.sigmoid(b.astype(jnp.float32))
    g = -jnp.exp(a_log.astype(jnp.float32)) * jax.nn.softplus(a.astype(jnp.float32) + dt_bias.astype(jnp.float32))
    o, s_fin = gated_delta_rule(q, k, v, g, beta, s0, chunk)
    o = rms_norm(o, head_gain) * jax.nn.silu(z.reshape(hs).astype(jnp.float32))
    return o.reshape(B, L, GDN_WIDTH).astype(qkv_raw.dtype), s_fin, conv_new


def token_mixer(h, w_in, conv_w, a_log, dt_bias, gdn_gain, w_branch_sb, w_branch_gdn, w_out,
                past_k, past_v, conv_hist, s0, chunk):
    B, L, _ = h.shape
    split_points = np.cumsum(IN_SIZES)[:-1].tolist()
    sb_q, sb_k, sb_v, gdn_qkv, gdn_z, gdn_b, gdn_a, gate_logits = jnp.split(h @ w_in, split_points, axis=-1)
    hs = (B, L, SB_HEADS, SB_HEAD_DIM)
    sb_q, sb_k, sb_v = sb_q.reshape(hs), sb_k.reshape(hs), sb_v.reshape(hs)
    if past_k is None:
        o_sb = stick_breaking_prompt(sb_q, sb_k, sb_v)
    else:
        P = past_k.shape[1]
        k_all = jnp.concatenate([past_k.astype(sb_k.dtype), sb_k], axis=1)
        v_all = jnp.concatenate([past_v.astype(sb_v.dtype), sb_v], axis=1)
        q_pos = P + jnp.arange(L, dtype=jnp.int32)
        k_pos = jnp.arange(P + L, dtype=jnp.int32)
        o_sb = stick_breaking_block(sb_q, k_all, v_all, q_pos, k_pos)
    o_gdn, s_fin, conv_new = gdn_branch(gdn_qkv, gdn_z, gdn_b, gdn_a, conv_hist, s0, conv_w,
                                        a_log, dt_bias, gdn_gain, chunk)
    gates = jax.nn.sigmoid(gate_logits.astype(jnp.float32)).reshape(B, L, N_BRANCH, D_MODEL)
    merged = (gates[:, :, 0] * (o_sb.reshape(B, L, SB_WIDTH) @ w_branch_sb)
              + gates[:, :, 1] * (o_gdn @ w_branch_gdn))
    return merged.astype(h.dtype) @ w_out, sb_k, sb_v, s_fin, conv_new


def encoder_layer(x, norm_g, w1_up, w1_down, w_in, conv_w, a_log, dt_bias, gdn_gain,
                  w_branch_sb, w_branch_gdn, w_out, w2_up, w2_down,
                  past_k, past_v, conv_hist, s0, chunk):
    x = x + 0.5 * rms_norm(swiglu_ffn(rms_norm(x, norm_g[0]), w1_up, w1_down), norm_g[1])
    m, k_rows, v_rows, s_fin, conv_new = token_mixer(
        rms_norm(x, norm_g[2]), w_in, conv_w, a_log, dt_bias, gdn_gain, w_branch_sb, w_branch_gdn,
        w_out, past_k, past_v, conv_hist, s0, chunk)
    x = x + rms_norm(m, norm_g[3])
    x = x + 0.5 * rms_norm(swiglu_ffn(rms_norm(x, norm_g[4]), w2_up, w2_down), norm_g[5])
    return x, k_rows, v_rows, s_fin, conv_new


def setup_inputs(seed: int = 0) -> dict:
    key = jax.random.key(seed)
    ks = jax.random.split(key, 20)
    f32 = jnp.float32

    def nrm(k, shape, scale):
        return jax.random.normal(k, shape, f32) * scale

    dt = jnp.exp(jax.random.uniform(ks[13], (DEPTH, GDN_HEADS), f32, math.log(1e-3), math.log(1e-1)))
    return {
        'x_prompt': nrm(ks[0], (BATCH, SEQ, D_MODEL), 1.0),
        'x_sample': nrm(ks[1], (DEC_BATCH, DEC_SEQ, D_MODEL), 1.0),
        'cache_sb_k': nrm(ks[2], (DEPTH, DEC_BATCH, PAST_LEN, SB_HEADS, SB_HEAD_DIM), 1.0),
        'cache_sb_v': nrm(ks[3], (DEPTH, DEC_BATCH, PAST_LEN, SB_HEADS, SB_HEAD_DIM), 1.0),
        'state_gdn': nrm(ks[4], (DEPTH, DEC_BATCH, GDN_HEADS, GDN_HEAD_DIM, GDN_HEAD_DIM), 0.1),
        'state_conv': nrm(ks[5], (DEPTH, DEC_BATCH, CONV_WIDTH - 1, CONV_DIM), 1.0),
        'norm_gains': 1.0 + nrm(ks[6], (DEPTH, 6, D_MODEL), 0.02),
        'w_ffn1_up': nrm(ks[7], (DEPTH, D_MODEL, 2 * D_FF), D_MODEL ** -0.5),
        'w_ffn1_down': nrm(ks[8], (DEPTH, D_FF, D_MODEL), D_FF ** -0.5),
        'w_in': nrm(ks[9], (DEPTH, D_MODEL, IN_DIM), D_MODEL ** -0.5),
        'conv_w': nrm(ks[10], (DEPTH, CONV_WIDTH, CONV_DIM), CONV_WIDTH ** -0.5),
        'gdn_a_log': jnp.log(jax.random.uniform(ks[11], (DEPTH, GDN_HEADS), f32, 1.0, 16.0)),
        'gdn_dt_bias': dt + jnp.log(-jnp.expm1(-dt)),
        'gdn_norm_gain': 1.0 + nrm(ks[12], (DEPTH, GDN_HEAD_DIM), 0.02),
        'w_branch_sb': nrm(ks[14], (DEPTH, SB_WIDTH, D_MODEL), SB_WIDTH ** -0.5),
        'w_branch_gdn': nrm(ks[15], (DEPTH, GDN_WIDTH, D_MODEL), GDN_WIDTH ** -0.5),
        'w_out': nrm(ks[16], (DEPTH, D_MODEL, D_MODEL), D_MODEL ** -0.5),
        'w_ffn2_up': nrm(ks[17], (DEPTH, D_MODEL, 2 * D_FF), D_MODEL ** -0.5),
        'w_ffn2_down': nrm(ks[18], (DEPTH, D_FF, D_MODEL), D_FF ** -0.5),
    }


def reference(x_prompt, x_sample, cache_sb_k, cache_sb_v, state_gdn, state_conv, norm_gains,
              w_ffn1_up, w_ffn1_down, w_in, conv_w, gdn_a_log, gdn_dt_bias, gdn_norm_gain,
              w_branch_sb, w_branch_gdn, w_out, w_ffn2_up, w_ffn2_down):
    def run(x, past_k, past_v, conv_hist, s0, chunk):
        k_list, v_list, s_list, c_list = [], [], [], []
        for l in range(DEPTH):
            pk = None if past_k is None else past_k[l]
            pv = None if past_v is None else past_v[l]
            x, k_rows, v_rows, s_fin, conv_new = encoder_layer(
                x, norm_gains[l], w_ffn1_up[l], w_ffn1_down[l], w_in[l], conv_w[l], gdn_a_log[l],
                gdn_dt_bias[l], gdn_norm_gain[l], w_branch_sb[l], w_branch_gdn[l], w_out[l],
                w_ffn2_up[l], w_ffn2_down[l], pk, pv, conv_hist[l], s0[l], chunk)
            k_list.append(k_rows)
            v_list.append(v_rows)
            s_list.append(s_fin)
            c_list.append(conv_new)
        return (x, jnp.stack(k_list).astype(cache_sb_k.dtype), jnp.stack(v_list).astype(cache_sb_v.dtype),
                jnp.stack(s_list).astype(state_gdn.dtype), jnp.stack(c_list).astype(state_conv.dtype))

    zero_conv = jnp.zeros((DEPTH, x_prompt.shape[0], CONV_WIDTH - 1, CONV_DIM), state_conv.dtype)
    zero_state = jnp.zeros((DEPTH, x_prompt.shape[0], GDN_HEADS, GDN_HEAD_DIM, GDN_HEAD_DIM), state_gdn.dtype)
    y_prompt, pk, pv, ps, pc = run(x_prompt, None, None, zero_conv, zero_state, CHUNK)
    y_sample, sk, sv, ss, sc = run(x_sample, cache_sb_k, cache_sb_v, state_conv, state_gdn, x_sample.shape[1])
    return (y_prompt, y_sample, pk, pv, ps, pc, sk, sv, ss, sc)
```

```python
import contextlib
import numpy as np
import concourse.bass as bass
import concourse.mybir as mybir
from concourse.bass_utils import run_bass_kernel_spmd

F32 = mybir.dt.float32
BF16 = mybir.dt.bfloat16
AF = mybir.ActivationFunctionType
ALU = mybir.AluOpType

D = 1024
KD = 8
FF = 2816
KF = 22
IN_DIM = 5640
EPS = 1e-6
NEGBIG = -30000.0


class Emitter:
    def __init__(self, nc):
        self.nc = nc
        self.E = {'pe': nc.tensor, 'act': nc.scalar, 'dve': nc.vector, 'pool': nc.gpsimd, 'sp': nc.sync}
        self.sem = {}
        self.cnt = {}
        self.waited = {k: {} for k in self.E}
        self.lastw = {}
        self.readers = {}
        self.pend = {k: ([], []) for k in self.E}
        self.n_ins = 0

    def _sem(self, key):
        if key not in self.sem:
            nm = "s_" + "".join(ch for ch in str(key) if ch.isalnum() or ch == "_")
            self.sem[key] = self.nc.alloc_semaphore(nm)
            self.cnt[key] = 0
        return self.sem[key]

    def _wait(self, eng, need):
        for sk, v in need.items():
            if self.waited[eng].get(sk, 0) < v:
                self.E[eng].wait_ge(self.sem[sk], v)
                self.waited[eng][sk] = v

    def op(self, eng, fn, r=(), w=(), dma=None, signal=True):
        need = {}
        for b in r:
            t = self.lastw.get(b)
            if t is not None:
                need[t[0]] = max(need.get(t[0], 0), t[1])
        for b in w:
            t = self.lastw.get(b)
            if t is not None:
                need[t[0]] = max(need.get(t[0], 0), t[1])
            for sk, v in self.readers.get(b, {}).items():
                need[sk] = max(need.get(sk, 0), v)
        for oe, (pr_, pw_) in self.pend.items():
            if oe != eng and (pr_ or pw_):
                for b in w:
                    assert b not in pr_ and b not in pw_, ("pending unsignaled access", oe, b)
                for b in r:
                    assert b not in pw_, ("pending unsignaled write", oe, b)
        for sk in list(need):
            if isinstance(sk, tuple) and sk[0] == 'dma':
                need[sk] = self.cnt[sk]
        if eng == 'pe':
            need.pop('pe', None)
        self._wait(eng, need)
        ins = fn(self.E[eng])
        self.n_ins += 1
        if dma is not None:
            sk = ('dma', dma)
            self._sem(sk)
            self.cnt[sk] += 16
            ins.then_inc(self.sem[sk], 16)
            val = self.cnt[sk]
            for b in w:
                self.lastw[b] = (sk, val)
                self.readers[b] = {}
            for b in r:
                self.readers.setdefault(b, {})[sk] = val
            return ins
        pr, pw = self.pend[eng]
        pr.extend(r)
        pw.extend(w)
        if signal:
            sk = eng
            self._sem(sk)
            self.cnt[sk] += 1
            ins.then_inc(self.sem[sk], 1)
            val = self.cnt[sk]
            for b in pw:
                self.lastw[b] = (sk, val)
                self.readers[b] = {}
            for b in pr:
                if b not in pw:
                    self.readers.setdefault(b, {})[sk] = val
            self.pend[eng] = ([], [])
        return ins

    def barrier(self, engines=('pe', 'act', 'dve', 'pool', 'sp')):
        need = {sk: v for sk, v in self.cnt.items() if v > 0}
        for e in engines:
            n2 = dict(need)
            self._wait(e, n2)
        self.lastw = {}
        self.readers = {}


def build(cfg):
    T_P, T_S, PAST, DEPTH = cfg['T_P'], cfg['T_S'], cfg['PAST'], cfg['DEPTH']
    FT = cfg.get('FT', 256)
    nc = bass.Bass("TRN2", target_bir_lowering=False)
    _orig_sbuf_tensor = nc.sbuf_tensor
    _uid = [0]

    def _sbuf_tensor(name, shape, dt):
        _uid[0] += 1
        return _orig_sbuf_tensor("%s_u%d" % (name, _uid[0]), shape, dt)
    em = Emitter(nc)
    op = em.op

    def din(name, shape):
        return nc.dram_tensor(name, list(shape), F32, kind="ExternalInput").ap()

    def dout(name, shape):
        return nc.dram_tensor(name, list(shape), F32, kind="ExternalOutput").ap()

    def dscr(name, shape, dt=F32):
        return nc.dram_tensor(name, list(shape), dt, kind="Internal").ap()

    groups = [('p', T_P), ('s', T_S)]
    TG = dict(groups)
    x_in = {'p': din("x_p", [T_P, D]), 's': din("x_s", [T_S, D])}
    cache_k = din("cache_k", [DEPTH, PAST, 512])
    cache_v = din("cache_v", [DEPTH, PAST, 512])
    state_gdn = din("state_gdn", [DEPTH, 4, 128, 128])
    state_conv = din("state_conv", [DEPTH, 3, 1536])
    norm_g = din("norm_g", [DEPTH, 6, D])
    w1u = din("w1u", [DEPTH, D, 2 * FF]); w1d = din("w1d", [DEPTH, FF, D])
    w2u = din("w2u", [DEPTH, D, 2 * FF]); w2d = din("w2d", [DEPTH, FF, D])
    w_in = din("w_in", [DEPTH, D, IN_DIM])
    conv_w = din("conv_w", [DEPTH, 4, 1536])
    a_log = din("a_log", [DEPTH, 4]); dt_bias = din("dt_bias", [DEPTH, 4])
    gdn_gain = din("gdn_gain", [DEPTH, 128])
    w_bsb = din("w_bsb", [DEPTH, 512, D]); w_bgdn = din("w_bgdn", [DEPTH, 512, D])
    w_out = din("w_out", [DEPTH, D, D])
    c_ident = din("c_ident", [128, 128]); c_linc = din("c_linc", [128, 128])
    c_masks = din("c_masks", [128, 4, 512])
    c_neg = din("c_neg", [64, 64]); c_mstrict = din("c_mstrict", [64, 256]); c_i4 = din("c_i4", [64, 256])
    c_sel = din("c_sel", [4, 4, 128]); c_reset = din("c_reset", [4, 512])

    y_out = {'p': dout("y_p", [T_P, D]), 's': dout("y_s", [T_S, D])}
    k_out = {'p': dout("k_p", [DEPTH, T_P, 512]), 's': dout("k_s", [DEPTH, T_S, 512])}
    v_out = {'p': dout("v_p", [DEPTH, T_P, 512]), 's': dout("v_s", [DEPTH, T_S, 512])}
    g_out = {'p': dout("g_p", [DEPTH, 4, 128, 128]), 's': dout("g_s", [DEPTH, 4, 128, 128])}
    c_out = {'p': dout("c_p", [DEPTH, 3, 1536]), 's': dout("c_s", [DEPTH, 3, 1536])}

    RES = {g: dscr("res_" + g, [T, D]) for g, T in groups}
    QTs = {g: dscr("qt_" + g, [4, 128, T], BF16) for g, T in groups}
    KTs = {g: dscr("kt_" + g, [4, 128, T], BF16) for g, T in groups}
    RAWs = {g: dscr("raw_" + g, [12, 128, T]) for g, T in groups}
    ZTs = {g: dscr("zt_" + g, [4, 128, T]) for g, T in groups}
    BAs = {g: dscr("ba_" + g, [2, 4, T]) for g, T in groups}
    OSTs = {g: dscr("ost_" + g, [4, 128, T], BF16) for g, T in groups}
    OGTs = {g: dscr("ogt_" + g, [4, 128, T], BF16) for g, T in groups}

    PSALL = nc.alloc_psum_tensor("psall", [128, 8, 512], F32)

    class _Bank:
        def __init__(self, i):
            self.i = i

        def __getitem__(self, idx):
            if not isinstance(idx, tuple):
                idx = (idx,)
            return PSALL[(idx[0], self.i) + tuple(idx[1:])]
    PS = [_Bank(i) for i in range(8)]

    def pk(i):
        return ('ps', i)

    out_dma_keys = set()

    ident = nc.alloc_sbuf_tensor("ident", [128, 128], F32)
    ones_bf = nc.alloc_sbuf_tensor("ones_bf", [128, 128], BF16)
    linc_bf = nc.alloc_sbuf_tensor("linc_bf", [128, 128], BF16)
    op('sp', lambda e: e.dma_start(out=ident[:], in_=c_ident), w=['ident'], dma='c0')
    op('pool', lambda e: e.dma_start(out=linc_bf[:], in_=c_linc), w=['linc'], dma='c1')
    op('dve', lambda e: e.memset(ones_bf[:], 1.0), w=['ones'])

    def load_weight_bf16(es, name, src, kchunks, ncols, c0=0, key=None):
        t = es.enter_context(_sbuf_tensor(name, [128, kchunks, ncols], BF16))
        key = key or name
        for k in range(kchunks):
            op('pool', lambda e, k=k: e.dma_start(out=t[:, k, :], in_=src[k * 128:(k + 1) * 128, c0:c0 + ncols]),
               w=[key], dma='w_' + key)
        return t

    def load_gain_bc(es, name, src_row):
        t = es.enter_context(_sbuf_tensor(name, [128, D], F32))
        op('sp', lambda e: e.dma_start(out=t[:], in_=src_row.partition_broadcast(128)), w=[name], dma='g_' + name)
        return t

    def norm_to_T(xt, xkey, pt, nsub, gain, gkey, xnT, xnTkey, scr, tagbase):
        junk, ss, lnv, rstd, xn = scr
        for s in range(nsub):
            op('dve', lambda e, s=s: e.scalar_tensor_tensor(out=junk[0:pt, :], in0=xt[0:pt, s, :], scalar=1.0, in1=xt[0:pt, s, :],
                                                            op0=ALU.mult, op1=ALU.mult, accum_out=ss[0:pt, s:s + 1]),
               r=[xkey], w=['junk', ('ss', s)])
        op('act', lambda e: e.activation(out=lnv[0:pt, 0:nsub], in_=ss[0:pt, 0:nsub], func=AF.Ln, bias=EPS, scale=1.0 / D),
           r=[('ss', s) for s in range(nsub)], w=['lnv'])
        op('act', lambda e: e.activation(out=rstd[0:pt, 0:nsub], in_=lnv[0:pt, 0:nsub], func=AF.Exp, scale=-0.5),
           r=['lnv'], w=['rstd'])
        for s in range(nsub):
            xs = s % 2
            op('dve', lambda e, s=s, xs=xs: e.scalar_tensor_tensor(out=xn[0:pt, xs, :], in0=xt[0:pt, s, :], scalar=rstd[0:pt, s:s + 1],
                                                                 in1=gain[0:pt, :], op0=ALU.mult, op1=ALU.mult),
               r=[xkey, 'rstd', gkey], w=[('xn', xs)])
            for half in range(2):
                b = 6 + half
                for i in range(4):
                    k = half * 4 + i
                    op('pe', lambda e, b=b, i=i, k=k, xs=xs: e.transpose(PS[b][:, i * 128:i * 128 + pt], xn[0:pt, xs, k * 128:(k + 1) * 128],
                                                                        ident[0:pt, 0:pt]),
                       r=[('xn', xs), 'ident'], w=[pk(b)], signal=(i == 3))
                eng = 'act' if half == 0 else 'dve'
                src = PS[b][:].rearrange("p (i t) -> p i t", i=4)[:, :, 0:pt]
                dst = xnT[:, half * 4:half * 4 + 4, s * pt:(s + 1) * pt]
                if eng == 'act':
                    op('act', lambda e, src=src, dst=dst: e.activation(out=dst, in_=src, func=AF.Copy), r=[pk(b)], w=[(xnTkey, s, half)])
                else:
                    op('dve', lambda e, src=src, dst=dst: e.tensor_copy(out=dst, in_=src), r=[pk(b)], w=[(xnTkey, s, half)])

    def xnT_keys(xnTkey, nsub):
        return [(xnTkey, s, h) for s in range(nsub) for h in range(2)]

    def alloc_norm_scr(es):
        junk = es.enter_context(_sbuf_tensor("n_junk", [128, D], F32))
        ss = es.enter_context(_sbuf_tensor("n_ss", [128, 8], F32))
        lnv = es.enter_context(_sbuf_tensor("n_lnv", [128, 8], F32))
        rstd = es.enter_context(_sbuf_tensor("n_rstd", [128, 8], F32))
        xn = es.enter_context(_sbuf_tensor("n_xn", [128, 2, D], F32))
        return (junk, ss, lnv, rstd, xn)

    def post_norm_residual(ycat, ykey, pt, xt, xkey, s, gain, gkey, scr2, coef):
        junk, ss2, lnv2, rstd2 = scr2
        op('dve', lambda e: e.scalar_tensor_tensor(out=junk[0:pt, :], in0=ycat[0:pt, :], scalar=1.0, in1=ycat[0:pt, :],
                                                   op0=ALU.mult, op1=ALU.mult, accum_out=ss2[0:pt, 0:1]),
           r=[ykey], w=['junk', 'ss2'])
        op('act', lambda e: e.activation(out=lnv2[0:pt, 0:1], in_=ss2[0:pt, 0:1], func=AF.Ln, bias=EPS, scale=1.0 / D),
           r=['ss2'], w=['lnv2'])
        op('act', lambda e: e.activation(out=rstd2[0:pt, 0:1], in_=lnv2[0:pt, 0:1], func=AF.Exp, scale=-0.5), r=['lnv2'], w=['rstd2'])
        op('dve', lambda e: e.scalar_tensor_tensor(out=ycat[0:pt, :], in0=ycat[0:pt, :], scalar=rstd2[0:pt, 0:1], in1=gain[0:pt, :],
                                                   op0=ALU.mult, op1=ALU.mult),
           r=[ykey, 'rstd2', gkey], w=[ykey])
        op('dve', lambda e: e.scalar_tensor_tensor(out=xt[0:pt, s, :], in0=ycat[0:pt, :], scalar=float(coef), in1=xt[0:pt, s, :],
                                                   op0=ALU.mult, op1=ALU.add),
           r=[ykey, xkey], w=[xkey])

    def ffn_phase(l, which, src, dst, final):
        wu = (w1u, w2u)[which][l]
        wd = (w1d, w2d)[which][l]
        gi0, gi1 = (0, 1) if which == 0 else (4, 5)
        with contextlib.ExitStack() as es:
            Wu = load_weight_bf16(es, "Wu", wu, KD, 2 * FF)
            Wd = load_weight_bf16(es, "Wd", wd, KF, D)
            g0 = load_gain_bc(es, "g0", norm_g[l, gi0])
            g1 = load_gain_bc(es, "g1", norm_g[l, gi1])
            scr = alloc_norm_scr(es)
            ss2 = es.enter_context(_sbuf_tensor("f_ss2", [128, 1], F32))
            lnv2 = es.enter_context(_sbuf_tensor("f_lnv2", [128, 1], F32))
            rstd2 = es.enter_context(_sbuf_tensor("f_rstd2", [128, 1], F32))
            scr2 = (scr[0], ss2, lnv2, rstd2)
            nsubF = FT // 128
            xts = [es.enter_context(_sbuf_tensor("f_x%d" % i, [128, nsubF, D], F32)) for i in range(2)]
            xnT = es.enter_context(_sbuf_tensor("f_xnT", [128, KD, FT], BF16))
            hT = es.enter_context(_sbuf_tensor("f_hT", [128, KF, FT], BF16))
            sg = [es.enter_context(_sbuf_tensor("f_sg%d" % i, [128, FT], F32)) for i in range(2)]
            ycat = [es.enter_context(_sbuf_tensor("f_y%d" % i, [128, D], F32)) for i in range(2)]
            tiles = []
            for g, T in groups:
                TT = min(FT, T)
                for ti in range(T // TT):
                    tiles.append((g, ti * TT, TT))

            def load(i):
                g, t0, TT = tiles[i]
                pt = min(128, TT); nsub = TT // pt
                xt = xts[i % 2]
                op('sp', lambda e: e.dma_start(out=xt[0:pt, 0:nsub, :], in_=src[g][t0:t0 + TT, :].rearrange("(s p) d -> p s d", p=pt)),
                   r=[('res', g, t0)], w=[('fx', i % 2)], dma='fx%d' % (i % 2))
            load(0)
            cnt = 0
            for i, (g, t0, TT) in enumerate(tiles):
                if i + 1 < len(tiles):
                    load(i + 1)
                pt = min(128, TT); nsub = TT // pt
                xt = xts[i % 2]; xkey = ('fx', i % 2)
                norm_to_T(xt, xkey, pt, nsub, g0, 'g0', xnT, 'xnT', scr, 'f')
                xk = xnT_keys('xnT', nsub)
                for m in range(KF):
                    bg = m % 2; bu = 2 + m % 2
                    for k in range(KD):
                        op('pe', lambda e, k=k, m=m, bg=bg: e.matmul(PS[bg][:, 0:TT], lhsT=Wu[:, k, m * 128:(m + 1) * 128], rhs=xnT[:, k, 0:TT],
                                                                      start=(k == 0), stop=(k == KD - 1)),
                           r=xk + ['Wu'], w=[pk(bg)], signal=(k == KD - 1))
                    for k in range(KD):
                        op('pe', lambda e, k=k, m=m, bu=bu: e.matmul(PS[bu][:, 0:TT], lhsT=Wu[:, k, FF + m * 128:FF + (m + 1) * 128], rhs=xnT[:, k, 0:TT],
                                                                      start=(k == 0), stop=(k == KD - 1)),
                           r=xk + ['Wu'], w=[pk(bu)], signal=(k == KD - 1))
                    sgt = sg[m % 2]
                    op('act', lambda e, bg=bg, sgt=sgt: e.activation(out=sgt[:, 0:TT], in_=PS[bg][:, 0:TT], func=AF.Silu),
                       r=[pk(bg)], w=[('sg', m % 2)])
                    op('dve', lambda e, bu=bu, sgt=sgt, m=m: e.tensor_tensor(out=hT[:, m, 0:TT], in0=PS[bu][:, 0:TT], in1=sgt[:, 0:TT], op=ALU.mult),
                       r=[pk(bu), ('sg', m % 2)], w=[('hT', m)])
                hk = [('hT', m) for m in range(KF)]
                for s in range(nsub):
                    yc = ycat[cnt % 2]; ykey = ('ycat', cnt % 2); cnt += 1
                    for half in range(2):
                        b = 4 + half
                        for m in range(KF):
                            op('pe', lambda e, m=m, b=b, s=s, half=half: e.matmul(PS[b][0:pt, :], lhsT=hT[:, m, s * pt:(s + 1) * pt],
                                                                                 rhs=Wd[:, m, half * 512:(half + 1) * 512],
                                                                                 start=(m == 0), stop=(m == KF - 1)),
                               r=hk + ['Wd'], w=[pk(b)], signal=(m == KF - 1))
                        op('act', lambda e, b=b, half=half, yc=yc: e.activation(out=yc[0:pt, half * 512:(half + 1) * 512], in_=PS[b][0:pt, :], func=AF.Copy),
                           r=[pk(b)], w=[ykey])
                    post_norm_residual(yc, ykey, pt, xt, xkey, s, g1, 'g1', scr2, 0.5)
                dd = y_out if final else dst
                dkey = 'fo%d' % (i % 2)
                if final:
                    out_dma_keys.add(('dma', dkey))
                op('sp', lambda e, dd=dd: e.dma_start(out=dd[g][t0:t0 + TT, :].rearrange("(s p) d -> p s d", p=pt), in_=xt[0:pt, 0:nsub, :]),
                   r=[xkey], w=[('res', g, t0)], dma=dkey)
            em.barrier()

    def m1_phase(l):
        NCOL = 3592
        with contextlib.ExitStack() as es:
            Wi = load_weight_bf16(es, "Wi", w_in[l], KD, NCOL)
            g2 = load_gain_bc(es, "g2", norm_g[l, 2])
            scr = alloc_norm_scr(es)
            xts = [es.enter_context(_sbuf_tensor("m_x%d" % i, [128, 4, D], F32)) for i in range(2)]
            xnT = es.enter_context(_sbuf_tensor("m_xnT", [128, KD, 512], BF16))
            stg_f = [es.enter_context(_sbuf_tensor("m_sf%d" % i, [128, 512], F32)) for i in range(4)]
            stg_b = [es.enter_context(_sbuf_tensor("m_sb%d" % i, [128, 512], BF16)) for i in range(4)]
            cst = es.enter_context(_sbuf_tensor("m_cst", [4, 1536], F32))
            tiles = []
            for g, T in groups:
                TT = min(512, T)
                for ti in range(T // TT):
                    tiles.append((g, ti * TT, TT, ti == T // TT - 1))

            def load(i):
                g, t0, TT, _ = tiles[i]
                pt = min(128, TT); nsub = TT // pt
                xt = xts[i % 2]
                op('sp', lambda e: e.dma_start(out=xt[0:pt, 0:nsub, :], in_=RES[g][t0:t0 + TT, :].rearrange("(s p) d -> p s d", p=pt)),
                   w=[('mx', i % 2)], dma='mx%d' % (i % 2))
            load(0)
            cf = 0; cb = 0; pb = 0
            for i, (g, t0, TT, last) in enumerate(tiles):
                if i + 1 < len(tiles):
                    load(i + 1)
                pt = min(128, TT); nsub = TT // pt
                xt = xts[i % 2]; xkey = ('mx', i % 2)
                norm_to_T(xt, xkey, pt, nsub, g2, 'g2', xnT, 'xnT', scr, 'm')
                xk = xnT_keys('xnT', nsub)
                for s in range(nsub):
                    for (c0, outd) in ((512, k_out[g]), (1024, v_out[g])):
                        b = pb % 6; pb += 1
                        for k in range(KD):
                            op('pe', lambda e, k=k, b=b, c0=c0, s=s: e.matmul(PS[b][0:pt, :], lhsT=xnT[:, k, s * pt:(s + 1) * pt], rhs=Wi[:, k, c0:c0 + 512],
                                                                             start=(k == 0), stop=(k == KD - 1)),
                               r=xk + ['Wi'], w=[pk(b)], signal=(k == KD - 1))
                        sf = stg_f[cf % 4]; sk = ('sf', cf % 4); dk = 'sf%d' % (cf % 4); cf += 1
                        op('act', lambda e, b=b, sf=sf: e.activation(out=sf[0:pt, :], in_=PS[b][0:pt, :], func=AF.Copy), r=[pk(b)], w=[sk])
                        out_dma_keys.add(('dma', dk))
                        op('sp', lambda e, sf=sf, outd=outd, s=s: e.dma_start(out=outd[l, t0 + s * pt:t0 + (s + 1) * pt, :], in_=sf[0:pt, :]),
                           r=[sk], dma=dk)
                fm = []
                for p in range(4):
                    fm.append((p * 128, 128, 'b', 0.125, QTs[g][p]))
                for p in range(4):
                    fm.append((512 + p * 128, 128, 'b', 1.0, KTs[g][p]))
                for c in range(12):
                    fm.append((1536 + c * 128, 128, 'f', 1.0, RAWs[g][c]))
                for h in range(4):
                    fm.append((3072 + h * 128, 128, 'f', 1.0, ZTs[g][h]))
                fm.append((3584, 4, 'f', 1.0, BAs[g][0]))
                fm.append((3588, 4, 'f', 1.0, BAs[g][1]))
                for (c0, mrows, kind, scale, dst) in fm:
                    b = pb % 6; pb += 1
                    for k in range(KD):
                        op('pe', lambda e, k=k, b=b, c0=c0, mrows=mrows: e.matmul(PS[b][0:mrows, 0:TT], lhsT=Wi[:, k, c0:c0 + mrows], rhs=xnT[:, k, 0:TT],
                                                                                 start=(k == 0), stop=(k == KD - 1)),
                           r=xk + ['Wi'], w=[pk(b)], signal=(k == KD - 1))
                    if kind == 'b':
                        st = stg_b[cb % 4]; sk = ('sb', cb % 4); dk = 'sb%d' % (cb % 4); cb += 1
                        op('dve', lambda e, b=b, st=st, scale=scale, mrows=mrows: e.tensor_scalar(out=st[0:mrows, 0:TT], in0=PS[b][0:mrows, 0:TT],
                                                                                                scalar1=float(scale), scalar2=None, op0=ALU.mult),
                           r=[pk(b)], w=[sk])
                    else:
                        st = stg_f[cf % 4]; sk = ('sf', cf % 4); dk = 'sf%d' % (cf % 4); cf += 1
                        op('act', lambda e, b=b, st=st, mrows=mrows: e.activation(out=st[0:mrows, 0:TT], in_=PS[b][0:mrows, 0:TT], func=AF.Copy),
                           r=[pk(b)], w=[sk])
                    op('sp', lambda e, st=st, dst=dst, mrows=mrows: e.dma_start(out=dst[0:mrows, t0:t0 + TT], in_=st[0:mrows, 0:TT]), r=[sk], dma=dk)
                if last:
                    for j in range(3):
                        b = pb % 6; pb += 1
                        for k in range(KD):
                            op('pe', lambda e, k=k, b=b, j=j: e.matmul(PS[b][0:3, :], lhsT=xnT[:, k, TT - 3:TT], rhs=Wi[:, k, 1536 + j * 512:1536 + (j + 1) * 512],
                                                                      start=(k == 0), stop=(k == KD - 1)),
                               r=xk + ['Wi'], w=[pk(b)], signal=(k == KD - 1))
                        op('act', lambda e, b=b, j=j: e.activation(out=cst[0:3, j * 512:(j + 1) * 512], in_=PS[b][0:3, :], func=AF.Copy), r=[pk(b)], w=['cst'])
                    out_dma_keys.add(('dma', 'cst'))
                    op('sp', lambda e: e.dma_start(out=c_out[g][l], in_=cst[0:3, :]), r=['cst'], dma='cst')
            em.barrier()

    def att_phase(l):
        with contextlib.ExitStack() as es:
            TKmax = max(T_P, PAST + 128)
            nblk_max = max(T_P // 128, PAST // 128 + 1)
            KT = es.enter_context(_sbuf_tensor("a_KT", [128, TKmax], BF16))
            NKT = es.enter_context(_sbuf_tensor("a_NKT", [128, TKmax], BF16))
            QT = es.enter_context(_sbuf_tensor("a_QT", [128, max(T_P, 128)], BF16))
            Vp = es.enter_context(_sbuf_tensor("a_Vp", [128, nblk_max, 2, 128], BF16))
            masks = es.enter_context(_sbuf_tensor("a_masks", [128, 4, 512], BF16))
            kc = es.enter_context(_sbuf_tensor("a_kc", [128, max(PAST // 128, 1), 128], F32))
            et2 = es.enter_context(_sbuf_tensor("a_e2", [128, 2, 512], F32))
            spt = [es.enter_context(_sbuf_tensor("a_sp%d" % i, [128, 2, 512], BF16)) for i in range(2)]
            tt = [es.enter_context(_sbuf_tensor("a_t%d" % i, [128, 2, 512], F32)) for i in range(2)]
            at = [es.enter_context(_sbuf_tensor("a_a%d" % i, [128, 2, 512], BF16)) for i in range(2)]
            carry2 = es.enter_context(_sbuf_tensor("a_c2", [128, 2, 512], F32))
            ost = [es.enter_context(_sbuf_tensor("a_o%d" % i, [128, 512], BF16)) for i in range(2)]
            op('pool', lambda e: e.dma_start(out=masks[:], in_=c_masks), w=['masks'], dma='amask')
            op('dve', lambda e: e.memset(Vp[:], 0.0), w=['Vp'])
            op('dve', lambda e: e.memset(KT[:], 0.0), w=['KT'])
            op('dve', lambda e: e.memset(QT[:], 0.0), w=['QT'])
            ocnt = 0
            bcnt = 0
            for g, T in groups:
                if g not in cfg.get('att_groups', 'ps'):
                    continue
                QW = min(512, max(T, 128))
                for p in range(4):
                    if g == 'p':
                        Tk = T
                        nseg = max(1, T // 2048)
                        seg = T // nseg
                        for q4 in range(nseg):
                            op('sp', lambda e, q4=q4: e.dma_start(out=KT[:, q4 * seg:(q4 + 1) * seg], in_=KTs[g][p][:, q4 * seg:(q4 + 1) * seg]),
                               w=['KT'], dma='aKT')
                        blocks_all = [(b * 128, 128, b) for b in range(T // 128)]
                        vsrc = v_out[g][l]
                        nb = T // 128
                        for j in range(2):
                            for b0 in range(0, nb, 8):
                                b1 = min(nb, b0 + 8)
                                op('pool', lambda e, j=j, b0=b0, b1=b1: e.dma_start(
                                    out=Vp[:, b0:b1, j, j * 64:(j + 1) * 64],
                                    in_=vsrc[b0 * 128:b1 * 128, (2 * p + j) * 64:(2 * p + j + 1) * 64].rearrange("(b k) d -> k b d", k=128)),
                                    w=['Vp'], dma='aV')
                    else:
                        Tk = PAST + 128
                        npb = PAST // 128
                        op('sp', lambda e: e.dma_start(out=kc[:, 0:npb, :], in_=cache_k[l][:, p * 128:(p + 1) * 128].rearrange("(b k) d -> k b d", k=128)),
                           w=['kc'], dma='akc')
                        for b in range(npb):
                            pb_ = 6 + b % 2
                            op('pe', lambda e, b=b, pb_=pb_: e.transpose(PS[pb_][:, 0:128], kc[:, b, :], ident[:]), r=['kc', 'ident'], w=[pk(pb_)])
                            op('act', lambda e, b=b, pb_=pb_: e.activation(out=KT[:, b * 128:(b + 1) * 128], in_=PS[pb_][:, 0:128], func=AF.Copy),
                               r=[pk(pb_)], w=['KT'])
                        op('sp', lambda e: e.dma_start(out=KT[:, PAST:PAST + T], in_=KTs[g][p][:, 0:T]), w=['KT'], dma='aKT')
                        for j in range(2):
                            op('pool', lambda e, j=j: e.dma_start(
                                out=Vp[:, 0:npb, j, j * 64:(j + 1) * 64],
                                in_=cache_v[l][:, (2 * p + j) * 64:(2 * p + j + 1) * 64].rearrange("(b k) d -> k b d", k=128)),
                                w=['Vp'], dma='aV')
                            op('pool', lambda e, j=j: e.dma_start(
                                out=Vp[0:T, npb, j, j * 64:(j + 1) * 64],
                                in_=v_out[g][l][0:T, (2 * p + j) * 64:(2 * p + j + 1) * 64]),
                                w=['Vp'], dma='aV')
                    op('sp', lambda e: e.dma_start(out=QT[:, 0:T], in_=QTs[g][p][:, 0:T]), w=['QT'], dma='aQT')
                    op('dve', lambda e: e.tensor_scalar(out=NKT[:, 0:Tk], in0=KT[:, 0:Tk], scalar1=-1.0, scalar2=None, op0=ALU.mult),
                       r=['KT'], w=['NKT'])
                    for qt in range(max(1, T // QW)):
                        if cfg.get('att_skipq', False):
                            continue
                        q0 = qt * QW
                        if g == 'p':
                            nvis = (q0 + QW) // 128
                            blks = []
                            for kb in range(nvis - 1, -1, -1):
                                j = kb - q0 // 128
                                blks.append((kb * 128, 128, kb, j if j >= 0 else None))
                        else:
                            blks = [(PAST, 128, PAST // 128, 0)] + [(b * 128, 128, b, None) for b in range(PAST // 128 - 1, -1, -1)]
                        nb_ = len(blks)
                        op('dve', lambda e: e.memset(carry2[:, :, 0:QW], 0.0), w=['carry'])
                        O = PS[6]

                        def pe_A(bi):
                            (k0, nk, vb, mj) = blks[bi]
                            for hh in range(2):
                                hs = slice(hh * 64, (hh + 1) * 64)
                                op('pe', lambda e, hh=hh, hs=hs: e.matmul(PS[hh][0:nk, 0:QW], lhsT=KT[hs, k0:k0 + nk], rhs=QT[hs, q0:q0 + QW], start=True, stop=True),
                                   r=['KT', 'QT'], w=[pk(hh)])

                        def pe_BC(bi):
                            (k0, nk, vb, mj) = blks[bi]
                            lastb = bi == nb_ - 1
                            sk = ('sp', bi % 2); spx = spt[bi % 2]
                            for hh in range(2):
                                hs = slice(hh * 64, (hh + 1) * 64)
                                op('pe', lambda e, hh=hh: e.matmul(PS[2 + hh][0:nk, 0:QW], lhsT=linc_bf[0:nk, 0:nk], rhs=spx[0:nk, hh, 0:QW], start=True, stop=False),
                                   r=[sk, 'linc'], w=[pk(2 + hh)], signal=False)
                                op('pe', lambda e, hh=hh, hs=hs: e.matmul(PS[2 + hh][0:nk, 0:QW], lhsT=NKT[hs, k0:k0 + nk], rhs=QT[hs, q0:q0 + QW], start=False, stop=True),
                                   r=['NKT', 'QT'], w=[pk(2 + hh)])
                                if not lastb:
                                    op('pe', lambda e, hh=hh: e.matmul(PS[4 + hh][:, 0:QW], lhsT=ones_bf[0:nk, :], rhs=spx[0:nk, hh, 0:QW], start=True, stop=True),
                                       r=[sk, 'ones'], w=[pk(4 + hh)])

                        def dve_t(bi):
                            (k0, nk, vb, mj) = blks[bi]
                            tx = tt[bi % 2]
                            op('dve', lambda e: e.tensor_tensor(out=tx[0:nk, :, 0:QW], in0=PSALL[0:nk, 2:4, 0:QW], in1=carry2[0:nk, :, 0:QW], op=ALU.add),
                               r=[pk(2), pk(3), 'carry'], w=[('t', bi % 2)])

                        def dve_cu(bi):
                            if bi == nb_ - 1:
                                return
                            op('dve', lambda e: e.tensor_tensor(out=carry2[:, :, 0:QW], in0=PSALL[:, 4:6, 0:QW], in1=carry2[:, :, 0:QW], op=ALU.add),
                               r=[pk(4), pk(5), 'carry'], w=['carry'])

                        def act_esp(bi):
                            (k0, nk, vb, mj) = blks[bi]
                            sk = ('sp', bi % 2); spx = spt[bi % 2]
                            op('act', lambda e: e.activation(out=et2[0:nk, :, 0:QW], in_=PSALL[0:nk, 0:2, 0:QW], func=AF.Exp), r=[pk(0), pk(1)], w=['e'])
                            op('act', lambda e: e.activation(out=spx[0:nk, :, 0:QW], in_=et2[0:nk, :, 0:QW], func=AF.Ln, bias=1.0, scale=1.0), r=['e'], w=[sk])
                            if mj is not None:
                                op('pool', lambda e: e.tensor_tensor(out=spx[0:nk, :, 0:QW], in0=spx[0:nk, :, 0:QW],
                                                                     in1=masks[0:nk, mj:mj + 1, 0:QW].to_broadcast([nk, 2, QW]), op=ALU.mult),
                                   r=[sk, 'masks'], w=[sk])

                        def act_a(bi):
                            (k0, nk, vb, mj) = blks[bi]
                            tx = tt[bi % 2]; ax = at[bi % 2]
                            op('act', lambda e: e.activation(out=ax[0:nk, :, 0:QW], in_=tx[0:nk, :, 0:QW], func=AF.Exp, scale=-1.0),
                               r=[('t', bi % 2)], w=[('a', bi % 2)])
                            if mj is not None:
                                op('pool', lambda e: e.tensor_tensor(out=ax[0:nk, :, 0:QW], in0=ax[0:nk, :, 0:QW],
                                                                     in1=masks[0:nk, mj:mj + 1, 0:QW].to_broadcast([nk, 2, QW]), op=ALU.mult),
                                   r=[('a', bi % 2), 'masks'], w=[('a', bi % 2)])

                        def pe_O(bi):
                            (k0, nk, vb, mj) = blks[bi]
                            ax = at[bi % 2]
                            for hh in range(2):
                                first = (bi == 0 and hh == 0); last_ = (bi == nb_ - 1 and hh == 1)
                                op('pe', lambda e, hh=hh: e.matmul(O[:, 0:QW], lhsT=Vp[0:nk, vb, hh, :], rhs=ax[0:nk, hh, 0:QW], start=first, stop=last_),
                                   r=[('a', bi % 2), 'Vp'], w=[pk(6)], signal=last_)
                        pe_A(0)
                        for s_ in range(nb_ + 2):
                            if 1 <= s_ <= nb_:
                                pe_BC(s_ - 1)
                                dve_t(s_ - 1)
                            if s_ < nb_:
                                act_esp(s_)
                            if 1 <= s_ <= nb_:
                                act_a(s_ - 1)
                                dve_cu(s_ - 1)
                            if 2 <= s_ <= nb_ + 1:
                                pe_O(s_ - 2)
                            if s_ + 1 < nb_:
                                pe_A(s_ + 1)
                        os_ = ost[ocnt % 2]; okey = ('ost', ocnt % 2); dk = 'ost%d' % (ocnt % 2); ocnt += 1
                        op('act', lambda e, os_=os_: e.activation(out=os_[:, 0:QW], in_=PS[6][:, 0:QW], func=AF.Copy), r=[pk(6)], w=[okey])
                        QS = min(QW, T)
                        op('sp', lambda e, os_=os_: e.dma_start(out=OSTs[g][p][:, q0:q0 + QS], in_=os_[:, 0:QS]), r=[okey], dma=dk)
            em.barrier()

    def gdn_phase(l):
        with contextlib.ExitStack() as es:
            def sb(name, shape, dt=F32):
                return es.enter_context(_sbuf_tensor(name, shape, dt))
            convw = sb("g_convw", [128, 12, 4])
            neg = sb("g_neg", [64, 64]); mstrict = sb("g_mstrict", [64, 256]); i4 = sb("g_i4", [64, 256])
            sel = sb("g_sel", [4, 4, 128]); reset = sb("g_reset", [4, 512])
            alog = sb("g_alog", [4, 1]); dtb = sb("g_dtb", [4, 1]); negA = sb("g_negA", [4, 1]); gng = sb("g_gng", [128, 1])
            S = sb("g_S", [128, 4, 128])
            raw = sb("g_raw", [128, 12, 515])
            qkv = sb("g_qkv", [128, 12, 512])
            ytmp = sb("g_ytmp", [128, 512])
            sqb = sb("g_sqb", [128, 512], BF16)
            lnt = sb("g_lnt", [128, 512])
            rows_b = sb("g_rb", [4, 512]); rows_a = sb("g_ra", [4, 512])
            r_beta = sb("g_rbeta", [4, 512]); r_g = sb("g_rg", [4, 512]); r_gc = sb("g_rgc", [4, 512]); r_ngc = sb("g_rngc", [4, 512])
            r_gl = sb("g_rgl", [4, 512]); r_eg = sb("g_reg", [4, 512]); r_kes = sb("g_rkes", [4, 512]); r_cd = sb("g_rcd", [4, 512])
            r_beg = sb("g_rbeg", [4, 512]); r_tmp = sb("g_rtmp", [4, 512])
            LL = sb("g_LL", [2, 4, 512]); RR = sb("g_RR", [2, 4, 512])
            vbT = sb("g_vbT", [128, 4, 512]); kbT = sb("g_kbT", [128, 4, 512]); kbgT = sb("g_kbgT", [128, 4, 512])
            qdT = sb("g_qdT", [128, 4, 512]); keT = sb("g_keT", [128, 4, 512]); cdc = sb("g_cdc", [128, 4, 8])
            zT = sb("g_zT", [128, 4, 512]); OT = sb("g_OT", [128, 4, 512]); ogs = sb("g_ogs", [128, 4, 512], BF16)
            gmS = [sb("g_gm%d" % i, [64, 256]) for i in range(2)]; BmS = [sb("g_B%d" % i, [64, 256]) for i in range(2)]
            BTmS = [sb("g_BT%d" % i, [64, 256]) for i in range(2)]
            PmS = [[sb("g_P%d_%d" % (j, i), [64, 256]) for i in range(2)] for j in range(2)]
            PTmS = [[sb("g_PT%d_%d" % (j, i), [64, 256]) for i in range(2)] for j in range(2)]
            XmS = [[sb("g_X%d_%d" % (j, i), [64, 256]) for i in range(2)] for j in range(2)]
            QKmO = [sb("g_QKm%d" % i, [64, 256]) for i in range(4)]; XO = [sb("g_XO%d" % i, [64, 256]) for i in range(4)]
            vbm = sb("g_vb", [64, 4, 128]); kem = sb("g_ke", [64, 4, 128]); rhs2 = sb("g_rhs2", [64, 4, 128]); vnew = sb("g_vnew", [64, 4, 128])

            for wi_ in range(4):
                op('sp', lambda e, wi_=wi_: e.dma_start(out=convw[:, :, wi_], in_=conv_w[l, wi_].rearrange("(c p) -> p c", p=128), allow_slow_non_contiguous=True),
                   w=['convw'], dma='gc0')
            for (t, s_, kname) in ((neg, c_neg, 'neg'), (mstrict, c_mstrict, 'mstrict'), (i4, c_i4, 'i4'), (sel, c_sel, 'sel'), (reset, c_reset, 'reset')):
                op('sp', lambda e, t=t, s_=s_: e.dma_start(out=t[:], in_=s_), w=[kname], dma='gc0')
            op('sp', lambda e: e.dma_start(out=alog[:], in_=a_log[l].rearrange("(h o) -> h o", o=1)), w=['alog'], dma='gc0')
            op('sp', lambda e: e.dma_start(out=dtb[:], in_=dt_bias[l].rearrange("(h o) -> h o", o=1)), w=['dtb'], dma='gc0')
            op('sp', lambda e: e.dma_start(out=gng[:], in_=gdn_gain[l].rearrange("(h o) -> h o", o=1)), w=['gng'], dma='gc0')
            op('act', lambda e: e.activation(out=negA[:], in_=alog[:], func=AF.Exp), r=['alog'], w=['negA'])
            op('dve', lambda e: e.tensor_scalar(out=negA[:], in0=negA[:], scalar1=-1.0, scalar2=None, op0=ALU.mult), r=['negA'], w=['negA'])
            op('dve', lambda e: e.memset(LL[:], 1.0), w=['LL'])
            op('dve', lambda e: e.memset(RR[:], 1.0), w=['RR'])
            pbc = [0]

            def nb():
                pbc[0] += 1
                return pbc[0] % 8

            for g, T in groups:
                TT = min(512, T)
                nch = TT // 64
                if g == 'p':
                    op('dve', lambda e: e.memset(S[:], 0.0), w=['S0', 'S1', 'S2', 'S3'])
                else:
                    op('sp', lambda e: e.dma_start(out=S[:], in_=state_gdn[l].rearrange("h k v -> k h v")), w=['S0', 'S1', 'S2', 'S3'], dma='gS')
                for ti in range(T // TT):
                    t0 = ti * TT
                    if ti == 0:
                        if g == 'p':
                            op('dve', lambda e: e.memset(raw[:, :, 0:3], 0.0), w=['raw'])
                        else:
                            for wi_ in range(3):
                                op('sp', lambda e, wi_=wi_: e.dma_start(out=raw[:, :, wi_], in_=state_conv[l, wi_].rearrange("(c p) -> p c", p=128),
                                                                        allow_slow_non_contiguous=True), w=['raw'], dma='graw')
                        for c in range(12):
                            op('sp', lambda e, c=c: e.dma_start(out=raw[:, c, 3:3 + TT], in_=RAWs[g][c][:, 0:TT]), w=['raw'], dma='graw')
                    else:
                        for c in range(12):
                            op('sp', lambda e, c=c: e.dma_start(out=raw[:, c, 0:3 + TT], in_=RAWs[g][c][:, t0 - 3:t0 + TT]), w=['raw'], dma='graw')
                    op('sp', lambda e: e.dma_start(out=rows_b[:, 0:TT], in_=BAs[g][0][:, t0:t0 + TT]), w=['rows_b'], dma='grow')
                    op('sp', lambda e: e.dma_start(out=rows_a[:, 0:TT], in_=BAs[g][1][:, t0:t0 + TT]), w=['rows_a'], dma='grow')
                    for h in range(4):
                        op('sp', lambda e, h=h: e.dma_start(out=zT[:, h, 0:TT], in_=ZTs[g][h][:, t0:t0 + TT]), w=['zT'], dma='gz')
                    for c in range(12):
                        op('dve', lambda e, c=c: e.tensor_scalar(out=ytmp[:, 0:TT], in0=raw[:, c, 0:TT], scalar1=convw[:, c, 0:1], scalar2=None, op0=ALU.mult),
                           r=['raw', 'convw'], w=['ytmp'])
                        for i in range(1, 4):
                            op('dve', lambda e, c=c, i=i: e.scalar_tensor_tensor(out=ytmp[:, 0:TT], in0=raw[:, c, i:i + TT], scalar=convw[:, c, i:i + 1],
                                                                               in1=ytmp[:, 0:TT], op0=ALU.mult, op1=ALU.add),
                               r=['raw', 'convw', 'ytmp'], w=['ytmp'])
                        op('act', lambda e, c=c: e.activation(out=qkv[:, c, 0:TT], in_=ytmp[:, 0:TT], func=AF.Silu), r=['ytmp'], w=[('qkv', c)])
                    for c in range(8):
                        b = nb()
                        op('dve', lambda e, c=c: e.tensor_tensor(out=sqb[:, 0:TT], in0=qkv[:, c, 0:TT], in1=qkv[:, c, 0:TT], op=ALU.mult),
                           r=[('qkv', c)], w=['sqb'])
                        op('pe', lambda e, b=b: e.matmul(PS[b][:, 0:TT], lhsT=ones_bf[:], rhs=sqb[:, 0:TT], start=True, stop=True), r=['sqb', 'ones'], w=[pk(b)])
                        op('act', lambda e, b=b: e.activation(out=lnt[:, 0:TT], in_=PS[b][:, 0:TT], func=AF.Ln, bias=EPS, scale=1.0), r=[pk(b)], w=['lnt'])
                        bias = float(-0.5 * np.log(128.0)) if c < 4 else 0.0
                        op('act', lambda e, bias=bias: e.activation(out=lnt[:, 0:TT], in_=lnt[:, 0:TT], func=AF.Exp, scale=-0.5, bias=bias), r=['lnt'], w=['lnt'])
                        op('dve', lambda e, c=c: e.tensor_tensor(out=qkv[:, c, 0:TT], in0=qkv[:, c, 0:TT], in1=lnt[:, 0:TT], op=ALU.mult),
                           r=[('qkv', c), 'lnt'], w=[('qkv', c)])
                    op('act', lambda e: e.activation(out=r_beta[:, 0:TT], in_=rows_b[:, 0:TT], func=AF.Exp, scale=-1.0), r=['rows_b'], w=['r_beta'])
                    op('dve', lambda e: e.tensor_scalar(out=r_beta[:, 0:TT], in0=r_beta[:, 0:TT], scalar1=1.0, scalar2=None, op0=ALU.add), r=['r_beta'], w=['r_beta'])
                    op('dve', lambda e: e.reciprocal(out=r_beta[:, 0:TT], in_=r_beta[:, 0:TT]), r=['r_beta'], w=['r_beta'])
                    op('act', lambda e: e.activation(out=r_tmp[:, 0:TT], in_=rows_a[:, 0:TT], func=AF.Exp, bias=dtb[:, 0:1], scale=1.0), r=['rows_a', 'dtb'], w=['r_tmp'])
                    op('act', lambda e: e.activation(out=r_tmp[:, 0:TT], in_=r_tmp[:, 0:TT], func=AF.Ln, bias=1.0, scale=1.0), r=['r_tmp'], w=['r_tmp'])
                    op('dve', lambda e: e.tensor_scalar(out=r_g[:, 0:TT], in0=r_tmp[:, 0:TT], scalar1=negA[:, 0:1], scalar2=None, op0=ALU.mult),
                       r=['r_tmp', 'negA'], w=['r_g'])
                    op('dve', lambda e: e.tensor_tensor_scan(out=r_gc[:, 0:TT], data0=reset[:, 0:TT], data1=r_g[:, 0:TT], initial=0.0, op0=ALU.mult, op1=ALU.add),
                       r=['r_g', 'reset'], w=['r_gc'])
                    op('dve', lambda e: e.tensor_scalar(out=r_ngc[:, 0:TT], in0=r_gc[:, 0:TT], scalar1=-1.0, scalar2=None, op0=ALU.mult), r=['r_gc'], w=['r_ngc'])
                    gc3 = r_gc[:, 0:TT].rearrange("h (n c) -> h n c", c=64)
                    op('dve', lambda e: e.tensor_copy(out=r_gl[:, 0:TT].rearrange("h (n c) -> h n c", c=64), in_=gc3[:, :, 63:64].to_broadcast([4, nch, 64])),
                       r=['r_gc'], w=['r_gl'])
                    op('act', lambda e: e.activation(out=r_eg[:, 0:TT], in_=r_gc[:, 0:TT], func=AF.Exp), r=['r_gc'], w=['r_eg'])
                    op('dve', lambda e: e.tensor_tensor(out=r_tmp[:, 0:TT], in0=r_gl[:, 0:TT], in1=r_gc[:, 0:TT], op=ALU.subtract), r=['r_gl', 'r_gc', 'r_tmp'], w=['r_tmp'])
                    op('act', lambda e: e.activation(out=r_kes[:, 0:TT], in_=r_tmp[:, 0:TT], func=AF.Exp), r=['r_tmp'], w=['r_kes'])
                    op('act', lambda e: e.activation(out=r_cd[:, 0:TT], in_=r_gl[:, 0:TT], func=AF.Exp), r=['r_gl'], w=['r_cd'])
                    op('dve', lambda e: e.tensor_tensor(out=r_beg[:, 0:TT], in0=r_beta[:, 0:TT], in1=r_eg[:, 0:TT], op=ALU.mult), r=['r_beta', 'r_eg'], w=['r_beg'])
                    for h in range(4):
                        op('sp', lambda e, h=h: e.dma_start(out=LL[1:2, h, 0:TT], in_=r_ngc[h:h + 1, 0:TT]), r=['r_ngc'], w=['LL'], dma='gLL')
                        op('sp', lambda e, h=h: e.dma_start(out=RR[0:1, h, 0:TT], in_=r_gc[h:h + 1, 0:TT]), r=['r_gc'], w=['RR'], dma='gLL')
                    for h in range(4):
                        def bc(rows, rkey):
                            b = nb()
                            op('pe', lambda e: e.matmul(PS[b][:, 0:TT], lhsT=sel[:, h, :], rhs=rows[:, 0:TT], start=True, stop=True), r=[rkey, 'sel'], w=[pk(b)])
                            return b
                        b = bc(r_beta, 'r_beta')
                        op('dve', lambda e: e.tensor_tensor(out=vbT[:, h, 0:TT], in0=qkv[:, 8 + h, 0:TT], in1=PS[b][:, 0:TT], op=ALU.mult),
                           r=[pk(b), ('qkv', 8 + h)], w=['vbT'])
                        op('dve', lambda e: e.tensor_tensor(out=kbT[:, h, 0:TT], in0=qkv[:, 4 + h, 0:TT], in1=PS[b][:, 0:TT], op=ALU.mult),
                           r=[pk(b), ('qkv', 4 + h)], w=['kbT'])
                        b = bc(r_beg, 'r_beg')
                        op('dve', lambda e: e.tensor_tensor(out=kbgT[:, h, 0:TT], in0=qkv[:, 4 + h, 0:TT], in1=PS[b][:, 0:TT], op=ALU.mult),
                           r=[pk(b), ('qkv', 4 + h)], w=['kbgT'])
                        b = bc(r_eg, 'r_eg')
                        op('dve', lambda e: e.tensor_tensor(out=qdT[:, h, 0:TT], in0=qkv[:, h, 0:TT], in1=PS[b][:, 0:TT], op=ALU.mult),
                           r=[pk(b), ('qkv', h)], w=['qdT'])
                        b = bc(r_kes, 'r_kes')
                        op('dve', lambda e: e.tensor_tensor(out=keT[:, h, 0:TT], in0=qkv[:, 4 + h, 0:TT], in1=PS[b][:, 0:TT], op=ALU.mult),
                           r=[pk(b), ('qkv', 4 + h)], w=['keT'])
                        b = bc(r_cd, 'r_cd')
                        op('act', lambda e: e.activation(out=cdc[:, h, 0:nch], in_=PS[b][:, 0:TT].rearrange("p (n c) -> p n c", c=64)[:, :, 0], func=AF.Copy),
                           r=[pk(b)], w=['cdc'])
                    qk_all = [('qkv', c) for c in range(12)]
                    Sk = ['S0', 'S1', 'S2', 'S3']

                    def prep(n):
                        cs = slice(n * 64, (n + 1) * 64)
                        st = n % 2; o4 = n % 4
                        c0 = 0
                        PB = {0: 3 * st, 3: 3 * st, 1: 3 * st + 1, 4: 3 * st + 1, 2: 3 * st + 2, 5: 3 * st + 2}
                        gm, Bm, BTm, Pm, PTm, Xm = gmS[st], BmS[st], BTmS[st], PmS[st], PTmS[st], XmS[st]
                        QKm = QKmO[o4]

                        def K(name, *a):
                            return (name, st) + a

                        def pp(b):
                            return pk(PB[b])
                        for h in range(4):
                            hc = slice(c0 + h * 64, c0 + (h + 1) * 64)
                            op('pe', lambda e, h=h, hc=hc: e.matmul(PS[PB[0]][0:64, hc], lhsT=LL[0:2, h, cs], rhs=RR[0:2, h, cs], start=True, stop=False),
                               r=['LL', 'RR'], w=[pp(0)], signal=False)
                            op('pe', lambda e, h=h, hc=hc: e.matmul(PS[PB[0]][0:64, hc], lhsT=ident[0:64, 0:64], rhs=neg[:, :], start=False, stop=True),
                               r=['ident', 'neg'], w=[pp(0)], signal=(h == 3))
                        for h in range(4):
                            hc = slice(c0 + h * 64, c0 + (h + 1) * 64)
                            op('pe', lambda e, h=h, hc=hc: e.matmul(PS[PB[1]][0:64, hc], lhsT=qkv[:, 4 + h, cs], rhs=kbT[:, h, cs], start=True, stop=True),
                               r=qk_all + ['kbT'], w=[pp(1)], signal=(h == 3))
                        for h in range(4):
                            hc = slice(c0 + h * 64, c0 + (h + 1) * 64)
                            op('pe', lambda e, h=h, hc=hc: e.matmul(PS[PB[2]][0:64, hc], lhsT=qkv[:, 4 + h, cs], rhs=qkv[:, h, cs], start=True, stop=True),
                               r=qk_all, w=[pp(2)], signal=(h == 3))
                        yield
                        op('act', lambda e: e.activation(out=gm[:, :], in_=PS[PB[0]][0:64, c0:c0 + 256], func=AF.Exp), r=[pp(0)], w=[K('gm')])
                        yield
                        op('dve', lambda e: e.scalar_tensor_tensor(out=Bm[:, :], in0=PS[PB[1]][0:64, c0:c0 + 256], scalar=-1.0, in1=gm[:, :], op0=ALU.mult, op1=ALU.mult),
                           r=[pp(1), K('gm')], w=[K('Bm')])
                        op('dve', lambda e: e.tensor_tensor(out=QKm[:, :], in0=PS[PB[2]][0:64, c0:c0 + 256], in1=gm[:, :], op=ALU.mult), r=[pp(2), K('gm')], w=[('QKm', o4)])
                        yield
                        op('dve', lambda e: e.tensor_tensor(out=Bm[:, :], in0=Bm[:, :], in1=mstrict[:, :], op=ALU.mult), r=[K('Bm'), 'mstrict'], w=[K('Bm')])
                        yield
                        for h in range(4):
                            hc = slice(h * 64, (h + 1) * 64)
                            pc = slice(c0 + h * 64, c0 + (h + 1) * 64)
                            op('pe', lambda e, hc=hc, pc=pc: e.transpose(PS[PB[3]][0:64, pc], Bm[:, hc], ident[0:64, 0:64]), r=[K('Bm'), 'ident'], w=[pp(3)], signal=(h == 3))
                        op('dve', lambda e: e.tensor_tensor(out=Xm[0][:, :], in0=Bm[:, :], in1=i4[:, :], op=ALU.add), r=[K('Bm'), 'i4'], w=[K('X', 0)])
                        yield
                        op('act', lambda e: e.activation(out=BTm[:, :], in_=PS[PB[3]][0:64, c0:c0 + 256], func=AF.Copy), r=[pp(3)], w=[K('BTm')])
                        yield
                        Pp, Ppk, PTp, PTpk = Bm, K('Bm'), BTm, K('BTm')
                        xi = 0
                        for lev in range(1, 6):
                            pi = lev % 2
                            if lev < 5:
                                for h in range(4):
                                    hc = slice(h * 64, (h + 1) * 64)
                                    pc = slice(c0 + h * 64, c0 + (h + 1) * 64)
                                    op('pe', lambda e, hc=hc, pc=pc, Pp=Pp, PTp=PTp: e.matmul(PS[PB[4]][0:64, pc], lhsT=PTp[:, hc], rhs=Pp[:, hc], start=True, stop=True),
                                       r=[Ppk, PTpk], w=[pp(4)], signal=(h == 3))
                            for h in range(4):
                                hc = slice(h * 64, (h + 1) * 64)
                                pc = slice(c0 + h * 64, c0 + (h + 1) * 64)
                                op('pe', lambda e, hc=hc, pc=pc, Pp=Pp, PTp=PTp: e.matmul(PS[PB[3]][0:64, pc], lhsT=Pp[:, hc], rhs=PTp[:, hc], start=True, stop=True),
                                   r=[Ppk, PTpk], w=[pp(3)], signal=(h == 3))
                            yield
                            if lev < 5:
                                op('dve', lambda e, pi=pi: e.tensor_copy(out=Pm[pi][:, :], in_=PS[PB[4]][0:64, c0:c0 + 256]), r=[pp(4)], w=[K('P', pi)])
                            op('act', lambda e, pi=pi: e.activation(out=PTm[pi][:, :], in_=PS[PB[3]][0:64, c0:c0 + 256], func=AF.Copy), r=[pp(3)], w=[K('PT', pi)])
                            yield
                            for h in range(4):
                                hc = slice(h * 64, (h + 1) * 64)
                                pc = slice(c0 + h * 64, c0 + (h + 1) * 64)
                                op('pe', lambda e, hc=hc, pc=pc, pi=pi, xi=xi: e.matmul(PS[PB[5]][0:64, pc], lhsT=PTm[pi][:, hc], rhs=Xm[xi][:, hc], start=True, stop=True),
                                   r=[K('PT', pi), K('X', xi)], w=[pp(5)], signal=(h == 3))
                            yield
                            if lev < 5:
                                op('dve', lambda e, xi=xi: e.tensor_tensor(out=Xm[1 - xi][:, :], in0=PS[PB[5]][0:64, c0:c0 + 256], in1=Xm[xi][:, :], op=ALU.add),
                                   r=[pp(5), K('X', xi)], w=[K('X', 1 - xi)])
                            else:
                                op('dve', lambda e, xi=xi: e.tensor_tensor(out=XO[o4][:, :], in0=PS[PB[5]][0:64, c0:c0 + 256], in1=Xm[xi][:, :], op=ALU.add),
                                   r=[pp(5), K('X', xi)], w=[('XO', o4)])
                            yield
                            xi = 1 - xi
                            Pp, Ppk, PTp, PTpk = Pm[pi], K('P', pi), PTm[pi], K('PT', pi)

                    def seq(n):
                        cs = slice(n * 64, (n + 1) * 64)
                        o4 = n % 4
                        X = XO[o4]; Xk = ('XO', o4); QKm = QKmO[o4]
                        for h in range(4):
                            op('pe', lambda e, h=h: e.transpose(PS[6][0:64, h * 128:(h + 1) * 128], vbT[:, h, cs], ident[:]), r=['vbT', 'ident'], w=[pk(6)], signal=(h == 3))
                        for h in range(4):
                            op('pe', lambda e, h=h: e.transpose(PS[7][0:64, h * 128:(h + 1) * 128], keT[:, h, cs], ident[:]), r=['keT', 'ident'], w=[pk(7)], signal=(h == 3))
                        yield
                        op('act', lambda e: e.activation(out=vbm[:].rearrange("p h d -> p (h d)"), in_=PS[6][0:64, :], func=AF.Copy), r=[pk(6)], w=['vbm'])
                        op('act', lambda e: e.activation(out=kem[:].rearrange("p h d -> p (h d)"), in_=PS[7][0:64, :], func=AF.Copy), r=[pk(7)], w=['kem'])
                        yield
                        for h in range(4):
                            op('pe', lambda e, h=h: e.matmul(PS[6][0:64, h * 128:(h + 1) * 128], lhsT=kbgT[:, h, cs], rhs=S[:, h, :], start=True, stop=True),
                               r=['kbgT'] + Sk, w=[pk(6)], signal=(h == 3))
                        yield
                        op('dve', lambda e: e.tensor_tensor(out=rhs2[:].rearrange("p h d -> p (h d)"), in0=vbm[:].rearrange("p h d -> p (h d)"), in1=PS[6][0:64, :], op=ALU.subtract),
                           r=[pk(6), 'vbm'], w=['rhs2'])
                        yield
                        for h in range(4):
                            hc = slice(h * 64, (h + 1) * 64)
                            op('pe', lambda e, h=h, hc=hc: e.matmul(PS[7][0:64, h * 128:(h + 1) * 128], lhsT=X[:, hc], rhs=rhs2[:, h, :], start=True, stop=True),
                               r=[Xk, 'rhs2'], w=[pk(7)], signal=(h == 3))
                        yield
                        op('act', lambda e: e.activation(out=vnew[:].rearrange("p h d -> p (h d)"), in_=PS[7][0:64, :], func=AF.Copy), r=[pk(7)], w=['vnew'])
                        yield
                        for h in range(4):
                            hc = slice(h * 64, (h + 1) * 64)
                            op('pe', lambda e, h=h, hc=hc: e.matmul(PS[6][:, hc], lhsT=S[:, h, :], rhs=qdT[:, h, cs], start=True, stop=False),
                               r=Sk + ['qdT'], w=[pk(6)], signal=False)
                            op('pe', lambda e, h=h, hc=hc: e.matmul(PS[6][:, hc], lhsT=vnew[:, h, :], rhs=QKm[:, hc], start=False, stop=True),
                               r=['vnew', ('QKm', o4)], w=[pk(6)], signal=(h == 3))
                        for h in range(4):
                            op('pe', lambda e, h=h: e.matmul(PS[7][:, h * 128:(h + 1) * 128], lhsT=kem[:, h, :], rhs=vnew[:, h, :], start=True, stop=True),
                               r=['kem', 'vnew'], w=[pk(7)], signal=(h == 3))
                        yield
                        op('act', lambda e: e.activation(out=OT[:, :, cs], in_=PS[6][:, 0:256].rearrange("p (h c) -> p h c", c=64), func=AF.Copy), r=[pk(6)], w=['OT'])
                        for h in range(4):
                            op('dve', lambda e, h=h: e.scalar_tensor_tensor(out=S[:, h, :], in0=S[:, h, :], scalar=cdc[:, h, n:n + 1], in1=PS[7][:, h * 128:(h + 1) * 128],
                                                                          op0=ALU.mult, op1=ALU.add),
                               r=[pk(7), 'cdc', Sk[h]], w=[Sk[h]])
                        yield

                    def chain(gens):
                        for g_ in gens:
                            yield from g_

                    def interleave(gens):
                        gens = [g_ for g_ in gens if g_ is not None]
                        while gens:
                            for g_ in list(gens):
                                try:
                                    next(g_)
                                except StopIteration:
                                    gens.remove(g_)
                    npair = (nch + 1) // 2
                    interleave([prep(0), prep(1) if nch > 1 else None])
                    for kp in range(npair):
                        nxt = []
                        if kp + 1 < npair:
                            nxt = [prep(2 * kp + 2), prep(2 * kp + 3)]
                        seqs = [seq(2 * kp)] + ([seq(2 * kp + 1)] if 2 * kp + 1 < nch else [])
                        interleave(nxt + [chain(seqs)])
                    for h in range(4):
                        b = nb()
                        op('dve', lambda e, h=h: e.tensor_tensor(out=sqb[:, 0:TT], in0=OT[:, h, 0:TT], in1=OT[:, h, 0:TT], op=ALU.mult), r=['OT'], w=['sqb'])
                        op('pe', lambda e, b=b: e.matmul(PS[b][:, 0:TT], lhsT=ones_bf[:], rhs=sqb[:, 0:TT], start=True, stop=True), r=['sqb', 'ones'], w=[pk(b)])
                        op('act', lambda e, b=b: e.activation(out=lnt[:, 0:TT], in_=PS[b][:, 0:TT], func=AF.Ln, bias=EPS, scale=1.0 / 128), r=[pk(b)], w=['lnt'])
                        op('act', lambda e: e.activation(out=lnt[:, 0:TT], in_=lnt[:, 0:TT], func=AF.Exp, scale=-0.5), r=['lnt'], w=['lnt'])
                        op('dve', lambda e, h=h: e.tensor_tensor(out=ytmp[:, 0:TT], in0=OT[:, h, 0:TT], in1=lnt[:, 0:TT], op=ALU.mult), r=['OT', 'lnt'], w=['ytmp'])
                        op('act', lambda e, h=h: e.activation(out=lnt[:, 0:TT], in_=zT[:, h, 0:TT], func=AF.Silu), r=['zT', 'lnt'], w=['lnt'])
                        op('dve', lambda e, h=h: e.scalar_tensor_tensor(out=ogs[:, h, 0:TT], in0=ytmp[:, 0:TT], scalar=gng[:, 0:1], in1=lnt[:, 0:TT], op0=ALU.mult, op1=ALU.mult),
                           r=['ytmp', 'lnt', 'gng'], w=[('ogs', h)])
                        op('sp', lambda e, h=h: e.dma_start(out=OGTs[g][h][:, t0:t0 + TT], in_=ogs[:, h, 0:TT]), r=[('ogs', h)], dma='gog%d' % h)
                out_dma_keys.add(('dma', 'gSo'))
                op('sp', lambda e: e.dma_start(out=g_out[g][l].rearrange("h k v -> k h v"), in_=S[:]), r=['S0', 'S1', 'S2', 'S3'], dma='gSo')
            em.barrier()

    def m2_phase(l):
        with contextlib.ExitStack() as es:
            Wg = load_weight_bf16(es, "Wg", w_in[l], KD, 2048, c0=3592)
            Wsb = load_weight_bf16(es, "Wsb", w_bsb[l], 4, D)
            Wgd = load_weight_bf16(es, "Wgd", w_bgdn[l], 4, D)
            Wo = load_weight_bf16(es, "Wo", w_out[l], KD, D)
            g2 = load_gain_bc(es, "g2", norm_g[l, 2])
            g3 = load_gain_bc(es, "g3", norm_g[l, 3])
            scr = alloc_norm_scr(es)
            ss2 = es.enter_context(_sbuf_tensor("f_ss2", [128, 1], F32))
            lnv2 = es.enter_context(_sbuf_tensor("f_lnv2", [128, 1], F32))
            rstd2 = es.enter_context(_sbuf_tensor("f_rstd2", [128, 1], F32))
            scr2 = (scr[0], ss2, lnv2, rstd2)
            xts = [es.enter_context(_sbuf_tensor("m_x%d" % i, [128, 4, D], F32)) for i in range(2)]
            xnT = es.enter_context(_sbuf_tensor("m_xnT", [128, KD, 512], BF16))
            osb = [es.enter_context(_sbuf_tensor("m_osb%d" % i, [128, 4, 512], BF16)) for i in range(2)]
            ogd = [es.enter_context(_sbuf_tensor("m_ogd%d" % i, [128, 4, 512], BF16)) for i in range(2)]
            sg0 = [es.enter_context(_sbuf_tensor("m_sg0%d" % i, [128, 512], F32)) for i in range(2)]
            sg1 = [es.enter_context(_sbuf_tensor("m_sg1%d" % i, [128, 512], F32)) for i in range(2)]
            tmp = [es.enter_context(_sbuf_tensor("m_tmp%d" % i, [128, 512], F32)) for i in range(2)]
            mT = es.enter_context(_sbuf_tensor("m_mT", [128, KD, 512], BF16))
            ycat = [es.enter_context(_sbuf_tensor("m_y%d" % i, [128, D], F32)) for i in range(2)]
            tiles = []
            for g, T in groups:
                TT = min(512, T)
                for ti in range(T // TT):
                    tiles.append((g, ti * TT, TT))

            def load(i):
                g, t0, TT = tiles[i]
                pt = min(128, TT); nsub = TT // pt
                xt = xts[i % 2]
                op('sp', lambda e: e.dma_start(out=xt[0:pt, 0:nsub, :], in_=RES[g][t0:t0 + TT, :].rearrange("(s p) d -> p s d", p=pt)),
                   r=[('res', g, t0)], w=[('mx', i % 2)], dma='mx%d' % (i % 2))
                for k in range(4):
                    op('sp', lambda e, k=k: e.dma_start(out=osb[i % 2][:, k, 0:TT], in_=OSTs[g][k][:, t0:t0 + TT]), w=[('osb', i % 2)], dma='mo%d' % (i % 2))
                    op('sp', lambda e, k=k: e.dma_start(out=ogd[i % 2][:, k, 0:TT], in_=OGTs[g][k][:, t0:t0 + TT]), w=[('ogd', i % 2)], dma='mo%d' % (i % 2))
            load(0)
            cnt = 0
            for i, (g, t0, TT) in enumerate(tiles):
                if i + 1 < len(tiles):
                    load(i + 1)
                pt = min(128, TT); nsub = TT // pt
                xt = xts[i % 2]; xkey = ('mx', i % 2)
                norm_to_T(xt, xkey, pt, nsub, g2, 'g2', xnT, 'xnT', scr, 'm')
                xk = xnT_keys('xnT', nsub)
                for m in range(KD):
                    j = m % 2
                    for (gi, b, sgt, sgk) in ((0, 0 + j, sg0[j], ('sg0', j)), (1, 2 + j, sg1[j], ('sg1', j))):
                        for k in range(KD):
                            op('pe', lambda e, k=k, b=b, gi=gi, m=m: e.matmul(PS[b][:, 0:TT], lhsT=Wg[:, k, gi * D + m * 128:gi * D + (m + 1) * 128], rhs=xnT[:, k, 0:TT],
                                                                             start=(k == 0), stop=(k == KD - 1)),
                               r=xk + ['Wg'], w=[pk(b)], signal=(k == KD - 1))
                        op('act', lambda e, b=b, sgt=sgt: e.activation(out=sgt[:, 0:TT], in_=PS[b][:, 0:TT], func=AF.Sigmoid), r=[pk(b)], w=[sgk])
                    for (b, Wb, wk, src, srck) in ((4, Wsb, 'Wsb', osb[i % 2], ('osb', i % 2)), (5, Wgd, 'Wgd', ogd[i % 2], ('ogd', i % 2))):
                        for k in range(4):
                            op('pe', lambda e, k=k, b=b, Wb=Wb, src=src, m=m: e.matmul(PS[b][:, 0:TT], lhsT=Wb[:, k, m * 128:(m + 1) * 128], rhs=src[:, k, 0:TT],
                                                                                      start=(k == 0), stop=(k == 3)),
                               r=[wk, srck], w=[pk(b)], signal=(k == 3))
                    op('dve', lambda e, j=j: e.tensor_tensor(out=tmp[j][:, 0:TT], in0=PS[4][:, 0:TT], in1=sg0[j][:, 0:TT], op=ALU.mult),
                       r=[pk(4), ('sg0', j)], w=[('tmp', j)])
                    op('dve', lambda e, j=j: e.tensor_tensor(out=sg1[j][:, 0:TT], in0=PS[5][:, 0:TT], in1=sg1[j][:, 0:TT], op=ALU.mult),
                       r=[pk(5), ('sg1', j)], w=[('sg1', j)])
                    op('dve', lambda e, j=j, m=m: e.tensor_tensor(out=mT[:, m, 0:TT], in0=tmp[j][:, 0:TT], in1=sg1[j][:, 0:TT], op=ALU.add),
                       r=[('tmp', j), ('sg1', j)], w=[('mT', m)])
                mk = [('mT', m) for m in range(KD)]
                for s in range(nsub):
                    yc = ycat[cnt % 2]; ykey = ('ycat', cnt % 2); cnt += 1
                    for half in range(2):
                        b = 6 + half
                        for k in range(KD):
                            op('pe', lambda e, k=k, b=b, s=s, half=half: e.matmul(PS[b][0:pt, :], lhsT=mT[:, k, s * pt:(s + 1) * pt], rhs=Wo[:, k, half * 512:(half + 1) * 512],
                                                                                 start=(k == 0), stop=(k == KD - 1)),
                               r=mk + ['Wo'], w=[pk(b)], signal=(k == KD - 1))
                        op('act', lambda e, b=b, half=half, yc=yc: e.activation(out=yc[0:pt, half * 512:(half + 1) * 512], in_=PS[b][0:pt, :], func=AF.Copy),
                           r=[pk(b)], w=[ykey])
                    post_norm_residual(yc, ykey, pt, xt, xkey, s, g3, 'g3', scr2, 1.0)
                op('sp', lambda e: e.dma_start(out=RES[g][t0:t0 + TT, :].rearrange("(s p) d -> p s d", p=pt), in_=xt[0:pt, 0:nsub, :]),
                   r=[xkey], w=[('res', g, t0)], dma='mo_%d' % (i % 2))
            em.barrier()

    nph = [0]
    stop_after = cfg.get('stop_after', 10 ** 9)

    def runp(fn, *a):
        if nph[0] < stop_after:
            fn(*a)
        nph[0] += 1
    for l in range(DEPTH):
        src = x_in if l == 0 else RES
        runp(ffn_phase, l, 0, src, RES, False)
        runp(m1_phase, l)
        runp(att_phase, l)
        runp(gdn_phase, l)
        runp(m2_phase, l)
        runp(ffn_phase, l, 1, RES, RES, l == DEPTH - 1)
    em.barrier()
    return nc, em


def make_consts():
    c = {}
    c['c_ident'] = np.eye(128, dtype=np.float32)
    j = np.arange(128)
    c['c_linc'] = (j[:, None] >= j[None, :]).astype(np.float32)
    r = np.arange(128)[:, None, None]; jj = np.arange(4)[None, :, None]; cc = np.arange(512)[None, None, :]
    c['c_masks'] = ((cc - r - 128 * jj) > 0).astype(np.float32)
    j64 = np.arange(64)
    c['c_neg'] = np.where(j64[:, None] <= j64[None, :], 0.0, NEGBIG).astype(np.float32)
    ms = (j64[:, None] < j64[None, :]).astype(np.float32)
    c['c_mstrict'] = np.tile(ms, (1, 4))
    c['c_i4'] = np.tile(np.eye(64, dtype=np.float32), (1, 4))
    sel = np.zeros((4, 4, 128), np.float32)
    for h in range(4):
        sel[h, h, :] = 1.0
    c['c_sel'] = sel
    rs = np.ones((4, 512), np.float32); rs[:, ::64] = 0.0
    c['c_reset'] = rs
    return c


def run(cfg, inputs, n_cores=8):
    nc, em = build(cfg)
    T_P, T_S, PAST, DEPTH = cfg['T_P'], cfg['T_S'], cfg['PAST'], cfg['DEPTH']
    f = lambda a: np.ascontiguousarray(np.asarray(a, dtype=np.float32))
    consts = make_consts()
    NB = inputs['x_prompt'].shape[0]
    shared = {
        'norm_g': f(inputs['norm_gains']), 'w1u': f(inputs['w_ffn1_up']), 'w1d': f(inputs['w_ffn1_down']),
        'w2u': f(inputs['w_ffn2_up']), 'w2d': f(inputs['w_ffn2_down']), 'w_in': f(inputs['w_in']),
        'conv_w': f(inputs['conv_w']), 'a_log': f(inputs['gdn_a_log']), 'dt_bias': f(inputs['gdn_dt_bias']),
        'gdn_gain': f(inputs['gdn_norm_gain']), 'w_bsb': f(inputs['w_branch_sb']), 'w_bgdn': f(inputs['w_branch_gdn']),
        'w_out': f(inputs['w_out']),
    }
    shared.update(consts)
    in_maps = []
    for c in range(n_cores):
        m = dict(shared)
        m['x_p'] = f(inputs['x_prompt'][c % NB])
        m['x_s'] = f(inputs['x_sample'][c])
        m['cache_k'] = f(np.asarray(inputs['cache_sb_k'])[:, c].reshape(DEPTH, PAST, 512))
        m['cache_v'] = f(np.asarray(inputs['cache_sb_v'])[:, c].reshape(DEPTH, PAST, 512))
        m['state_gdn'] = f(np.asarray(inputs['state_gdn'])[:, c])
        m['state_conv'] = f(np.asarray(inputs['state_conv'])[:, c])
        in_maps.append(m)
    res = run_bass_kernel_spmd(nc, in_maps, core_ids=list(range(n_cores)))
    R = res.results
    NS = n_cores
    y_p = np.stack([R[b]['y_p'] for b in range(NB)])
    y_s = np.stack([R[c]['y_s'] for c in range(NS)])
    k_p = np.stack([R[b]['k_p'] for b in range(NB)], axis=1).reshape(DEPTH, NB, T_P, 8, 64)
    v_p = np.stack([R[b]['v_p'] for b in range(NB)], axis=1).reshape(DEPTH, NB, T_P, 8, 64)
    g_p = np.stack([R[b]['g_p'] for b in range(NB)], axis=1)
    c_p = np.stack([R[b]['c_p'] for b in range(NB)], axis=1)
    k_s = np.stack([R[c]['k_s'] for c in range(NS)], axis=1).reshape(DEPTH, NS, T_S, 8, 64)
    v_s = np.stack([R[c]['v_s'] for c in range(NS)], axis=1).reshape(DEPTH, NS, T_S, 8, 64)
    g_s = np.stack([R[c]['g_s'] for c in range(NS)], axis=1)
    c_s = np.stack([R[c]['c_s'] for c in range(NS)], axis=1)
    outs = (y_p, y_s, k_p, v_p, g_p, c_p, k_s, v_s, g_s, c_s)
    return tuple(np.ascontiguousarray(o, dtype=np.float32) for o in outs)


def kernel(**inputs):
    cfg = dict(T_P=8192, T_S=64, PAST=2048, DEPTH=2, FT=256)
    return run(cfg, inputs, n_cores=8)
```

```python
import contextlib
import numpy as np
import concourse.bass as bass
import concourse.mybir as mybir
from concourse.bass_utils import run_bass_kernel_spmd

F32 = mybir.dt.float32
BF16 = mybir.dt.bfloat16
AF = mybir.ActivationFunctionType
ALU = mybir.AluOpType

D = 1024
KD = 8
FF = 2816
KF = 22
IN_DIM = 5640
EPS = 1e-6
NEGBIG = -30000.0


class Emitter:
    def __init__(self, nc):
        self.nc = nc
        self.E = {'pe': nc.tensor, 'act': nc.scalar, 'dve': nc.vector, 'pool': nc.gpsimd, 'sp': nc.sync}
        self.sem = {}
        self.cnt = {}
        self.waited = {k: {} for k in self.E}
        self.lastw = {}
        self.readers = {}
        self.pend = {k: ([], []) for k in self.E}
        self.n_ins = 0

    def _sem(self, key):
        if key not in self.sem:
            nm = "s_" + "".join(ch for ch in str(key) if ch.isalnum() or ch == "_")
            self.sem[key] = self.nc.alloc_semaphore(nm)
            self.cnt[key] = 0
        return self.sem[key]

    def _wait(self, eng, need):
        for sk, v in need.items():
            if self.waited[eng].get(sk, 0) < v:
                self.E[eng].wait_ge(self.sem[sk], v)
                self.waited[eng][sk] = v

    def op(self, eng, fn, r=(), w=(), dma=None, signal=True):
        need = {}
        for b in r:
            t = self.lastw.get(b)
            if t is not None:
                need[t[0]] = max(need.get(t[0], 0), t[1])
        for b in w:
            t = self.lastw.get(b)
            if t is not None:
                need[t[0]] = max(need.get(t[0], 0), t[1])
            for sk, v in self.readers.get(b, {}).items():
                need[sk] = max(need.get(sk, 0), v)
        for oe, (pr_, pw_) in self.pend.items():
            if oe != eng and (pr_ or pw_):
                for b in w:
                    assert b not in pr_ and b not in pw_, ("pending unsignaled access", oe, b)
                for b in r:
                    assert b not in pw_, ("pending unsignaled write", oe, b)
        for sk in list(need):
            if isinstance(sk, tuple) and sk[0] == 'dma':
                need[sk] = self.cnt[sk]
        if eng == 'pe':
            need.pop('pe', None)
        self._wait(eng, need)
        ins = fn(self.E[eng])
        self.n_ins += 1
        if dma is not None:
            sk = ('dma', dma)
            self._sem(sk)
            self.cnt[sk] += 16
            ins.then_inc(self.sem[sk], 16)
            val = self.cnt[sk]
            for b in w:
                self.lastw[b] = (sk, val)
                self.readers[b] = {}
            for b in r:
                self.readers.setdefault(b, {})[sk] = val
            return ins
        pr, pw = self.pend[eng]
        pr.extend(r)
        pw.extend(w)
        if signal:
            sk = eng
            self._sem(sk)
            self.cnt[sk] += 1
            ins.then_inc(self.sem[sk], 1)
            val = self.cnt[sk]
            for b in pw:
                self.lastw[b] = (sk, val)
                self.readers[b] = {}
            for b in pr:
                if b not in pw:
                    self.readers.setdefault(b, {})[sk] = val
            self.pend[eng] = ([], [])
        return ins

    def barrier(self, engines=('pe', 'act', 'dve', 'pool', 'sp')):
        need = {sk: v for sk, v in self.cnt.items() if v > 0}
        for e in engines:
            n2 = dict(need)
            self._wait(e, n2)
        self.lastw = {}
        self.readers = {}


def build(cfg):
    T_P, T_S, PAST, DEPTH = cfg['T_P'], cfg['T_S'], cfg['PAST'], cfg['DEPTH']
    FT = cfg.get('FT', 256)
    nc = bass.Bass("TRN2", target_bir_lowering=False)
    _orig_sbuf_tensor = nc.sbuf_tensor
    _uid = [0]

    def _sbuf_tensor(name, shape, dt):
        _uid[0] += 1
        return _orig_sbuf_tensor("%s_u%d" % (name, _uid[0]), shape, dt)
    em = Emitter(nc)
    op = em.op

    def din(name, shape):
        return nc.dram_tensor(name, list(shape), F32, kind="ExternalInput").ap()

    def dout(name, shape):
        return nc.dram_tensor(name, list(shape), F32, kind="ExternalOutput").ap()

    def dscr(name, shape, dt=F32):
        return nc.dram_tensor(name, list(shape), dt, kind="Internal").ap()

    groups = [('p', T_P), ('s', T_S)]
    TG = dict(groups)
    x_in = {'p': din("x_p", [T_P, D]), 's': din("x_s", [T_S, D])}
    cache_k = din("cache_k", [DEPTH, PAST, 512])
    cache_v = din("cache_v", [DEPTH, PAST, 512])
    state_gdn = din("state_gdn", [DEPTH, 4, 128, 128])
    state_conv = din("state_conv", [DEPTH, 3, 1536])
    norm_g = din("norm_g", [DEPTH, 6, D])
    w1u = din("w1u", [DEPTH, D, 2 * FF]); w1d = din("w1d", [DEPTH, FF, D])
    w2u = din("w2u", [DEPTH, D, 2 * FF]); w2d = din("w2d", [DEPTH, FF, D])
    w_in = din("w_in", [DEPTH, D, IN_DIM])
    conv_w = din("conv_w", [DEPTH, 4, 1536])
    a_log = din("a_log", [DEPTH, 4]); dt_bias = din("dt_bias", [DEPTH, 4])
    gdn_gain = din("gdn_gain", [DEPTH, 128])
    w_bsb = din("w_bsb", [DEPTH, 512, D]); w_bgdn = din("w_bgdn", [DEPTH, 512, D])
    w_out = din("w_out", [DEPTH, D, D])
    c_ident = din("c_ident", [128, 128]); c_linc = din("c_linc", [128, 128])
    c_masks = din("c_masks", [128, 4, 512])
    c_neg = din("c_neg", [64, 64]); c_mstrict = din("c_mstrict", [64, 256]); c_i4 = din("c_i4", [64, 256])
    c_sel = din("c_sel", [4, 4, 128]); c_reset = din("c_reset", [4, 512])

    y_out = {'p': dout("y_p", [T_P, D]), 's': dout("y_s", [T_S, D])}
    k_out = {'p': dout("k_p", [DEPTH, T_P, 512]), 's': dout("k_s", [DEPTH, T_S, 512])}
    v_out = {'p': dout("v_p", [DEPTH, T_P, 512]), 's': dout("v_s", [DEPTH, T_S, 512])}
    g_out = {'p': dout("g_p", [DEPTH, 4, 128, 128]), 's': dout("g_s", [DEPTH, 4, 128, 128])}
    c_out = {'p': dout("c_p", [DEPTH, 3, 1536]), 's': dout("c_s", [DEPTH, 3, 1536])}

    RES = {g: dscr("res_" + g, [T, D]) for g, T in groups}
    QTs = {g: dscr("qt_" + g, [4, 128, T], BF16) for g, T in groups}
    KTs = {g: dscr("kt_" + g, [4, 128, T], BF16) for g, T in groups}
    RAWs = {g: dscr("raw_" + g, [12, 128, T]) for g, T in groups}
    ZTs = {g: dscr("zt_" + g, [4, 128, T]) for g, T in groups}
    BAs = {g: dscr("ba_" + g, [2, 4, T]) for g, T in groups}
    OSTs = {g: dscr("ost_" + g, [4, 128, T], BF16) for g, T in groups}
    OGTs = {g: dscr("ogt_" + g, [4, 128, T], BF16) for g, T in groups}

    PS = [nc.alloc_psum_tensor("psb%d" % i, [128, 512], F32) for i in range(8)]

    def pk(i):
        return ('ps', i)

    out_dma_keys = set()

    ident = nc.alloc_sbuf_tensor("ident", [128, 128], F32)
    ones_bf = nc.alloc_sbuf_tensor("ones_bf", [128, 128], BF16)
    linc_bf = nc.alloc_sbuf_tensor("linc_bf", [128, 128], BF16)
    op('sp', lambda e: e.dma_start(out=ident[:], in_=c_ident), w=['ident'], dma='c0')
    op('pool', lambda e: e.dma_start(out=linc_bf[:], in_=c_linc), w=['linc'], dma='c1')
    op('dve', lambda e: e.memset(ones_bf[:], 1.0), w=['ones'])

    def load_weight_bf16(es, name, src, kchunks, ncols, c0=0, key=None):
        t = es.enter_context(_sbuf_tensor(name, [128, kchunks, ncols], BF16))
        key = key or name
        for k in range(kchunks):
            op('pool', lambda e, k=k: e.dma_start(out=t[:, k, :], in_=src[k * 128:(k + 1) * 128, c0:c0 + ncols]),
               w=[key], dma='w_' + key)
        return t

    def load_gain_bc(es, name, src_row):
        t = es.enter_context(_sbuf_tensor(name, [128, D], F32))
        op('sp', lambda e: e.dma_start(out=t[:], in_=src_row.partition_broadcast(128)), w=[name], dma='g_' + name)
        return t

    def norm_to_T(xt, xkey, pt, nsub, gain, gkey, xnT, xnTkey, scr, tagbase):
        junk, ss, lnv, rstd, xn = scr
        for s in range(nsub):
            op('dve', lambda e, s=s: e.scalar_tensor_tensor(out=junk[0:pt, :], in0=xt[0:pt, s, :], scalar=1.0, in1=xt[0:pt, s, :],
                                                            op0=ALU.mult, op1=ALU.mult, accum_out=ss[0:pt, s:s + 1]),
               r=[xkey], w=['junk', ('ss', s)])
        op('act', lambda e: e.activation(out=lnv[0:pt, 0:nsub], in_=ss[0:pt, 0:nsub], func=AF.Ln, bias=EPS, scale=1.0 / D),
           r=[('ss', s) for s in range(nsub)], w=['lnv'])
        op('act', lambda e: e.activation(out=rstd[0:pt, 0:nsub], in_=lnv[0:pt, 0:nsub], func=AF.Exp, scale=-0.5),
           r=['lnv'], w=['rstd'])
        for s in range(nsub):
            xs = s % 2
            op('dve', lambda e, s=s, xs=xs: e.scalar_tensor_tensor(out=xn[0:pt, xs, :], in0=xt[0:pt, s, :], scalar=rstd[0:pt, s:s + 1],
                                                                 in1=gain[0:pt, :], op0=ALU.mult, op1=ALU.mult),
               r=[xkey, 'rstd', gkey], w=[('xn', xs)])
            for half in range(2):
                b = 6 + half
                for i in range(4):
                    k = half * 4 + i
                    op('pe', lambda e, b=b, i=i, k=k, xs=xs: e.transpose(PS[b][:, i * 128:i * 128 + pt], xn[0:pt, xs, k * 128:(k + 1) * 128],
                                                                        ident[0:pt, 0:pt]),
                       r=[('xn', xs), 'ident'], w=[pk(b)], signal=(i == 3))
                eng = 'act' if half == 0 else 'dve'
                src = PS[b][:].rearrange("p (i t) -> p i t", i=4)[:, :, 0:pt]
                dst = xnT[:, half * 4:half * 4 + 4, s * pt:(s + 1) * pt]
                if eng == 'act':
                    op('act', lambda e, src=src, dst=dst: e.activation(out=dst, in_=src, func=AF.Copy), r=[pk(b)], w=[(xnTkey, s, half)])
                else:
                    op('dve', lambda e, src=src, dst=dst: e.tensor_copy(out=dst, in_=src), r=[pk(b)], w=[(xnTkey, s, half)])

    def xnT_keys(xnTkey, nsub):
        return [(xnTkey, s, h) for s in range(nsub) for h in range(2)]

    def alloc_norm_scr(es):
        junk = es.enter_context(_sbuf_tensor("n_junk", [128, D], F32))
        ss = es.enter_context(_sbuf_tensor("n_ss", [128, 8], F32))
        lnv = es.enter_context(_sbuf_tensor("n_lnv", [128, 8], F32))
        rstd = es.enter_context(_sbuf_tensor("n_rstd", [128, 8], F32))
        xn = es.enter_context(_sbuf_tensor("n_xn", [128, 2, D], F32))
        return (junk, ss, lnv, rstd, xn)

    def post_norm_residual(ycat, ykey, pt, xt, xkey, s, gain, gkey, scr2, coef):
        junk, ss2, lnv2, rstd2 = scr2
        op('dve', lambda e: e.scalar_tensor_tensor(out=junk[0:pt, :], in0=ycat[0:pt, :], scalar=1.0, in1=ycat[0:pt, :],
                                                   op0=ALU.mult, op1=ALU.mult, accum_out=ss2[0:pt, 0:1]),
           r=[ykey], w=['junk', 'ss2'])
        op('act', lambda e: e.activation(out=lnv2[0:pt, 0:1], in_=ss2[0:pt, 0:1], func=AF.Ln, bias=EPS, scale=1.0 / D),
           r=['ss2'], w=['lnv2'])
        op('act', lambda e: e.activation(out=rstd2[0:pt, 0:1], in_=lnv2[0:pt, 0:1], func=AF.Exp, scale=-0.5), r=['lnv2'], w=['rstd2'])
        op('dve', lambda e: e.scalar_tensor_tensor(out=ycat[0:pt, :], in0=ycat[0:pt, :], scalar=rstd2[0:pt, 0:1], in1=gain[0:pt, :],
                                                   op0=ALU.mult, op1=ALU.mult),
           r=[ykey, 'rstd2', gkey], w=[ykey])
        op('dve', lambda e: e.scalar_tensor_tensor(out=xt[0:pt, s, :], in0=ycat[0:pt, :], scalar=float(coef), in1=xt[0:pt, s, :],
                                                   op0=ALU.mult, op1=ALU.add),
           r=[ykey, xkey], w=[xkey])

    def ffn_phase(l, which, src, dst, final):
        wu = (w1u, w2u)[which][l]
        wd = (w1d, w2d)[which][l]
        gi0, gi1 = (0, 1) if which == 0 else (4, 5)
        with contextlib.ExitStack() as es:
            Wu = load_weight_bf16(es, "Wu", wu, KD, 2 * FF)
            Wd = load_weight_bf16(es, "Wd", wd, KF, D)
            g0 = load_gain_bc(es, "g0", norm_g[l, gi0])
            g1 = load_gain_bc(es, "g1", norm_g[l, gi1])
            scr = alloc_norm_scr(es)
            ss2 = es.enter_context(_sbuf_tensor("f_ss2", [128, 1], F32))
            lnv2 = es.enter_context(_sbuf_tensor("f_lnv2", [128, 1], F32))
            rstd2 = es.enter_context(_sbuf_tensor("f_rstd2", [128, 1], F32))
            scr2 = (scr[0], ss2, lnv2, rstd2)
            nsubF = FT // 128
            xts = [es.enter_context(_sbuf_tensor("f_x%d" % i, [128, nsubF, D], F32)) for i in range(2)]
            xnT = es.enter_context(_sbuf_tensor("f_xnT", [128, KD, FT], BF16))
            hT = es.enter_context(_sbuf_tensor("f_hT", [128, KF, FT], BF16))
            sg = [es.enter_context(_sbuf_tensor("f_sg%d" % i, [128, FT], F32)) for i in range(2)]
            ycat = [es.enter_context(_sbuf_tensor("f_y%d" % i, [128, D], F32)) for i in range(2)]
            tiles = []
            for g, T in groups:
                TT = min(FT, T)
                for ti in range(T // TT):
                    tiles.append((g, ti * TT, TT))

            def load(i):
                g, t0, TT = tiles[i]
                pt = min(128, TT); nsub = TT // pt
                xt = xts[i % 2]
                op('sp', lambda e: e.dma_start(out=xt[0:pt, 0:nsub, :], in_=src[g][t0:t0 + TT, :].rearrange("(s p) d -> p s d", p=pt)),
                   r=[('res', g, t0)], w=[('fx', i % 2)], dma='fx%d' % (i % 2))
            load(0)
            cnt = 0
            for i, (g, t0, TT) in enumerate(tiles):
                if i + 1 < len(tiles):
                    load(i + 1)
                pt = min(128, TT); nsub = TT // pt
                xt = xts[i % 2]; xkey = ('fx', i % 2)
                norm_to_T(xt, xkey, pt, nsub, g0, 'g0', xnT, 'xnT', scr, 'f')
                xk = xnT_keys('xnT', nsub)
                for m in range(KF):
                    bg = m % 2; bu = 2 + m % 2
                    for k in range(KD):
                        op('pe', lambda e, k=k, m=m, bg=bg: e.matmul(PS[bg][:, 0:TT], lhsT=Wu[:, k, m * 128:(m + 1) * 128], rhs=xnT[:, k, 0:TT],
                                                                      start=(k == 0), stop=(k == KD - 1)),
                           r=xk + ['Wu'], w=[pk(bg)], signal=(k == KD - 1))
                    for k in range(KD):
                        op('pe', lambda e, k=k, m=m, bu=bu: e.matmul(PS[bu][:, 0:TT], lhsT=Wu[:, k, FF + m * 128:FF + (m + 1) * 128], rhs=xnT[:, k, 0:TT],
                                                                      start=(k == 0), stop=(k == KD - 1)),
                           r=xk + ['Wu'], w=[pk(bu)], signal=(k == KD - 1))
                    sgt = sg[m % 2]
                    op('act', lambda e, bg=bg, sgt=sgt: e.activation(out=sgt[:, 0:TT], in_=PS[bg][:, 0:TT], func=AF.Silu),
                       r=[pk(bg)], w=[('sg', m % 2)])
                    op('dve', lambda e, bu=bu, sgt=sgt, m=m: e.tensor_tensor(out=hT[:, m, 0:TT], in0=PS[bu][:, 0:TT], in1=sgt[:, 0:TT], op=ALU.mult),
                       r=[pk(bu), ('sg', m % 2)], w=[('hT', m)])
                hk = [('hT', m) for m in range(KF)]
                for s in range(nsub):
                    yc = ycat[cnt % 2]; ykey = ('ycat', cnt % 2); cnt += 1
                    for half in range(2):
                        b = 4 + half
                        for m in range(KF):
                            op('pe', lambda e, m=m, b=b, s=s, half=half: e.matmul(PS[b][0:pt, :], lhsT=hT[:, m, s * pt:(s + 1) * pt],
                                                                                 rhs=Wd[:, m, half * 512:(half + 1) * 512],
                                                                                 start=(m == 0), stop=(m == KF - 1)),
                               r=hk + ['Wd'], w=[pk(b)], signal=(m == KF - 1))
                        op('act', lambda e, b=b, half=half, yc=yc: e.activation(out=yc[0:pt, half * 512:(half + 1) * 512], in_=PS[b][0:pt, :], func=AF.Copy),
                           r=[pk(b)], w=[ykey])
                    post_norm_residual(yc, ykey, pt, xt, xkey, s, g1, 'g1', scr2, 0.5)
                dd = y_out if final else dst
                dkey = 'fo%d' % (i % 2)
                if final:
                    out_dma_keys.add(('dma', dkey))
                op('sp', lambda e, dd=dd: e.dma_start(out=dd[g][t0:t0 + TT, :].rearrange("(s p) d -> p s d", p=pt), in_=xt[0:pt, 0:nsub, :]),
                   r=[xkey], w=[('res', g, t0)], dma=dkey)
            em.barrier()

    def m1_phase(l):
        NCOL = 3592
        with contextlib.ExitStack() as es:
            Wi = load_weight_bf16(es, "Wi", w_in[l], KD, NCOL)
            g2 = load_gain_bc(es, "g2", norm_g[l, 2])
            scr = alloc_norm_scr(es)
            xts = [es.enter_context(_sbuf_tensor("m_x%d" % i, [128, 4, D], F32)) for i in range(2)]
            xnT = es.enter_context(_sbuf_tensor("m_xnT", [128, KD, 512], BF16))
            stg_f = [es.enter_context(_sbuf_tensor("m_sf%d" % i, [128, 512], F32)) for i in range(4)]
            stg_b = [es.enter_context(_sbuf_tensor("m_sb%d" % i, [128, 512], BF16)) for i in range(4)]
            cst = es.enter_context(_sbuf_tensor("m_cst", [4, 1536], F32))
            tiles = []
            for g, T in groups:
                TT = min(512, T)
                for ti in range(T // TT):
                    tiles.append((g, ti * TT, TT, ti == T // TT - 1))

            def load(i):
                g, t0, TT, _ = tiles[i]
                pt = min(128, TT); nsub = TT // pt
                xt = xts[i % 2]
                op('sp', lambda e: e.dma_start(out=xt[0:pt, 0:nsub, :], in_=RES[g][t0:t0 + TT, :].rearrange("(s p) d -> p s d", p=pt)),
                   w=[('mx', i % 2)], dma='mx%d' % (i % 2))
            load(0)
            cf = 0; cb = 0; pb = 0
            for i, (g, t0, TT, last) in enumerate(tiles):
                if i + 1 < len(tiles):
                    load(i + 1)
                pt = min(128, TT); nsub = TT // pt
                xt = xts[i % 2]; xkey = ('mx', i % 2)
                norm_to_T(xt, xkey, pt, nsub, g2, 'g2', xnT, 'xnT', scr, 'm')
                xk = xnT_keys('xnT', nsub)
                for s in range(nsub):
                    for (c0, outd) in ((512, k_out[g]), (1024, v_out[g])):
                        b = pb % 6; pb += 1
                        for k in range(KD):
                            op('pe', lambda e, k=k, b=b, c0=c0, s=s: e.matmul(PS[b][0:pt, :], lhsT=xnT[:, k, s * pt:(s + 1) * pt], rhs=Wi[:, k, c0:c0 + 512],
                                                                             start=(k == 0), stop=(k == KD - 1)),
                               r=xk + ['Wi'], w=[pk(b)], signal=(k == KD - 1))
                        sf = stg_f[cf % 4]; sk = ('sf', cf % 4); dk = 'sf%d' % (cf % 4); cf += 1
                        op('act', lambda e, b=b, sf=sf: e.activation(out=sf[0:pt, :], in_=PS[b][0:pt, :], func=AF.Copy), r=[pk(b)], w=[sk])
                        out_dma_keys.add(('dma', dk))
                        op('sp', lambda e, sf=sf, outd=outd, s=s: e.dma_start(out=outd[l, t0 + s * pt:t0 + (s + 1) * pt, :], in_=sf[0:pt, :]),
                           r=[sk], dma=dk)
                fm = []
                for p in range(4):
                    fm.append((p * 128, 128, 'b', 0.125, QTs[g][p]))
                for p in range(4):
                    fm.append((512 + p * 128, 128, 'b', 1.0, KTs[g][p]))
                for c in range(12):
                    fm.append((1536 + c * 128, 128, 'f', 1.0, RAWs[g][c]))
                for h in range(4):
                    fm.append((3072 + h * 128, 128, 'f', 1.0, ZTs[g][h]))
                fm.append((3584, 4, 'f', 1.0, BAs[g][0]))
                fm.append((3588, 4, 'f', 1.0, BAs[g][1]))
                for (c0, mrows, kind, scale, dst) in fm:
                    b = pb % 6; pb += 1
                    for k in range(KD):
                        op('pe', lambda e, k=k, b=b, c0=c0, mrows=mrows: e.matmul(PS[b][0:mrows, 0:TT], lhsT=Wi[:, k, c0:c0 + mrows], rhs=xnT[:, k, 0:TT],
                                                                                 start=(k == 0), stop=(k == KD - 1)),
                           r=xk + ['Wi'], w=[pk(b)], signal=(k == KD - 1))
                    if kind == 'b':
                        st = stg_b[cb % 4]; sk = ('sb', cb % 4); dk = 'sb%d' % (cb % 4); cb += 1
                        op('dve', lambda e, b=b, st=st, scale=scale, mrows=mrows: e.tensor_scalar(out=st[0:mrows, 0:TT], in0=PS[b][0:mrows, 0:TT],
                                                                                                scalar1=float(scale), scalar2=None, op0=ALU.mult),
                           r=[pk(b)], w=[sk])
                    else:
                        st = stg_f[cf % 4]; sk = ('sf', cf % 4); dk = 'sf%d' % (cf % 4); cf += 1
                        op('act', lambda e, b=b, st=st, mrows=mrows: e.activation(out=st[0:mrows, 0:TT], in_=PS[b][0:mrows, 0:TT], func=AF.Copy),
                           r=[pk(b)], w=[sk])
                    op('sp', lambda e, st=st, dst=dst, mrows=mrows: e.dma_start(out=dst[0:mrows, t0:t0 + TT], in_=st[0:mrows, 0:TT]), r=[sk], dma=dk)
                if last:
                    for j in range(3):
                        b = pb % 6; pb += 1
                        for k in range(KD):
                            op('pe', lambda e, k=k, b=b, j=j: e.matmul(PS[b][0:3, :], lhsT=xnT[:, k, TT - 3:TT], rhs=Wi[:, k, 1536 + j * 512:1536 + (j + 1) * 512],
                                                                      start=(k == 0), stop=(k == KD - 1)),
                               r=xk + ['Wi'], w=[pk(b)], signal=(k == KD - 1))
                        op('act', lambda e, b=b, j=j: e.activation(out=cst[0:3, j * 512:(j + 1) * 512], in_=PS[b][0:3, :], func=AF.Copy), r=[pk(b)], w=['cst'])
                    out_dma_keys.add(('dma', 'cst'))
                    op('sp', lambda e: e.dma_start(out=c_out[g][l], in_=cst[0:3, :]), r=['cst'], dma='cst')
            em.barrier()

    def att_phase(l):
        with contextlib.ExitStack() as es:
            TKmax = max(T_P, PAST + 128)
            nblk_max = max(T_P // 128, PAST // 128 + 1)
            KT = es.enter_context(_sbuf_tensor("a_KT", [128, TKmax], BF16))
            NKT = es.enter_context(_sbuf_tensor("a_NKT", [128, TKmax], BF16))
            QT = es.enter_context(_sbuf_tensor("a_QT", [128, max(T_P, 128)], BF16))
            Vp = es.enter_context(_sbuf_tensor("a_Vp", [128, nblk_max, 2, 128], BF16))
            masks = es.enter_context(_sbuf_tensor("a_masks", [128, 4, 512], BF16))
            kc = es.enter_context(_sbuf_tensor("a_kc", [128, max(PAST // 128, 1), 128], F32))
            et = [es.enter_context(_sbuf_tensor("a_e%d" % i, [128, 512], F32)) for i in range(2)]
            spt = [es.enter_context(_sbuf_tensor("a_sp%d" % i, [128, 512], BF16)) for i in range(4)]
            tt = [es.enter_context(_sbuf_tensor("a_t%d" % i, [128, 512], F32)) for i in range(4)]
            at = [es.enter_context(_sbuf_tensor("a_a%d" % i, [128, 512], BF16)) for i in range(4)]
            carry = [es.enter_context(_sbuf_tensor("a_c%d" % i, [128, 512], F32)) for i in range(2)]
            ost = [es.enter_context(_sbuf_tensor("a_o%d" % i, [128, 512], BF16)) for i in range(2)]
            op('pool', lambda e: e.dma_start(out=masks[:], in_=c_masks), w=['masks'], dma='amask')
            op('dve', lambda e: e.memset(Vp[:], 0.0), w=['Vp'])
            op('dve', lambda e: e.memset(KT[:], 0.0), w=['KT'])
            op('dve', lambda e: e.memset(QT[:], 0.0), w=['QT'])
            ocnt = 0
            bcnt = 0
            for g, T in groups:
                if g not in cfg.get('att_groups', 'ps'):
                    continue
                QW = min(512, max(T, 128))
                for p in range(4):
                    if g == 'p':
                        Tk = T
                        nseg = max(1, T // 2048)
                        seg = T // nseg
                        for q4 in range(nseg):
                            op('sp', lambda e, q4=q4: e.dma_start(out=KT[:, q4 * seg:(q4 + 1) * seg], in_=KTs[g][p][:, q4 * seg:(q4 + 1) * seg]),
                               w=['KT'], dma='aKT')
                        blocks_all = [(b * 128, 128, b) for b in range(T // 128)]
                        vsrc = v_out[g][l]
                        nb = T // 128
                        for j in range(2):
                            for b0 in range(0, nb, 8):
                                b1 = min(nb, b0 + 8)
                                op('pool', lambda e, j=j, b0=b0, b1=b1: e.dma_start(
                                    out=Vp[:, b0:b1, j, j * 64:(j + 1) * 64],
                                    in_=vsrc[b0 * 128:b1 * 128, (2 * p + j) * 64:(2 * p + j + 1) * 64].rearrange("(b k) d -> k b d", k=128)),
                                    w=['Vp'], dma='aV')
                    else:
                        Tk = PAST + 128
                        npb = PAST // 128
                        op('sp', lambda e: e.dma_start(out=kc[:, 0:npb, :], in_=cache_k[l][:, p * 128:(p + 1) * 128].rearrange("(b k) d -> k b d", k=128)),
                           w=['kc'], dma='akc')
                        for b in range(npb):
                            pb_ = 6 + b % 2
                            op('pe', lambda e, b=b, pb_=pb_: e.transpose(PS[pb_][:, 0:128], kc[:, b, :], ident[:]), r=['kc', 'ident'], w=[pk(pb_)])
                            op('act', lambda e, b=b, pb_=pb_: e.activation(out=KT[:, b * 128:(b + 1) * 128], in_=PS[pb_][:, 0:128], func=AF.Copy),
                               r=[pk(pb_)], w=['KT'])
                        op('sp', lambda e: e.dma_start(out=KT[:, PAST:PAST + T], in_=KTs[g][p][:, 0:T]), w=['KT'], dma='aKT')
                        for j in range(2):
                            op('pool', lambda e, j=j: e.dma_start(
                                out=Vp[:, 0:npb, j, j * 64:(j + 1) * 64],
                                in_=cache_v[l][:, (2 * p + j) * 64:(2 * p + j + 1) * 64].rearrange("(b k) d -> k b d", k=128)),
                                w=['Vp'], dma='aV')
                            op('pool', lambda e, j=j: e.dma_start(
                                out=Vp[0:T, npb, j, j * 64:(j + 1) * 64],
                                in_=v_out[g][l][0:T, (2 * p + j) * 64:(2 * p + j + 1) * 64]),
                                w=['Vp'], dma='aV')
                    op('sp', lambda e: e.dma_start(out=QT[:, 0:T], in_=QTs[g][p][:, 0:T]), w=['QT'], dma='aQT')
                    op('dve', lambda e: e.tensor_scalar(out=NKT[:, 0:Tk], in0=KT[:, 0:Tk], scalar1=-1.0, scalar2=None, op0=ALU.mult),
                       r=['KT'], w=['NKT'])
                    for qt in range(max(1, T // QW)):
                        if cfg.get('att_skipq', False):
                            continue
                        q0 = qt * QW
                        if g == 'p':
                            nvis = (q0 + QW) // 128
                            blks = []
                            for kb in range(nvis - 1, -1, -1):
                                j = kb - q0 // 128
                                blks.append((kb * 128, 128, kb, j if j >= 0 else None))
                        else:
                            blks = [(PAST, 128, PAST // 128, 0)] + [(b * 128, 128, b, None) for b in range(PAST // 128 - 1, -1, -1)]
                        nb_ = len(blks)
                        for hh in range(2):
                            op('dve', lambda e, hh=hh: e.memset(carry[hh][:, 0:QW], 0.0), w=[('carry', hh)])
                        O = PS[6]

                        def pe_A(bi):
                            (k0, nk, vb, mj) = blks[bi]
                            for hh in range(2):
                                hs = slice(hh * 64, (hh + 1) * 64)
                                op('pe', lambda e, hh=hh, hs=hs: e.matmul(PS[hh][0:nk, 0:QW], lhsT=KT[hs, k0:k0 + nk], rhs=QT[hs, q0:q0 + QW], start=True, stop=True),
                                   r=['KT', 'QT'], w=[pk(hh)])

                        def pe_BC(bi):
                            (k0, nk, vb, mj) = blks[bi]
                            lastb = bi == nb_ - 1
                            for hh in range(2):
                                hs = slice(hh * 64, (hh + 1) * 64)
                                sk = ('sp', bi % 2, hh); spx = spt[(bi % 2) * 2 + hh]
                                op('pe', lambda e, hh=hh, spx=spx: e.matmul(PS[2 + hh][0:nk, 0:QW], lhsT=linc_bf[0:nk, 0:nk], rhs=spx[0:nk, 0:QW], start=True, stop=False),
                                   r=[sk, 'linc'], w=[pk(2 + hh)], signal=False)
                                op('pe', lambda e, hh=hh, hs=hs: e.matmul(PS[2 + hh][0:nk, 0:QW], lhsT=NKT[hs, k0:k0 + nk], rhs=QT[hs, q0:q0 + QW], start=False, stop=True),
                                   r=['NKT', 'QT'], w=[pk(2 + hh)])
                                if not lastb:
                                    op('pe', lambda e, hh=hh, spx=spx: e.matmul(PS[4 + hh][:, 0:QW], lhsT=ones_bf[0:nk, :], rhs=spx[0:nk, 0:QW], start=True, stop=True),
                                       r=[sk, 'ones'], w=[pk(4 + hh)])

                        def dve_t(bi):
                            (k0, nk, vb, mj) = blks[bi]
                            for hh in range(2):
                                tx = tt[(bi % 2) * 2 + hh]
                                op('dve', lambda e, hh=hh, tx=tx: e.tensor_tensor(out=tx[0:nk, 0:QW], in0=PS[2 + hh][0:nk, 0:QW], in1=carry[hh][0:nk, 0:QW], op=ALU.add),
                                   r=[pk(2 + hh), ('carry', hh)], w=[('t', bi % 2, hh)])

                        def dve_cu(bi):
                            if bi == nb_ - 1:
                                return
                            for hh in range(2):
                                op('dve', lambda e, hh=hh: e.tensor_tensor(out=carry[hh][:, 0:QW], in0=PS[4 + hh][:, 0:QW], in1=carry[hh][:, 0:QW], op=ALU.add),
                                   r=[pk(4 + hh), ('carry', hh)], w=[('carry', hh)])

                        def act_esp(bi):
                            (k0, nk, vb, mj) = blks[bi]
                            for hh in range(2):
                                op('act', lambda e, hh=hh: e.activation(out=et[hh][0:nk, 0:QW], in_=PS[hh][0:nk, 0:QW], func=AF.Exp), r=[pk(hh)], w=[('e', hh)])
                            for hh in range(2):
                                sk = ('sp', bi % 2, hh); spx = spt[(bi % 2) * 2 + hh]
                                op('act', lambda e, hh=hh, spx=spx: e.activation(out=spx[0:nk, 0:QW], in_=et[hh][0:nk, 0:QW], func=AF.Ln, bias=1.0, scale=1.0),
                                   r=[('e', hh)], w=[sk])
                            if mj is not None:
                                for hh in range(2):
                                    sk = ('sp', bi % 2, hh); spx = spt[(bi % 2) * 2 + hh]
                                    op('pool', lambda e, spx=spx: e.tensor_tensor(out=spx[0:nk, 0:QW], in0=spx[0:nk, 0:QW], in1=masks[0:nk, mj, 0:QW], op=ALU.mult),
                                       r=[sk, 'masks'], w=[sk])

                        def act_a(bi):
                            (k0, nk, vb, mj) = blks[bi]
                            for hh in range(2):
                                tx = tt[(bi % 2) * 2 + hh]; ax = at[(bi % 2) * 2 + hh]
                                op('act', lambda e, tx=tx, ax=ax: e.activation(out=ax[0:nk, 0:QW], in_=tx[0:nk, 0:QW], func=AF.Exp, scale=-1.0),
                                   r=[('t', bi % 2, hh)], w=[('a', bi % 2, hh)])
                            if mj is not None:
                                for hh in range(2):
                                    ax = at[(bi % 2) * 2 + hh]
                                    op('pool', lambda e, ax=ax: e.tensor_tensor(out=ax[0:nk, 0:QW], in0=ax[0:nk, 0:QW], in1=masks[0:nk, mj, 0:QW], op=ALU.mult),
                                       r=[('a', bi % 2, hh), 'masks'], w=[('a', bi % 2, hh)])

                        def pe_O(bi):
                            (k0, nk, vb, mj) = blks[bi]
                            for hh in range(2):
                                ax = at[(bi % 2) * 2 + hh]
                                first = (bi == 0 and hh == 0); last_ = (bi == nb_ - 1 and hh == 1)
                                op('pe', lambda e, hh=hh, ax=ax: e.matmul(O[:, 0:QW], lhsT=Vp[0:nk, vb, hh, :], rhs=ax[0:nk, 0:QW], start=first, stop=last_),
                                   r=[('a', bi % 2, hh), 'Vp'], w=[pk(6)], signal=last_)
                        pe_A(0)
                        for s_ in range(nb_ + 2):
                            if 1 <= s_ <= nb_:
                                pe_BC(s_ - 1)
                                dve_t(s_ - 1)
                            if s_ < nb_:
                                act_esp(s_)
                            if 1 <= s_ <= nb_:
                                act_a(s_ - 1)
                                dve_cu(s_ - 1)
                            if 2 <= s_ <= nb_ + 1:
                                pe_O(s_ - 2)
                            if s_ + 1 < nb_:
                                pe_A(s_ + 1)
                        os_ = ost[ocnt % 2]; okey = ('ost', ocnt % 2); dk = 'ost%d' % (ocnt % 2); ocnt += 1
                        op('act', lambda e, os_=os_: e.activation(out=os_[:, 0:QW], in_=PS[6][:, 0:QW], func=AF.Copy), r=[pk(6)], w=[okey])
                        QS = min(QW, T)
                        op('sp', lambda e, os_=os_: e.dma_start(out=OSTs[g][p][:, q0:q0 + QS], in_=os_[:, 0:QS]), r=[okey], dma=dk)
            em.barrier()

    def gdn_phase(l):
        with contextlib.ExitStack() as es:
            def sb(name, shape, dt=F32):
                return es.enter_context(_sbuf_tensor(name, shape, dt))
            convw = sb("g_convw", [128, 12, 4])
            neg = sb("g_neg", [64, 64]); mstrict = sb("g_mstrict", [64, 256]); i4 = sb("g_i4", [64, 256])
            sel = sb("g_sel", [4, 4, 128]); reset = sb("g_reset", [4, 512])
            alog = sb("g_alog", [4, 1]); dtb = sb("g_dtb", [4, 1]); negA = sb("g_negA", [4, 1]); gng = sb("g_gng", [128, 1])
            S = sb("g_S", [128, 4, 128])
            raw = sb("g_raw", [128, 12, 515])
            qkv = sb("g_qkv", [128, 12, 512])
            ytmp = sb("g_ytmp", [128, 512])
            sqb = sb("g_sqb", [128, 512], BF16)
            lnt = sb("g_lnt", [128, 512])
            rows_b = sb("g_rb", [4, 512]); rows_a = sb("g_ra", [4, 512])
            r_beta = sb("g_rbeta", [4, 512]); r_g = sb("g_rg", [4, 512]); r_gc = sb("g_rgc", [4, 512]); r_ngc = sb("g_rngc", [4, 512])
            r_gl = sb("g_rgl", [4, 512]); r_eg = sb("g_reg", [4, 512]); r_kes = sb("g_rkes", [4, 512]); r_cd = sb("g_rcd", [4, 512])
            r_beg = sb("g_rbeg", [4, 512]); r_tmp = sb("g_rtmp", [4, 512])
            LL = sb("g_LL", [2, 4, 512]); RR = sb("g_RR", [2, 4, 512])
            vbT = sb("g_vbT", [128, 4, 512]); kbT = sb("g_kbT", [128, 4, 512], BF16); kbgT = sb("g_kbgT", [128, 4, 512], BF16)
            qdT = sb("g_qdT", [128, 4, 512], BF16); keT = sb("g_keT", [128, 4, 512]); cdc = sb("g_cdc", [128, 4, 8])
            qk16 = sb("g_qk16", [128, 8, 512], BF16); Sb = sb("g_Sb", [128, 4, 128], BF16)
            zT = sb("g_zT", [128, 4, 512]); OT = sb("g_OT", [128, 4, 512]); ogs = sb("g_ogs", [128, 4, 512], BF16)
            gmS = [sb("g_gm%d" % i, [64, 256]) for i in range(2)]; BmS = [sb("g_B%d" % i, [64, 256]) for i in range(2)]
            BTmS = [sb("g_BT%d" % i, [64, 256]) for i in range(2)]
            PmS = [[sb("g_P%d_%d" % (j, i), [64, 256]) for i in range(2)] for j in range(2)]
            PTmS = [[sb("g_PT%d_%d" % (j, i), [64, 256]) for i in range(2)] for j in range(2)]
            XmS = [[sb("g_X%d_%d" % (j, i), [64, 256]) for i in range(2)] for j in range(2)]
            QKmO = [sb("g_QKm%d" % i, [64, 256], BF16) for i in range(4)]; XO = [sb("g_XO%d" % i, [64, 256], BF16) for i in range(4)]
            vbm = sb("g_vb", [64, 4, 128]); kem = sb("g_ke", [64, 4, 128], BF16); rhs2 = sb("g_rhs2", [64, 4, 128], BF16); vnew = sb("g_vnew", [64, 4, 128], BF16)

            for wi_ in range(4):
                op('sp', lambda e, wi_=wi_: e.dma_start(out=convw[:, :, wi_], in_=conv_w[l, wi_].rearrange("(c p) -> p c", p=128), allow_slow_non_contiguous=True),
                   w=['convw'], dma='gc0')
            for (t, s_, kname) in ((neg, c_neg, 'neg'), (mstrict, c_mstrict, 'mstrict'), (i4, c_i4, 'i4'), (sel, c_sel, 'sel'), (reset, c_reset, 'reset')):
                op('sp', lambda e, t=t, s_=s_: e.dma_start(out=t[:], in_=s_), w=[kname], dma='gc0')
            op('sp', lambda e: e.dma_start(out=alog[:], in_=a_log[l].rearrange("(h o) -> h o", o=1)), w=['alog'], dma='gc0')
            op('sp', lambda e: e.dma_start(out=dtb[:], in_=dt_bias[l].rearrange("(h o) -> h o", o=1)), w=['dtb'], dma='gc0')
            op('sp', lambda e: e.dma_start(out=gng[:], in_=gdn_gain[l].rearrange("(h o) -> h o", o=1)), w=['gng'], dma='gc0')
            op('act', lambda e: e.activation(out=negA[:], in_=alog[:], func=AF.Exp), r=['alog'], w=['negA'])
            op('dve', lambda e: e.tensor_scalar(out=negA[:], in0=negA[:], scalar1=-1.0, scalar2=None, op0=ALU.mult), r=['negA'], w=['negA'])
            op('dve', lambda e: e.memset(LL[:], 1.0), w=['LL'])
            op('dve', lambda e: e.memset(RR[:], 1.0), w=['RR'])
            pbc = [0]

            def nb():
                pbc[0] += 1
                return pbc[0] % 8

            for g, T in groups:
                TT = min(512, T)
                nch = TT // 64
                if g == 'p':
                    op('dve', lambda e: e.memset(S[:], 0.0), w=['S0', 'S1', 'S2', 'S3'])
                else:
                    op('sp', lambda e: e.dma_start(out=S[:], in_=state_gdn[l].rearrange("h k v -> k h v")), w=['S0', 'S1', 'S2', 'S3'], dma='gS')
                op('pool', lambda e: e.tensor_copy(out=Sb[:], in_=S[:]), r=['S0', 'S1', 'S2', 'S3'], w=['Sb'])
                for ti in range(T // TT):
                    t0 = ti * TT
                    if ti == 0:
                        if g == 'p':
                            op('dve', lambda e: e.memset(raw[:, :, 0:3], 0.0), w=['raw'])
                        else:
                            for wi_ in range(3):
                                op('sp', lambda e, wi_=wi_: e.dma_start(out=raw[:, :, wi_], in_=state_conv[l, wi_].rearrange("(c p) -> p c", p=128),
                                                                        allow_slow_non_contiguous=True), w=['raw'], dma='graw')
                        for c in range(12):
                            op('sp', lambda e, c=c: e.dma_start(out=raw[:, c, 3:3 + TT], in_=RAWs[g][c][:, 0:TT]), w=['raw'], dma='graw')
                    else:
                        for c in range(12):
                            op('sp', lambda e, c=c: e.dma_start(out=raw[:, c, 0:3 + TT], in_=RAWs[g][c][:, t0 - 3:t0 + TT]), w=['raw'], dma='graw')
                    op('sp', lambda e: e.dma_start(out=rows_b[:, 0:TT], in_=BAs[g][0][:, t0:t0 + TT]), w=['rows_b'], dma='grow')
                    op('sp', lambda e: e.dma_start(out=rows_a[:, 0:TT], in_=BAs[g][1][:, t0:t0 + TT]), w=['rows_a'], dma='grow')
                    for h in range(4):
                        op('sp', lambda e, h=h: e.dma_start(out=zT[:, h, 0:TT], in_=ZTs[g][h][:, t0:t0 + TT]), w=['zT'], dma='gz')
                    for c in range(12):
                        op('dve', lambda e, c=c: e.tensor_scalar(out=ytmp[:, 0:TT], in0=raw[:, c, 0:TT], scalar1=convw[:, c, 0:1], scalar2=None, op0=ALU.mult),
                           r=['raw', 'convw'], w=['ytmp'])
                        for i in range(1, 4):
                            op('dve', lambda e, c=c, i=i: e.scalar_tensor_tensor(out=ytmp[:, 0:TT], in0=raw[:, c, i:i + TT], scalar=convw[:, c, i:i + 1],
                                                                               in1=ytmp[:, 0:TT], op0=ALU.mult, op1=ALU.add),
                               r=['raw', 'convw', 'ytmp'], w=['ytmp'])
                        op('act', lambda e, c=c: e.activation(out=qkv[:, c, 0:TT], in_=ytmp[:, 0:TT], func=AF.Silu), r=['ytmp'], w=[('qkv', c)])
                    for c in range(8):
                        b = nb()
                        op('dve', lambda e, c=c: e.tensor_tensor(out=sqb[:, 0:TT], in0=qkv[:, c, 0:TT], in1=qkv[:, c, 0:TT], op=ALU.mult),
                           r=[('qkv', c)], w=['sqb'])
                        op('pe', lambda e, b=b: e.matmul(PS[b][:, 0:TT], lhsT=ones_bf[:], rhs=sqb[:, 0:TT], start=True, stop=True), r=['sqb', 'ones'], w=[pk(b)])
                        op('act', lambda e, b=b: e.activation(out=lnt[:, 0:TT], in_=PS[b][:, 0:TT], func=AF.Ln, bias=EPS, scale=1.0), r=[pk(b)], w=['lnt'])
                        bias = float(-0.5 * np.log(128.0)) if c < 4 else 0.0
                        op('act', lambda e, bias=bias: e.activation(out=lnt[:, 0:TT], in_=lnt[:, 0:TT], func=AF.Exp, scale=-0.5, bias=bias), r=['lnt'], w=['lnt'])
                        op('dve', lambda e, c=c: e.tensor_tensor(out=qkv[:, c, 0:TT], in0=qkv[:, c, 0:TT], in1=lnt[:, 0:TT], op=ALU.mult),
                           r=[('qkv', c), 'lnt'], w=[('qkv', c)])
                        op('pool', lambda e, c=c: e.tensor_copy(out=qk16[:, c, 0:TT], in_=qkv[:, c, 0:TT]), r=[('qkv', c)], w=[('qk16', c)])
                    op('act', lambda e: e.activation(out=r_beta[:, 0:TT], in_=rows_b[:, 0:TT], func=AF.Exp, scale=-1.0), r=['rows_b'], w=['r_beta'])
                    op('dve', lambda e: e.tensor_scalar(out=r_beta[:, 0:TT], in0=r_beta[:, 0:TT], scalar1=1.0, scalar2=None, op0=ALU.add), r=['r_beta'], w=['r_beta'])
                    op('dve', lambda e: e.reciprocal(out=r_beta[:, 0:TT], in_=r_beta[:, 0:TT]), r=['r_beta'], w=['r_beta'])
                    op('act', lambda e: e.activation(out=r_tmp[:, 0:TT], in_=rows_a[:, 0:TT], func=AF.Exp, bias=dtb[:, 0:1], scale=1.0), r=['rows_a', 'dtb'], w=['r_tmp'])
                    op('act', lambda e: e.activation(out=r_tmp[:, 0:TT], in_=r_tmp[:, 0:TT], func=AF.Ln, bias=1.0, scale=1.0), r=['r_tmp'], w=['r_tmp'])
                    op('dve', lambda e: e.tensor_scalar(out=r_g[:, 0:TT], in0=r_tmp[:, 0:TT], scalar1=negA[:, 0:1], scalar2=None, op0=ALU.mult),
                       r=['r_tmp', 'negA'], w=['r_g'])
                    op('dve', lambda e: e.tensor_tensor_scan(out=r_gc[:, 0:TT], data0=reset[:, 0:TT], data1=r_g[:, 0:TT], initial=0.0, op0=ALU.mult, op1=ALU.add),
                       r=['r_g', 'reset'], w=['r_gc'])
                    op('dve', lambda e: e.tensor_scalar(out=r_ngc[:, 0:TT], in0=r_gc[:, 0:TT], scalar1=-1.0, scalar2=None, op0=ALU.mult), r=['r_gc'], w=['r_ngc'])
                    gc3 = r_gc[:, 0:TT].rearrange("h (n c) -> h n c", c=64)
                    op('dve', lambda e: e.tensor_copy(out=r_gl[:, 0:TT].rearrange("h (n c) -> h n c", c=64), in_=gc3[:, :, 63:64].to_broadcast([4, nch, 64])),
                       r=['r_gc'], w=['r_gl'])
                    op('act', lambda e: e.activation(out=r_eg[:, 0:TT], in_=r_gc[:, 0:TT], func=AF.Exp), r=['r_gc'], w=['r_eg'])
                    op('dve', lambda e: e.tensor_tensor(out=r_tmp[:, 0:TT], in0=r_gl[:, 0:TT], in1=r_gc[:, 0:TT], op=ALU.subtract), r=['r_gl', 'r_gc', 'r_tmp'], w=['r_tmp'])
                    op('act', lambda e: e.activation(out=r_kes[:, 0:TT], in_=r_tmp[:, 0:TT], func=AF.Exp), r=['r_tmp'], w=['r_kes'])
                    op('act', lambda e: e.activation(out=r_cd[:, 0:TT], in_=r_gl[:, 0:TT], func=AF.Exp), r=['r_gl'], w=['r_cd'])
                    op('dve', lambda e: e.tensor_tensor(out=r_beg[:, 0:TT], in0=r_beta[:, 0:TT], in1=r_eg[:, 0:TT], op=ALU.mult), r=['r_beta', 'r_eg'], w=['r_beg'])
                    for h in range(4):
                        op('sp', lambda e, h=h: e.dma_start(out=LL[1:2, h, 0:TT], in_=r_ngc[h:h + 1, 0:TT]), r=['r_ngc'], w=['LL'], dma='gLL')
                        op('sp', lambda e, h=h: e.dma_start(out=RR[0:1, h, 0:TT], in_=r_gc[h:h + 1, 0:TT]), r=['r_gc'], w=['RR'], dma='gLL')
                    for h in range(4):
                        def bc(rows, rkey):
                            b = nb()
                            op('pe', lambda e: e.matmul(PS[b][:, 0:TT], lhsT=sel[:, h, :], rhs=rows[:, 0:TT], start=True, stop=True), r=[rkey, 'sel'], w=[pk(b)])
                            return b
                        b = bc(r_beta, 'r_beta')
                        op('dve', lambda e: e.tensor_tensor(out=vbT[:, h, 0:TT], in0=qkv[:, 8 + h, 0:TT], in1=PS[b][:, 0:TT], op=ALU.mult),
                           r=[pk(b), ('qkv', 8 + h)], w=['vbT'])
                        op('dve', lambda e: e.tensor_tensor(out=kbT[:, h, 0:TT], in0=qkv[:, 4 + h, 0:TT], in1=PS[b][:, 0:TT], op=ALU.mult),
                           r=[pk(b), ('qkv', 4 + h)], w=['kbT'])
                        b = bc(r_beg, 'r_beg')
                        op('dve', lambda e: e.tensor_tensor(out=kbgT[:, h, 0:TT], in0=qkv[:, 4 + h, 0:TT], in1=PS[b][:, 0:TT], op=ALU.mult),
                           r=[pk(b), ('qkv', 4 + h)], w=['kbgT'])
                        b = bc(r_eg, 'r_eg')
                        op('dve', lambda e: e.tensor_tensor(out=qdT[:, h, 0:TT], in0=qkv[:, h, 0:TT], in1=PS[b][:, 0:TT], op=ALU.mult),
                           r=[pk(b), ('qkv', h)], w=['qdT'])
                        b = bc(r_kes, 'r_kes')
                        op('dve', lambda e: e.tensor_tensor(out=keT[:, h, 0:TT], in0=qkv[:, 4 + h, 0:TT], in1=PS[b][:, 0:TT], op=ALU.mult),
                           r=[pk(b), ('qkv', 4 + h)], w=['keT'])
                        b = bc(r_cd, 'r_cd')
                        op('act', lambda e: e.activation(out=cdc[:, h, 0:nch], in_=PS[b][:, 0:TT].rearrange("p (n c) -> p n c", c=64)[:, :, 0], func=AF.Copy),
                           r=[pk(b)], w=['cdc'])
                    qk_all = [('qkv', c) for c in range(12)]
                    qk16_all = [('qk16', c) for c in range(8)]
                    Sk = ['S0', 'S1', 'S2', 'S3']

                    def prep(n):
                        cs = slice(n * 64, (n + 1) * 64)
                        st = n % 2; o4 = n % 4
                        c0 = 0
                        PB = {0: 3 * st, 3: 3 * st, 1: 3 * st + 1, 4: 3 * st + 1, 2: 3 * st + 2, 5: 3 * st + 2}
                        gm, Bm, BTm, Pm, PTm, Xm = gmS[st], BmS[st], BTmS[st], PmS[st], PTmS[st], XmS[st]
                        QKm = QKmO[o4]

                        def K(name, *a):
                            return (name, st) + a

                        def pp(b):
                            return pk(PB[b])
                        for h in range(4):
                            hc = slice(c0 + h * 64, c0 + (h + 1) * 64)
                            op('pe', lambda e, h=h, hc=hc: e.matmul(PS[PB[0]][0:64, hc], lhsT=LL[0:2, h, cs], rhs=RR[0:2, h, cs], start=True, stop=False),
                               r=['LL', 'RR'], w=[pp(0)], signal=False)
                            op('pe', lambda e, h=h, hc=hc: e.matmul(PS[PB[0]][0:64, hc], lhsT=ident[0:64, 0:64], rhs=neg[:, :], start=False, stop=True),
                               r=['ident', 'neg'], w=[pp(0)], signal=(h == 3))
                        for h in range(4):
                            hc = slice(c0 + h * 64, c0 + (h + 1) * 64)
                            op('pe', lambda e, h=h, hc=hc: e.matmul(PS[PB[1]][0:64, hc], lhsT=qk16[:, 4 + h, cs], rhs=kbT[:, h, cs], start=True, stop=True),
                               r=qk16_all + ['kbT'], w=[pp(1)], signal=(h == 3))
                        for h in range(4):
                            hc = slice(c0 + h * 64, c0 + (h + 1) * 64)
                            op('pe', lambda e, h=h, hc=hc: e.matmul(PS[PB[2]][0:64, hc], lhsT=qk16[:, 4 + h, cs], rhs=qk16[:, h, cs], start=True, stop=True),
                               r=qk16_all, w=[pp(2)], signal=(h == 3))
                        yield
                        op('act', lambda e: e.activation(out=gm[:, :], in_=PS[PB[0]][0:64, c0:c0 + 256], func=AF.Exp), r=[pp(0)], w=[K('gm')])
                        yield
                        op('dve', lambda e: e.scalar_tensor_tensor(out=Bm[:, :], in0=PS[PB[1]][0:64, c0:c0 + 256], scalar=-1.0, in1=gm[:, :], op0=ALU.mult, op1=ALU.mult),
                           r=[pp(1), K('gm')], w=[K('Bm')])
                        op('dve', lambda e: e.tensor_tensor(out=QKm[:, :], in0=PS[PB[2]][0:64, c0:c0 + 256], in1=gm[:, :], op=ALU.mult), r=[pp(2), K('gm')], w=[('QKm', o4)])
                        yield
                        op('dve', lambda e: e.tensor_tensor(out=Bm[:, :], in0=Bm[:, :], in1=mstrict[:, :], op=ALU.mult), r=[K('Bm'), 'mstrict'], w=[K('Bm')])
                        yield
                        for h in range(4):
                            hc = slice(h * 64, (h + 1) * 64)
                            pc = slice(c0 + h * 64, c0 + (h + 1) * 64)
                            op('pe', lambda e, hc=hc, pc=pc: e.transpose(PS[PB[3]][0:64, pc], Bm[:, hc], ident[0:64, 0:64]), r=[K('Bm'), 'ident'], w=[pp(3)], signal=(h == 3))
                        op('dve', lambda e: e.tensor_tensor(out=Xm[0][:, :], in0=Bm[:, :], in1=i4[:, :], op=ALU.add), r=[K('Bm'), 'i4'], w=[K('X', 0)])
                        yield
                        op('act', lambda e: e.activation(out=BTm[:, :], in_=PS[PB[3]][0:64, c0:c0 + 256], func=AF.Copy), r=[pp(3)], w=[K('BTm')])
                        yield
                        Pp, Ppk, PTp, PTpk = Bm, K('Bm'), BTm, K('BTm')
                        xi = 0
                        for lev in range(1, 6):
                            pi = lev % 2
                            if lev < 5:
                                for h in range(4):
                                    hc = slice(h * 64, (h + 1) * 64)
                                    pc = slice(c0 + h * 64, c0 + (h + 1) * 64)
                                    op('pe', lambda e, hc=hc, pc=pc, Pp=Pp, PTp=PTp: e.matmul(PS[PB[4]][0:64, pc], lhsT=PTp[:, hc], rhs=Pp[:, hc], start=True, stop=True),
                                       r=[Ppk, PTpk], w=[pp(4)], signal=(h == 3))
                            for h in range(4):
                                hc = slice(h * 64, (h + 1) * 64)
                                pc = slice(c0 + h * 64, c0 + (h + 1) * 64)
                                op('pe', lambda e, hc=hc, pc=pc, Pp=Pp, PTp=PTp: e.matmul(PS[PB[3]][0:64, pc], lhsT=Pp[:, hc], rhs=PTp[:, hc], start=True, stop=True),
                                   r=[Ppk, PTpk], w=[pp(3)], signal=(h == 3))
                            yield
                            if lev < 5:
                                op('dve', lambda e, pi=pi: e.tensor_copy(out=Pm[pi][:, :], in_=PS[PB[4]][0:64, c0:c0 + 256]), r=[pp(4)], w=[K('P', pi)])
                            op('act', lambda e, pi=pi: e.activation(out=PTm[pi][:, :], in_=PS[PB[3]][0:64, c0:c0 + 256], func=AF.Copy), r=[pp(3)], w=[K('PT', pi)])
                            yield
                            for h in range(4):
                                hc = slice(h * 64, (h + 1) * 64)
                                pc = slice(c0 + h * 64, c0 + (h + 1) * 64)
                                op('pe', lambda e, hc=hc, pc=pc, pi=pi, xi=xi: e.matmul(PS[PB[5]][0:64, pc], lhsT=PTm[pi][:, hc], rhs=Xm[xi][:, hc], start=True, stop=True),
                                   r=[K('PT', pi), K('X', xi)], w=[pp(5)], signal=(h == 3))
                            yield
                            if lev < 5:
                                op('dve', lambda e, xi=xi: e.tensor_tensor(out=Xm[1 - xi][:, :], in0=PS[PB[5]][0:64, c0:c0 + 256], in1=Xm[xi][:, :], op=ALU.add),
                                   r=[pp(5), K('X', xi)], w=[K('X', 1 - xi)])
                            else:
                                op('dve', lambda e, xi=xi: e.tensor_tensor(out=XO[o4][:, :], in0=PS[PB[5]][0:64, c0:c0 + 256], in1=Xm[xi][:, :], op=ALU.add),
                                   r=[pp(5), K('X', xi)], w=[('XO', o4)])
                            yield
                            xi = 1 - xi
                            Pp, Ppk, PTp, PTpk = Pm[pi], K('P', pi), PTm[pi], K('PT', pi)

                    def seq(n):
                        cs = slice(n * 64, (n + 1) * 64)
                        o4 = n % 4
                        X = XO[o4]; Xk = ('XO', o4); QKm = QKmO[o4]
                        for h in range(4):
                            op('pe', lambda e, h=h: e.transpose(PS[6][0:64, h * 128:(h + 1) * 128], vbT[:, h, cs], ident[:]), r=['vbT', 'ident'], w=[pk(6)], signal=(h == 3))
                        for h in range(4):
                            op('pe', lambda e, h=h: e.transpose(PS[7][0:64, h * 128:(h + 1) * 128], keT[:, h, cs], ident[:]), r=['keT', 'ident'], w=[pk(7)], signal=(h == 3))
                        yield
                        op('act', lambda e: e.activation(out=vbm[:].rearrange("p h d -> p (h d)"), in_=PS[6][0:64, :], func=AF.Copy), r=[pk(6)], w=['vbm'])
                        op('act', lambda e: e.activation(out=kem[:].rearrange("p h d -> p (h d)"), in_=PS[7][0:64, :], func=AF.Copy), r=[pk(7)], w=['kem'])
                        yield
                        for h in range(4):
                            op('pe', lambda e, h=h: e.matmul(PS[6][0:64, h * 128:(h + 1) * 128], lhsT=kbgT[:, h, cs], rhs=Sb[:, h, :], start=True, stop=True),
                               r=['kbgT', 'Sb'], w=[pk(6)], signal=(h == 3))
                        yield
                        op('dve', lambda e: e.tensor_tensor(out=rhs2[:].rearrange("p h d -> p (h d)"), in0=vbm[:].rearrange("p h d -> p (h d)"), in1=PS[6][0:64, :], op=ALU.subtract),
                           r=[pk(6), 'vbm'], w=['rhs2'])
                        yield
                        for h in range(4):
                            hc = slice(h * 64, (h + 1) * 64)
                            op('pe', lambda e, h=h, hc=hc: e.matmul(PS[7][0:64, h * 128:(h + 1) * 128], lhsT=X[:, hc], rhs=rhs2[:, h, :], start=True, stop=True),
                               r=[Xk, 'rhs2'], w=[pk(7)], signal=(h == 3))
                        yield
                        op('act', lambda e: e.activation(out=vnew[:].rearrange("p h d -> p (h d)"), in_=PS[7][0:64, :], func=AF.Copy), r=[pk(7)], w=['vnew'])
                        yield
                        for h in range(4):
                            hc = slice(h * 64, (h + 1) * 64)
                            op('pe', lambda e, h=h, hc=hc: e.matmul(PS[6][:, hc], lhsT=Sb[:, h, :], rhs=qdT[:, h, cs], start=True, stop=False),
                               r=['Sb', 'qdT'], w=[pk(6)], signal=False)
                            op('pe', lambda e, h=h, hc=hc: e.matmul(PS[6][:, hc], lhsT=vnew[:, h, :], rhs=QKm[:, hc], start=False, stop=True),
                               r=['vnew', ('QKm', o4)], w=[pk(6)], signal=(h == 3))
                        for h in range(4):
                            op('pe', lambda e, h=h: e.matmul(PS[7][:, h * 128:(h + 1) * 128], lhsT=kem[:, h, :], rhs=vnew[:, h, :], start=True, stop=True),
                               r=['kem', 'vnew'], w=[pk(7)], signal=(h == 3))
                        yield
                        op('act', lambda e: e.activation(out=OT[:, :, cs], in_=PS[6][:, 0:256].rearrange("p (h c) -> p h c", c=64), func=AF.Copy), r=[pk(6)], w=['OT'])
                        for h in range(4):
                            op('dve', lambda e, h=h: e.scalar_tensor_tensor(out=S[:, h, :], in0=S[:, h, :], scalar=cdc[:, h, n:n + 1], in1=PS[7][:, h * 128:(h + 1) * 128],
                                                                          op0=ALU.mult, op1=ALU.add),
                               r=[pk(7), 'cdc', Sk[h]], w=[Sk[h]])
                        op('pool', lambda e: e.tensor_copy(out=Sb[:], in_=S[:]), r=Sk, w=['Sb'])
                        yield

                    def chain(gens):
                        for g_ in gens:
                            yield from g_

                    def interleave(gens):
                        gens = [g_ for g_ in gens if g_ is not None]
                        while gens:
                            for g_ in list(gens):
                                try:
                                    next(g_)
                                except StopIteration:
                                    gens.remove(g_)
                    npair = (nch + 1) // 2
                    interleave([prep(0), prep(1) if nch > 1 else None])
                    for kp in range(npair):
                        nxt = []
                        if kp + 1 < npair:
                            nxt = [prep(2 * kp + 2), prep(2 * kp + 3)]
                        seqs = [seq(2 * kp)] + ([seq(2 * kp + 1)] if 2 * kp + 1 < nch else [])
                        interleave(nxt + [chain(seqs)])
                    for h in range(4):
                        b = nb()
                        op('dve', lambda e, h=h: e.tensor_tensor(out=sqb[:, 0:TT], in0=OT[:, h, 0:TT], in1=OT[:, h, 0:TT], op=ALU.mult), r=['OT'], w=['sqb'])
                        op('pe', lambda e, b=b: e.matmul(PS[b][:, 0:TT], lhsT=ones_bf[:], rhs=sqb[:, 0:TT], start=True, stop=True), r=['sqb', 'ones'], w=[pk(b)])
                        op('act', lambda e, b=b: e.activation(out=lnt[:, 0:TT], in_=PS[b][:, 0:TT], func=AF.Ln, bias=EPS, scale=1.0 / 128), r=[pk(b)], w=['lnt'])
                        op('act', lambda e: e.activation(out=lnt[:, 0:TT], in_=lnt[:, 0:TT], func=AF.Exp, scale=-0.5), r=['lnt'], w=['lnt'])
                        op('dve', lambda e, h=h: e.tensor_tensor(out=ytmp[:, 0:TT], in0=OT[:, h, 0:TT], in1=lnt[:, 0:TT], op=ALU.mult), r=['OT', 'lnt'], w=['ytmp'])
                        op('act', lambda e, h=h: e.activation(out=lnt[:, 0:TT], in_=zT[:, h, 0:TT], func=AF.Silu), r=['zT', 'lnt'], w=['lnt'])
                        op('dve', lambda e, h=h: e.scalar_tensor_tensor(out=ogs[:, h, 0:TT], in0=ytmp[:, 0:TT], scalar=gng[:, 0:1], in1=lnt[:, 0:TT], op0=ALU.mult, op1=ALU.mult),
                           r=['ytmp', 'lnt', 'gng'], w=[('ogs', h)])
                        op('sp', lambda e, h=h: e.dma_start(out=OGTs[g][h][:, t0:t0 + TT], in_=ogs[:, h, 0:TT]), r=[('ogs', h)], dma='gog%d' % h)
                out_dma_keys.add(('dma', 'gSo'))
                op('sp', lambda e: e.dma_start(out=g_out[g][l].rearrange("h k v -> k h v"), in_=S[:]), r=['S0', 'S1', 'S2', 'S3'], dma='gSo')
            em.barrier()

    def m2_phase(l):
        with contextlib.ExitStack() as es:
            Wg = load_weight_bf16(es, "Wg", w_in[l], KD, 2048, c0=3592)
            Wsb = load_weight_bf16(es, "Wsb", w_bsb[l], 4, D)
            Wgd = load_weight_bf16(es, "Wgd", w_bgdn[l], 4, D)
            Wo = load_weight_bf16(es, "Wo", w_out[l], KD, D)
            g2 = load_gain_bc(es, "g2", norm_g[l, 2])
            g3 = load_gain_bc(es, "g3", norm_g[l, 3])
            scr = alloc_norm_scr(es)
            ss2 = es.enter_context(_sbuf_tensor("f_ss2", [128, 1], F32))
            lnv2 = es.enter_context(_sbuf_tensor("f_lnv2", [128, 1], F32))
            rstd2 = es.enter_context(_sbuf_tensor("f_rstd2", [128, 1], F32))
            scr2 = (scr[0], ss2, lnv2, rstd2)
            xts = [es.enter_context(_sbuf_tensor("m_x%d" % i, [128, 4, D], F32)) for i in range(2)]
            xnT = es.enter_context(_sbuf_tensor("m_xnT", [128, KD, 512], BF16))
            osb = [es.enter_context(_sbuf_tensor("m_osb%d" % i, [128, 4, 512], BF16)) for i in range(2)]
            ogd = [es.enter_context(_sbuf_tensor("m_ogd%d" % i, [128, 4, 512], BF16)) for i in range(2)]
            sg0 = [es.enter_context(_sbuf_tensor("m_sg0%d" % i, [128, 512], F32)) for i in range(2)]
            sg1 = [es.enter_context(_sbuf_tensor("m_sg1%d" % i, [128, 512], F32)) for i in range(2)]
            tmp = [es.enter_context(_sbuf_tensor("m_tmp%d" % i, [128, 512], F32)) for i in range(2)]
            mT = es.enter_context(_sbuf_tensor("m_mT", [128, KD, 512], BF16))
            ycat = [es.enter_context(_sbuf_tensor("m_y%d" % i, [128, D], F32)) for i in range(2)]
            tiles = []
            for g, T in groups:
                TT = min(512, T)
                for ti in range(T // TT):
                    tiles.append((g, ti * TT, TT))

            def load(i):
                g, t0, TT = tiles[i]
                pt = min(128, TT); nsub = TT // pt
                xt = xts[i % 2]
                op('sp', lambda e: e.dma_start(out=xt[0:pt, 0:nsub, :], in_=RES[g][t0:t0 + TT, :].rearrange("(s p) d -> p s d", p=pt)),
                   r=[('res', g, t0)], w=[('mx', i % 2)], dma='mx%d' % (i % 2))
                for k in range(4):
                    op('sp', lambda e, k=k: e.dma_start(out=osb[i % 2][:, k, 0:TT], in_=OSTs[g][k][:, t0:t0 + TT]), w=[('osb', i % 2)], dma='mo%d' % (i % 2))
                    op('sp', lambda e, k=k: e.dma_start(out=ogd[i % 2][:, k, 0:TT], in_=OGTs[g][k][:, t0:t0 + TT]), w=[('ogd', i % 2)], dma='mo%d' % (i % 2))
            load(0)
            cnt = 0
            for i, (g, t0, TT) in enumerate(tiles):
                if i + 1 < len(tiles):
                    load(i + 1)
                pt = min(128, TT); nsub = TT // pt
                xt = xts[i % 2]; xkey = ('mx', i % 2)
                norm_to_T(xt, xkey, pt, nsub, g2, 'g2', xnT, 'xnT', scr, 'm')
                xk = xnT_keys('xnT', nsub)
                for m in range(KD):
                    j = m % 2
                    for (gi, b, sgt, sgk) in ((0, 0 + j, sg0[j], ('sg0', j)), (1, 2 + j, sg1[j], ('sg1', j))):
                        for k in range(KD):
                            op('pe', lambda e, k=k, b=b, gi=gi, m=m: e.matmul(PS[b][:, 0:TT], lhsT=Wg[:, k, gi * D + m * 128:gi * D + (m + 1) * 128], rhs=xnT[:, k, 0:TT],
                                                                             start=(k == 0), stop=(k == KD - 1)),
                               r=xk + ['Wg'], w=[pk(b)], signal=(k == KD - 1))
                        op('act', lambda e, b=b, sgt=sgt: e.activation(out=sgt[:, 0:TT], in_=PS[b][:, 0:TT], func=AF.Sigmoid), r=[pk(b)], w=[sgk])
                    for (b, Wb, wk, src, srck) in ((4, Wsb, 'Wsb', osb[i % 2], ('osb', i % 2)), (5, Wgd, 'Wgd', ogd[i % 2], ('ogd', i % 2))):
                        for k in range(4):
                            op('pe', lambda e, k=k, b=b, Wb=Wb, src=src, m=m: e.matmul(PS[b][:, 0:TT], lhsT=Wb[:, k, m * 128:(m + 1) * 128], rhs=src[:, k, 0:TT],
                                                                                      start=(k == 0), stop=(k == 3)),
                               r=[wk, srck], w=[pk(b)], signal=(k == 3))
                    op('dve', lambda e, j=j: e.tensor_tensor(out=tmp[j][:, 0:TT], in0=PS[4][:, 0:TT], in1=sg0[j][:, 0:TT], op=ALU.mult),
                       r=[pk(4), ('sg0', j)], w=[('tmp', j)])
                    op('dve', lambda e, j=j: e.tensor_tensor(out=sg1[j][:, 0:TT], in0=PS[5][:, 0:TT], in1=sg1[j][:, 0:TT], op=ALU.mult),
                       r=[pk(5), ('sg1', j)], w=[('sg1', j)])
                    op('dve', lambda e, j=j, m=m: e.tensor_tensor(out=mT[:, m, 0:TT], in0=tmp[j][:, 0:TT], in1=sg1[j][:, 0:TT], op=ALU.add),
                       r=[('tmp', j), ('sg1', j)], w=[('mT', m)])
                mk = [('mT', m) for m in range(KD)]
                for s in range(nsub):
                    yc = ycat[cnt % 2]; ykey = ('ycat', cnt % 2); cnt += 1
                    for half in range(2):
                        b = 6 + half
                        for k in range(KD):
                            op('pe', lambda e, k=k, b=b, s=s, half=half: e.matmul(PS[b][0:pt, :], lhsT=mT[:, k, s * pt:(s + 1) * pt], rhs=Wo[:, k, half * 512:(half + 1) * 512],
                                                                                 start=(k == 0), stop=(k == KD - 1)),
                               r=mk + ['Wo'], w=[pk(b)], signal=(k == KD - 1))
                        op('act', lambda e, b=b, half=half, yc=yc: e.activation(out=yc[0:pt, half * 512:(half + 1) * 512], in_=PS[b][0:pt, :], func=AF.Copy),
                           r=[pk(b)], w=[ykey])
                    post_norm_residual(yc, ykey, pt, xt, xkey, s, g3, 'g3', scr2, 1.0)
                op('sp', lambda e: e.dma_start(out=RES[g][t0:t0 + TT, :].rearrange("(s p) d -> p s d", p=pt), in_=xt[0:pt, 0:nsub, :]),
                   r=[xkey], w=[('res', g, t0)], dma='mo_%d' % (i % 2))
            em.barrier()

    nph = [0]
    stop_after = cfg.get('stop_after', 10 ** 9)

    def runp(fn, *a):
        if nph[0] < stop_after:
            fn(*a)
        nph[0] += 1
    for l in range(DEPTH):
        src = x_in if l == 0 else RES
        runp(ffn_phase, l, 0, src, RES, False)
        runp(m1_phase, l)
        runp(att_phase, l)
        runp(gdn_phase, l)
        runp(m2_phase, l)
        runp(ffn_phase, l, 1, RES, RES, l == DEPTH - 1)
    em.barrier()
    return nc, em


def make_consts():
    c = {}
    c['c_ident'] = np.eye(128, dtype=np.float32)
    j = np.arange(128)
    c['c_linc'] = (j[:, None] >= j[None, :]).astype(np.float32)
    r = np.arange(128)[:, None, None]; jj = np.arange(4)[None, :, None]; cc = np.arange(512)[None, None, :]
    c['c_masks'] = ((cc - r - 128 * jj) > 0).astype(np.float32)
    j64 = np.arange(64)
    c['c_neg'] = np.where(j64[:, None] <= j64[None, :], 0.0, NEGBIG).astype(np.float32)
    ms = (j64[:, None] < j64[None, :]).astype(np.float32)
    c['c_mstrict'] = np.tile(ms, (1, 4))
    c['c_i4'] = np.tile(np.eye(64, dtype=np.float32), (1, 4))
    sel = np.zeros((4, 4, 128), np.float32)
    for h in range(4):
        sel[h, h, :] = 1.0
    c['c_sel'] = sel
    rs = np.ones((4, 512), np.float32); rs[:, ::64] = 0.0
    c['c_reset'] = rs
    return c


def run(cfg, inputs, n_cores=8):
    nc, em = build(cfg)
    T_P, T_S, PAST, DEPTH = cfg['T_P'], cfg['T_S'], cfg['PAST'], cfg['DEPTH']
    f = lambda a: np.ascontiguousarray(np.asarray(a, dtype=np.float32))
    consts = make_consts()
    NB = inputs['x_prompt'].shape[0]
    shared = {
        'norm_g': f(inputs['norm_gains']), 'w1u': f(inputs['w_ffn1_up']), 'w1d': f(inputs['w_ffn1_down']),
        'w2u': f(inputs['w_ffn2_up']), 'w2d': f(inputs['w_ffn2_down']), 'w_in': f(inputs['w_in']),
        'conv_w': f(inputs['conv_w']), 'a_log': f(inputs['gdn_a_log']), 'dt_bias': f(inputs['gdn_dt_bias']),
        'gdn_gain': f(inputs['gdn_norm_gain']), 'w_bsb': f(inputs['w_branch_sb']), 'w_bgdn': f(inputs['w_branch_gdn']),
        'w_out': f(inputs['w_out']),
    }
    shared.update(consts)
    in_maps = []
    for c in range(n_cores):
        m = dict(shared)
        m['x_p'] = f(inputs['x_prompt'][c % NB])
        m['x_s'] = f(inputs['x_sample'][c])
        m['cache_k'] = f(np.asarray(inputs['cache_sb_k'])[:, c].reshape(DEPTH, PAST, 512))
        m['cache_v'] = f(np.asarray(inputs['cache_sb_v'])[:, c].reshape(DEPTH, PAST, 512))
        m['state_gdn'] = f(np.asarray(inputs['state_gdn'])[:, c])
        m['state_conv'] = f(np.asarray(inputs['state_conv'])[:, c])
        in_maps.append(m)
    res = run_bass_kernel_spmd(nc, in_maps, core_ids=list(range(n_cores)))
    R = res.results
    NS = n_cores
    y_p = np.stack([R[b]['y_p'] for b in range(NB)])
    y_s = np.stack([R[c]['y_s'] for c in range(NS)])
    k_p = np.stack([R[b]['k_p'] for b in range(NB)], axis=1).reshape(DEPTH, NB, T_P, 8, 64)
    v_p = np.stack([R[b]['v_p'] for b in range(NB)], axis=1).reshape(DEPTH, NB, T_P, 8, 64)
    g_p = np.stack([R[b]['g_p'] for b in range(NB)], axis=1)
    c_p = np.stack([R[b]['c_p'] for b in range(NB)], axis=1)
    k_s = np.stack([R[c]['k_s'] for c in range(NS)], axis=1).reshape(DEPTH, NS, T_S, 8, 64)
    v_s = np.stack([R[c]['v_s'] for c in range(NS)], axis=1).reshape(DEPTH, NS, T_S, 8, 64)
    g_s = np.stack([R[c]['g_s'] for c in range(NS)], axis=1)
    c_s = np.stack([R[c]['c_s'] for c in range(NS)], axis=1)
    outs = (y_p, y_s, k_p, v_p, g_p, c_p, k_s, v_s, g_s, c_s)
    return tuple(np.ascontiguousarray(o, dtype=np.float32) for o in outs)


def kernel(**inputs):
    cfg = dict(T_P=8192, T_S=64, PAST=2048, DEPTH=2, FT=256)
    return run(cfg, inputs, n_cores=8)
```

```python
import contextlib
import numpy as np
import concourse.bass as bass
import concourse.mybir as mybir
from concourse.bass_utils import run_bass_kernel_spmd

F32 = mybir.dt.float32
BF16 = mybir.dt.bfloat16
AF = mybir.ActivationFunctionType
ALU = mybir.AluOpType

D = 1024
KD = 8
FF = 2816
KF = 22
IN_DIM = 5640
EPS = 1e-6
NEGBIG = -30000.0


class Emitter:
    def __init__(self, nc):
        self.nc = nc
        self.E = {'pe': nc.tensor, 'act': nc.scalar, 'dve': nc.vector, 'pool': nc.gpsimd, 'sp': nc.sync}
        self.sem = {}
        self.cnt = {}
        self.waited = {k: {} for k in self.E}
        self.lastw = {}
        self.readers = {}
        self.pend = {k: ([], []) for k in self.E}
        self.n_ins = 0

    def _sem(self, key):
        if key not in self.sem:
            nm = "s_" + "".join(ch for ch in str(key) if ch.isalnum() or ch == "_")
            self.sem[key] = self.nc.alloc_semaphore(nm)
            self.cnt[key] = 0
        return self.sem[key]

    def _wait(self, eng, need):
        for sk, v in need.items():
            if self.waited[eng].get(sk, 0) < v:
                self.E[eng].wait_ge(self.sem[sk], v)
                self.waited[eng][sk] = v

    def op(self, eng, fn, r=(), w=(), dma=None, signal=True):
        need = {}
        for b in r:
            t = self.lastw.get(b)
            if t is not None:
                need[t[0]] = max(need.get(t[0], 0), t[1])
        for b in w:
            t = self.lastw.get(b)
            if t is not None:
                need[t[0]] = max(need.get(t[0], 0), t[1])
            for sk, v in self.readers.get(b, {}).items():
                need[sk] = max(need.get(sk, 0), v)
        for oe, (pr_, pw_) in self.pend.items():
            if oe != eng and (pr_ or pw_):
                for b in w:
                    assert b not in pr_ and b not in pw_, ("pending unsignaled access", oe, b)
                for b in r:
                    assert b not in pw_, ("pending unsignaled write", oe, b)
        for sk in list(need):
            if isinstance(sk, tuple) and sk[0] == 'dma':
                need[sk] = self.cnt[sk]
        if eng == 'pe':
            need.pop('pe', None)
        self._wait(eng, need)
        ins = fn(self.E[eng])
        self.n_ins += 1
        if dma is not None:
            sk = ('dma', dma)
            self._sem(sk)
            self.cnt[sk] += 16
            ins.then_inc(self.sem[sk], 16)
            val = self.cnt[sk]
            for b in w:
                self.lastw[b] = (sk, val)
                self.readers[b] = {}
            for b in r:
                self.readers.setdefault(b, {})[sk] = val
            return ins
        pr, pw = self.pend[eng]
        pr.extend(r)
        pw.extend(w)
        if signal:
            sk = eng
            self._sem(sk)
            self.cnt[sk] += 1
            ins.then_inc(self.sem[sk], 1)
            val = self.cnt[sk]
            for b in pw:
                self.lastw[b] = (sk, val)
                self.readers[b] = {}
            for b in pr:
                if b not in pw:
                    self.readers.setdefault(b, {})[sk] = val
            self.pend[eng] = ([], [])
        return ins

    def barrier(self, engines=('pe', 'act', 'dve', 'pool', 'sp')):
        need = {sk: v for sk, v in self.cnt.items() if v > 0}
        for e in engines:
            n2 = dict(need)
            self._wait(e, n2)
        self.lastw = {}
        self.readers = {}


def build(cfg):
    T_P, T_S, PAST, DEPTH = cfg['T_P'], cfg['T_S'], cfg['PAST'], cfg['DEPTH']
    FT = cfg.get('FT', 256)
    nc = bass.Bass("TRN2", target_bir_lowering=False)
    _orig_sbuf_tensor = nc.sbuf_tensor
    _uid = [0]

    def _sbuf_tensor(name, shape, dt):
        _uid[0] += 1
        return _orig_sbuf_tensor("%s_u%d" % (name, _uid[0]), shape, dt)
    em = Emitter(nc)
    op = em.op

    def din(name, shape):
        return nc.dram_tensor(name, list(shape), F32, kind="ExternalInput").ap()

    def dout(name, shape):
        return nc.dram_tensor(name, list(shape), F32, kind="ExternalOutput").ap()

    def dscr(name, shape, dt=F32):
        return nc.dram_tensor(name, list(shape), dt, kind="Internal").ap()

    groups = [('p', T_P), ('s', T_S)]
    TG = dict(groups)
    x_in = {'p': din("x_p", [T_P, D]), 's': din("x_s", [T_S, D])}
    cache_k = din("cache_k", [DEPTH, PAST, 512])
    cache_v = din("cache_v", [DEPTH, PAST, 512])
    state_gdn = din("state_gdn", [DEPTH, 4, 128, 128])
    state_conv = din("state_conv", [DEPTH, 3, 1536])
    norm_g = din("norm_g", [DEPTH, 6, D])
    w1u = din("w1u", [DEPTH, D, 2 * FF]); w1d = din("w1d", [DEPTH, FF, D])
    w2u = din("w2u", [DEPTH, D, 2 * FF]); w2d = din("w2d", [DEPTH, FF, D])
    w_in = din("w_in", [DEPTH, D, IN_DIM])
    conv_w = din("conv_w", [DEPTH, 4, 1536])
    a_log = din("a_log", [DEPTH, 4]); dt_bias = din("dt_bias", [DEPTH, 4])
    gdn_gain = din("gdn_gain", [DEPTH, 128])
    w_bsb = din("w_bsb", [DEPTH, 512, D]); w_bgdn = din("w_bgdn", [DEPTH, 512, D])
    w_out = din("w_out", [DEPTH, D, D])
    c_ident = din("c_ident", [128, 128]); c_linc = din("c_linc", [128, 128])
    c_masks = din("c_masks", [128, 4, 512])
    c_neg = din("c_neg", [64, 64]); c_mstrict = din("c_mstrict", [64, 256]); c_i4 = din("c_i4", [64, 256])
    c_sel = din("c_sel", [4, 4, 128]); c_reset = din("c_reset", [4, 512])

    y_out = {'p': dout("y_p", [T_P, D]), 's': dout("y_s", [T_S, D])}
    k_out = {'p': dout("k_p", [DEPTH, T_P, 512]), 's': dout("k_s", [DEPTH, T_S, 512])}
    v_out = {'p': dout("v_p", [DEPTH, T_P, 512]), 's': dout("v_s", [DEPTH, T_S, 512])}
    g_out = {'p': dout("g_p", [DEPTH, 4, 128, 128]), 's': dout("g_s", [DEPTH, 4, 128, 128])}
    c_out = {'p': dout("c_p", [DEPTH, 3, 1536]), 's': dout("c_s", [DEPTH, 3, 1536])}

    RES = {g: dscr("res_" + g, [T, D]) for g, T in groups}
    QTs = {g: dscr("qt_" + g, [4, 128, T], BF16) for g, T in groups}
    KTs = {g: dscr("kt_" + g, [4, 128, T], BF16) for g, T in groups}
    RAWs = {g: dscr("raw_" + g, [12, 128, T]) for g, T in groups}
    ZTs = {g: dscr("zt_" + g, [4, 128, T]) for g, T in groups}
    BAs = {g: dscr("ba_" + g, [2, 4, T]) for g, T in groups}
    OSTs = {g: dscr("ost_" + g, [4, 128, T], BF16) for g, T in groups}
    OGTs = {g: dscr("ogt_" + g, [4, 128, T], BF16) for g, T in groups}

    PS = [nc.alloc_psum_tensor("psb%d" % i, [128, 512], F32) for i in range(8)]

    def pk(i):
        return ('ps', i)

    out_dma_keys = set()

    ident = nc.alloc_sbuf_tensor("ident", [128, 128], F32)
    ones_bf = nc.alloc_sbuf_tensor("ones_bf", [128, 128], BF16)
    linc_bf = nc.alloc_sbuf_tensor("linc_bf", [128, 128], BF16)
    op('sp', lambda e: e.dma_start(out=ident[:], in_=c_ident), w=['ident'], dma='c0')
    op('pool', lambda e: e.dma_start(out=linc_bf[:], in_=c_linc), w=['linc'], dma='c1')
    op('dve', lambda e: e.memset(ones_bf[:], 1.0), w=['ones'])

    def load_weight_bf16(es, name, src, kchunks, ncols, c0=0, key=None):
        t = es.enter_context(_sbuf_tensor(name, [128, kchunks, ncols], BF16))
        key = key or name
        for k in range(kchunks):
            op('pool', lambda e, k=k: e.dma_start(out=t[:, k, :], in_=src[k * 128:(k + 1) * 128, c0:c0 + ncols]),
               w=[key], dma='w_' + key)
        return t

    def load_gain_bc(es, name, src_row):
        t = es.enter_context(_sbuf_tensor(name, [128, D], F32))
        op('sp', lambda e: e.dma_start(out=t[:], in_=src_row.partition_broadcast(128)), w=[name], dma='g_' + name)
        return t

    def norm_to_T(xt, xkey, pt, nsub, gain, gkey, xnT, xnTkey, scr, tagbase):
        junk, ss, lnv, rstd, xn, nxs = scr
        for s in range(nsub):
            op('dve', lambda e, s=s: e.scalar_tensor_tensor(out=junk[0:pt, :], in0=xt[0:pt, s, :], scalar=1.0, in1=xt[0:pt, s, :],
                                                            op0=ALU.mult, op1=ALU.mult, accum_out=ss[0:pt, s:s + 1]),
               r=[xkey], w=['junk', ('ss', s)])
        op('act', lambda e: e.activation(out=lnv[0:pt, 0:nsub], in_=ss[0:pt, 0:nsub], func=AF.Ln, bias=EPS, scale=1.0 / D),
           r=[('ss', s) for s in range(nsub)], w=['lnv'])
        op('act', lambda e: e.activation(out=rstd[0:pt, 0:nsub], in_=lnv[0:pt, 0:nsub], func=AF.Exp, scale=-0.5),
           r=['lnv'], w=['rstd'])
        for s in range(nsub):
            xs = s % nxs
            op('dve', lambda e, s=s, xs=xs: e.scalar_tensor_tensor(out=xn[0:pt, xs, :], in0=xt[0:pt, s, :], scalar=rstd[0:pt, s:s + 1],
                                                                 in1=gain[0:pt, :], op0=ALU.mult, op1=ALU.mult),
               r=[xkey, 'rstd', gkey], w=[('xn', xs)])
            for half in range(2):
                b = 6 + half
                for i in range(4):
                    k = half * 4 + i
                    op('pe', lambda e, b=b, i=i, k=k, xs=xs: e.transpose(PS[b][:, i * 128:i * 128 + pt], xn[0:pt, xs, k * 128:(k + 1) * 128],
                                                                        ident[0:pt, 0:pt]),
                       r=[('xn', xs), 'ident'], w=[pk(b)], signal=(i == 3))
                eng = 'act' if half == 0 else 'dve'
                src = PS[b][:].rearrange("p (i t) -> p i t", i=4)[:, :, 0:pt]
                dst = xnT[:, half * 4:half * 4 + 4, s * pt:(s + 1) * pt]
                if eng == 'act':
                    op('act', lambda e, src=src, dst=dst: e.activation(out=dst, in_=src, func=AF.Copy), r=[pk(b)], w=[(xnTkey, s, half)])
                else:
                    op('dve', lambda e, src=src, dst=dst: e.tensor_copy(out=dst, in_=src), r=[pk(b)], w=[(xnTkey, s, half)])

    def xnT_keys(xnTkey, nsub):
        return [(xnTkey, s, h) for s in range(nsub) for h in range(2)]

    def alloc_norm_scr(es, slim=False):
        junk = es.enter_context(_sbuf_tensor("n_junk", [128, D], BF16 if slim else F32))
        ss = es.enter_context(_sbuf_tensor("n_ss", [128, 8], F32))
        lnv = es.enter_context(_sbuf_tensor("n_lnv", [128, 8], F32))
        rstd = es.enter_context(_sbuf_tensor("n_rstd", [128, 8], F32))
        xn = es.enter_context(_sbuf_tensor("n_xn", [128, 1 if slim else 2, D], F32))
        return (junk, ss, lnv, rstd, xn, 1 if slim else 2)

    def post_norm_residual(ycat, ykey, pt, xt, xkey, s, gain, gkey, scr2, coef):
        junk, ss2, lnv2, rstd2 = scr2
        op('dve', lambda e: e.scalar_tensor_tensor(out=junk[0:pt, :], in0=ycat[0:pt, :], scalar=1.0, in1=ycat[0:pt, :],
                                                   op0=ALU.mult, op1=ALU.mult, accum_out=ss2[0:pt, 0:1]),
           r=[ykey], w=['junk', 'ss2'])
        op('act', lambda e: e.activation(out=lnv2[0:pt, 0:1], in_=ss2[0:pt, 0:1], func=AF.Ln, bias=EPS, scale=1.0 / D),
           r=['ss2'], w=['lnv2'])
        op('act', lambda e: e.activation(out=rstd2[0:pt, 0:1], in_=lnv2[0:pt, 0:1], func=AF.Exp, scale=-0.5), r=['lnv2'], w=['rstd2'])
        op('dve', lambda e: e.scalar_tensor_tensor(out=ycat[0:pt, :], in0=ycat[0:pt, :], scalar=rstd2[0:pt, 0:1], in1=gain[0:pt, :],
                                                   op0=ALU.mult, op1=ALU.mult),
           r=[ykey, 'rstd2', gkey], w=[ykey])
        op('dve', lambda e: e.scalar_tensor_tensor(out=xt[0:pt, s, :], in0=ycat[0:pt, :], scalar=float(coef), in1=xt[0:pt, s, :],
                                                   op0=ALU.mult, op1=ALU.add),
           r=[ykey, xkey], w=[xkey])

    def ffn_phase(l, which, src, dst, final):
        wu = (w1u, w2u)[which][l]
        wd = (w1d, w2d)[which][l]
        gi0, gi1 = (0, 1) if which == 0 else (4, 5)
        with contextlib.ExitStack() as es:
            Wu = load_weight_bf16(es, "Wu", wu, KD, 2 * FF)
            Wd = load_weight_bf16(es, "Wd", wd, KF, D)
            g0 = load_gain_bc(es, "g0", norm_g[l, gi0])
            g1 = load_gain_bc(es, "g1", norm_g[l, gi1])
            scr = alloc_norm_scr(es, slim=True)
            ss2 = es.enter_context(_sbuf_tensor("f_ss2", [128, 1], F32))
            lnv2 = es.enter_context(_sbuf_tensor("f_lnv2", [128, 1], F32))
            rstd2 = es.enter_context(_sbuf_tensor("f_rstd2", [128, 1], F32))
            scr2 = (scr[0], ss2, lnv2, rstd2)
            nsubF = max(1, FT // 128)
            xts = [es.enter_context(_sbuf_tensor("f_x%d" % i, [128, nsubF, D], F32)) for i in range(2)]
            xnT = es.enter_context(_sbuf_tensor("f_xnT", [128, KD, FT], BF16))
            hT = es.enter_context(_sbuf_tensor("f_hT", [128, KF, FT], BF16))
            sg = [es.enter_context(_sbuf_tensor("f_sg%d" % i, [128, FT], F32)) for i in range(2)]
            ycat = [es.enter_context(_sbuf_tensor("f_y%d" % i, [128, D], F32)) for i in range(2)]
            tiles = []
            for g, T in groups:
                t0_ = 0
                while t0_ < T:
                    tt_ = min(FT, T - t0_)
                    tiles.append((g, t0_, tt_))
                    t0_ += tt_

            def load(i):
                g, t0, TT = tiles[i]
                pt = min(128, TT); nsub = TT // pt
                xt = xts[i % 2]
                op('sp', lambda e: e.dma_start(out=xt[0:pt, 0:nsub, :], in_=src[g][t0:t0 + TT, :].rearrange("(s p) d -> p s d", p=pt)),
                   r=[('res', g, t0)], w=[('fx', i % 2)], dma='fx%d' % (i % 2))
            load(0)
            cnt = 0
            for i, (g, t0, TT) in enumerate(tiles):
                if i + 1 < len(tiles):
                    load(i + 1)
                pt = min(128, TT); nsub = TT // pt
                xt = xts[i % 2]; xkey = ('fx', i % 2)
                norm_to_T(xt, xkey, pt, nsub, g0, 'g0', xnT, 'xnT', scr, 'f')
                xk = xnT_keys('xnT', nsub)
                for m in range(KF):
                    bg = m % 2; bu = 2 + m % 2
                    for k in range(KD):
                        op('pe', lambda e, k=k, m=m, bg=bg: e.matmul(PS[bg][:, 0:TT], lhsT=Wu[:, k, m * 128:(m + 1) * 128], rhs=xnT[:, k, 0:TT],
                                                                      start=(k == 0), stop=(k == KD - 1)),
                           r=xk + ['Wu'], w=[pk(bg)], signal=(k == KD - 1))
                    for k in range(KD):
                        op('pe', lambda e, k=k, m=m, bu=bu: e.matmul(PS[bu][:, 0:TT], lhsT=Wu[:, k, FF + m * 128:FF + (m + 1) * 128], rhs=xnT[:, k, 0:TT],
                                                                      start=(k == 0), stop=(k == KD - 1)),
                           r=xk + ['Wu'], w=[pk(bu)], signal=(k == KD - 1))
                    sgt = sg[m % 2]
                    op('act', lambda e, bg=bg, sgt=sgt: e.activation(out=sgt[:, 0:TT], in_=PS[bg][:, 0:TT], func=AF.Silu),
                       r=[pk(bg)], w=[('sg', m % 2)])
                    op('dve', lambda e, bu=bu, sgt=sgt, m=m: e.tensor_tensor(out=hT[:, m, 0:TT], in0=PS[bu][:, 0:TT], in1=sgt[:, 0:TT], op=ALU.mult),
                       r=[pk(bu), ('sg', m % 2)], w=[('hT', m)])
                hk = [('hT', m) for m in range(KF)]
                for s in range(nsub):
                    yc = ycat[cnt % 2]; ykey = ('ycat', cnt % 2); cnt += 1
                    for half in range(2):
                        b = 4 + half
                        for m in range(KF):
                            op('pe', lambda e, m=m, b=b, s=s, half=half: e.matmul(PS[b][0:pt, :], lhsT=hT[:, m, s * pt:(s + 1) * pt],
                                                                                 rhs=Wd[:, m, half * 512:(half + 1) * 512],
                                                                                 start=(m == 0), stop=(m == KF - 1)),
                               r=hk + ['Wd'], w=[pk(b)], signal=(m == KF - 1))
                        op('act', lambda e, b=b, half=half, yc=yc: e.activation(out=yc[0:pt, half * 512:(half + 1) * 512], in_=PS[b][0:pt, :], func=AF.Copy),
                           r=[pk(b)], w=[ykey])
                    post_norm_residual(yc, ykey, pt, xt, xkey, s, g1, 'g1', scr2, 0.5)
                dd = y_out if final else dst
                dkey = 'fo%d' % (i % 2)
                if final:
                    out_dma_keys.add(('dma', dkey))
                op('sp', lambda e, dd=dd: e.dma_start(out=dd[g][t0:t0 + TT, :].rearrange("(s p) d -> p s d", p=pt), in_=xt[0:pt, 0:nsub, :]),
                   r=[xkey], w=[('res', g, t0)], dma=dkey)
            em.barrier()

    def m1_phase(l):
        NCOL = 3592
        with contextlib.ExitStack() as es:
            Wi = load_weight_bf16(es, "Wi", w_in[l], KD, NCOL)
            g2 = load_gain_bc(es, "g2", norm_g[l, 2])
            scr = alloc_norm_scr(es)
            xts = [es.enter_context(_sbuf_tensor("m_x%d" % i, [128, 4, D], F32)) for i in range(2)]
            xnT = es.enter_context(_sbuf_tensor("m_xnT", [128, KD, 512], BF16))
            stg_f = [es.enter_context(_sbuf_tensor("m_sf%d" % i, [128, 512], F32)) for i in range(4)]
            stg_b = [es.enter_context(_sbuf_tensor("m_sb%d" % i, [128, 512], BF16)) for i in range(4)]
            cst = es.enter_context(_sbuf_tensor("m_cst", [4, 1536], F32))
            tiles = []
            for g, T in groups:
                TT = min(512, T)
                for ti in range(T // TT):
                    tiles.append((g, ti * TT, TT, ti == T // TT - 1))

            def load(i):
                g, t0, TT, _ = tiles[i]
                pt = min(128, TT); nsub = TT // pt
                xt = xts[i % 2]
                op('sp', lambda e: e.dma_start(out=xt[0:pt, 0:nsub, :], in_=RES[g][t0:t0 + TT, :].rearrange("(s p) d -> p s d", p=pt)),
                   w=[('mx', i % 2)], dma='mx%d' % (i % 2))
            load(0)
            cf = 0; cb = 0; pb = 0
            for i, (g, t0, TT, last) in enumerate(tiles):
                if i + 1 < len(tiles):
                    load(i + 1)
                pt = min(128, TT); nsub = TT // pt
                xt = xts[i % 2]; xkey = ('mx', i % 2)
                norm_to_T(xt, xkey, pt, nsub, g2, 'g2', xnT, 'xnT', scr, 'm')
                xk = xnT_keys('xnT', nsub)
                for s in range(nsub):
                    for (c0, outd) in ((512, k_out[g]), (1024, v_out[g])):
                        b = pb % 6; pb += 1
                        for k in range(KD):
                            op('pe', lambda e, k=k, b=b, c0=c0, s=s: e.matmul(PS[b][0:pt, :], lhsT=xnT[:, k, s * pt:(s + 1) * pt], rhs=Wi[:, k, c0:c0 + 512],
                                                                             start=(k == 0), stop=(k == KD - 1)),
                               r=xk + ['Wi'], w=[pk(b)], signal=(k == KD - 1))
                        sf = stg_f[cf % 4]; sk = ('sf', cf % 4); dk = 'sf%d' % (cf % 4); cf += 1
                        op('act', lambda e, b=b, sf=sf: e.activation(out=sf[0:pt, :], in_=PS[b][0:pt, :], func=AF.Copy), r=[pk(b)], w=[sk])
                        out_dma_keys.add(('dma', dk))
                        op('sp', lambda e, sf=sf, outd=outd, s=s: e.dma_start(out=outd[l, t0 + s * pt:t0 + (s + 1) * pt, :], in_=sf[0:pt, :]),
                           r=[sk], dma=dk)
                fm = []
                for p in range(4):
                    fm.append((p * 128, 128, 'b', 0.125, QTs[g][p]))
                for p in range(4):
                    fm.append((512 + p * 128, 128, 'b', 1.0, KTs[g][p]))
                for c in range(12):
                    fm.append((1536 + c * 128, 128, 'f', 1.0, RAWs[g][c]))
                for h in range(4):
                    fm.append((3072 + h * 128, 128, 'f', 1.0, ZTs[g][h]))
                fm.append((3584, 4, 'f', 1.0, BAs[g][0]))
                fm.append((3588, 4, 'f', 1.0, BAs[g][1]))
                for (c0, mrows, kind, scale, dst) in fm:
                    b = pb % 6; pb += 1
                    for k in range(KD):
                        op('pe', lambda e, k=k, b=b, c0=c0, mrows=mrows: e.matmul(PS[b][0:mrows, 0:TT], lhsT=Wi[:, k, c0:c0 + mrows], rhs=xnT[:, k, 0:TT],
                                                                                 start=(k == 0), stop=(k == KD - 1)),
                           r=xk + ['Wi'], w=[pk(b)], signal=(k == KD - 1))
                    if kind == 'b':
                        st = stg_b[cb % 4]; sk = ('sb', cb % 4); dk = 'sb%d' % (cb % 4); cb += 1
                        op('dve', lambda e, b=b, st=st, scale=scale, mrows=mrows: e.tensor_scalar(out=st[0:mrows, 0:TT], in0=PS[b][0:mrows, 0:TT],
                                                                                                scalar1=float(scale), scalar2=None, op0=ALU.mult),
                           r=[pk(b)], w=[sk])
                    else:
                        st = stg_f[cf % 4]; sk = ('sf', cf % 4); dk = 'sf%d' % (cf % 4); cf += 1
                        op('act', lambda e, b=b, st=st, mrows=mrows: e.activation(out=st[0:mrows, 0:TT], in_=PS[b][0:mrows, 0:TT], func=AF.Copy),
                           r=[pk(b)], w=[sk])
                    op('sp', lambda e, st=st, dst=dst, mrows=mrows: e.dma_start(out=dst[0:mrows, t0:t0 + TT], in_=st[0:mrows, 0:TT]), r=[sk], dma=dk)
                if last:
                    for j in range(3):
                        b = pb % 6; pb += 1
                        for k in range(KD):
                            op('pe', lambda e, k=k, b=b, j=j: e.matmul(PS[b][0:3, :], lhsT=xnT[:, k, TT - 3:TT], rhs=Wi[:, k, 1536 + j * 512:1536 + (j + 1) * 512],
                                                                      start=(k == 0), stop=(k == KD - 1)),
                               r=xk + ['Wi'], w=[pk(b)], signal=(k == KD - 1))
                        op('act', lambda e, b=b, j=j: e.activation(out=cst[0:3, j * 512:(j + 1) * 512], in_=PS[b][0:3, :], func=AF.Copy), r=[pk(b)], w=['cst'])
                    out_dma_keys.add(('dma', 'cst'))
                    op('sp', lambda e: e.dma_start(out=c_out[g][l], in_=cst[0:3, :]), r=['cst'], dma='cst')
            em.barrier()

    def att_phase(l):
        with contextlib.ExitStack() as es:
            TKmax = max(T_P, PAST + 128)
            nblk_max = max(T_P // 128, PAST // 128 + 1)
            KT = es.enter_context(_sbuf_tensor("a_KT", [128, TKmax], BF16))
            NKT = es.enter_context(_sbuf_tensor("a_NKT", [128, TKmax], BF16))
            QT = es.enter_context(_sbuf_tensor("a_QT", [128, max(T_P, 128)], BF16))
            Vp = es.enter_context(_sbuf_tensor("a_Vp", [128, nblk_max, 2, 128], BF16))
            masks = es.enter_context(_sbuf_tensor("a_masks", [128, 4, 512], BF16))
            kc = es.enter_context(_sbuf_tensor("a_kc", [128, max(PAST // 128, 1), 128], F32))
            et = [es.enter_context(_sbuf_tensor("a_e%d" % i, [128, 512], F32)) for i in range(2)]
            spt = [es.enter_context(_sbuf_tensor("a_sp%d" % i, [128, 512], BF16)) for i in range(4)]
            tt = [es.enter_context(_sbuf_tensor("a_t%d" % i, [128, 512], F32)) for i in range(4)]
            at = [es.enter_context(_sbuf_tensor("a_a%d" % i, [128, 512], BF16)) for i in range(4)]
            carry = [es.enter_context(_sbuf_tensor("a_c%d" % i, [128, 512], F32)) for i in range(2)]
            ost = [es.enter_context(_sbuf_tensor("a_o%d" % i, [128, 512], BF16)) for i in range(2)]
            op('pool', lambda e: e.dma_start(out=masks[:], in_=c_masks), w=['masks'], dma='amask')
            op('dve', lambda e: e.memset(Vp[:], 0.0), w=['Vp'])
            op('dve', lambda e: e.memset(KT[:], 0.0), w=['KT'])
            op('dve', lambda e: e.memset(QT[:], 0.0), w=['QT'])
            ocnt = 0
            bcnt = 0
            for g, T in groups:
                if g not in cfg.get('att_groups', 'ps'):
                    continue
                QW = min(512, max(T, 128))
                for p in range(4):
                    if g == 'p':
                        Tk = T
                        nseg = max(1, T // 2048)
                        seg = T // nseg
                        for q4 in range(nseg):
                            op('sp', lambda e, q4=q4: e.dma_start(out=KT[:, q4 * seg:(q4 + 1) * seg], in_=KTs[g][p][:, q4 * seg:(q4 + 1) * seg]),
                               w=['KT'], dma='aKT')
                        blocks_all = [(b * 128, 128, b) for b in range(T // 128)]
                        vsrc = v_out[g][l]
                        nb = T // 128
                        for j in range(2):
                            for b0 in range(0, nb, 8):
                                b1 = min(nb, b0 + 8)
                                op('pool', lambda e, j=j, b0=b0, b1=b1: e.dma_start(
                                    out=Vp[:, b0:b1, j, j * 64:(j + 1) * 64],
                                    in_=vsrc[b0 * 128:b1 * 128, (2 * p + j) * 64:(2 * p + j + 1) * 64].rearrange("(b k) d -> k b d", k=128)),
                                    w=['Vp'], dma='aV')
                    else:
                        Tk = PAST + 128
                        npb = PAST // 128
                        op('sp', lambda e: e.dma_start(out=kc[:, 0:npb, :], in_=cache_k[l][:, p * 128:(p + 1) * 128].rearrange("(b k) d -> k b d", k=128)),
                           w=['kc'], dma='akc')
                        for b in range(npb):
                            pb_ = 6 + b % 2
                            op('pe', lambda e, b=b, pb_=pb_: e.transpose(PS[pb_][:, 0:128], kc[:, b, :], ident[:]), r=['kc', 'ident'], w=[pk(pb_)])
                            op('act', lambda e, b=b, pb_=pb_: e.activation(out=KT[:, b * 128:(b + 1) * 128], in_=PS[pb_][:, 0:128], func=AF.Copy),
                               r=[pk(pb_)], w=['KT'])
                        op('sp', lambda e: e.dma_start(out=KT[:, PAST:PAST + T], in_=KTs[g][p][:, 0:T]), w=['KT'], dma='aKT')
                        for j in range(2):
                            op('pool', lambda e, j=j: e.dma_start(
                                out=Vp[:, 0:npb, j, j * 64:(j + 1) * 64],
                                in_=cache_v[l][:, (2 * p + j) * 64:(2 * p + j + 1) * 64].rearrange("(b k) d -> k b d", k=128)),
                                w=['Vp'], dma='aV')
                            op('pool', lambda e, j=j: e.dma_start(
                                out=Vp[0:T, npb, j, j * 64:(j + 1) * 64],
                                in_=v_out[g][l][0:T, (2 * p + j) * 64:(2 * p + j + 1) * 64]),
                                w=['Vp'], dma='aV')
                    op('sp', lambda e: e.dma_start(out=QT[:, 0:T], in_=QTs[g][p][:, 0:T]), w=['QT'], dma='aQT')
                    op('dve', lambda e: e.tensor_scalar(out=NKT[:, 0:Tk], in0=KT[:, 0:Tk], scalar1=-1.0, scalar2=None, op0=ALU.mult),
                       r=['KT'], w=['NKT'])
                    for qt in range(max(1, T // QW)):
                        if cfg.get('att_skipq', False):
                            continue
                        q0 = qt * QW
                        if g == 'p':
                            nvis = (q0 + QW) // 128
                            blks = []
                            for kb in range(nvis - 1, -1, -1):
                                j = kb - q0 // 128
                                blks.append((kb * 128, 128, kb, j if j >= 0 else None))
                        else:
                            blks = [(PAST, 128, PAST // 128, 0)] + [(b * 128, 128, b, None) for b in range(PAST // 128 - 1, -1, -1)]
                        nb_ = len(blks)
                        for hh in range(2):
                            op('dve', lambda e, hh=hh: e.memset(carry[hh][:, 0:QW], 0.0), w=[('carry', hh)])
                        O = PS[6]

                        def pe_A(bi):
                            (k0, nk, vb, mj) = blks[bi]
                            for hh in range(2):
                                hs = slice(hh * 64, (hh + 1) * 64)
                                op('pe', lambda e, hh=hh, hs=hs: e.matmul(PS[hh][0:nk, 0:QW], lhsT=KT[hs, k0:k0 + nk], rhs=QT[hs, q0:q0 + QW], start=True, stop=True),
                                   r=['KT', 'QT'], w=[pk(hh)])

                        def pe_BC(bi):
                            (k0, nk, vb, mj) = blks[bi]
                            lastb = bi == nb_ - 1
                            for hh in range(2):
                                hs = slice(hh * 64, (hh + 1) * 64)
                                sk = ('sp', bi % 2, hh); spx = spt[(bi % 2) * 2 + hh]
                                op('pe', lambda e, hh=hh, spx=spx: e.matmul(PS[2 + hh][0:nk, 0:QW], lhsT=linc_bf[0:nk, 0:nk], rhs=spx[0:nk, 0:QW], start=True, stop=False),
                                   r=[sk, 'linc'], w=[pk(2 + hh)], signal=False)
                                op('pe', lambda e, hh=hh, hs=hs: e.matmul(PS[2 + hh][0:nk, 0:QW], lhsT=NKT[hs, k0:k0 + nk], rhs=QT[hs, q0:q0 + QW], start=False, stop=True),
                                   r=['NKT', 'QT'], w=[pk(2 + hh)])
                                if not lastb:
                                    op('pe', lambda e, hh=hh, spx=spx: e.matmul(PS[4 + hh][:, 0:QW], lhsT=ones_bf[0:nk, :], rhs=spx[0:nk, 0:QW], start=True, stop=True),
                                       r=[sk, 'ones'], w=[pk(4 + hh)])

                        def dve_t(bi):
                            (k0, nk, vb, mj) = blks[bi]
                            for hh in range(2):
                                tx = tt[(bi % 2) * 2 + hh]
                                op('dve', lambda e, hh=hh, tx=tx: e.tensor_tensor(out=tx[0:nk, 0:QW], in0=PS[2 + hh][0:nk, 0:QW], in1=carry[hh][0:nk, 0:QW], op=ALU.add),
                                   r=[pk(2 + hh), ('carry', hh)], w=[('t', bi % 2, hh)])

                        def dve_cu(bi):
                            if bi == nb_ - 1:
                                return
                            for hh in range(2):
                                op('dve', lambda e, hh=hh: e.tensor_tensor(out=carry[hh][:, 0:QW], in0=PS[4 + hh][:, 0:QW], in1=carry[hh][:, 0:QW], op=ALU.add),
                                   r=[pk(4 + hh), ('carry', hh)], w=[('carry', hh)])

                        def act_esp(bi):
                            (k0, nk, vb, mj) = blks[bi]
                            for hh in range(2):
                                op('act', lambda e, hh=hh: e.activation(out=et[hh][0:nk, 0:QW], in_=PS[hh][0:nk, 0:QW], func=AF.Exp), r=[pk(hh)], w=[('e', hh)])
                            for hh in range(2):
                                sk = ('sp', bi % 2, hh); spx = spt[(bi % 2) * 2 + hh]
                                op('act', lambda e, hh=hh, spx=spx: e.activation(out=spx[0:nk, 0:QW], in_=et[hh][0:nk, 0:QW], func=AF.Ln, bias=1.0, scale=1.0),
                                   r=[('e', hh)], w=[sk])
                            if mj is not None:
                                for hh in range(2):
                                    sk = ('sp', bi % 2, hh); spx = spt[(bi % 2) * 2 + hh]
                                    op('pool', lambda e, spx=spx: e.tensor_tensor(out=spx[0:nk, 0:QW], in0=spx[0:nk, 0:QW], in1=masks[0:nk, mj, 0:QW], op=ALU.mult),
                                       r=[sk, 'masks'], w=[sk])

                        def act_a(bi):
                            (k0, nk, vb, mj) = blks[bi]
                            for hh in range(2):
                                tx = tt[(bi % 2) * 2 + hh]; ax = at[(bi % 2) * 2 + hh]
                                op('act', lambda e, tx=tx, ax=ax: e.activation(out=ax[0:nk, 0:QW], in_=tx[0:nk, 0:QW], func=AF.Exp, scale=-1.0),
                                   r=[('t', bi % 2, hh)], w=[('a', bi % 2, hh)])
                            if mj is not None:
                                for hh in range(2):
                                    ax = at[(bi % 2) * 2 + hh]
                                    op('pool', lambda e, ax=ax: e.tensor_tensor(out=ax[0:nk, 0:QW], in0=ax[0:nk, 0:QW], in1=masks[0:nk, mj, 0:QW], op=ALU.mult),
                                       r=[('a', bi % 2, hh), 'masks'], w=[('a', bi % 2, hh)])

                        def pe_O(bi):
                            (k0, nk, vb, mj) = blks[bi]
                            for hh in range(2):
                                ax = at[(bi % 2) * 2 + hh]
                                first = (bi == 0 and hh == 0); last_ = (bi == nb_ - 1 and hh == 1)
                                op('pe', lambda e, hh=hh, ax=ax: e.matmul(O[:, 0:QW], lhsT=Vp[0:nk, vb, hh, :], rhs=ax[0:nk, 0:QW], start=first, stop=last_),
                                   r=[('a', bi % 2, hh), 'Vp'], w=[pk(6)], signal=last_)
                        pe_A(0)
                        for s_ in range(nb_ + 2):
                            if 1 <= s_ <= nb_:
                                pe_BC(s_ - 1)
                                dve_t(s_ - 1)
                            if s_ < nb_:
                                act_esp(s_)
                            if 1 <= s_ <= nb_:
                                act_a(s_ - 1)
                                dve_cu(s_ - 1)
                            if 2 <= s_ <= nb_ + 1:
                                pe_O(s_ - 2)
                            if s_ + 1 < nb_:
                                pe_A(s_ + 1)
                        os_ = ost[ocnt % 2]; okey = ('ost', ocnt % 2); dk = 'ost%d' % (ocnt % 2); ocnt += 1
                        op('act', lambda e, os_=os_: e.activation(out=os_[:, 0:QW], in_=PS[6][:, 0:QW], func=AF.Copy), r=[pk(6)], w=[okey])
                        QS = min(QW, T)
                        op('sp', lambda e, os_=os_: e.dma_start(out=OSTs[g][p][:, q0:q0 + QS], in_=os_[:, 0:QS]), r=[okey], dma=dk)
            em.barrier()

    def gdn_phase(l):
        with contextlib.ExitStack() as es:
            def sb(name, shape, dt=F32):
                return es.enter_context(_sbuf_tensor(name, shape, dt))
            convw = sb("g_convw", [128, 12, 4])
            neg = sb("g_neg", [64, 64]); mstrict = sb("g_mstrict", [64, 256]); i4 = sb("g_i4", [64, 256])
            sel = sb("g_sel", [4, 4, 128]); reset = sb("g_reset", [4, 512])
            alog = sb("g_alog", [4, 1]); dtb = sb("g_dtb", [4, 1]); negA = sb("g_negA", [4, 1]); gng = sb("g_gng", [128, 1])
            S = sb("g_S", [128, 4, 128])
            raw = sb("g_raw", [128, 12, 515])
            qkv = sb("g_qkv", [128, 12, 512])
            ytmp = sb("g_ytmp", [128, 512])
            sqb = sb("g_sqb", [128, 512], BF16)
            lnt = sb("g_lnt", [128, 512])
            rows_b = sb("g_rb", [4, 512]); rows_a = sb("g_ra", [4, 512])
            r_beta = sb("g_rbeta", [4, 512]); r_g = sb("g_rg", [4, 512]); r_gc = sb("g_rgc", [4, 512]); r_ngc = sb("g_rngc", [4, 512])
            r_gl = sb("g_rgl", [4, 512]); r_eg = sb("g_reg", [4, 512]); r_kes = sb("g_rkes", [4, 512]); r_cd = sb("g_rcd", [4, 512])
            r_beg = sb("g_rbeg", [4, 512]); r_tmp = sb("g_rtmp", [4, 512])
            LL = sb("g_LL", [2, 4, 512]); RR = sb("g_RR", [2, 4, 512])
            vbT = sb("g_vbT", [128, 4, 512]); kbT = sb("g_kbT", [128, 4, 512], BF16); kbgT = sb("g_kbgT", [128, 4, 512], BF16)
            qdT = sb("g_qdT", [128, 4, 512], BF16); keT = sb("g_keT", [128, 4, 512]); cdc = sb("g_cdc", [128, 4, 8])
            qk16 = sb("g_qk16", [128, 8, 512], BF16); Sb = sb("g_Sb", [128, 4, 128], BF16)
            zT = sb("g_zT", [128, 4, 512]); OT = sb("g_OT", [128, 4, 512]); ogs = sb("g_ogs", [128, 4, 512], BF16)
            gmS = [sb("g_gm%d" % i, [64, 256]) for i in range(2)]; BmS = [sb("g_B%d" % i, [64, 256]) for i in range(2)]
            BTmS = [sb("g_BT%d" % i, [64, 256]) for i in range(2)]
            PmS = [[sb("g_P%d_%d" % (j, i), [64, 256]) for i in range(2)] for j in range(2)]
            PTmS = [[sb("g_PT%d_%d" % (j, i), [64, 256]) for i in range(2)] for j in range(2)]
            XmS = [[sb("g_X%d_%d" % (j, i), [64, 256]) for i in range(2)] for j in range(2)]
            QKmO = [sb("g_QKm%d" % i, [64, 256], BF16) for i in range(4)]; XO = [sb("g_XO%d" % i, [64, 256], BF16) for i in range(4)]
            vbm = sb("g_vb", [64, 4, 128]); kem = sb("g_ke", [64, 4, 128], BF16); rhs2 = sb("g_rhs2", [64, 4, 128], BF16); vnew = sb("g_vnew", [64, 4, 128], BF16)

            for wi_ in range(4):
                op('sp', lambda e, wi_=wi_: e.dma_start(out=convw[:, :, wi_], in_=conv_w[l, wi_].rearrange("(c p) -> p c", p=128), allow_slow_non_contiguous=True),
                   w=['convw'], dma='gc0')
            for (t, s_, kname) in ((neg, c_neg, 'neg'), (mstrict, c_mstrict, 'mstrict'), (i4, c_i4, 'i4'), (sel, c_sel, 'sel'), (reset, c_reset, 'reset')):
                op('sp', lambda e, t=t, s_=s_: e.dma_start(out=t[:], in_=s_), w=[kname], dma='gc0')
            op('sp', lambda e: e.dma_start(out=alog[:], in_=a_log[l].rearrange("(h o) -> h o", o=1)), w=['alog'], dma='gc0')
            op('sp', lambda e: e.dma_start(out=dtb[:], in_=dt_bias[l].rearrange("(h o) -> h o", o=1)), w=['dtb'], dma='gc0')
            op('sp', lambda e: e.dma_start(out=gng[:], in_=gdn_gain[l].rearrange("(h o) -> h o", o=1)), w=['gng'], dma='gc0')
            op('act', lambda e: e.activation(out=negA[:], in_=alog[:], func=AF.Exp), r=['alog'], w=['negA'])
            op('dve', lambda e: e.tensor_scalar(out=negA[:], in0=negA[:], scalar1=-1.0, scalar2=None, op0=ALU.mult), r=['negA'], w=['negA'])
            op('dve', lambda e: e.memset(LL[:], 1.0), w=['LL'])
            op('dve', lambda e: e.memset(RR[:], 1.0), w=['RR'])
            pbc = [0]

            def nb():
                pbc[0] += 1
                return pbc[0] % 8

            for g, T in groups:
                TT = min(512, T)
                nch = TT // 64
                if g == 'p':
                    op('dve', lambda e: e.memset(S[:], 0.0), w=['S0', 'S1', 'S2', 'S3'])
                else:
                    op('sp', lambda e: e.dma_start(out=S[:], in_=state_gdn[l].rearrange("h k v -> k h v")), w=['S0', 'S1', 'S2', 'S3'], dma='gS')
                op('pool', lambda e: e.tensor_copy(out=Sb[:], in_=S[:]), r=['S0', 'S1', 'S2', 'S3'], w=['Sb'])
                for ti in range(T // TT):
                    t0 = ti * TT
                    if ti == 0:
                        if g == 'p':
                            op('dve', lambda e: e.memset(raw[:, :, 0:3], 0.0), w=['raw'])
                        else:
                            for wi_ in range(3):
                                op('sp', lambda e, wi_=wi_: e.dma_start(out=raw[:, :, wi_], in_=state_conv[l, wi_].rearrange("(c p) -> p c", p=128),
                                                                        allow_slow_non_contiguous=True), w=['raw'], dma='graw')
                        for c in range(12):
                            op('sp', lambda e, c=c: e.dma_start(out=raw[:, c, 3:3 + TT], in_=RAWs[g][c][:, 0:TT]), w=['raw'], dma='graw')
                    else:
                        for c in range(12):
                            op('sp', lambda e, c=c: e.dma_start(out=raw[:, c, 0:3 + TT], in_=RAWs[g][c][:, t0 - 3:t0 + TT]), w=['raw'], dma='graw')
                    op('sp', lambda e: e.dma_start(out=rows_b[:, 0:TT], in_=BAs[g][0][:, t0:t0 + TT]), w=['rows_b'], dma='grow')
                    op('sp', lambda e: e.dma_start(out=rows_a[:, 0:TT], in_=BAs[g][1][:, t0:t0 + TT]), w=['rows_a'], dma='grow')
                    for h in range(4):
                        op('sp', lambda e, h=h: e.dma_start(out=zT[:, h, 0:TT], in_=ZTs[g][h][:, t0:t0 + TT]), w=['zT'], dma='gz')
                    for c in range(12):
                        op('dve', lambda e, c=c: e.tensor_scalar(out=ytmp[:, 0:TT], in0=raw[:, c, 0:TT], scalar1=convw[:, c, 0:1], scalar2=None, op0=ALU.mult),
                           r=['raw', 'convw'], w=['ytmp'])
                        for i in range(1, 4):
                            op('dve', lambda e, c=c, i=i: e.scalar_tensor_tensor(out=ytmp[:, 0:TT], in0=raw[:, c, i:i + TT], scalar=convw[:, c, i:i + 1],
                                                                               in1=ytmp[:, 0:TT], op0=ALU.mult, op1=ALU.add),
                               r=['raw', 'convw', 'ytmp'], w=['ytmp'])
                        op('act', lambda e, c=c: e.activation(out=qkv[:, c, 0:TT], in_=ytmp[:, 0:TT], func=AF.Silu), r=['ytmp'], w=[('qkv', c)])
                    for c in range(8):
                        b = nb()
                        op('dve', lambda e, c=c: e.tensor_tensor(out=sqb[:, 0:TT], in0=qkv[:, c, 0:TT], in1=qkv[:, c, 0:TT], op=ALU.mult),
                           r=[('qkv', c)], w=['sqb'])
                        op('pe', lambda e, b=b: e.matmul(PS[b][:, 0:TT], lhsT=ones_bf[:], rhs=sqb[:, 0:TT], start=True, stop=True), r=['sqb', 'ones'], w=[pk(b)])
                        op('act', lambda e, b=b: e.activation(out=lnt[:, 0:TT], in_=PS[b][:, 0:TT], func=AF.Ln, bias=EPS, scale=1.0), r=[pk(b)], w=['lnt'])
                        bias = float(-0.5 * np.log(128.0)) if c < 4 else 0.0
                        op('act', lambda e, bias=bias: e.activation(out=lnt[:, 0:TT], in_=lnt[:, 0:TT], func=AF.Exp, scale=-0.5, bias=bias), r=['lnt'], w=['lnt'])
                        op('dve', lambda e, c=c: e.tensor_tensor(out=qkv[:, c, 0:TT], in0=qkv[:, c, 0:TT], in1=lnt[:, 0:TT], op=ALU.mult),
                           r=[('qkv', c), 'lnt'], w=[('qkv', c)])
                        op('pool', lambda e, c=c: e.tensor_copy(out=qk16[:, c, 0:TT], in_=qkv[:, c, 0:TT]), r=[('qkv', c)], w=[('qk16', c)])
                    op('act', lambda e: e.activation(out=r_beta[:, 0:TT], in_=rows_b[:, 0:TT], func=AF.Exp, scale=-1.0), r=['rows_b'], w=['r_beta'])
                    op('dve', lambda e: e.tensor_scalar(out=r_beta[:, 0:TT], in0=r_beta[:, 0:TT], scalar1=1.0, scalar2=None, op0=ALU.add), r=['r_beta'], w=['r_beta'])
                    op('dve', lambda e: e.reciprocal(out=r_beta[:, 0:TT], in_=r_beta[:, 0:TT]), r=['r_beta'], w=['r_beta'])
                    op('act', lambda e: e.activation(out=r_tmp[:, 0:TT], in_=rows_a[:, 0:TT], func=AF.Exp, bias=dtb[:, 0:1], scale=1.0), r=['rows_a', 'dtb'], w=['r_tmp'])
                    op('act', lambda e: e.activation(out=r_tmp[:, 0:TT], in_=r_tmp[:, 0:TT], func=AF.Ln, bias=1.0, scale=1.0), r=['r_tmp'], w=['r_tmp'])
                    op('dve', lambda e: e.tensor_scalar(out=r_g[:, 0:TT], in0=r_tmp[:, 0:TT], scalar1=negA[:, 0:1], scalar2=None, op0=ALU.mult),
                       r=['r_tmp', 'negA'], w=['r_g'])
                    op('dve', lambda e: e.tensor_tensor_scan(out=r_gc[:, 0:TT], data0=reset[:, 0:TT], data1=r_g[:, 0:TT], initial=0.0, op0=ALU.mult, op1=ALU.add),
                       r=['r_g', 'reset'], w=['r_gc'])
                    op('dve', lambda e: e.tensor_scalar(out=r_ngc[:, 0:TT], in0=r_gc[:, 0:TT], scalar1=-1.0, scalar2=None, op0=ALU.mult), r=['r_gc'], w=['r_ngc'])
                    gc3 = r_gc[:, 0:TT].rearrange("h (n c) -> h n c", c=64)
                    op('dve', lambda e: e.tensor_copy(out=r_gl[:, 0:TT].rearrange("h (n c) -> h n c", c=64), in_=gc3[:, :, 63:64].to_broadcast([4, nch, 64])),
                       r=['r_gc'], w=['r_gl'])
                    op('act', lambda e: e.activation(out=r_eg[:, 0:TT], in_=r_gc[:, 0:TT], func=AF.Exp), r=['r_gc'], w=['r_eg'])
                    op('dve', lambda e: e.tensor_tensor(out=r_tmp[:, 0:TT], in0=r_gl[:, 0:TT], in1=r_gc[:, 0:TT], op=ALU.subtract), r=['r_gl', 'r_gc', 'r_tmp'], w=['r_tmp'])
                    op('act', lambda e: e.activation(out=r_kes[:, 0:TT], in_=r_tmp[:, 0:TT], func=AF.Exp), r=['r_tmp'], w=['r_kes'])
                    op('act', lambda e: e.activation(out=r_cd[:, 0:TT], in_=r_gl[:, 0:TT], func=AF.Exp), r=['r_gl'], w=['r_cd'])
                    op('dve', lambda e: e.tensor_tensor(out=r_beg[:, 0:TT], in0=r_beta[:, 0:TT], in1=r_eg[:, 0:TT], op=ALU.mult), r=['r_beta', 'r_eg'], w=['r_beg'])
                    for h in range(4):
                        op('sp', lambda e, h=h: e.dma_start(out=LL[1:2, h, 0:TT], in_=r_ngc[h:h + 1, 0:TT]), r=['r_ngc'], w=['LL'], dma='gLL')
                        op('sp', lambda e, h=h: e.dma_start(out=RR[0:1, h, 0:TT], in_=r_gc[h:h + 1, 0:TT]), r=['r_gc'], w=['RR'], dma='gLL')
                    for h in range(4):
                        def bc(rows, rkey):
                            b = nb()
                            op('pe', lambda e: e.matmul(PS[b][:, 0:TT], lhsT=sel[:, h, :], rhs=rows[:, 0:TT], start=True, stop=True), r=[rkey, 'sel'], w=[pk(b)])
                            return b
                        b = bc(r_beta, 'r_beta')
                        op('dve', lambda e: e.tensor_tensor(out=vbT[:, h, 0:TT], in0=qkv[:, 8 + h, 0:TT], in1=PS[b][:, 0:TT], op=ALU.mult),
                           r=[pk(b), ('qkv', 8 + h)], w=['vbT'])
                        op('dve', lambda e: e.tensor_tensor(out=kbT[:, h, 0:TT], in0=qkv[:, 4 + h, 0:TT], in1=PS[b][:, 0:TT], op=ALU.mult),
                           r=[pk(b), ('qkv', 4 + h)], w=['kbT'])
                        b = bc(r_beg, 'r_beg')
                        op('dve', lambda e: e.tensor_tensor(out=kbgT[:, h, 0:TT], in0=qkv[:, 4 + h, 0:TT], in1=PS[b][:, 0:TT], op=ALU.mult),
                           r=[pk(b), ('qkv', 4 + h)], w=['kbgT'])
                        b = bc(r_eg, 'r_eg')
                        op('dve', lambda e: e.tensor_tensor(out=qdT[:, h, 0:TT], in0=qkv[:, h, 0:TT], in1=PS[b][:, 0:TT], op=ALU.mult),
                           r=[pk(b), ('qkv', h)], w=['qdT'])
                        b = bc(r_kes, 'r_kes')
                        op('dve', lambda e: e.tensor_tensor(out=keT[:, h, 0:TT], in0=qkv[:, 4 + h, 0:TT], in1=PS[b][:, 0:TT], op=ALU.mult),
                           r=[pk(b), ('qkv', 4 + h)], w=['keT'])
                        b = bc(r_cd, 'r_cd')
                        op('act', lambda e: e.activation(out=cdc[:, h, 0:nch], in_=PS[b][:, 0:TT].rearrange("p (n c) -> p n c", c=64)[:, :, 0], func=AF.Copy),
                           r=[pk(b)], w=['cdc'])
                    qk_all = [('qkv', c) for c in range(12)]
                    qk16_all = [('qk16', c) for c in range(8)]
                    Sk = ['S0', 'S1', 'S2', 'S3']

                    def prep(n):
                        cs = slice(n * 64, (n + 1) * 64)
                        st = n % 2; o4 = n % 4
                        c0 = 0
                        PB = {0: 3 * st, 3: 3 * st, 1: 3 * st + 1, 4: 3 * st + 1, 2: 3 * st + 2, 5: 3 * st + 2}
                        gm, Bm, BTm, Pm, PTm, Xm = gmS[st], BmS[st], BTmS[st], PmS[st], PTmS[st], XmS[st]
                        QKm = QKmO[o4]

                        def K(name, *a):
                            return (name, st) + a

                        def pp(b):
                            return pk(PB[b])
                        for h in range(4):
                            hc = slice(c0 + h * 64, c0 + (h + 1) * 64)
                            op('pe', lambda e, h=h, hc=hc: e.matmul(PS[PB[0]][0:64, hc], lhsT=LL[0:2, h, cs], rhs=RR[0:2, h, cs], start=True, stop=False),
                               r=['LL', 'RR'], w=[pp(0)], signal=False)
                            op('pe', lambda e, h=h, hc=hc: e.matmul(PS[PB[0]][0:64, hc], lhsT=ident[0:64, 0:64], rhs=neg[:, :], start=False, stop=True),
                               r=['ident', 'neg'], w=[pp(0)], signal=(h == 3))
                        for h in range(4):
                            hc = slice(c0 + h * 64, c0 + (h + 1) * 64)
                            op('pe', lambda e, h=h, hc=hc: e.matmul(PS[PB[1]][0:64, hc], lhsT=qk16[:, 4 + h, cs], rhs=kbT[:, h, cs], start=True, stop=True),
                               r=qk16_all + ['kbT'], w=[pp(1)], signal=(h == 3))
                        for h in range(4):
                            hc = slice(c0 + h * 64, c0 + (h + 1) * 64)
                            op('pe', lambda e, h=h, hc=hc: e.matmul(PS[PB[2]][0:64, hc], lhsT=qk16[:, 4 + h, cs], rhs=qk16[:, h, cs], start=True, stop=True),
                               r=qk16_all, w=[pp(2)], signal=(h == 3))
                        yield
                        op('act', lambda e: e.activation(out=gm[:, :], in_=PS[PB[0]][0:64, c0:c0 + 256], func=AF.Exp), r=[pp(0)], w=[K('gm')])
                        yield
                        op('dve', lambda e: e.scalar_tensor_tensor(out=Bm[:, :], in0=PS[PB[1]][0:64, c0:c0 + 256], scalar=-1.0, in1=gm[:, :], op0=ALU.mult, op1=ALU.mult),
                           r=[pp(1), K('gm')], w=[K('Bm')])
                        op('dve', lambda e: e.tensor_tensor(out=QKm[:, :], in0=PS[PB[2]][0:64, c0:c0 + 256], in1=gm[:, :], op=ALU.mult), r=[pp(2), K('gm')], w=[('QKm', o4)])
                        yield
                        op('dve', lambda e: e.tensor_tensor(out=Bm[:, :], in0=Bm[:, :], in1=mstrict[:, :], op=ALU.mult), r=[K('Bm'), 'mstrict'], w=[K('Bm')])
                        yield
                        for h in range(4):
                            hc = slice(h * 64, (h + 1) * 64)
                            pc = slice(c0 + h * 64, c0 + (h + 1) * 64)
                            op('pe', lambda e, hc=hc, pc=pc: e.transpose(PS[PB[3]][0:64, pc], Bm[:, hc], ident[0:64, 0:64]), r=[K('Bm'), 'ident'], w=[pp(3)], signal=(h == 3))
                        op('dve', lambda e: e.tensor_tensor(out=Xm[0][:, :], in0=Bm[:, :], in1=i4[:, :], op=ALU.add), r=[K('Bm'), 'i4'], w=[K('X', 0)])
                        yield
                        op('act', lambda e: e.activation(out=BTm[:, :], in_=PS[PB[3]][0:64, c0:c0 + 256], func=AF.Copy), r=[pp(3)], w=[K('BTm')])
                        yield
                        Pp, Ppk, PTp, PTpk = Bm, K('Bm'), BTm, K('BTm')
                        xi = 0
                        for lev in range(1, 6):
                            pi = lev % 2
                            if lev < 5:
                                for h in range(4):
                                    hc = slice(h * 64, (h + 1) * 64)
                                    pc = slice(c0 + h * 64, c0 + (h + 1) * 64)
                                    op('pe', lambda e, hc=hc, pc=pc, Pp=Pp, PTp=PTp: e.matmul(PS[PB[4]][0:64, pc], lhsT=PTp[:, hc], rhs=Pp[:, hc], start=True, stop=True),
                                       r=[Ppk, PTpk], w=[pp(4)], signal=(h == 3))
                            for h in range(4):
                                hc = slice(h * 64, (h + 1) * 64)
                                pc = slice(c0 + h * 64, c0 + (h + 1) * 64)
                                op('pe', lambda e, hc=hc, pc=pc, Pp=Pp, PTp=PTp: e.matmul(PS[PB[3]][0:64, pc], lhsT=Pp[:, hc], rhs=PTp[:, hc], start=True, stop=True),
                                   r=[Ppk, PTpk], w=[pp(3)], signal=(h == 3))
                            yield
                            if lev < 5:
                                op('dve', lambda e, pi=pi: e.tensor_copy(out=Pm[pi][:, :], in_=PS[PB[4]][0:64, c0:c0 + 256]), r=[pp(4)], w=[K('P', pi)])
                            op('act', lambda e, pi=pi: e.activation(out=PTm[pi][:, :], in_=PS[PB[3]][0:64, c0:c0 + 256], func=AF.Copy), r=[pp(3)], w=[K('PT', pi)])
                            yield
                            for h in range(4):
                                hc = slice(h * 64, (h + 1) * 64)
                                pc = slice(c0 + h * 64, c0 + (h + 1) * 64)
                                op('pe', lambda e, hc=hc, pc=pc, pi=pi, xi=xi: e.matmul(PS[PB[5]][0:64, pc], lhsT=PTm[pi][:, hc], rhs=Xm[xi][:, hc], start=True, stop=True),
                                   r=[K('PT', pi), K('X', xi)], w=[pp(5)], signal=(h == 3))
                            yield
                            if lev < 5:
                                op('dve', lambda e, xi=xi: e.tensor_tensor(out=Xm[1 - xi][:, :], in0=PS[PB[5]][0:64, c0:c0 + 256], in1=Xm[xi][:, :], op=ALU.add),
                                   r=[pp(5), K('X', xi)], w=[K('X', 1 - xi)])
                            else:
                                op('dve', lambda e, xi=xi: e.tensor_tensor(out=XO[o4][:, :], in0=PS[PB[5]][0:64, c0:c0 + 256], in1=Xm[xi][:, :], op=ALU.add),
                                   r=[pp(5), K('X', xi)], w=[('XO', o4)])
                            yield
                            xi = 1 - xi
                            Pp, Ppk, PTp, PTpk = Pm[pi], K('P', pi), PTm[pi], K('PT', pi)

                    def seq(n):
                        cs = slice(n * 64, (n + 1) * 64)
                        o4 = n % 4
                        X = XO[o4]; Xk = ('XO', o4); QKm = QKmO[o4]
                        for h in range(4):
                            op('pe', lambda e, h=h: e.transpose(PS[6][0:64, h * 128:(h + 1) * 128], vbT[:, h, cs], ident[:]), r=['vbT', 'ident'], w=[pk(6)], signal=(h == 3))
                        for h in range(4):
                            op('pe', lambda e, h=h: e.transpose(PS[7][0:64, h * 128:(h + 1) * 128], keT[:, h, cs], ident[:]), r=['keT', 'ident'], w=[pk(7)], signal=(h == 3))
                        yield
                        op('act', lambda e: e.activation(out=vbm[:].rearrange("p h d -> p (h d)"), in_=PS[6][0:64, :], func=AF.Copy), r=[pk(6)], w=['vbm'])
                        op('act', lambda e: e.activation(out=kem[:].rearrange("p h d -> p (h d)"), in_=PS[7][0:64, :], func=AF.Copy), r=[pk(7)], w=['kem'])
                        yield
                        for h in range(4):
                            op('pe', lambda e, h=h: e.matmul(PS[6][0:64, h * 128:(h + 1) * 128], lhsT=kbgT[:, h, cs], rhs=Sb[:, h, :], start=True, stop=True),
                               r=['kbgT', 'Sb'], w=[pk(6)], signal=(h == 3))
                        yield
                        op('dve', lambda e: e.tensor_tensor(out=rhs2[:].rearrange("p h d -> p (h d)"), in0=vbm[:].rearrange("p h d -> p (h d)"), in1=PS[6][0:64, :], op=ALU.subtract),
                           r=[pk(6), 'vbm'], w=['rhs2'])
                        yield
                        for h in range(4):
                            hc = slice(h * 64, (h + 1) * 64)
                            op('pe', lambda e, h=h, hc=hc: e.matmul(PS[7][0:64, h * 128:(h + 1) * 128], lhsT=X[:, hc], rhs=rhs2[:, h, :], start=True, stop=True),
                               r=[Xk, 'rhs2'], w=[pk(7)], signal=(h == 3))
                        yield
                        op('act', lambda e: e.activation(out=vnew[:].rearrange("p h d -> p (h d)"), in_=PS[7][0:64, :], func=AF.Copy), r=[pk(7)], w=['vnew'])
                        yield
                        for h in range(4):
                            hc = slice(h * 64, (h + 1) * 64)
                            op('pe', lambda e, h=h, hc=hc: e.matmul(PS[6][:, hc], lhsT=Sb[:, h, :], rhs=qdT[:, h, cs], start=True, stop=False),
                               r=['Sb', 'qdT'], w=[pk(6)], signal=False)
                            op('pe', lambda e, h=h, hc=hc: e.matmul(PS[6][:, hc], lhsT=vnew[:, h, :], rhs=QKm[:, hc], start=False, stop=True),
                               r=['vnew', ('QKm', o4)], w=[pk(6)], signal=(h == 3))
                        for h in range(4):
                            op('pe', lambda e, h=h: e.matmul(PS[7][:, h * 128:(h + 1) * 128], lhsT=kem[:, h, :], rhs=vnew[:, h, :], start=True, stop=True),
                               r=['kem', 'vnew'], w=[pk(7)], signal=(h == 3))
                        yield
                        op('act', lambda e: e.activation(out=OT[:, :, cs], in_=PS[6][:, 0:256].rearrange("p (h c) -> p h c", c=64), func=AF.Copy), r=[pk(6)], w=['OT'])
                        for h in range(4):
                            op('dve', lambda e, h=h: e.scalar_tensor_tensor(out=S[:, h, :], in0=S[:, h, :], scalar=cdc[:, h, n:n + 1], in1=PS[7][:, h * 128:(h + 1) * 128],
                                                                          op0=ALU.mult, op1=ALU.add),
                               r=[pk(7), 'cdc', Sk[h]], w=[Sk[h]])
                        op('pool', lambda e: e.tensor_copy(out=Sb[:], in_=S[:]), r=Sk, w=['Sb'])
                        yield

                    def chain(gens):
                        for g_ in gens:
                            yield from g_

                    def interleave(gens):
                        gens = [g_ for g_ in gens if g_ is not None]
                        while gens:
                            for g_ in list(gens):
                                try:
                                    next(g_)
                                except StopIteration:
                                    gens.remove(g_)
                    npair = (nch + 1) // 2
                    interleave([prep(0), prep(1) if nch > 1 else None])
                    for kp in range(npair):
                        nxt = []
                        if kp + 1 < npair:
                            nxt = [prep(2 * kp + 2), prep(2 * kp + 3)]
                        seqs = [seq(2 * kp)] + ([seq(2 * kp + 1)] if 2 * kp + 1 < nch else [])
                        interleave(nxt + [chain(seqs)])
                    for h in range(4):
                        b = nb()
                        op('dve', lambda e, h=h: e.tensor_tensor(out=sqb[:, 0:TT], in0=OT[:, h, 0:TT], in1=OT[:, h, 0:TT], op=ALU.mult), r=['OT'], w=['sqb'])
                        op('pe', lambda e, b=b: e.matmul(PS[b][:, 0:TT], lhsT=ones_bf[:], rhs=sqb[:, 0:TT], start=True, stop=True), r=['sqb', 'ones'], w=[pk(b)])
                        op('act', lambda e, b=b: e.activation(out=lnt[:, 0:TT], in_=PS[b][:, 0:TT], func=AF.Ln, bias=EPS, scale=1.0 / 128), r=[pk(b)], w=['lnt'])
                        op('act', lambda e: e.activation(out=lnt[:, 0:TT], in_=lnt[:, 0:TT], func=AF.Exp, scale=-0.5), r=['lnt'], w=['lnt'])
                        op('dve', lambda e, h=h: e.tensor_tensor(out=ytmp[:, 0:TT], in0=OT[:, h, 0:TT], in1=lnt[:, 0:TT], op=ALU.mult), r=['OT', 'lnt'], w=['ytmp'])
                        op('act', lambda e, h=h: e.activation(out=lnt[:, 0:TT], in_=zT[:, h, 0:TT], func=AF.Silu), r=['zT', 'lnt'], w=['lnt'])
                        op('dve', lambda e, h=h: e.scalar_tensor_tensor(out=ogs[:, h, 0:TT], in0=ytmp[:, 0:TT], scalar=gng[:, 0:1], in1=lnt[:, 0:TT], op0=ALU.mult, op1=ALU.mult),
                           r=['ytmp', 'lnt', 'gng'], w=[('ogs', h)])
                        op('sp', lambda e, h=h: e.dma_start(out=OGTs[g][h][:, t0:t0 + TT], in_=ogs[:, h, 0:TT]), r=[('ogs', h)], dma='gog%d' % h)
                out_dma_keys.add(('dma', 'gSo'))
                op('sp', lambda e: e.dma_start(out=g_out[g][l].rearrange("h k v -> k h v"), in_=S[:]), r=['S0', 'S1', 'S2', 'S3'], dma='gSo')
            em.barrier()

    def m2_phase(l):
        with contextlib.ExitStack() as es:
            Wg = load_weight_bf16(es, "Wg", w_in[l], KD, 2048, c0=3592)
            Wsb = load_weight_bf16(es, "Wsb", w_bsb[l], 4, D)
            Wgd = load_weight_bf16(es, "Wgd", w_bgdn[l], 4, D)
            Wo = load_weight_bf16(es, "Wo", w_out[l], KD, D)
            g2 = load_gain_bc(es, "g2", norm_g[l, 2])
            g3 = load_gain_bc(es, "g3", norm_g[l, 3])
            scr = alloc_norm_scr(es)
            ss2 = es.enter_context(_sbuf_tensor("f_ss2", [128, 1], F32))
            lnv2 = es.enter_context(_sbuf_tensor("f_lnv2", [128, 1], F32))
            rstd2 = es.enter_context(_sbuf_tensor("f_rstd2", [128, 1], F32))
            scr2 = (scr[0], ss2, lnv2, rstd2)
            xts = [es.enter_context(_sbuf_tensor("m_x%d" % i, [128, 4, D], F32)) for i in range(2)]
            xnT = es.enter_context(_sbuf_tensor("m_xnT", [128, KD, 512], BF16))
            osb = [es.enter_context(_sbuf_tensor("m_osb%d" % i, [128, 4, 512], BF16)) for i in range(2)]
            ogd = [es.enter_context(_sbuf_tensor("m_ogd%d" % i, [128, 4, 512], BF16)) for i in range(2)]
            sg0 = [es.enter_context(_sbuf_tensor("m_sg0%d" % i, [128, 512], F32)) for i in range(2)]
            sg1 = [es.enter_context(_sbuf_tensor("m_sg1%d" % i, [128, 512], F32)) for i in range(2)]
            tmp = [es.enter_context(_sbuf_tensor("m_tmp%d" % i, [128, 512], F32)) for i in range(2)]
            mT = es.enter_context(_sbuf_tensor("m_mT", [128, KD, 512], BF16))
            ycat = [es.enter_context(_sbuf_tensor("m_y%d" % i, [128, D], F32)) for i in range(2)]
            tiles = []
            for g, T in groups:
                TT = min(512, T)
                for ti in range(T // TT):
                    tiles.append((g, ti * TT, TT))

            def load(i):
                g, t0, TT = tiles[i]
                pt = min(128, TT); nsub = TT // pt
                xt = xts[i % 2]
                op('sp', lambda e: e.dma_start(out=xt[0:pt, 0:nsub, :], in_=RES[g][t0:t0 + TT, :].rearrange("(s p) d -> p s d", p=pt)),
                   r=[('res', g, t0)], w=[('mx', i % 2)], dma='mx%d' % (i % 2))
                for k in range(4):
                    op('sp', lambda e, k=k: e.dma_start(out=osb[i % 2][:, k, 0:TT], in_=OSTs[g][k][:, t0:t0 + TT]), w=[('osb', i % 2)], dma='mo%d' % (i % 2))
                    op('sp', lambda e, k=k: e.dma_start(out=ogd[i % 2][:, k, 0:TT], in_=OGTs[g][k][:, t0:t0 + TT]), w=[('ogd', i % 2)], dma='mo%d' % (i % 2))
            load(0)
            cnt = 0
            for i, (g, t0, TT) in enumerate(tiles):
                if i + 1 < len(tiles):
                    load(i + 1)
                pt = min(128, TT); nsub = TT // pt
                xt = xts[i % 2]; xkey = ('mx', i % 2)
                norm_to_T(xt, xkey, pt, nsub, g2, 'g2', xnT, 'xnT', scr, 'm')
                xk = xnT_keys('xnT', nsub)
                for m in range(KD):
                    j = m % 2
                    for (gi, b, sgt, sgk) in ((0, 0 + j, sg0[j], ('sg0', j)), (1, 2 + j, sg1[j], ('sg1', j))):
                        for k in range(KD):
                            op('pe', lambda e, k=k, b=b, gi=gi, m=m: e.matmul(PS[b][:, 0:TT], lhsT=Wg[:, k, gi * D + m * 128:gi * D + (m + 1) * 128], rhs=xnT[:, k, 0:TT],
                                                                             start=(k == 0), stop=(k == KD - 1)),
                               r=xk + ['Wg'], w=[pk(b)], signal=(k == KD - 1))
                        op('act', lambda e, b=b, sgt=sgt: e.activation(out=sgt[:, 0:TT], in_=PS[b][:, 0:TT], func=AF.Sigmoid), r=[pk(b)], w=[sgk])
                    for (b, Wb, wk, src, srck) in ((4, Wsb, 'Wsb', osb[i % 2], ('osb', i % 2)), (5, Wgd, 'Wgd', ogd[i % 2], ('ogd', i % 2))):
                        for k in range(4):
                            op('pe', lambda e, k=k, b=b, Wb=Wb, src=src, m=m: e.matmul(PS[b][:, 0:TT], lhsT=Wb[:, k, m * 128:(m + 1) * 128], rhs=src[:, k, 0:TT],
                                                                                      start=(k == 0), stop=(k == 3)),
                               r=[wk, srck], w=[pk(b)], signal=(k == 3))
                    op('dve', lambda e, j=j: e.tensor_tensor(out=tmp[j][:, 0:TT], in0=PS[4][:, 0:TT], in1=sg0[j][:, 0:TT], op=ALU.mult),
                       r=[pk(4), ('sg0', j)], w=[('tmp', j)])
                    op('dve', lambda e, j=j: e.tensor_tensor(out=sg1[j][:, 0:TT], in0=PS[5][:, 0:TT], in1=sg1[j][:, 0:TT], op=ALU.mult),
                       r=[pk(5), ('sg1', j)], w=[('sg1', j)])
                    op('dve', lambda e, j=j, m=m: e.tensor_tensor(out=mT[:, m, 0:TT], in0=tmp[j][:, 0:TT], in1=sg1[j][:, 0:TT], op=ALU.add),
                       r=[('tmp', j), ('sg1', j)], w=[('mT', m)])
                mk = [('mT', m) for m in range(KD)]
                for s in range(nsub):
                    yc = ycat[cnt % 2]; ykey = ('ycat', cnt % 2); cnt += 1
                    for half in range(2):
                        b = 6 + half
                        for k in range(KD):
                            op('pe', lambda e, k=k, b=b, s=s, half=half: e.matmul(PS[b][0:pt, :], lhsT=mT[:, k, s * pt:(s + 1) * pt], rhs=Wo[:, k, half * 512:(half + 1) * 512],
                                                                                 start=(k == 0), stop=(k == KD - 1)),
                               r=mk + ['Wo'], w=[pk(b)], signal=(k == KD - 1))
                        op('act', lambda e, b=b, half=half, yc=yc: e.activation(out=yc[0:pt, half * 512:(half + 1) * 512], in_=PS[b][0:pt, :], func=AF.Copy),
                           r=[pk(b)], w=[ykey])
                    post_norm_residual(yc, ykey, pt, xt, xkey, s, g3, 'g3', scr2, 1.0)
                op('sp', lambda e: e.dma_start(out=RES[g][t0:t0 + TT, :].rearrange("(s p) d -> p s d", p=pt), in_=xt[0:pt, 0:nsub, :]),
                   r=[xkey], w=[('res', g, t0)], dma='mo_%d' % (i % 2))
            em.barrier()

    nph = [0]
    stop_after = cfg.get('stop_after', 10 ** 9)

    def runp(fn, *a):
        if nph[0] < stop_after:
            fn(*a)
        nph[0] += 1
    for l in range(DEPTH):
        src = x_in if l == 0 else RES
        runp(ffn_phase, l, 0, src, RES, False)
        runp(m1_phase, l)
        runp(att_phase, l)
        runp(gdn_phase, l)
        runp(m2_phase, l)
        runp(ffn_phase, l, 1, RES, RES, l == DEPTH - 1)
    em.barrier()
    return nc, em


def make_consts():
    c = {}
    c['c_ident'] = np.eye(128, dtype=np.float32)
    j = np.arange(128)
    c['c_linc'] = (j[:, None] >= j[None, :]).astype(np.float32)
    r = np.arange(128)[:, None, None]; jj = np.arange(4)[None, :, None]; cc = np.arange(512)[None, None, :]
    c['c_masks'] = ((cc - r - 128 * jj) > 0).astype(np.float32)
    j64 = np.arange(64)
    c['c_neg'] = np.where(j64[:, None] <= j64[None, :], 0.0, NEGBIG).astype(np.float32)
    ms = (j64[:, None] < j64[None, :]).astype(np.float32)
    c['c_mstrict'] = np.tile(ms, (1, 4))
    c['c_i4'] = np.tile(np.eye(64, dtype=np.float32), (1, 4))
    sel = np.zeros((4, 4, 128), np.float32)
    for h in range(4):
        sel[h, h, :] = 1.0
    c['c_sel'] = sel
    rs = np.ones((4, 512), np.float32); rs[:, ::64] = 0.0
    c['c_reset'] = rs
    return c


def run(cfg, inputs, n_cores=8):
    nc, em = build(cfg)
    T_P, T_S, PAST, DEPTH = cfg['T_P'], cfg['T_S'], cfg['PAST'], cfg['DEPTH']
    f = lambda a: np.ascontiguousarray(np.asarray(a, dtype=np.float32))
    consts = make_consts()
    NB = inputs['x_prompt'].shape[0]
    shared = {
        'norm_g': f(inputs['norm_gains']), 'w1u': f(inputs['w_ffn1_up']), 'w1d': f(inputs['w_ffn1_down']),
        'w2u': f(inputs['w_ffn2_up']), 'w2d': f(inputs['w_ffn2_down']), 'w_in': f(inputs['w_in']),
        'conv_w': f(inputs['conv_w']), 'a_log': f(inputs['gdn_a_log']), 'dt_bias': f(inputs['gdn_dt_bias']),
        'gdn_gain': f(inputs['gdn_norm_gain']), 'w_bsb': f(inputs['w_branch_sb']), 'w_bgdn': f(inputs['w_branch_gdn']),
        'w_out': f(inputs['w_out']),
    }
    shared.update(consts)
    in_maps = []
    for c in range(n_cores):
        m = dict(shared)
        m['x_p'] = f(inputs['x_prompt'][c % NB])
        m['x_s'] = f(inputs['x_sample'][c])
        m['cache_k'] = f(np.asarray(inputs['cache_sb_k'])[:, c].reshape(DEPTH, PAST, 512))
        m['cache_v'] = f(np.asarray(inputs['cache_sb_v'])[:, c].reshape(DEPTH, PAST, 512))
        m['state_gdn'] = f(np.asarray(inputs['state_gdn'])[:, c])
        m['state_conv'] = f(np.asarray(inputs['state_conv'])[:, c])
        in_maps.append(m)
    res = run_bass_kernel_spmd(nc, in_maps, core_ids=list(range(n_cores)))
    R = res.results
    NS = n_cores
    y_p = np.stack([R[b]['y_p'] for b in range(NB)])
    y_s = np.stack([R[c]['y_s'] for c in range(NS)])
    k_p = np.stack([R[b]['k_p'] for b in range(NB)], axis=1).reshape(DEPTH, NB, T_P, 8, 64)
    v_p = np.stack([R[b]['v_p'] for b in range(NB)], axis=1).reshape(DEPTH, NB, T_P, 8, 64)
    g_p = np.stack([R[b]['g_p'] for b in range(NB)], axis=1)
    c_p = np.stack([R[b]['c_p'] for b in range(NB)], axis=1)
    k_s = np.stack([R[c]['k_s'] for c in range(NS)], axis=1).reshape(DEPTH, NS, T_S, 8, 64)
    v_s = np.stack([R[c]['v_s'] for c in range(NS)], axis=1).reshape(DEPTH, NS, T_S, 8, 64)
    g_s = np.stack([R[c]['g_s'] for c in range(NS)], axis=1)
    c_s = np.stack([R[c]['c_s'] for c in range(NS)], axis=1)
    outs = (y_p, y_s, k_p, v_p, g_p, c_p, k_s, v_s, g_s, c_s)
    return tuple(np.ascontiguousarray(o, dtype=np.float32) for o in outs)


def kernel(**inputs):
    cfg = dict(T_P=8192, T_S=64, PAST=2048, DEPTH=2, FT=384)
    return run(cfg, inputs, n_cores=8)
```

```python
import contextlib
import numpy as np
import concourse.bass as bass
import concourse.mybir as mybir
from concourse.bass_utils import run_bass_kernel_spmd

F32 = mybir.dt.float32
BF16 = mybir.dt.bfloat16
AF = mybir.ActivationFunctionType
ALU = mybir.AluOpType

D = 1024
KD = 8
FF = 2816
KF = 22
IN_DIM = 5640
EPS = 1e-6
NEGBIG = -30000.0


class Emitter:
    def __init__(self, nc):
        self.nc = nc
        self.E = {'pe': nc.tensor, 'act': nc.scalar, 'dve': nc.vector, 'pool': nc.gpsimd, 'sp': nc.sync}
        self.sem = {}
        self.cnt = {}
        self.waited = {k: {} for k in self.E}
        self.lastw = {}
        self.readers = {}
        self.pend = {k: ([], []) for k in self.E}
        self.n_ins = 0

    def _sem(self, key):
        if key not in self.sem:
            nm = "s_" + "".join(ch for ch in str(key) if ch.isalnum() or ch == "_")
            self.sem[key] = self.nc.alloc_semaphore(nm)
            self.cnt[key] = 0
        return self.sem[key]

    def _wait(self, eng, need):
        for sk, v in need.items():
            if self.waited[eng].get(sk, 0) < v:
                self.E[eng].wait_ge(self.sem[sk], v)
                self.waited[eng][sk] = v

    def op(self, eng, fn, r=(), w=(), dma=None, signal=True):
        need = {}
        for b in r:
            t = self.lastw.get(b)
            if t is not None:
                need[t[0]] = max(need.get(t[0], 0), t[1])
        for b in w:
            t = self.lastw.get(b)
            if t is not None:
                need[t[0]] = max(need.get(t[0], 0), t[1])
            for sk, v in self.readers.get(b, {}).items():
                need[sk] = max(need.get(sk, 0), v)
        for oe, (pr_, pw_) in self.pend.items():
            if oe != eng and (pr_ or pw_):
                for b in w:
                    assert b not in pr_ and b not in pw_, ("pending unsignaled access", oe, b)
                for b in r:
                    assert b not in pw_, ("pending unsignaled write", oe, b)
        for sk in list(need):
            if isinstance(sk, tuple) and sk[0] == 'dma':
                need[sk] = self.cnt[sk]
        if eng == 'pe':
            need.pop('pe', None)
        self._wait(eng, need)
        ins = fn(self.E[eng])
        self.n_ins += 1
        if dma is not None:
            sk = ('dma', dma)
            self._sem(sk)
            self.cnt[sk] += 16
            ins.then_inc(self.sem[sk], 16)
            val = self.cnt[sk]
            for b in w:
                self.lastw[b] = (sk, val)
                self.readers[b] = {}
            for b in r:
                self.readers.setdefault(b, {})[sk] = val
            return ins
        pr, pw = self.pend[eng]
        pr.extend(r)
        pw.extend(w)
        if signal:
            sk = eng
            self._sem(sk)
            self.cnt[sk] += 1
            ins.then_inc(self.sem[sk], 1)
            val = self.cnt[sk]
            for b in pw:
                self.lastw[b] = (sk, val)
                self.readers[b] = {}
            for b in pr:
                if b not in pw:
                    self.readers.setdefault(b, {})[sk] = val
            self.pend[eng] = ([], [])
        return ins

    def barrier(self, engines=('pe', 'act', 'dve', 'pool', 'sp')):
        need = {sk: v for sk, v in self.cnt.items() if v > 0}
        for e in engines:
            n2 = dict(need)
            self._wait(e, n2)
        self.lastw = {}
        self.readers = {}


def build(cfg):
    T_P, T_S, PAST, DEPTH = cfg['T_P'], cfg['T_S'], cfg['PAST'], cfg['DEPTH']
    FT = cfg.get('FT', 256)
    nc = bass.Bass("TRN2", target_bir_lowering=False)
    _orig_sbuf_tensor = nc.sbuf_tensor
    _uid = [0]

    def _sbuf_tensor(name, shape, dt):
        _uid[0] += 1
        return _orig_sbuf_tensor("%s_u%d" % (name, _uid[0]), shape, dt)
    em = Emitter(nc)
    op = em.op

    def din(name, shape):
        return nc.dram_tensor(name, list(shape), F32, kind="ExternalInput").ap()

    def dout(name, shape):
        return nc.dram_tensor(name, list(shape), F32, kind="ExternalOutput").ap()

    def dscr(name, shape, dt=F32):
        return nc.dram_tensor(name, list(shape), dt, kind="Internal").ap()

    groups = [('p', T_P), ('s', T_S)]
    TG = dict(groups)
    x_in = {'p': din("x_p", [T_P, D]), 's': din("x_s", [T_S, D])}
    cache_k = din("cache_k", [DEPTH, PAST, 512])
    cache_v = din("cache_v", [DEPTH, PAST, 512])
    state_gdn = din("state_gdn", [DEPTH, 4, 128, 128])
    state_conv = din("state_conv", [DEPTH, 3, 1536])
    norm_g = din("norm_g", [DEPTH, 6, D])
    w1u = din("w1u", [DEPTH, D, 2 * FF]); w1d = din("w1d", [DEPTH, FF, D])
    w2u = din("w2u", [DEPTH, D, 2 * FF]); w2d = din("w2d", [DEPTH, FF, D])
    w_in = din("w_in", [DEPTH, D, IN_DIM])
    conv_w = din("conv_w", [DEPTH, 4, 1536])
    a_log = din("a_log", [DEPTH, 4]); dt_bias = din("dt_bias", [DEPTH, 4])
    gdn_gain = din("gdn_gain", [DEPTH, 128])
    w_bsb = din("w_bsb", [DEPTH, 512, D]); w_bgdn = din("w_bgdn", [DEPTH, 512, D])
    w_out = din("w_out", [DEPTH, D, D])
    c_ident = din("c_ident", [128, 128]); c_linc = din("c_linc", [128, 128])
    c_masks = din("c_masks", [128, 4, 512])
    c_neg = din("c_neg", [64, 64]); c_mstrict = din("c_mstrict", [64, 256]); c_i4 = din("c_i4", [64, 256])
    c_sel = din("c_sel", [4, 4, 128]); c_reset = din("c_reset", [4, 512])

    y_out = {'p': dout("y_p", [T_P, D]), 's': dout("y_s", [T_S, D])}
    k_out = {'p': dout("k_p", [DEPTH, T_P, 512]), 's': dout("k_s", [DEPTH, T_S, 512])}
    v_out = {'p': dout("v_p", [DEPTH, T_P, 512]), 's': dout("v_s", [DEPTH, T_S, 512])}
    g_out = {'p': dout("g_p", [DEPTH, 4, 128, 128]), 's': dout("g_s", [DEPTH, 4, 128, 128])}
    c_out = {'p': dout("c_p", [DEPTH, 3, 1536]), 's': dout("c_s", [DEPTH, 3, 1536])}

    RES = {g: dscr("res_" + g, [T, D]) for g, T in groups}
    QTs = {g: dscr("qt_" + g, [4, 128, T], BF16) for g, T in groups}
    KTs = {g: dscr("kt_" + g, [4, 128, T], BF16) for g, T in groups}
    RAWs = {g: dscr("raw_" + g, [12, 128, T]) for g, T in groups}
    ZTs = {g: dscr("zt_" + g, [4, 128, T]) for g, T in groups}
    BAs = {g: dscr("ba_" + g, [2, 4, T]) for g, T in groups}
    OSTs = {g: dscr("ost_" + g, [4, 128, T], BF16) for g, T in groups}
    OGTs = {g: dscr("ogt_" + g, [4, 128, T], BF16) for g, T in groups}

    PS = [nc.alloc_psum_tensor("psb%d" % i, [128, 512], F32) for i in range(8)]

    def pk(i):
        return ('ps', i)

    out_dma_keys = set()

    ident = nc.alloc_sbuf_tensor("ident", [128, 128], F32)
    ones_bf = nc.alloc_sbuf_tensor("ones_bf", [128, 128], BF16)
    linc_bf = nc.alloc_sbuf_tensor("linc_bf", [128, 128], BF16)
    op('sp', lambda e: e.dma_start(out=ident[:], in_=c_ident), w=['ident'], dma='c0')
    op('pool', lambda e: e.dma_start(out=linc_bf[:], in_=c_linc), w=['linc'], dma='c1')
    op('dve', lambda e: e.memset(ones_bf[:], 1.0), w=['ones'])

    def load_weight_bf16(es, name, src, kchunks, ncols, c0=0, key=None):
        t = es.enter_context(_sbuf_tensor(name, [128, kchunks, ncols], BF16))
        key = key or name
        for k in range(kchunks):
            op('pool', lambda e, k=k: e.dma_start(out=t[:, k, :], in_=src[k * 128:(k + 1) * 128, c0:c0 + ncols]),
               w=[key], dma='w_' + key)
        return t

    def load_gain_bc(es, name, src_row):
        t = es.enter_context(_sbuf_tensor(name, [128, D], F32))
        op('sp', lambda e: e.dma_start(out=t[:], in_=src_row.partition_broadcast(128)), w=[name], dma='g_' + name)
        return t

    def norm_to_T(xt, xkey, pt, nsub, gain, gkey, xnT, xnTkey, scr, tagbase):
        junk, ss, lnv, rstd, xn, nxs = scr
        for s in range(nsub):
            op('dve', lambda e, s=s: e.scalar_tensor_tensor(out=junk[0:pt, :], in0=xt[0:pt, s, :], scalar=1.0, in1=xt[0:pt, s, :],
                                                            op0=ALU.mult, op1=ALU.mult, accum_out=ss[0:pt, s:s + 1]),
               r=[xkey], w=['junk', ('ss', s)])
        op('act', lambda e: e.activation(out=lnv[0:pt, 0:nsub], in_=ss[0:pt, 0:nsub], func=AF.Ln, bias=EPS, scale=1.0 / D),
           r=[('ss', s) for s in range(nsub)], w=['lnv'])
        op('act', lambda e: e.activation(out=rstd[0:pt, 0:nsub], in_=lnv[0:pt, 0:nsub], func=AF.Exp, scale=-0.5),
           r=['lnv'], w=['rstd'])
        for s in range(nsub):
            xs = s % nxs
            op('dve', lambda e, s=s, xs=xs: e.scalar_tensor_tensor(out=xn[0:pt, xs, :], in0=xt[0:pt, s, :], scalar=rstd[0:pt, s:s + 1],
                                                                 in1=gain[0:pt, :], op0=ALU.mult, op1=ALU.mult),
               r=[xkey, 'rstd', gkey], w=[('xn', xs)])
            for half in range(2):
                b = 6 + half
                for i in range(4):
                    k = half * 4 + i
                    op('pe', lambda e, b=b, i=i, k=k, xs=xs: e.transpose(PS[b][:, i * 128:i * 128 + pt], xn[0:pt, xs, k * 128:(k + 1) * 128],
                                                                        ident[0:pt, 0:pt]),
                       r=[('xn', xs), 'ident'], w=[pk(b)], signal=(i == 3))
                eng = 'act' if half == 0 else 'dve'
                src = PS[b][:].rearrange("p (i t) -> p i t", i=4)[:, :, 0:pt]
                dst = xnT[:, half * 4:half * 4 + 4, s * pt:(s + 1) * pt]
                if eng == 'act':
                    op('act', lambda e, src=src, dst=dst: e.activation(out=dst, in_=src, func=AF.Copy), r=[pk(b)], w=[(xnTkey, s, half)])
                else:
                    op('dve', lambda e, src=src, dst=dst: e.tensor_copy(out=dst, in_=src), r=[pk(b)], w=[(xnTkey, s, half)])

    def xnT_keys(xnTkey, nsub):
        return [(xnTkey, s, h) for s in range(nsub) for h in range(2)]

    def alloc_norm_scr(es, slim=False):
        junk = es.enter_context(_sbuf_tensor("n_junk", [128, D], BF16 if slim else F32))
        ss = es.enter_context(_sbuf_tensor("n_ss", [128, 8], F32))
        lnv = es.enter_context(_sbuf_tensor("n_lnv", [128, 8], F32))
        rstd = es.enter_context(_sbuf_tensor("n_rstd", [128, 8], F32))
        xn = es.enter_context(_sbuf_tensor("n_xn", [128, 1 if slim else 2, D], F32))
        return (junk, ss, lnv, rstd, xn, 1 if slim else 2)

    def post_norm_residual(ycat, ykey, pt, xt, xkey, s, gain, gkey, scr2, coef):
        junk, ss2, lnv2, rstd2 = scr2
        op('dve', lambda e: e.scalar_tensor_tensor(out=junk[0:pt, :], in0=ycat[0:pt, :], scalar=1.0, in1=ycat[0:pt, :],
                                                   op0=ALU.mult, op1=ALU.mult, accum_out=ss2[0:pt, 0:1]),
           r=[ykey], w=['junk', 'ss2'])
        op('act', lambda e: e.activation(out=lnv2[0:pt, 0:1], in_=ss2[0:pt, 0:1], func=AF.Ln, bias=EPS, scale=1.0 / D),
           r=['ss2'], w=['lnv2'])
        op('act', lambda e: e.activation(out=rstd2[0:pt, 0:1], in_=lnv2[0:pt, 0:1], func=AF.Exp, scale=-0.5), r=['lnv2'], w=['rstd2'])
        op('dve', lambda e: e.scalar_tensor_tensor(out=ycat[0:pt, :], in0=ycat[0:pt, :], scalar=rstd2[0:pt, 0:1], in1=gain[0:pt, :],
                                                   op0=ALU.mult, op1=ALU.mult),
           r=[ykey, 'rstd2', gkey], w=[ykey])
        op('dve', lambda e: e.scalar_tensor_tensor(out=xt[0:pt, s, :], in0=ycat[0:pt, :], scalar=float(coef), in1=xt[0:pt, s, :],
                                                   op0=ALU.mult, op1=ALU.add),
           r=[ykey, xkey], w=[xkey])

    def ffn_phase(l, which, src, dst, final):
        wu = (w1u, w2u)[which][l]
        wd = (w1d, w2d)[which][l]
        gi0, gi1 = (0, 1) if which == 0 else (4, 5)
        with contextlib.ExitStack() as es:
            Wu = load_weight_bf16(es, "Wu", wu, KD, 2 * FF)
            Wd = load_weight_bf16(es, "Wd", wd, KF, D)
            g0 = load_gain_bc(es, "g0", norm_g[l, gi0])
            g1 = load_gain_bc(es, "g1", norm_g[l, gi1])
            scr = alloc_norm_scr(es, slim=True)
            ss2 = es.enter_context(_sbuf_tensor("f_ss2", [128, 1], F32))
            lnv2 = es.enter_context(_sbuf_tensor("f_lnv2", [128, 1], F32))
            rstd2 = es.enter_context(_sbuf_tensor("f_rstd2", [128, 1], F32))
            scr2 = (scr[0], ss2, lnv2, rstd2)
            nsubF = max(1, FT // 128)
            xts = [es.enter_context(_sbuf_tensor("f_x%d" % i, [128, nsubF, D], F32)) for i in range(2)]
            xnT = es.enter_context(_sbuf_tensor("f_xnT", [128, KD, FT], BF16))
            hT = es.enter_context(_sbuf_tensor("f_hT", [128, KF, FT], BF16))
            sg = [es.enter_context(_sbuf_tensor("f_sg%d" % i, [128, FT], F32)) for i in range(2)]
            ycat = [es.enter_context(_sbuf_tensor("f_y%d" % i, [128, D], F32)) for i in range(2)]
            tiles = []
            for g, T in groups:
                t0_ = 0
                while t0_ < T:
                    tt_ = min(FT, T - t0_)
                    tiles.append((g, t0_, tt_))
                    t0_ += tt_

            def load(i):
                g, t0, TT = tiles[i]
                pt = min(128, TT); nsub = TT // pt
                xt = xts[i % 2]
                op('sp', lambda e: e.dma_start(out=xt[0:pt, 0:nsub, :], in_=src[g][t0:t0 + TT, :].rearrange("(s p) d -> p s d", p=pt)),
                   r=[('res', g, t0)], w=[('fx', i % 2)], dma='fx%d' % (i % 2))
            load(0)
            cnt = 0
            for i, (g, t0, TT) in enumerate(tiles):
                if i + 1 < len(tiles):
                    load(i + 1)
                pt = min(128, TT); nsub = TT // pt
                xt = xts[i % 2]; xkey = ('fx', i % 2)
                norm_to_T(xt, xkey, pt, nsub, g0, 'g0', xnT, 'xnT', scr, 'f')
                xk = xnT_keys('xnT', nsub)
                for m in range(KF):
                    bg = m % 2; bu = 2 + m % 2
                    for k in range(KD):
                        op('pe', lambda e, k=k, m=m, bg=bg: e.matmul(PS[bg][:, 0:TT], lhsT=Wu[:, k, m * 128:(m + 1) * 128], rhs=xnT[:, k, 0:TT],
                                                                      start=(k == 0), stop=(k == KD - 1)),
                           r=xk + ['Wu'], w=[pk(bg)], signal=(k == KD - 1))
                    for k in range(KD):
                        op('pe', lambda e, k=k, m=m, bu=bu: e.matmul(PS[bu][:, 0:TT], lhsT=Wu[:, k, FF + m * 128:FF + (m + 1) * 128], rhs=xnT[:, k, 0:TT],
                                                                      start=(k == 0), stop=(k == KD - 1)),
                           r=xk + ['Wu'], w=[pk(bu)], signal=(k == KD - 1))
                    sgt = sg[m % 2]
                    op('act', lambda e, bg=bg, sgt=sgt: e.activation(out=sgt[:, 0:TT], in_=PS[bg][:, 0:TT], func=AF.Silu),
                       r=[pk(bg)], w=[('sg', m % 2)])
                    op('dve', lambda e, bu=bu, sgt=sgt, m=m: e.tensor_tensor(out=hT[:, m, 0:TT], in0=PS[bu][:, 0:TT], in1=sgt[:, 0:TT], op=ALU.mult),
                       r=[pk(bu), ('sg', m % 2)], w=[('hT', m)])
                hk = [('hT', m) for m in range(KF)]
                for s in range(nsub):
                    yc = ycat[cnt % 2]; ykey = ('ycat', cnt % 2); cnt += 1
                    for half in range(2):
                        b = 4 + half
                        for m in range(KF):
                            op('pe', lambda e, m=m, b=b, s=s, half=half: e.matmul(PS[b][0:pt, :], lhsT=hT[:, m, s * pt:(s + 1) * pt],
                                                                                 rhs=Wd[:, m, half * 512:(half + 1) * 512],
                                                                                 start=(m == 0), stop=(m == KF - 1)),
                               r=hk + ['Wd'], w=[pk(b)], signal=(m == KF - 1))
                        op('act', lambda e, b=b, half=half, yc=yc: e.activation(out=yc[0:pt, half * 512:(half + 1) * 512], in_=PS[b][0:pt, :], func=AF.Copy),
                           r=[pk(b)], w=[ykey])
                    post_norm_residual(yc, ykey, pt, xt, xkey, s, g1, 'g1', scr2, 0.5)
                dd = y_out if final else dst
                dkey = 'fo%d' % (i % 2)
                if final:
                    out_dma_keys.add(('dma', dkey))
                op('sp', lambda e, dd=dd: e.dma_start(out=dd[g][t0:t0 + TT, :].rearrange("(s p) d -> p s d", p=pt), in_=xt[0:pt, 0:nsub, :]),
                   r=[xkey], w=[('res', g, t0)], dma=dkey)
            em.barrier()

    def m1_phase(l):
        NCOL = 3592
        with contextlib.ExitStack() as es:
            Wi = load_weight_bf16(es, "Wi", w_in[l], KD, NCOL)
            g2 = load_gain_bc(es, "g2", norm_g[l, 2])
            scr = alloc_norm_scr(es)
            xts = [es.enter_context(_sbuf_tensor("m_x%d" % i, [128, 4, D], F32)) for i in range(2)]
            xnT = es.enter_context(_sbuf_tensor("m_xnT", [128, KD, 512], BF16))
            stg_f = [es.enter_context(_sbuf_tensor("m_sf%d" % i, [128, 512], F32)) for i in range(4)]
            stg_b = [es.enter_context(_sbuf_tensor("m_sb%d" % i, [128, 512], BF16)) for i in range(4)]
            cst = es.enter_context(_sbuf_tensor("m_cst", [4, 1536], F32))
            tiles = []
            for g, T in groups:
                TT = min(512, T)
                for ti in range(T // TT):
                    tiles.append((g, ti * TT, TT, ti == T // TT - 1))

            def load(i):
                g, t0, TT, _ = tiles[i]
                pt = min(128, TT); nsub = TT // pt
                xt = xts[i % 2]
                op('sp', lambda e: e.dma_start(out=xt[0:pt, 0:nsub, :], in_=RES[g][t0:t0 + TT, :].rearrange("(s p) d -> p s d", p=pt)),
                   w=[('mx', i % 2)], dma='mx%d' % (i % 2))
            load(0)
            cf = 0; cb = 0; pb = 0
            for i, (g, t0, TT, last) in enumerate(tiles):
                if i + 1 < len(tiles):
                    load(i + 1)
                pt = min(128, TT); nsub = TT // pt
                xt = xts[i % 2]; xkey = ('mx', i % 2)
                norm_to_T(xt, xkey, pt, nsub, g2, 'g2', xnT, 'xnT', scr, 'm')
                xk = xnT_keys('xnT', nsub)
                for s in range(nsub):
                    for (c0, outd) in ((512, k_out[g]), (1024, v_out[g])):
                        b = pb % 6; pb += 1
                        for k in range(KD):
                            op('pe', lambda e, k=k, b=b, c0=c0, s=s: e.matmul(PS[b][0:pt, :], lhsT=xnT[:, k, s * pt:(s + 1) * pt], rhs=Wi[:, k, c0:c0 + 512],
                                                                             start=(k == 0), stop=(k == KD - 1)),
                               r=xk + ['Wi'], w=[pk(b)], signal=(k == KD - 1))
                        sf = stg_f[cf % 4]; sk = ('sf', cf % 4); dk = 'sf%d' % (cf % 4); cf += 1
                        op('act', lambda e, b=b, sf=sf: e.activation(out=sf[0:pt, :], in_=PS[b][0:pt, :], func=AF.Copy), r=[pk(b)], w=[sk])
                        out_dma_keys.add(('dma', dk))
                        op('sp', lambda e, sf=sf, outd=outd, s=s: e.dma_start(out=outd[l, t0 + s * pt:t0 + (s + 1) * pt, :], in_=sf[0:pt, :]),
                           r=[sk], dma=dk)
                fm = []
                for p in range(4):
                    fm.append((p * 128, 128, 'b', 0.125, QTs[g][p]))
                for p in range(4):
                    fm.append((512 + p * 128, 128, 'b', 1.0, KTs[g][p]))
                for c in range(12):
                    fm.append((1536 + c * 128, 128, 'f', 1.0, RAWs[g][c]))
                for h in range(4):
                    fm.append((3072 + h * 128, 128, 'f', 1.0, ZTs[g][h]))
                fm.append((3584, 4, 'f', 1.0, BAs[g][0]))
                fm.append((3588, 4, 'f', 1.0, BAs[g][1]))
                for (c0, mrows, kind, scale, dst) in fm:
                    b = pb % 6; pb += 1
                    for k in range(KD):
                        op('pe', lambda e, k=k, b=b, c0=c0, mrows=mrows: e.matmul(PS[b][0:mrows, 0:TT], lhsT=Wi[:, k, c0:c0 + mrows], rhs=xnT[:, k, 0:TT],
                                                                                 start=(k == 0), stop=(k == KD - 1)),
                           r=xk + ['Wi'], w=[pk(b)], signal=(k == KD - 1))
                    if kind == 'b':
                        st = stg_b[cb % 4]; sk = ('sb', cb % 4); dk = 'sb%d' % (cb % 4); cb += 1
                        op('dve', lambda e, b=b, st=st, scale=scale, mrows=mrows: e.tensor_scalar(out=st[0:mrows, 0:TT], in0=PS[b][0:mrows, 0:TT],
                                                                                                scalar1=float(scale), scalar2=None, op0=ALU.mult),
                           r=[pk(b)], w=[sk])
                    else:
                        st = stg_f[cf % 4]; sk = ('sf', cf % 4); dk = 'sf%d' % (cf % 4); cf += 1
                        op('act', lambda e, b=b, st=st, mrows=mrows: e.activation(out=st[0:mrows, 0:TT], in_=PS[b][0:mrows, 0:TT], func=AF.Copy),
                           r=[pk(b)], w=[sk])
                    op('sp', lambda e, st=st, dst=dst, mrows=mrows: e.dma_start(out=dst[0:mrows, t0:t0 + TT], in_=st[0:mrows, 0:TT]), r=[sk], dma=dk)
                if last:
                    for j in range(3):
                        b = pb % 6; pb += 1
                        for k in range(KD):
                            op('pe', lambda e, k=k, b=b, j=j: e.matmul(PS[b][0:3, :], lhsT=xnT[:, k, TT - 3:TT], rhs=Wi[:, k, 1536 + j * 512:1536 + (j + 1) * 512],
                                                                      start=(k == 0), stop=(k == KD - 1)),
                               r=xk + ['Wi'], w=[pk(b)], signal=(k == KD - 1))
                        op('act', lambda e, b=b, j=j: e.activation(out=cst[0:3, j * 512:(j + 1) * 512], in_=PS[b][0:3, :], func=AF.Copy), r=[pk(b)], w=['cst'])
                    out_dma_keys.add(('dma', 'cst'))
                    op('sp', lambda e: e.dma_start(out=c_out[g][l], in_=cst[0:3, :]), r=['cst'], dma='cst')
            em.barrier()

    def att_phase(l):
        with contextlib.ExitStack() as es:
            TKmax = max(T_P, PAST + 128)
            nblk_max = max(T_P // 128, PAST // 128 + 1)
            KT = es.enter_context(_sbuf_tensor("a_KT", [128, TKmax], BF16))
            NKT = es.enter_context(_sbuf_tensor("a_NKT", [128, TKmax], BF16))
            QT = es.enter_context(_sbuf_tensor("a_QT", [128, max(T_P, 128)], BF16))
            Vp = es.enter_context(_sbuf_tensor("a_Vp", [128, nblk_max, 2, 128], BF16))
            masks = es.enter_context(_sbuf_tensor("a_masks", [128, 4, 512], BF16))
            kc = es.enter_context(_sbuf_tensor("a_kc", [128, max(PAST // 128, 1), 128], F32))
            et = [es.enter_context(_sbuf_tensor("a_e%d" % i, [128, 512], F32)) for i in range(2)]
            spt = [es.enter_context(_sbuf_tensor("a_sp%d" % i, [128, 512], BF16)) for i in range(4)]
            tt = [es.enter_context(_sbuf_tensor("a_t%d" % i, [128, 512], F32)) for i in range(4)]
            at = [es.enter_context(_sbuf_tensor("a_a%d" % i, [128, 512], BF16)) for i in range(4)]
            carry = [es.enter_context(_sbuf_tensor("a_c%d" % i, [128, 512], F32)) for i in range(2)]
            ost = [es.enter_context(_sbuf_tensor("a_o%d" % i, [128, 512], BF16)) for i in range(2)]
            op('pool', lambda e: e.dma_start(out=masks[:], in_=c_masks), w=['masks'], dma='amask')
            op('dve', lambda e: e.memset(Vp[:], 0.0), w=['Vp'])
            op('dve', lambda e: e.memset(KT[:], 0.0), w=['KT'])
            op('dve', lambda e: e.memset(QT[:], 0.0), w=['QT'])
            ocnt = 0
            bcnt = 0
            for g, T in groups:
                if g not in cfg.get('att_groups', 'ps'):
                    continue
                QW = min(512, max(T, 128))
                for p in range(4):
                    if g == 'p':
                        Tk = T
                        nseg = max(1, T // 2048)
                        seg = T // nseg
                        for q4 in range(nseg):
                            op('sp', lambda e, q4=q4: e.dma_start(out=KT[:, q4 * seg:(q4 + 1) * seg], in_=KTs[g][p][:, q4 * seg:(q4 + 1) * seg]),
                               w=['KT'], dma='aKT')
                        blocks_all = [(b * 128, 128, b) for b in range(T // 128)]
                        vsrc = v_out[g][l]
                        nb = T // 128
                        for j in range(2):
                            for b0 in range(0, nb, 8):
                                b1 = min(nb, b0 + 8)
                                op('pool', lambda e, j=j, b0=b0, b1=b1: e.dma_start(
                                    out=Vp[:, b0:b1, j, j * 64:(j + 1) * 64],
                                    in_=vsrc[b0 * 128:b1 * 128, (2 * p + j) * 64:(2 * p + j + 1) * 64].rearrange("(b k) d -> k b d", k=128)),
                                    w=['Vp'], dma='aV')
                    else:
                        Tk = PAST + 128
                        npb = PAST // 128
                        op('sp', lambda e: e.dma_start(out=kc[:, 0:npb, :], in_=cache_k[l][:, p * 128:(p + 1) * 128].rearrange("(b k) d -> k b d", k=128)),
                           w=['kc'], dma='akc')
                        for b in range(npb):
                            pb_ = 6 + b % 2
                            op('pe', lambda e, b=b, pb_=pb_: e.transpose(PS[pb_][:, 0:128], kc[:, b, :], ident[:]), r=['kc', 'ident'], w=[pk(pb_)])
                            op('act', lambda e, b=b, pb_=pb_: e.activation(out=KT[:, b * 128:(b + 1) * 128], in_=PS[pb_][:, 0:128], func=AF.Copy),
                               r=[pk(pb_)], w=['KT'])
                        op('sp', lambda e: e.dma_start(out=KT[:, PAST:PAST + T], in_=KTs[g][p][:, 0:T]), w=['KT'], dma='aKT')
                        for j in range(2):
                            op('pool', lambda e, j=j: e.dma_start(
                                out=Vp[:, 0:npb, j, j * 64:(j + 1) * 64],
                                in_=cache_v[l][:, (2 * p + j) * 64:(2 * p + j + 1) * 64].rearrange("(b k) d -> k b d", k=128)),
                                w=['Vp'], dma='aV')
                            op('pool', lambda e, j=j: e.dma_start(
                                out=Vp[0:T, npb, j, j * 64:(j + 1) * 64],
                                in_=v_out[g][l][0:T, (2 * p + j) * 64:(2 * p + j + 1) * 64]),
                                w=['Vp'], dma='aV')
                    op('sp', lambda e: e.dma_start(out=QT[:, 0:T], in_=QTs[g][p][:, 0:T]), w=['QT'], dma='aQT')
                    op('dve', lambda e: e.tensor_scalar(out=NKT[:, 0:Tk], in0=KT[:, 0:Tk], scalar1=-1.0, scalar2=None, op0=ALU.mult),
                       r=['KT'], w=['NKT'])
                    for qt in range(max(1, T // QW)):
                        if cfg.get('att_skipq', False):
                            continue
                        q0 = qt * QW
                        if g == 'p':
                            nvis = (q0 + QW) // 128
                            blks = []
                            for kb in range(nvis - 1, -1, -1):
                                j = kb - q0 // 128
                                blks.append((kb * 128, 128, kb, j if j >= 0 else None))
                        else:
                            blks = [(PAST, 128, PAST // 128, 0)] + [(b * 128, 128, b, None) for b in range(PAST // 128 - 1, -1, -1)]
                        nb_ = len(blks)
                        for hh in range(2):
                            op('dve', lambda e, hh=hh: e.memset(carry[hh][:, 0:QW], 0.0), w=[('carry', hh)])
                        O = PS[6]

                        def pe_A(bi):
                            (k0, nk, vb, mj) = blks[bi]
                            for hh in range(2):
                                hs = slice(hh * 64, (hh + 1) * 64)
                                op('pe', lambda e, hh=hh, hs=hs: e.matmul(PS[hh][0:nk, 0:QW], lhsT=KT[hs, k0:k0 + nk], rhs=QT[hs, q0:q0 + QW], start=True, stop=True),
                                   r=['KT', 'QT'], w=[pk(hh)])

                        def pe_BC(bi):
                            (k0, nk, vb, mj) = blks[bi]
                            lastb = bi == nb_ - 1
                            for hh in range(2):
                                hs = slice(hh * 64, (hh + 1) * 64)
                                sk = ('sp', bi % 2, hh); spx = spt[(bi % 2) * 2 + hh]
                                op('pe', lambda e, hh=hh, spx=spx: e.matmul(PS[2 + hh][0:nk, 0:QW], lhsT=linc_bf[0:nk, 0:nk], rhs=spx[0:nk, 0:QW], start=True, stop=False),
                                   r=[sk, 'linc'], w=[pk(2 + hh)], signal=False)
                                op('pe', lambda e, hh=hh, hs=hs: e.matmul(PS[2 + hh][0:nk, 0:QW], lhsT=NKT[hs, k0:k0 + nk], rhs=QT[hs, q0:q0 + QW], start=False, stop=True),
                                   r=['NKT', 'QT'], w=[pk(2 + hh)])
                                if not lastb:
                                    op('pe', lambda e, hh=hh, spx=spx: e.matmul(PS[4 + hh][:, 0:QW], lhsT=ones_bf[0:nk, :], rhs=spx[0:nk, 0:QW], start=True, stop=True),
                                       r=[sk, 'ones'], w=[pk(4 + hh)])

                        def dve_t(bi):
                            (k0, nk, vb, mj) = blks[bi]
                            for hh in range(2):
                                tx = tt[(bi % 2) * 2 + hh]
                                op('dve', lambda e, hh=hh, tx=tx: e.tensor_tensor(out=tx[0:nk, 0:QW], in0=PS[2 + hh][0:nk, 0:QW], in1=carry[hh][0:nk, 0:QW], op=ALU.add),
                                   r=[pk(2 + hh), ('carry', hh)], w=[('t', bi % 2, hh)])

                        def dve_cu(bi):
                            if bi == nb_ - 1:
                                return
                            for hh in range(2):
                                op('dve', lambda e, hh=hh: e.tensor_tensor(out=carry[hh][:, 0:QW], in0=PS[4 + hh][:, 0:QW], in1=carry[hh][:, 0:QW], op=ALU.add),
                                   r=[pk(4 + hh), ('carry', hh)], w=[('carry', hh)])

                        def act_esp(bi):
                            (k0, nk, vb, mj) = blks[bi]
                            for hh in range(2):
                                op('act', lambda e, hh=hh: e.activation(out=et[hh][0:nk, 0:QW], in_=PS[hh][0:nk, 0:QW], func=AF.Exp), r=[pk(hh)], w=[('e', hh)])
                            for hh in range(2):
                                sk = ('sp', bi % 2, hh); spx = spt[(bi % 2) * 2 + hh]
                                op('act', lambda e, hh=hh, spx=spx: e.activation(out=spx[0:nk, 0:QW], in_=et[hh][0:nk, 0:QW], func=AF.Ln, bias=1.0, scale=1.0),
                                   r=[('e', hh)], w=[sk])
                            if mj is not None:
                                for hh in range(2):
                                    sk = ('sp', bi % 2, hh); spx = spt[(bi % 2) * 2 + hh]
                                    op('pool', lambda e, spx=spx: e.tensor_tensor(out=spx[0:nk, 0:QW], in0=spx[0:nk, 0:QW], in1=masks[0:nk, mj, 0:QW], op=ALU.mult),
                                       r=[sk, 'masks'], w=[sk])

                        def act_a(bi):
                            (k0, nk, vb, mj) = blks[bi]
                            for hh in range(2):
                                tx = tt[(bi % 2) * 2 + hh]; ax = at[(bi % 2) * 2 + hh]
                                op('act', lambda e, tx=tx, ax=ax: e.activation(out=ax[0:nk, 0:QW], in_=tx[0:nk, 0:QW], func=AF.Exp, scale=-1.0),
                                   r=[('t', bi % 2, hh)], w=[('a', bi % 2, hh)])
                            if mj is not None:
                                for hh in range(2):
                                    ax = at[(bi % 2) * 2 + hh]
                                    op('pool', lambda e, ax=ax: e.tensor_tensor(out=ax[0:nk, 0:QW], in0=ax[0:nk, 0:QW], in1=masks[0:nk, mj, 0:QW], op=ALU.mult),
                                       r=[('a', bi % 2, hh), 'masks'], w=[('a', bi % 2, hh)])

                        def pe_O(bi):
                            (k0, nk, vb, mj) = blks[bi]
                            for hh in range(2):
                                ax = at[(bi % 2) * 2 + hh]
                                first = (bi == 0 and hh == 0); last_ = (bi == nb_ - 1 and hh == 1)
                                op('pe', lambda e, hh=hh, ax=ax: e.matmul(O[:, 0:QW], lhsT=Vp[0:nk, vb, hh, :], rhs=ax[0:nk, 0:QW], start=first, stop=last_),
                                   r=[('a', bi % 2, hh), 'Vp'], w=[pk(6)], signal=last_)
                        pe_A(0)
                        for s_ in range(nb_ + 2):
                            if 1 <= s_ <= nb_:
                                pe_BC(s_ - 1)
                                dve_t(s_ - 1)
                            if s_ < nb_:
                                act_esp(s_)
                            if 1 <= s_ <= nb_:
                                act_a(s_ - 1)
                                dve_cu(s_ - 1)
                            if 2 <= s_ <= nb_ + 1:
                                pe_O(s_ - 2)
                            if s_ + 1 < nb_:
                                pe_A(s_ + 1)
                        os_ = ost[ocnt % 2]; okey = ('ost', ocnt % 2); dk = 'ost%d' % (ocnt % 2); ocnt += 1
                        op('act', lambda e, os_=os_: e.activation(out=os_[:, 0:QW], in_=PS[6][:, 0:QW], func=AF.Copy), r=[pk(6)], w=[okey])
                        QS = min(QW, T)
                        op('sp', lambda e, os_=os_: e.dma_start(out=OSTs[g][p][:, q0:q0 + QS], in_=os_[:, 0:QS]), r=[okey], dma=dk)
            em.barrier()

    def gdn_phase(l):
        with contextlib.ExitStack() as es:
            def sb(name, shape, dt=F32):
                return es.enter_context(_sbuf_tensor(name, shape, dt))
            convw = sb("g_convw", [128, 12, 4])
            neg = sb("g_neg", [64, 64]); mstrict = sb("g_mstrict", [64, 256]); i4 = sb("g_i4", [64, 256])
            sel = sb("g_sel", [4, 4, 128]); reset = sb("g_reset", [4, 512])
            alog = sb("g_alog", [4, 1]); dtb = sb("g_dtb", [4, 1]); negA = sb("g_negA", [4, 1]); gng = sb("g_gng", [128, 1])
            S = sb("g_S", [128, 4, 128])
            raw = sb("g_raw", [128, 12, 515])
            qkv = sb("g_qkv", [128, 12, 512])
            ytmp = sb("g_ytmp", [128, 512])
            sqb = sb("g_sqb", [128, 512], BF16)
            lnt = sb("g_lnt", [128, 512])
            rows_b = sb("g_rb", [4, 512]); rows_a = sb("g_ra", [4, 512])
            r_beta = sb("g_rbeta", [4, 512]); r_g = sb("g_rg", [4, 512]); r_gc = sb("g_rgc", [4, 512]); r_ngc = sb("g_rngc", [4, 512])
            r_gl = sb("g_rgl", [4, 512]); r_eg = sb("g_reg", [4, 512]); r_kes = sb("g_rkes", [4, 512]); r_cd = sb("g_rcd", [4, 512])
            r_beg = sb("g_rbeg", [4, 512]); r_tmp = sb("g_rtmp", [4, 512])
            LL = sb("g_LL", [2, 4, 512]); RR = sb("g_RR", [2, 4, 512])
            vbT = sb("g_vbT", [128, 4, 512]); kbT = sb("g_kbT", [128, 4, 512], BF16); kbgT = sb("g_kbgT", [128, 4, 512], BF16)
            qdT = sb("g_qdT", [128, 4, 512], BF16); keT = sb("g_keT", [128, 4, 512]); cdc = sb("g_cdc", [128, 4, 8])
            qk16 = sb("g_qk16", [128, 8, 512], BF16); Sb = sb("g_Sb", [128, 4, 128], BF16)
            zT = sb("g_zT", [128, 4, 512]); OT = sb("g_OT", [128, 4, 512]); ogs = sb("g_ogs", [128, 4, 512], BF16)
            gmS = [sb("g_gm%d" % i, [64, 256]) for i in range(2)]; BmS = [sb("g_B%d" % i, [64, 256]) for i in range(2)]
            BTmS = [sb("g_BT%d" % i, [64, 256]) for i in range(2)]
            PmS = [[sb("g_P%d_%d" % (j, i), [64, 256]) for i in range(2)] for j in range(2)]
            PTmS = [[sb("g_PT%d_%d" % (j, i), [64, 256]) for i in range(2)] for j in range(2)]
            XmS = [[sb("g_X%d_%d" % (j, i), [64, 256]) for i in range(2)] for j in range(2)]
            QKmO = [sb("g_QKm%d" % i, [64, 256], BF16) for i in range(4)]; XO = [sb("g_XO%d" % i, [64, 256], BF16) for i in range(4)]
            vbm = sb("g_vb", [64, 4, 128]); kem = sb("g_ke", [64, 4, 128], BF16); rhs2 = sb("g_rhs2", [64, 4, 128], BF16); vnew = sb("g_vnew", [64, 4, 128], BF16)

            for wi_ in range(4):
                op('sp', lambda e, wi_=wi_: e.dma_start(out=convw[:, :, wi_], in_=conv_w[l, wi_].rearrange("(c p) -> p c", p=128), allow_slow_non_contiguous=True),
                   w=['convw'], dma='gc0')
            for (t, s_, kname) in ((neg, c_neg, 'neg'), (mstrict, c_mstrict, 'mstrict'), (i4, c_i4, 'i4'), (sel, c_sel, 'sel'), (reset, c_reset, 'reset')):
                op('sp', lambda e, t=t, s_=s_: e.dma_start(out=t[:], in_=s_), w=[kname], dma='gc0')
            op('sp', lambda e: e.dma_start(out=alog[:], in_=a_log[l].rearrange("(h o) -> h o", o=1)), w=['alog'], dma='gc0')
            op('sp', lambda e: e.dma_start(out=dtb[:], in_=dt_bias[l].rearrange("(h o) -> h o", o=1)), w=['dtb'], dma='gc0')
            op('sp', lambda e: e.dma_start(out=gng[:], in_=gdn_gain[l].rearrange("(h o) -> h o", o=1)), w=['gng'], dma='gc0')
            op('act', lambda e: e.activation(out=negA[:], in_=alog[:], func=AF.Exp), r=['alog'], w=['negA'])
            op('dve', lambda e: e.tensor_scalar(out=negA[:], in0=negA[:], scalar1=-1.0, scalar2=None, op0=ALU.mult), r=['negA'], w=['negA'])
            op('dve', lambda e: e.memset(LL[:], 1.0), w=['LL'])
            op('dve', lambda e: e.memset(RR[:], 1.0), w=['RR'])
            pbc = [0]

            def nb():
                pbc[0] += 1
                return pbc[0] % 8

            for g, T in groups:
                TT = min(512, T)
                nch = TT // 64
                if g == 'p':
                    op('dve', lambda e: e.memset(S[:], 0.0), w=['S0', 'S1', 'S2', 'S3'])
                else:
                    op('sp', lambda e: e.dma_start(out=S[:], in_=state_gdn[l].rearrange("h k v -> k h v")), w=['S0', 'S1', 'S2', 'S3'], dma='gS')
                op('pool', lambda e: e.tensor_copy(out=Sb[:], in_=S[:]), r=['S0', 'S1', 'S2', 'S3'], w=['Sb'])
                def partA(ti_):
                    t0_ = ti_ * TT
                    if ti_ == 0:
                        if g == 'p':
                            op('dve', lambda e: e.memset(raw[:, :, 0:3], 0.0), w=['raw'])
                        else:
                            for wi_ in range(3):
                                op('sp', lambda e, wi_=wi_: e.dma_start(out=raw[:, :, wi_], in_=state_conv[l, wi_].rearrange("(c p) -> p c", p=128),
                                                                        allow_slow_non_contiguous=True), w=['raw'], dma='graw')
                        for c in range(12):
                            op('sp', lambda e, c=c: e.dma_start(out=raw[:, c, 3:3 + TT], in_=RAWs[g][c][:, 0:TT]), w=['raw'], dma='graw')
                    else:
                        for c in range(12):
                            op('sp', lambda e, c=c: e.dma_start(out=raw[:, c, 0:3 + TT], in_=RAWs[g][c][:, t0_ - 3:t0_ + TT]), w=['raw'], dma='graw')
                    yield
                    for c in range(12):
                        op('dve', lambda e, c=c: e.tensor_scalar(out=qkv[:, c, 0:TT], in0=raw[:, c, 0:TT], scalar1=convw[:, c, 0:1], scalar2=None, op0=ALU.mult),
                           r=['raw', 'convw'], w=[('qkv', c)])
                        for i in range(1, 4):
                            op('dve', lambda e, c=c, i=i: e.scalar_tensor_tensor(out=qkv[:, c, 0:TT], in0=raw[:, c, i:i + TT], scalar=convw[:, c, i:i + 1],
                                                                               in1=qkv[:, c, 0:TT], op0=ALU.mult, op1=ALU.add),
                               r=['raw', 'convw', ('qkv', c)], w=[('qkv', c)])
                        yield

                def silu_all():
                    for c3 in range(0, 12, 4):
                        op('act', lambda e, c3=c3: e.activation(out=qkv[:, c3:c3 + 4, 0:TT], in_=qkv[:, c3:c3 + 4, 0:TT], func=AF.Silu),
                           r=[('qkv', c) for c in range(c3, c3 + 4)], w=[('qkv', c) for c in range(c3, c3 + 4)])

                for ti in range(T // TT):
                    t0 = ti * TT
                    if ti == 0:
                        for _ in partA(0):
                            pass
                        silu_all()
                    op('sp', lambda e: e.dma_start(out=rows_b[:, 0:TT], in_=BAs[g][0][:, t0:t0 + TT]), w=['rows_b'], dma='grow')
                    op('sp', lambda e: e.dma_start(out=rows_a[:, 0:TT], in_=BAs[g][1][:, t0:t0 + TT]), w=['rows_a'], dma='grow')
                    for h in range(4):
                        op('sp', lambda e, h=h: e.dma_start(out=zT[:, h, 0:TT], in_=ZTs[g][h][:, t0:t0 + TT]), w=['zT'], dma='gz')
                    for c in range(8):
                        b = nb()
                        op('dve', lambda e, c=c: e.tensor_tensor(out=sqb[:, 0:TT], in0=qkv[:, c, 0:TT], in1=qkv[:, c, 0:TT], op=ALU.mult),
                           r=[('qkv', c)], w=['sqb'])
                        op('pe', lambda e, b=b: e.matmul(PS[b][:, 0:TT], lhsT=ones_bf[:], rhs=sqb[:, 0:TT], start=True, stop=True), r=['sqb', 'ones'], w=[pk(b)])
                        op('act', lambda e, b=b: e.activation(out=lnt[:, 0:TT], in_=PS[b][:, 0:TT], func=AF.Ln, bias=EPS, scale=1.0), r=[pk(b)], w=['lnt'])
                        bias = float(-0.5 * np.log(128.0)) if c < 4 else 0.0
                        op('act', lambda e, bias=bias: e.activation(out=lnt[:, 0:TT], in_=lnt[:, 0:TT], func=AF.Exp, scale=-0.5, bias=bias), r=['lnt'], w=['lnt'])
                        op('dve', lambda e, c=c: e.tensor_tensor(out=qkv[:, c, 0:TT], in0=qkv[:, c, 0:TT], in1=lnt[:, 0:TT], op=ALU.mult),
                           r=[('qkv', c), 'lnt'], w=[('qkv', c)])
                        op('pool', lambda e, c=c: e.tensor_copy(out=qk16[:, c, 0:TT], in_=qkv[:, c, 0:TT]), r=[('qkv', c)], w=[('qk16', c)])
                    op('act', lambda e: e.activation(out=r_beta[:, 0:TT], in_=rows_b[:, 0:TT], func=AF.Exp, scale=-1.0), r=['rows_b'], w=['r_beta'])
                    op('dve', lambda e: e.tensor_scalar(out=r_beta[:, 0:TT], in0=r_beta[:, 0:TT], scalar1=1.0, scalar2=None, op0=ALU.add), r=['r_beta'], w=['r_beta'])
                    op('dve', lambda e: e.reciprocal(out=r_beta[:, 0:TT], in_=r_beta[:, 0:TT]), r=['r_beta'], w=['r_beta'])
                    op('act', lambda e: e.activation(out=r_tmp[:, 0:TT], in_=rows_a[:, 0:TT], func=AF.Exp, bias=dtb[:, 0:1], scale=1.0), r=['rows_a', 'dtb'], w=['r_tmp'])
                    op('act', lambda e: e.activation(out=r_tmp[:, 0:TT], in_=r_tmp[:, 0:TT], func=AF.Ln, bias=1.0, scale=1.0), r=['r_tmp'], w=['r_tmp'])
                    op('dve', lambda e: e.tensor_scalar(out=r_g[:, 0:TT], in0=r_tmp[:, 0:TT], scalar1=negA[:, 0:1], scalar2=None, op0=ALU.mult),
                       r=['r_tmp', 'negA'], w=['r_g'])
                    op('dve', lambda e: e.tensor_tensor_scan(out=r_gc[:, 0:TT], data0=reset[:, 0:TT], data1=r_g[:, 0:TT], initial=0.0, op0=ALU.mult, op1=ALU.add),
                       r=['r_g', 'reset'], w=['r_gc'])
                    op('dve', lambda e: e.tensor_scalar(out=r_ngc[:, 0:TT], in0=r_gc[:, 0:TT], scalar1=-1.0, scalar2=None, op0=ALU.mult), r=['r_gc'], w=['r_ngc'])
                    gc3 = r_gc[:, 0:TT].rearrange("h (n c) -> h n c", c=64)
                    op('dve', lambda e: e.tensor_copy(out=r_gl[:, 0:TT].rearrange("h (n c) -> h n c", c=64), in_=gc3[:, :, 63:64].to_broadcast([4, nch, 64])),
                       r=['r_gc'], w=['r_gl'])
                    op('act', lambda e: e.activation(out=r_eg[:, 0:TT], in_=r_gc[:, 0:TT], func=AF.Exp), r=['r_gc'], w=['r_eg'])
                    op('dve', lambda e: e.tensor_tensor(out=r_tmp[:, 0:TT], in0=r_gl[:, 0:TT], in1=r_gc[:, 0:TT], op=ALU.subtract), r=['r_gl', 'r_gc', 'r_tmp'], w=['r_tmp'])
                    op('act', lambda e: e.activation(out=r_kes[:, 0:TT], in_=r_tmp[:, 0:TT], func=AF.Exp), r=['r_tmp'], w=['r_kes'])
                    op('act', lambda e: e.activation(out=r_cd[:, 0:TT], in_=r_gl[:, 0:TT], func=AF.Exp), r=['r_gl'], w=['r_cd'])
                    op('dve', lambda e: e.tensor_tensor(out=r_beg[:, 0:TT], in0=r_beta[:, 0:TT], in1=r_eg[:, 0:TT], op=ALU.mult), r=['r_beta', 'r_eg'], w=['r_beg'])
                    for h in range(4):
                        op('sp', lambda e, h=h: e.dma_start(out=LL[1:2, h, 0:TT], in_=r_ngc[h:h + 1, 0:TT]), r=['r_ngc'], w=['LL'], dma='gLL')
                        op('sp', lambda e, h=h: e.dma_start(out=RR[0:1, h, 0:TT], in_=r_gc[h:h + 1, 0:TT]), r=['r_gc'], w=['RR'], dma='gLL')
                    for h in range(4):
                        def bc(rows, rkey):
                            b = nb()
                            op('pe', lambda e: e.matmul(PS[b][:, 0:TT], lhsT=sel[:, h, :], rhs=rows[:, 0:TT], start=True, stop=True), r=[rkey, 'sel'], w=[pk(b)])
                            return b
                        b = bc(r_beta, 'r_beta')
                        op('dve', lambda e: e.tensor_tensor(out=vbT[:, h, 0:TT], in0=qkv[:, 8 + h, 0:TT], in1=PS[b][:, 0:TT], op=ALU.mult),
                           r=[pk(b), ('qkv', 8 + h)], w=['vbT'])
                        op('dve', lambda e: e.tensor_tensor(out=kbT[:, h, 0:TT], in0=qkv[:, 4 + h, 0:TT], in1=PS[b][:, 0:TT], op=ALU.mult),
                           r=[pk(b), ('qkv', 4 + h)], w=['kbT'])
                        b = bc(r_beg, 'r_beg')
                        op('dve', lambda e: e.tensor_tensor(out=kbgT[:, h, 0:TT], in0=qkv[:, 4 + h, 0:TT], in1=PS[b][:, 0:TT], op=ALU.mult),
                           r=[pk(b), ('qkv', 4 + h)], w=['kbgT'])
                        b = bc(r_eg, 'r_eg')
                        op('dve', lambda e: e.tensor_tensor(out=qdT[:, h, 0:TT], in0=qkv[:, h, 0:TT], in1=PS[b][:, 0:TT], op=ALU.mult),
                           r=[pk(b), ('qkv', h)], w=['qdT'])
                        b = bc(r_kes, 'r_kes')
                        op('dve', lambda e: e.tensor_tensor(out=keT[:, h, 0:TT], in0=qkv[:, 4 + h, 0:TT], in1=PS[b][:, 0:TT], op=ALU.mult),
                           r=[pk(b), ('qkv', 4 + h)], w=['keT'])
                        b = bc(r_cd, 'r_cd')
                        op('act', lambda e: e.activation(out=cdc[:, h, 0:nch], in_=PS[b][:, 0:TT].rearrange("p (n c) -> p n c", c=64)[:, :, 0], func=AF.Copy),
                           r=[pk(b)], w=['cdc'])
                    qk_all = [('qkv', c) for c in range(12)]
                    qk16_all = [('qk16', c) for c in range(8)]
                    Sk = ['S0', 'S1', 'S2', 'S3']

                    def prep(n):
                        cs = slice(n * 64, (n + 1) * 64)
                        st = n % 2; o4 = n % 4
                        c0 = 0
                        PB = {0: 3 * st, 3: 3 * st, 1: 3 * st + 1, 4: 3 * st + 1, 2: 3 * st + 2, 5: 3 * st + 2}
                        gm, Bm, BTm, Pm, PTm, Xm = gmS[st], BmS[st], BTmS[st], PmS[st], PTmS[st], XmS[st]
                        QKm = QKmO[o4]

                        def K(name, *a):
                            return (name, st) + a

                        def pp(b):
                            return pk(PB[b])
                        for h in range(4):
                            hc = slice(c0 + h * 64, c0 + (h + 1) * 64)
                            op('pe', lambda e, h=h, hc=hc: e.matmul(PS[PB[0]][0:64, hc], lhsT=LL[0:2, h, cs], rhs=RR[0:2, h, cs], start=True, stop=False),
                               r=['LL', 'RR'], w=[pp(0)], signal=False)
                            op('pe', lambda e, h=h, hc=hc: e.matmul(PS[PB[0]][0:64, hc], lhsT=ident[0:64, 0:64], rhs=neg[:, :], start=False, stop=True),
                               r=['ident', 'neg'], w=[pp(0)], signal=(h == 3))
                        for h in range(4):
                            hc = slice(c0 + h * 64, c0 + (h + 1) * 64)
                            op('pe', lambda e, h=h, hc=hc: e.matmul(PS[PB[1]][0:64, hc], lhsT=qk16[:, 4 + h, cs], rhs=kbT[:, h, cs], start=True, stop=True),
                               r=qk16_all + ['kbT'], w=[pp(1)], signal=(h == 3))
                        for h in range(4):
                            hc = slice(c0 + h * 64, c0 + (h + 1) * 64)
                            op('pe', lambda e, h=h, hc=hc: e.matmul(PS[PB[2]][0:64, hc], lhsT=qk16[:, 4 + h, cs], rhs=qk16[:, h, cs], start=True, stop=True),
                               r=qk16_all, w=[pp(2)], signal=(h == 3))
                        yield
                        op('act', lambda e: e.activation(out=gm[:, :], in_=PS[PB[0]][0:64, c0:c0 + 256], func=AF.Exp), r=[pp(0)], w=[K('gm')])
                        yield
                        op('dve', lambda e: e.scalar_tensor_tensor(out=Bm[:, :], in0=PS[PB[1]][0:64, c0:c0 + 256], scalar=-1.0, in1=gm[:, :], op0=ALU.mult, op1=ALU.mult),
                           r=[pp(1), K('gm')], w=[K('Bm')])
                        op('dve', lambda e: e.tensor_tensor(out=QKm[:, :], in0=PS[PB[2]][0:64, c0:c0 + 256], in1=gm[:, :], op=ALU.mult), r=[pp(2), K('gm')], w=[('QKm', o4)])
                        yield
                        op('dve', lambda e: e.tensor_tensor(out=Bm[:, :], in0=Bm[:, :], in1=mstrict[:, :], op=ALU.mult), r=[K('Bm'), 'mstrict'], w=[K('Bm')])
                        yield
                        for h in range(4):
                            hc = slice(h * 64, (h + 1) * 64)
                            pc = slice(c0 + h * 64, c0 + (h + 1) * 64)
                            op('pe', lambda e, hc=hc, pc=pc: e.transpose(PS[PB[3]][0:64, pc], Bm[:, hc], ident[0:64, 0:64]), r=[K('Bm'), 'ident'], w=[pp(3)], signal=(h == 3))
                        op('dve', lambda e: e.tensor_tensor(out=Xm[0][:, :], in0=Bm[:, :], in1=i4[:, :], op=ALU.add), r=[K('Bm'), 'i4'], w=[K('X', 0)])
                        yield
                        op('act', lambda e: e.activation(out=BTm[:, :], in_=PS[PB[3]][0:64, c0:c0 + 256], func=AF.Copy), r=[pp(3)], w=[K('BTm')])
                        yield
                        Pp, Ppk, PTp, PTpk = Bm, K('Bm'), BTm, K('BTm')
                        xi = 0
                        for lev in range(1, 6):
                            pi = lev % 2
                            if lev < 5:
                                for h in range(4):
                                    hc = slice(h * 64, (h + 1) * 64)
                                    pc = slice(c0 + h * 64, c0 + (h + 1) * 64)
                                    op('pe', lambda e, hc=hc, pc=pc, Pp=Pp, PTp=PTp: e.matmul(PS[PB[4]][0:64, pc], lhsT=PTp[:, hc], rhs=Pp[:, hc], start=True, stop=True),
                                       r=[Ppk, PTpk], w=[pp(4)], signal=(h == 3))
                            for h in range(4):
                                hc = slice(h * 64, (h + 1) * 64)
                                pc = slice(c0 + h * 64, c0 + (h + 1) * 64)
                                op('pe', lambda e, hc=hc, pc=pc, Pp=Pp, PTp=PTp: e.matmul(PS[PB[3]][0:64, pc], lhsT=Pp[:, hc], rhs=PTp[:, hc], start=True, stop=True),
                                   r=[Ppk, PTpk], w=[pp(3)], signal=(h == 3))
                            yield
                            if lev < 5:
                                op('dve', lambda e, pi=pi: e.tensor_copy(out=Pm[pi][:, :], in_=PS[PB[4]][0:64, c0:c0 + 256]), r=[pp(4)], w=[K('P', pi)])
                            op('act', lambda e, pi=pi: e.activation(out=PTm[pi][:, :], in_=PS[PB[3]][0:64, c0:c0 + 256], func=AF.Copy), r=[pp(3)], w=[K('PT', pi)])
                            yield
                            for h in range(4):
                                hc = slice(h * 64, (h + 1) * 64)
                                pc = slice(c0 + h * 64, c0 + (h + 1) * 64)
                                op('pe', lambda e, hc=hc, pc=pc, pi=pi, xi=xi: e.matmul(PS[PB[5]][0:64, pc], lhsT=PTm[pi][:, hc], rhs=Xm[xi][:, hc], start=True, stop=True),
                                   r=[K('PT', pi), K('X', xi)], w=[pp(5)], signal=(h == 3))
                            yield
                            if lev < 5:
                                op('dve', lambda e, xi=xi: e.tensor_tensor(out=Xm[1 - xi][:, :], in0=PS[PB[5]][0:64, c0:c0 + 256], in1=Xm[xi][:, :], op=ALU.add),
                                   r=[pp(5), K('X', xi)], w=[K('X', 1 - xi)])
                            else:
                                op('dve', lambda e, xi=xi: e.tensor_tensor(out=XO[o4][:, :], in0=PS[PB[5]][0:64, c0:c0 + 256], in1=Xm[xi][:, :], op=ALU.add),
                                   r=[pp(5), K('X', xi)], w=[('XO', o4)])
                            yield
                            xi = 1 - xi
                            Pp, Ppk, PTp, PTpk = Pm[pi], K('P', pi), PTm[pi], K('PT', pi)

                    def seq(n):
                        cs = slice(n * 64, (n + 1) * 64)
                        o4 = n % 4
                        X = XO[o4]; Xk = ('XO', o4); QKm = QKmO[o4]
                        for h in range(4):
                            op('pe', lambda e, h=h: e.transpose(PS[6][0:64, h * 128:(h + 1) * 128], vbT[:, h, cs], ident[:]), r=['vbT', 'ident'], w=[pk(6)], signal=(h == 3))
                        for h in range(4):
                            op('pe', lambda e, h=h: e.transpose(PS[7][0:64, h * 128:(h + 1) * 128], keT[:, h, cs], ident[:]), r=['keT', 'ident'], w=[pk(7)], signal=(h == 3))
                        yield
                        op('act', lambda e: e.activation(out=vbm[:].rearrange("p h d -> p (h d)"), in_=PS[6][0:64, :], func=AF.Copy), r=[pk(6)], w=['vbm'])
                        op('act', lambda e: e.activation(out=kem[:].rearrange("p h d -> p (h d)"), in_=PS[7][0:64, :], func=AF.Copy), r=[pk(7)], w=['kem'])
                        yield
                        for h in range(4):
                            op('pe', lambda e, h=h: e.matmul(PS[6][0:64, h * 128:(h + 1) * 128], lhsT=kbgT[:, h, cs], rhs=Sb[:, h, :], start=True, stop=True),
                               r=['kbgT', 'Sb'], w=[pk(6)], signal=(h == 3))
                        yield
                        op('dve', lambda e: e.tensor_tensor(out=rhs2[:].rearrange("p h d -> p (h d)"), in0=vbm[:].rearrange("p h d -> p (h d)"), in1=PS[6][0:64, :], op=ALU.subtract),
                           r=[pk(6), 'vbm'], w=['rhs2'])
                        yield
                        for h in range(4):
                            hc = slice(h * 64, (h + 1) * 64)
                            op('pe', lambda e, h=h, hc=hc: e.matmul(PS[7][0:64, h * 128:(h + 1) * 128], lhsT=X[:, hc], rhs=rhs2[:, h, :], start=True, stop=True),
                               r=[Xk, 'rhs2'], w=[pk(7)], signal=(h == 3))
                        yield
                        op('act', lambda e: e.activation(out=vnew[:].rearrange("p h d -> p (h d)"), in_=PS[7][0:64, :], func=AF.Copy), r=[pk(7)], w=['vnew'])
                        yield
                        for h in range(4):
                            hc = slice(h * 64, (h + 1) * 64)
                            op('pe', lambda e, h=h, hc=hc: e.matmul(PS[6][:, hc], lhsT=Sb[:, h, :], rhs=qdT[:, h, cs], start=True, stop=False),
                               r=['Sb', 'qdT'], w=[pk(6)], signal=False)
                            op('pe', lambda e, h=h, hc=hc: e.matmul(PS[6][:, hc], lhsT=vnew[:, h, :], rhs=QKm[:, hc], start=False, stop=True),
                               r=['vnew', ('QKm', o4)], w=[pk(6)], signal=(h == 3))
                        for h in range(4):
                            op('pe', lambda e, h=h: e.matmul(PS[7][:, h * 128:(h + 1) * 128], lhsT=kem[:, h, :], rhs=vnew[:, h, :], start=True, stop=True),
                               r=['kem', 'vnew'], w=[pk(7)], signal=(h == 3))
                        yield
                        op('act', lambda e: e.activation(out=OT[:, :, cs], in_=PS[6][:, 0:256].rearrange("p (h c) -> p h c", c=64), func=AF.Copy), r=[pk(6)], w=['OT'])
                        for h in range(4):
                            op('dve', lambda e, h=h: e.scalar_tensor_tensor(out=S[:, h, :], in0=S[:, h, :], scalar=cdc[:, h, n:n + 1], in1=PS[7][:, h * 128:(h + 1) * 128],
                                                                          op0=ALU.mult, op1=ALU.add),
                               r=[pk(7), 'cdc', Sk[h]], w=[Sk[h]])
                        op('pool', lambda e: e.tensor_copy(out=Sb[:], in_=S[:]), r=Sk, w=['Sb'])
                        yield

                    def chain(gens):
                        for g_ in gens:
                            yield from g_

                    def interleave(gens):
                        gens = [g_ for g_ in gens if g_ is not None]
                        while gens:
                            for g_ in list(gens):
                                try:
                                    next(g_)
                                except StopIteration:
                                    gens.remove(g_)
                    npair = (nch + 1) // 2
                    genA = partA(ti + 1) if ti + 1 < T // TT else None
                    interleave([prep(0), prep(1) if nch > 1 else None, genA])
                    for kp in range(npair):
                        nxt = []
                        if kp + 1 < npair:
                            nxt = [prep(2 * kp + 2), prep(2 * kp + 3)]
                        seqs = [seq(2 * kp)] + ([seq(2 * kp + 1)] if 2 * kp + 1 < nch else [])
                        interleave(nxt + [chain(seqs), genA])
                    if genA is not None:
                        for _ in genA:
                            pass
                        silu_all()
                    for h in range(4):
                        b = nb()
                        op('dve', lambda e, h=h: e.tensor_tensor(out=sqb[:, 0:TT], in0=OT[:, h, 0:TT], in1=OT[:, h, 0:TT], op=ALU.mult), r=['OT'], w=['sqb'])
                        op('pe', lambda e, b=b: e.matmul(PS[b][:, 0:TT], lhsT=ones_bf[:], rhs=sqb[:, 0:TT], start=True, stop=True), r=['sqb', 'ones'], w=[pk(b)])
                        op('act', lambda e, b=b: e.activation(out=lnt[:, 0:TT], in_=PS[b][:, 0:TT], func=AF.Ln, bias=EPS, scale=1.0 / 128), r=[pk(b)], w=['lnt'])
                        op('act', lambda e: e.activation(out=lnt[:, 0:TT], in_=lnt[:, 0:TT], func=AF.Exp, scale=-0.5), r=['lnt'], w=['lnt'])
                        op('dve', lambda e, h=h: e.tensor_tensor(out=ytmp[:, 0:TT], in0=OT[:, h, 0:TT], in1=lnt[:, 0:TT], op=ALU.mult), r=['OT', 'lnt'], w=['ytmp'])
                        op('act', lambda e, h=h: e.activation(out=lnt[:, 0:TT], in_=zT[:, h, 0:TT], func=AF.Silu), r=['zT', 'lnt'], w=['lnt'])
                        op('dve', lambda e, h=h: e.scalar_tensor_tensor(out=ogs[:, h, 0:TT], in0=ytmp[:, 0:TT], scalar=gng[:, 0:1], in1=lnt[:, 0:TT], op0=ALU.mult, op1=ALU.mult),
                           r=['ytmp', 'lnt', 'gng'], w=[('ogs', h)])
                        op('sp', lambda e, h=h: e.dma_start(out=OGTs[g][h][:, t0:t0 + TT], in_=ogs[:, h, 0:TT]), r=[('ogs', h)], dma='gog%d' % h)
                out_dma_keys.add(('dma', 'gSo'))
                op('sp', lambda e: e.dma_start(out=g_out[g][l].rearrange("h k v -> k h v"), in_=S[:]), r=['S0', 'S1', 'S2', 'S3'], dma='gSo')
            em.barrier()

    def m2_phase(l):
        with contextlib.ExitStack() as es:
            Wg = load_weight_bf16(es, "Wg", w_in[l], KD, 2048, c0=3592)
            Wsb = load_weight_bf16(es, "Wsb", w_bsb[l], 4, D)
            Wgd = load_weight_bf16(es, "Wgd", w_bgdn[l], 4, D)
            Wo = load_weight_bf16(es, "Wo", w_out[l], KD, D)
            g2 = load_gain_bc(es, "g2", norm_g[l, 2])
            g3 = load_gain_bc(es, "g3", norm_g[l, 3])
            scr = alloc_norm_scr(es)
            ss2 = es.enter_context(_sbuf_tensor("f_ss2", [128, 1], F32))
            lnv2 = es.enter_context(_sbuf_tensor("f_lnv2", [128, 1], F32))
            rstd2 = es.enter_context(_sbuf_tensor("f_rstd2", [128, 1], F32))
            scr2 = (scr[0], ss2, lnv2, rstd2)
            xts = [es.enter_context(_sbuf_tensor("m_x%d" % i, [128, 4, D], F32)) for i in range(2)]
            xnT = es.enter_context(_sbuf_tensor("m_xnT", [128, KD, 512], BF16))
            osb = [es.enter_context(_sbuf_tensor("m_osb%d" % i, [128, 4, 512], BF16)) for i in range(2)]
            ogd = [es.enter_context(_sbuf_tensor("m_ogd%d" % i, [128, 4, 512], BF16)) for i in range(2)]
            sg0 = [es.enter_context(_sbuf_tensor("m_sg0%d" % i, [128, 512], F32)) for i in range(2)]
            sg1 = [es.enter_context(_sbuf_tensor("m_sg1%d" % i, [128, 512], F32)) for i in range(2)]
            tmp = [es.enter_context(_sbuf_tensor("m_tmp%d" % i, [128, 512], F32)) for i in range(2)]
            mT = es.enter_context(_sbuf_tensor("m_mT", [128, KD, 512], BF16))
            ycat = [es.enter_context(_sbuf_tensor("m_y%d" % i, [128, D], F32)) for i in range(2)]
            tiles = []
            for g, T in groups:
                TT = min(512, T)
                for ti in range(T // TT):
                    tiles.append((g, ti * TT, TT))

            def load(i):
                g, t0, TT = tiles[i]
                pt = min(128, TT); nsub = TT // pt
                xt = xts[i % 2]
                op('sp', lambda e: e.dma_start(out=xt[0:pt, 0:nsub, :], in_=RES[g][t0:t0 + TT, :].rearrange("(s p) d -> p s d", p=pt)),
                   r=[('res', g, t0)], w=[('mx', i % 2)], dma='mx%d' % (i % 2))
                for k in range(4):
                    op('sp', lambda e, k=k: e.dma_start(out=osb[i % 2][:, k, 0:TT], in_=OSTs[g][k][:, t0:t0 + TT]), w=[('osb', i % 2)], dma='mo%d' % (i % 2))
                    op('sp', lambda e, k=k: e.dma_start(out=ogd[i % 2][:, k, 0:TT], in_=OGTs[g][k][:, t0:t0 + TT]), w=[('ogd', i % 2)], dma='mo%d' % (i % 2))
            load(0)
            cnt = 0
            for i, (g, t0, TT) in enumerate(tiles):
                if i + 1 < len(tiles):
                    load(i + 1)
                pt = min(128, TT); nsub = TT // pt
                xt = xts[i % 2]; xkey = ('mx', i % 2)
                norm_to_T(xt, xkey, pt, nsub, g2, 'g2', xnT, 'xnT', scr, 'm')
                xk = xnT_keys('xnT', nsub)
                for m in range(KD):
                    j = m % 2
                    for (gi, b, sgt, sgk) in ((0, 0 + j, sg0[j], ('sg0', j)), (1, 2 + j, sg1[j], ('sg1', j))):
                        for k in range(KD):
                            op('pe', lambda e, k=k, b=b, gi=gi, m=m: e.matmul(PS[b][:, 0:TT], lhsT=Wg[:, k, gi * D + m * 128:gi * D + (m + 1) * 128], rhs=xnT[:, k, 0:TT],
                                                                             start=(k == 0), stop=(k == KD - 1)),
                               r=xk + ['Wg'], w=[pk(b)], signal=(k == KD - 1))
                        op('act', lambda e, b=b, sgt=sgt: e.activation(out=sgt[:, 0:TT], in_=PS[b][:, 0:TT], func=AF.Sigmoid), r=[pk(b)], w=[sgk])
                    for (b, Wb, wk, src, srck) in ((4, Wsb, 'Wsb', osb[i % 2], ('osb', i % 2)), (5, Wgd, 'Wgd', ogd[i % 2], ('ogd', i % 2))):
                        for k in range(4):
                            op('pe', lambda e, k=k, b=b, Wb=Wb, src=src, m=m: e.matmul(PS[b][:, 0:TT], lhsT=Wb[:, k, m * 128:(m + 1) * 128], rhs=src[:, k, 0:TT],
                                                                                      start=(k == 0), stop=(k == 3)),
                               r=[wk, srck], w=[pk(b)], signal=(k == 3))
                    op('dve', lambda e, j=j: e.tensor_tensor(out=tmp[j][:, 0:TT], in0=PS[4][:, 0:TT], in1=sg0[j][:, 0:TT], op=ALU.mult),
                       r=[pk(4), ('sg0', j)], w=[('tmp', j)])
                    op('dve', lambda e, j=j: e.tensor_tensor(out=sg1[j][:, 0:TT], in0=PS[5][:, 0:TT], in1=sg1[j][:, 0:TT], op=ALU.mult),
                       r=[pk(5), ('sg1', j)], w=[('sg1', j)])
                    op('dve', lambda e, j=j, m=m: e.tensor_tensor(out=mT[:, m, 0:TT], in0=tmp[j][:, 0:TT], in1=sg1[j][:, 0:TT], op=ALU.add),
                       r=[('tmp', j), ('sg1', j)], w=[('mT', m)])
                mk = [('mT', m) for m in range(KD)]
                for s in range(nsub):
                    yc = ycat[cnt % 2]; ykey = ('ycat', cnt % 2); cnt += 1
                    for half in range(2):
                        b = 6 + half
                        for k in range(KD):
                            op('pe', lambda e, k=k, b=b, s=s, half=half: e.matmul(PS[b][0:pt, :], lhsT=mT[:, k, s * pt:(s + 1) * pt], rhs=Wo[:, k, half * 512:(half + 1) * 512],
                                                                                 start=(k == 0), stop=(k == KD - 1)),
                               r=mk + ['Wo'], w=[pk(b)], signal=(k == KD - 1))
                        op('act', lambda e, b=b, half=half, yc=yc: e.activation(out=yc[0:pt, half * 512:(half + 1) * 512], in_=PS[b][0:pt, :], func=AF.Copy),
                           r=[pk(b)], w=[ykey])
                    post_norm_residual(yc, ykey, pt, xt, xkey, s, g3, 'g3', scr2, 1.0)
                op('sp', lambda e: e.dma_start(out=RES[g][t0:t0 + TT, :].rearrange("(s p) d -> p s d", p=pt), in_=xt[0:pt, 0:nsub, :]),
                   r=[xkey], w=[('res', g, t0)], dma='mo_%d' % (i % 2))
            em.barrier()

    nph = [0]
    stop_after = cfg.get('stop_after', 10 ** 9)

    def runp(fn, *a):
        if nph[0] < stop_after:
            fn(*a)
        nph[0] += 1
    for l in range(DEPTH):
        src = x_in if l == 0 else RES
        runp(ffn_phase, l, 0, src, RES, False)
        runp(m1_phase, l)
        runp(att_phase, l)
        runp(gdn_phase, l)
        runp(m2_phase, l)
        runp(ffn_phase, l, 1, RES, RES, l == DEPTH - 1)
    em.barrier()
    return nc, em


def make_consts():
    c = {}
    c['c_ident'] = np.eye(128, dtype=np.float32)
    j = np.arange(128)
    c['c_linc'] = (j[:, None] >= j[None, :]).astype(np.float32)
    r = np.arange(128)[:, None, None]; jj = np.arange(4)[None, :, None]; cc = np.arange(512)[None, None, :]
    c['c_masks'] = ((cc - r - 128 * jj) > 0).astype(np.float32)
    j64 = np.arange(64)
    c['c_neg'] = np.where(j64[:, None] <= j64[None, :], 0.0, NEGBIG).astype(np.float32)
    ms = (j64[:, None] < j64[None, :]).astype(np.float32)
    c['c_mstrict'] = np.tile(ms, (1, 4))
    c['c_i4'] = np.tile(np.eye(64, dtype=np.float32), (1, 4))
    sel = np.zeros((4, 4, 128), np.float32)
    for h in range(4):
        sel[h, h, :] = 1.0
    c['c_sel'] = sel
    rs = np.ones((4, 512), np.float32); rs[:, ::64] = 0.0
    c['c_reset'] = rs
    return c


def run(cfg, inputs, n_cores=8):
    nc, em = build(cfg)
    T_P, T_S, PAST, DEPTH = cfg['T_P'], cfg['T_S'], cfg['PAST'], cfg['DEPTH']
    f = lambda a: np.ascontiguousarray(np.asarray(a, dtype=np.float32))
    consts = make_consts()
    NB = inputs['x_prompt'].shape[0]
    shared = {
        'norm_g': f(inputs['norm_gains']), 'w1u': f(inputs['w_ffn1_up']), 'w1d': f(inputs['w_ffn1_down']),
        'w2u': f(inputs['w_ffn2_up']), 'w2d': f(inputs['w_ffn2_down']), 'w_in': f(inputs['w_in']),
        'conv_w': f(inputs['conv_w']), 'a_log': f(inputs['gdn_a_log']), 'dt_bias': f(inputs['gdn_dt_bias']),
        'gdn_gain': f(inputs['gdn_norm_gain']), 'w_bsb': f(inputs['w_branch_sb']), 'w_bgdn': f(inputs['w_branch_gdn']),
        'w_out': f(inputs['w_out']),
    }
    shared.update(consts)
    in_maps = []
    for c in range(n_cores):
        m = dict(shared)
        m['x_p'] = f(inputs['x_prompt'][c % NB])
        m['x_s'] = f(inputs['x_sample'][c])
        m['cache_k'] = f(np.asarray(inputs['cache_sb_k'])[:, c].reshape(DEPTH, PAST, 512))
        m['cache_v'] = f(np.asarray(inputs['cache_sb_v'])[:, c].reshape(DEPTH, PAST, 512))
        m['state_gdn'] = f(np.asarray(inputs['state_gdn'])[:, c])
        m['state_conv'] = f(np.asarray(inputs['state_conv'])[:, c])
        in_maps.append(m)
    res = run_bass_kernel_spmd(nc, in_maps, core_ids=list(range(n_cores)))
    R = res.results
    NS = n_cores
    y_p = np.stack([R[b]['y_p'] for b in range(NB)])
    y_s = np.stack([R[c]['y_s'] for c in range(NS)])
    k_p = np.stack([R[b]['k_p'] for b in range(NB)], axis=1).reshape(DEPTH, NB, T_P, 8, 64)
    v_p = np.stack([R[b]['v_p'] for b in range(NB)], axis=1).reshape(DEPTH, NB, T_P, 8, 64)
    g_p = np.stack([R[b]['g_p'] for b in range(NB)], axis=1)
    c_p = np.stack([R[b]['c_p'] for b in range(NB)], axis=1)
    k_s = np.stack([R[c]['k_s'] for c in range(NS)], axis=1).reshape(DEPTH, NS, T_S, 8, 64)
    v_s = np.stack([R[c]['v_s'] for c in range(NS)], axis=1).reshape(DEPTH, NS, T_S, 8, 64)
    g_s = np.stack([R[c]['g_s'] for c in range(NS)], axis=1)
    c_s = np.stack([R[c]['c_s'] for c in range(NS)], axis=1)
    outs = (y_p, y_s, k_p, v_p, g_p, c_p, k_s, v_s, g_s, c_s)
    return tuple(np.ascontiguousarray(o, dtype=np.float32) for o in outs)


def kernel(**inputs):
    cfg = dict(T_P=8192, T_S=64, PAST=2048, DEPTH=2, FT=384)
    return run(cfg, inputs, n_cores=8)
```

```python
import contextlib
import numpy as np
import concourse.bass as bass
import concourse.mybir as mybir
from concourse.bass_utils import run_bass_kernel_spmd

F32 = mybir.dt.float32
BF16 = mybir.dt.bfloat16
AF = mybir.ActivationFunctionType
ALU = mybir.AluOpType

D = 1024
KD = 8
FF = 2816
KF = 22
IN_DIM = 5640
EPS = 1e-6
NEGBIG = -30000.0


class Emitter:
    def __init__(self, nc):
        self.nc = nc
        self.E = {'pe': nc.tensor, 'act': nc.scalar, 'dve': nc.vector, 'pool': nc.gpsimd, 'sp': nc.sync}
        self.sem = {}
        self.cnt = {}
        self.waited = {k: {} for k in self.E}
        self.lastw = {}
        self.readers = {}
        self.pend = {k: ([], []) for k in self.E}
        self.n_ins = 0

    def _sem(self, key):
        if key not in self.sem:
            nm = "s_" + "".join(ch for ch in str(key) if ch.isalnum() or ch == "_")
            self.sem[key] = self.nc.alloc_semaphore(nm)
            self.cnt[key] = 0
        return self.sem[key]

    def _wait(self, eng, need):
        for sk, v in need.items():
            if self.waited[eng].get(sk, 0) < v:
                self.E[eng].wait_ge(self.sem[sk], v)
                self.waited[eng][sk] = v

    def op(self, eng, fn, r=(), w=(), dma=None, signal=True):
        need = {}
        for b in r:
            t = self.lastw.get(b)
            if t is not None:
                need[t[0]] = max(need.get(t[0], 0), t[1])
        for b in w:
            t = self.lastw.get(b)
            if t is not None:
                need[t[0]] = max(need.get(t[0], 0), t[1])
            for sk, v in self.readers.get(b, {}).items():
                need[sk] = max(need.get(sk, 0), v)
        for oe, (pr_, pw_) in self.pend.items():
            if oe != eng and (pr_ or pw_):
                for b in w:
                    assert b not in pr_ and b not in pw_, ("pending unsignaled access", oe, b)
                for b in r:
                    assert b not in pw_, ("pending unsignaled write", oe, b)
        for sk in list(need):
            if isinstance(sk, tuple) and sk[0] == 'dma':
                need[sk] = self.cnt[sk]
        if eng == 'pe':
            need.pop('pe', None)
        self._wait(eng, need)
        ins = fn(self.E[eng])
        self.n_ins += 1
        if dma is not None:
            sk = ('dma', dma)
            self._sem(sk)
            self.cnt[sk] += 16
            ins.then_inc(self.sem[sk], 16)
            val = self.cnt[sk]
            for b in w:
                self.lastw[b] = (sk, val)
                self.readers[b] = {}
            for b in r:
                self.readers.setdefault(b, {})[sk] = val
            return ins
        pr, pw = self.pend[eng]
        pr.extend(r)
        pw.extend(w)
        if signal:
            sk = eng
            self._sem(sk)
            self.cnt[sk] += 1
            ins.then_inc(self.sem[sk], 1)
            val = self.cnt[sk]
            for b in pw:
                self.lastw[b] = (sk, val)
                self.readers[b] = {}
            for b in pr:
                if b not in pw:
                    self.readers.setdefault(b, {})[sk] = val
            self.pend[eng] = ([], [])
        return ins

    def barrier(self, engines=('pe', 'act', 'dve', 'pool', 'sp')):
        need = {sk: v for sk, v in self.cnt.items() if v > 0}
        for e in engines:
            n2 = dict(need)
            self._wait(e, n2)
        self.lastw = {}
        self.readers = {}


def build(cfg):
    T_P, T_S, PAST, DEPTH = cfg['T_P'], cfg['T_S'], cfg['PAST'], cfg['DEPTH']
    FT = cfg.get('FT', 256)
    nc = bass.Bass("TRN2", target_bir_lowering=False)
    _orig_sbuf_tensor = nc.sbuf_tensor
    _uid = [0]

    def _sbuf_tensor(name, shape, dt):
        _uid[0] += 1
        return _orig_sbuf_tensor("%s_u%d" % (name, _uid[0]), shape, dt)
    em = Emitter(nc)
    op = em.op

    def din(name, shape):
        return nc.dram_tensor(name, list(shape), F32, kind="ExternalInput").ap()

    def dout(name, shape):
        return nc.dram_tensor(name, list(shape), F32, kind="ExternalOutput").ap()

    def dscr(name, shape, dt=F32):
        return nc.dram_tensor(name, list(shape), dt, kind="Internal").ap()

    groups = [('p', T_P), ('s', T_S)]
    TG = dict(groups)
    x_in = {'p': din("x_p", [T_P, D]), 's': din("x_s", [T_S, D])}
    cache_k = din("cache_k", [DEPTH, PAST, 512])
    cache_v = din("cache_v", [DEPTH, PAST, 512])
    state_gdn = din("state_gdn", [DEPTH, 4, 128, 128])
    state_conv = din("state_conv", [DEPTH, 3, 1536])
    norm_g = din("norm_g", [DEPTH, 6, D])
    w1u = din("w1u", [DEPTH, D, 2 * FF]); w1d = din("w1d", [DEPTH, FF, D])
    w2u = din("w2u", [DEPTH, D, 2 * FF]); w2d = din("w2d", [DEPTH, FF, D])
    w_in = din("w_in", [DEPTH, D, IN_DIM])
    conv_w = din("conv_w", [DEPTH, 4, 1536])
    a_log = din("a_log", [DEPTH, 4]); dt_bias = din("dt_bias", [DEPTH, 4])
    gdn_gain = din("gdn_gain", [DEPTH, 128])
    w_bsb = din("w_bsb", [DEPTH, 512, D]); w_bgdn = din("w_bgdn", [DEPTH, 512, D])
    w_out = din("w_out", [DEPTH, D, D])
    c_ident = din("c_ident", [128, 128]); c_linc = din("c_linc", [128, 128])
    c_masks = din("c_masks", [128, 4, 512])
    c_neg = din("c_neg", [64, 64]); c_mstrict = din("c_mstrict", [64, 256]); c_i4 = din("c_i4", [64, 256])
    c_sel = din("c_sel", [4, 4, 128]); c_reset = din("c_reset", [4, 512])

    y_out = {'p': dout("y_p", [T_P, D]), 's': dout("y_s", [T_S, D])}
    k_out = {'p': dout("k_p", [DEPTH, T_P, 512]), 's': dout("k_s", [DEPTH, T_S, 512])}
    v_out = {'p': dout("v_p", [DEPTH, T_P, 512]), 's': dout("v_s", [DEPTH, T_S, 512])}
    g_out = {'p': dout("g_p", [DEPTH, 4, 128, 128]), 's': dout("g_s", [DEPTH, 4, 128, 128])}
    c_out = {'p': dout("c_p", [DEPTH, 3, 1536]), 's': dout("c_s", [DEPTH, 3, 1536])}

    RES = {g: dscr("res_" + g, [T, D]) for g, T in groups}
    QTs = {g: dscr("qt_" + g, [4, 128, T], BF16) for g, T in groups}
    KTs = {g: dscr("kt_" + g, [4, 128, T], BF16) for g, T in groups}
    RAWs = {g: dscr("raw_" + g, [12, 128, T]) for g, T in groups}
    ZTs = {g: dscr("zt_" + g, [4, 128, T]) for g, T in groups}
    BAs = {g: dscr("ba_" + g, [2, 4, T]) for g, T in groups}
    OSTs = {g: dscr("ost_" + g, [4, 128, T], BF16) for g, T in groups}
    OGTs = {g: dscr("ogt_" + g, [4, 128, T], BF16) for g, T in groups}

    PS = [nc.alloc_psum_tensor("psb%d" % i, [128, 512], F32) for i in range(8)]

    def pk(i):
        return ('ps', i)

    out_dma_keys = set()

    ident = nc.alloc_sbuf_tensor("ident", [128, 128], F32)
    ones_bf = nc.alloc_sbuf_tensor("ones_bf", [128, 128], BF16)
    linc_bf = nc.alloc_sbuf_tensor("linc_bf", [128, 128], BF16)
    op('sp', lambda e: e.dma_start(out=ident[:], in_=c_ident), w=['ident'], dma='c0')
    op('pool', lambda e: e.dma_start(out=linc_bf[:], in_=c_linc), w=['linc'], dma='c1')
    op('dve', lambda e: e.memset(ones_bf[:], 1.0), w=['ones'])

    def load_weight_bf16(es, name, src, kchunks, ncols, c0=0, key=None):
        t = es.enter_context(_sbuf_tensor(name, [128, kchunks, ncols], BF16))
        key = key or name
        for k in range(kchunks):
            op('pool', lambda e, k=k: e.dma_start(out=t[:, k, :], in_=src[k * 128:(k + 1) * 128, c0:c0 + ncols]),
               w=[key], dma='w_' + key)
        return t

    def load_gain_bc(es, name, src_row):
        t = es.enter_context(_sbuf_tensor(name, [128, D], F32))
        op('sp', lambda e: e.dma_start(out=t[:], in_=src_row.partition_broadcast(128)), w=[name], dma='g_' + name)
        return t

    def norm_to_T(xt, xkey, pt, nsub, gain, gkey, xnT, xnTkey, scr, tagbase):
        junk, ss, lnv, rstd, xn, nxs = scr
        for s in range(nsub):
            op('dve', lambda e, s=s: e.scalar_tensor_tensor(out=junk[0:pt, :], in0=xt[0:pt, s, :], scalar=1.0, in1=xt[0:pt, s, :],
                                                            op0=ALU.mult, op1=ALU.mult, accum_out=ss[0:pt, s:s + 1]),
               r=[xkey], w=['junk', ('ss', s)])
        op('act', lambda e: e.activation(out=lnv[0:pt, 0:nsub], in_=ss[0:pt, 0:nsub], func=AF.Ln, bias=EPS, scale=1.0 / D),
           r=[('ss', s) for s in range(nsub)], w=['lnv'])
        op('act', lambda e: e.activation(out=rstd[0:pt, 0:nsub], in_=lnv[0:pt, 0:nsub], func=AF.Exp, scale=-0.5),
           r=['lnv'], w=['rstd'])
        for s in range(nsub):
            xs = s % nxs
            op('dve', lambda e, s=s, xs=xs: e.scalar_tensor_tensor(out=xn[0:pt, xs, :], in0=xt[0:pt, s, :], scalar=rstd[0:pt, s:s + 1],
                                                                 in1=gain[0:pt, :], op0=ALU.mult, op1=ALU.mult),
               r=[xkey, 'rstd', gkey], w=[('xn', xs)])
            for half in range(2):
                b = 6 + half
                for i in range(4):
                    k = half * 4 + i
                    op('pe', lambda e, b=b, i=i, k=k, xs=xs: e.transpose(PS[b][:, i * 128:i * 128 + pt], xn[0:pt, xs, k * 128:(k + 1) * 128],
                                                                        ident[0:pt, 0:pt]),
                       r=[('xn', xs), 'ident'], w=[pk(b)], signal=(i == 3))
                eng = 'act' if half == 0 else 'dve'
                src = PS[b][:].rearrange("p (i t) -> p i t", i=4)[:, :, 0:pt]
                dst = xnT[:, half * 4:half * 4 + 4, s * pt:(s + 1) * pt]
                if eng == 'act':
                    op('act', lambda e, src=src, dst=dst: e.activation(out=dst, in_=src, func=AF.Copy), r=[pk(b)], w=[(xnTkey, s, half)])
                else:
                    op('dve', lambda e, src=src, dst=dst: e.tensor_copy(out=dst, in_=src), r=[pk(b)], w=[(xnTkey, s, half)])

    def xnT_keys(xnTkey, nsub):
        return [(xnTkey, s, h) for s in range(nsub) for h in range(2)]

    def alloc_norm_scr(es, slim=False):
        junk = es.enter_context(_sbuf_tensor("n_junk", [128, D], BF16 if slim else F32))
        ss = es.enter_context(_sbuf_tensor("n_ss", [128, 8], F32))
        lnv = es.enter_context(_sbuf_tensor("n_lnv", [128, 8], F32))
        rstd = es.enter_context(_sbuf_tensor("n_rstd", [128, 8], F32))
        xn = es.enter_context(_sbuf_tensor("n_xn", [128, 1 if slim else 2, D], F32))
        return (junk, ss, lnv, rstd, xn, 1 if slim else 2)

    def post_norm_residual(ycat, ykey, pt, xt, xkey, s, gain, gkey, scr2, coef):
        junk, ss2, lnv2, rstd2 = scr2
        op('dve', lambda e: e.scalar_tensor_tensor(out=junk[0:pt, :], in0=ycat[0:pt, :], scalar=1.0, in1=ycat[0:pt, :],
                                                   op0=ALU.mult, op1=ALU.mult, accum_out=ss2[0:pt, 0:1]),
           r=[ykey], w=['junk', 'ss2'])
        op('act', lambda e: e.activation(out=lnv2[0:pt, 0:1], in_=ss2[0:pt, 0:1], func=AF.Ln, bias=EPS, scale=1.0 / D),
           r=['ss2'], w=['lnv2'])
        op('act', lambda e: e.activation(out=rstd2[0:pt, 0:1], in_=lnv2[0:pt, 0:1], func=AF.Exp, scale=-0.5), r=['lnv2'], w=['rstd2'])
        op('dve', lambda e: e.scalar_tensor_tensor(out=ycat[0:pt, :], in0=ycat[0:pt, :], scalar=rstd2[0:pt, 0:1], in1=gain[0:pt, :],
                                                   op0=ALU.mult, op1=ALU.mult),
           r=[ykey, 'rstd2', gkey], w=[ykey])
        op('dve', lambda e: e.scalar_tensor_tensor(out=xt[0:pt, s, :], in0=ycat[0:pt, :], scalar=float(coef), in1=xt[0:pt, s, :],
                                                   op0=ALU.mult, op1=ALU.add),
           r=[ykey, xkey], w=[xkey])

    def ffn_phase(l, which, src, dst, final):
        wu = (w1u, w2u)[which][l]
        wd = (w1d, w2d)[which][l]
        gi0, gi1 = (0, 1) if which == 0 else (4, 5)
        with contextlib.ExitStack() as es:
            Wu = load_weight_bf16(es, "Wu", wu, KD, 2 * FF)
            Wd = load_weight_bf16(es, "Wd", wd, KF, D)
            g0 = load_gain_bc(es, "g0", norm_g[l, gi0])
            g1 = load_gain_bc(es, "g1", norm_g[l, gi1])
            scr = alloc_norm_scr(es, slim=True)
            ss2 = es.enter_context(_sbuf_tensor("f_ss2", [128, 1], F32))
            lnv2 = es.enter_context(_sbuf_tensor("f_lnv2", [128, 1], F32))
            rstd2 = es.enter_context(_sbuf_tensor("f_rstd2", [128, 1], F32))
            scr2 = (scr[0], ss2, lnv2, rstd2)
            nsubF = max(1, FT // 128)
            xts = [es.enter_context(_sbuf_tensor("f_x%d" % i, [128, nsubF, D], F32)) for i in range(2)]
            xnT = es.enter_context(_sbuf_tensor("f_xnT", [128, KD, FT], BF16))
            hT = es.enter_context(_sbuf_tensor("f_hT", [128, KF, FT], BF16))
            sg = [es.enter_context(_sbuf_tensor("f_sg%d" % i, [128, FT], F32)) for i in range(2)]
            ycat = [es.enter_context(_sbuf_tensor("f_y%d" % i, [128, D], F32)) for i in range(2)]
            tiles = []
            for g, T in groups:
                t0_ = 0
                while t0_ < T:
                    tt_ = min(FT, T - t0_)
                    tiles.append((g, t0_, tt_))
                    t0_ += tt_

            def load(i):
                g, t0, TT = tiles[i]
                pt = min(128, TT); nsub = TT // pt
                xt = xts[i % 2]
                op('sp', lambda e: e.dma_start(out=xt[0:pt, 0:nsub, :], in_=src[g][t0:t0 + TT, :].rearrange("(s p) d -> p s d", p=pt)),
                   r=[('res', g, t0)], w=[('fx', i % 2)], dma='fx%d' % (i % 2))
            load(0)
            cnt = 0
            for i, (g, t0, TT) in enumerate(tiles):
                if i + 1 < len(tiles):
                    load(i + 1)
                pt = min(128, TT); nsub = TT // pt
                xt = xts[i % 2]; xkey = ('fx', i % 2)
                norm_to_T(xt, xkey, pt, nsub, g0, 'g0', xnT, 'xnT', scr, 'f')
                xk = xnT_keys('xnT', nsub)
                for m in range(KF):
                    bg = m % 2; bu = 2 + m % 2
                    for k in range(KD):
                        op('pe', lambda e, k=k, m=m, bg=bg: e.matmul(PS[bg][:, 0:TT], lhsT=Wu[:, k, m * 128:(m + 1) * 128], rhs=xnT[:, k, 0:TT],
                                                                      start=(k == 0), stop=(k == KD - 1)),
                           r=xk + ['Wu'], w=[pk(bg)], signal=(k == KD - 1))
                    for k in range(KD):
                        op('pe', lambda e, k=k, m=m, bu=bu: e.matmul(PS[bu][:, 0:TT], lhsT=Wu[:, k, FF + m * 128:FF + (m + 1) * 128], rhs=xnT[:, k, 0:TT],
                                                                      start=(k == 0), stop=(k == KD - 1)),
                           r=xk + ['Wu'], w=[pk(bu)], signal=(k == KD - 1))
                    sgt = sg[m % 2]
                    op('act', lambda e, bg=bg, sgt=sgt: e.activation(out=sgt[:, 0:TT], in_=PS[bg][:, 0:TT], func=AF.Silu),
                       r=[pk(bg)], w=[('sg', m % 2)])
                    op('dve', lambda e, bu=bu, sgt=sgt, m=m: e.tensor_tensor(out=hT[:, m, 0:TT], in0=PS[bu][:, 0:TT], in1=sgt[:, 0:TT], op=ALU.mult),
                       r=[pk(bu), ('sg', m % 2)], w=[('hT', m)])
                hk = [('hT', m) for m in range(KF)]
                for s in range(nsub):
                    yc = ycat[cnt % 2]; ykey = ('ycat', cnt % 2); cnt += 1
                    for half in range(2):
                        b = 4 + half
                        for m in range(KF):
                            op('pe', lambda e, m=m, b=b, s=s, half=half: e.matmul(PS[b][0:pt, :], lhsT=hT[:, m, s * pt:(s + 1) * pt],
                                                                                 rhs=Wd[:, m, half * 512:(half + 1) * 512],
                                                                                 start=(m == 0), stop=(m == KF - 1)),
                               r=hk + ['Wd'], w=[pk(b)], signal=(m == KF - 1))
                        op('act', lambda e, b=b, half=half, yc=yc: e.activation(out=yc[0:pt, half * 512:(half + 1) * 512], in_=PS[b][0:pt, :], func=AF.Copy),
                           r=[pk(b)], w=[ykey])
                    post_norm_residual(yc, ykey, pt, xt, xkey, s, g1, 'g1', scr2, 0.5)
                dd = y_out if final else dst
                dkey = 'fo%d' % (i % 2)
                if final:
                    out_dma_keys.add(('dma', dkey))
                op('sp', lambda e, dd=dd: e.dma_start(out=dd[g][t0:t0 + TT, :].rearrange("(s p) d -> p s d", p=pt), in_=xt[0:pt, 0:nsub, :]),
                   r=[xkey], w=[('res', g, t0)], dma=dkey)
            em.barrier()

    def m1_phase(l):
        NCOL = 3592
        with contextlib.ExitStack() as es:
            Wi = load_weight_bf16(es, "Wi", w_in[l], KD, NCOL)
            g2 = load_gain_bc(es, "g2", norm_g[l, 2])
            scr = alloc_norm_scr(es)
            xts = [es.enter_context(_sbuf_tensor("m_x%d" % i, [128, 4, D], F32)) for i in range(2)]
            xnT = es.enter_context(_sbuf_tensor("m_xnT", [128, KD, 512], BF16))
            stg_f = [es.enter_context(_sbuf_tensor("m_sf%d" % i, [128, 512], F32)) for i in range(4)]
            stg_b = [es.enter_context(_sbuf_tensor("m_sb%d" % i, [128, 512], BF16)) for i in range(4)]
            cst = es.enter_context(_sbuf_tensor("m_cst", [4, 1536], F32))
            tiles = []
            for g, T in groups:
                TT = min(512, T)
                for ti in range(T // TT):
                    tiles.append((g, ti * TT, TT, ti == T // TT - 1))

            def load(i):
                g, t0, TT, _ = tiles[i]
                pt = min(128, TT); nsub = TT // pt
                xt = xts[i % 2]
                op('sp', lambda e: e.dma_start(out=xt[0:pt, 0:nsub, :], in_=RES[g][t0:t0 + TT, :].rearrange("(s p) d -> p s d", p=pt)),
                   w=[('mx', i % 2)], dma='mx%d' % (i % 2))
            load(0)
            cf = 0; cb = 0; pb = 0
            for i, (g, t0, TT, last) in enumerate(tiles):
                if i + 1 < len(tiles):
                    load(i + 1)
                pt = min(128, TT); nsub = TT // pt
                xt = xts[i % 2]; xkey = ('mx', i % 2)
                norm_to_T(xt, xkey, pt, nsub, g2, 'g2', xnT, 'xnT', scr, 'm')
                xk = xnT_keys('xnT', nsub)
                for s in range(nsub):
                    for (c0, outd) in ((512, k_out[g]), (1024, v_out[g])):
                        b = pb % 6; pb += 1
                        for k in range(KD):
                            op('pe', lambda e, k=k, b=b, c0=c0, s=s: e.matmul(PS[b][0:pt, :], lhsT=xnT[:, k, s * pt:(s + 1) * pt], rhs=Wi[:, k, c0:c0 + 512],
                                                                             start=(k == 0), stop=(k == KD - 1)),
                               r=xk + ['Wi'], w=[pk(b)], signal=(k == KD - 1))
                        sf = stg_f[cf % 4]; sk = ('sf', cf % 4); dk = 'sf%d' % (cf % 4); cf += 1
                        op('act', lambda e, b=b, sf=sf: e.activation(out=sf[0:pt, :], in_=PS[b][0:pt, :], func=AF.Copy), r=[pk(b)], w=[sk])
                        out_dma_keys.add(('dma', dk))
                        op('sp', lambda e, sf=sf, outd=outd, s=s: e.dma_start(out=outd[l, t0 + s * pt:t0 + (s + 1) * pt, :], in_=sf[0:pt, :]),
                           r=[sk], dma=dk)
                fm = []
                for p in range(4):
                    fm.append((p * 128, 128, 'b', 0.125, QTs[g][p]))
                for p in range(4):
                    fm.append((512 + p * 128, 128, 'b', 1.0, KTs[g][p]))
                for c in range(12):
                    fm.append((1536 + c * 128, 128, 'f', 1.0, RAWs[g][c]))
                for h in range(4):
                    fm.append((3072 + h * 128, 128, 'f', 1.0, ZTs[g][h]))
                fm.append((3584, 4, 'f', 1.0, BAs[g][0]))
                fm.append((3588, 4, 'f', 1.0, BAs[g][1]))
                for (c0, mrows, kind, scale, dst) in fm:
                    b = pb % 6; pb += 1
                    for k in range(KD):
                        op('pe', lambda e, k=k, b=b, c0=c0, mrows=mrows: e.matmul(PS[b][0:mrows, 0:TT], lhsT=Wi[:, k, c0:c0 + mrows], rhs=xnT[:, k, 0:TT],
                                                                                 start=(k == 0), stop=(k == KD - 1)),
                           r=xk + ['Wi'], w=[pk(b)], signal=(k == KD - 1))
                    if kind == 'b':
                        st = stg_b[cb % 4]; sk = ('sb', cb % 4); dk = 'sb%d' % (cb % 4); cb += 1
                        op('dve', lambda e, b=b, st=st, scale=scale, mrows=mrows: e.tensor_scalar(out=st[0:mrows, 0:TT], in0=PS[b][0:mrows, 0:TT],
                                                                                                scalar1=float(scale), scalar2=None, op0=ALU.mult),
                           r=[pk(b)], w=[sk])
                    else:
                        st = stg_f[cf % 4]; sk = ('sf', cf % 4); dk = 'sf%d' % (cf % 4); cf += 1
                        op('act', lambda e, b=b, st=st, mrows=mrows: e.activation(out=st[0:mrows, 0:TT], in_=PS[b][0:mrows, 0:TT], func=AF.Copy),
                           r=[pk(b)], w=[sk])
                    op('sp', lambda e, st=st, dst=dst, mrows=mrows: e.dma_start(out=dst[0:mrows, t0:t0 + TT], in_=st[0:mrows, 0:TT]), r=[sk], dma=dk)
                if last:
                    for j in range(3):
                        b = pb % 6; pb += 1
                        for k in range(KD):
                            op('pe', lambda e, k=k, b=b, j=j: e.matmul(PS[b][0:3, :], lhsT=xnT[:, k, TT - 3:TT], rhs=Wi[:, k, 1536 + j * 512:1536 + (j + 1) * 512],
                                                                      start=(k == 0), stop=(k == KD - 1)),
                               r=xk + ['Wi'], w=[pk(b)], signal=(k == KD - 1))
                        op('act', lambda e, b=b, j=j: e.activation(out=cst[0:3, j * 512:(j + 1) * 512], in_=PS[b][0:3, :], func=AF.Copy), r=[pk(b)], w=['cst'])
                    out_dma_keys.add(('dma', 'cst'))
                    op('sp', lambda e: e.dma_start(out=c_out[g][l], in_=cst[0:3, :]), r=['cst'], dma='cst')
            em.barrier()

    def att_phase(l):
        with contextlib.ExitStack() as es:
            TKmax = max(T_P, PAST + 128)
            nblk_max = max(T_P // 128, PAST // 128 + 1)
            KT = es.enter_context(_sbuf_tensor("a_KT", [128, TKmax], BF16))
            NKT = es.enter_context(_sbuf_tensor("a_NKT", [128, TKmax], BF16))
            QT = es.enter_context(_sbuf_tensor("a_QT", [128, max(T_P, 128)], BF16))
            Vp = es.enter_context(_sbuf_tensor("a_Vp", [128, nblk_max, 2, 128], BF16))
            masks = es.enter_context(_sbuf_tensor("a_masks", [128, 4, 512], BF16))
            kc = es.enter_context(_sbuf_tensor("a_kc", [128, max(PAST // 128, 1), 128], F32))
            et = [es.enter_context(_sbuf_tensor("a_e%d" % i, [128, 512], F32)) for i in range(2)]
            spt = [es.enter_context(_sbuf_tensor("a_sp%d" % i, [128, 512], BF16)) for i in range(4)]
            tt = [es.enter_context(_sbuf_tensor("a_t%d" % i, [128, 512], F32)) for i in range(4)]
            at = [es.enter_context(_sbuf_tensor("a_a%d" % i, [128, 512], BF16)) for i in range(4)]
            carry = [es.enter_context(_sbuf_tensor("a_c%d" % i, [128, 512], F32)) for i in range(2)]
            ost = [es.enter_context(_sbuf_tensor("a_o%d" % i, [128, 512], BF16)) for i in range(2)]
            op('pool', lambda e: e.dma_start(out=masks[:], in_=c_masks), w=['masks'], dma='amask')
            op('dve', lambda e: e.memset(Vp[:], 0.0), w=['Vp'])
            op('dve', lambda e: e.memset(KT[:], 0.0), w=['KT'])
            op('dve', lambda e: e.memset(QT[:], 0.0), w=['QT'])
            ocnt = 0
            bcnt = 0
            for g, T in groups:
                if g not in cfg.get('att_groups', 'ps'):
                    continue
                QW = min(512, max(T, 128))
                for p in range(4):
                    if g == 'p':
                        Tk = T
                        nseg = max(1, T // 2048)
                        seg = T // nseg
                        for q4 in range(nseg):
                            op('sp', lambda e, q4=q4: e.dma_start(out=KT[:, q4 * seg:(q4 + 1) * seg], in_=KTs[g][p][:, q4 * seg:(q4 + 1) * seg]),
                               w=['KT'], dma='aKT')
                        blocks_all = [(b * 128, 128, b) for b in range(T // 128)]
                        vsrc = v_out[g][l]
                        nb = T // 128
                        for j in range(2):
                            for b0 in range(0, nb, 8):
                                b1 = min(nb, b0 + 8)
                                op('pool', lambda e, j=j, b0=b0, b1=b1: e.dma_start(
                                    out=Vp[:, b0:b1, j, j * 64:(j + 1) * 64],
                                    in_=vsrc[b0 * 128:b1 * 128, (2 * p + j) * 64:(2 * p + j + 1) * 64].rearrange("(b k) d -> k b d", k=128)),
                                    w=['Vp'], dma='aV')
                    else:
                        Tk = PAST + 128
                        npb = PAST // 128
                        op('sp', lambda e: e.dma_start(out=kc[:, 0:npb, :], in_=cache_k[l][:, p * 128:(p + 1) * 128].rearrange("(b k) d -> k b d", k=128)),
                           w=['kc'], dma='akc')
                        for b in range(npb):
                            pb_ = 6 + b % 2
                            op('pe', lambda e, b=b, pb_=pb_: e.transpose(PS[pb_][:, 0:128], kc[:, b, :], ident[:]), r=['kc', 'ident'], w=[pk(pb_)])
                            op('act', lambda e, b=b, pb_=pb_: e.activation(out=KT[:, b * 128:(b + 1) * 128], in_=PS[pb_][:, 0:128], func=AF.Copy),
                               r=[pk(pb_)], w=['KT'])
                        op('sp', lambda e: e.dma_start(out=KT[:, PAST:PAST + T], in_=KTs[g][p][:, 0:T]), w=['KT'], dma='aKT')
                        for j in range(2):
                            op('pool', lambda e, j=j: e.dma_start(
                                out=Vp[:, 0:npb, j, j * 64:(j + 1) * 64],
                                in_=cache_v[l][:, (2 * p + j) * 64:(2 * p + j + 1) * 64].rearrange("(b k) d -> k b d", k=128)),
                                w=['Vp'], dma='aV')
                            op('pool', lambda e, j=j: e.dma_start(
                                out=Vp[0:T, npb, j, j * 64:(j + 1) * 64],
                                in_=v_out[g][l][0:T, (2 * p + j) * 64:(2 * p + j + 1) * 64]),
                                w=['Vp'], dma='aV')
                    op('sp', lambda e: e.dma_start(out=QT[:, 0:T], in_=QTs[g][p][:, 0:T]), w=['QT'], dma='aQT')
                    op('dve', lambda e: e.tensor_scalar(out=NKT[:, 0:Tk], in0=KT[:, 0:Tk], scalar1=-1.0, scalar2=None, op0=ALU.mult),
                       r=['KT'], w=['NKT'])
                    for qt in range(max(1, T // QW)):
                        if cfg.get('att_skipq', False):
                            continue
                        q0 = qt * QW
                        if g == 'p':
                            nvis = (q0 + QW) // 128
                            blks = []
                            for kb in range(nvis - 1, -1, -1):
                                j = kb - q0 // 128
                                blks.append((kb * 128, 128, kb, j if j >= 0 else None))
                        else:
                            blks = [(PAST, 128, PAST // 128, 0)] + [(b * 128, 128, b, None) for b in range(PAST // 128 - 1, -1, -1)]
                        nb_ = len(blks)
                        for hh in range(2):
                            op('dve', lambda e, hh=hh: e.memset(carry[hh][:, 0:QW], 0.0), w=[('carry', hh)])
                        O = PS[6]

                        def pe_A(bi):
                            (k0, nk, vb, mj) = blks[bi]
                            for hh in range(2):
                                hs = slice(hh * 64, (hh + 1) * 64)
                                op('pe', lambda e, hh=hh, hs=hs: e.matmul(PS[hh][0:nk, 0:QW], lhsT=KT[hs, k0:k0 + nk], rhs=QT[hs, q0:q0 + QW], start=True, stop=True),
                                   r=['KT', 'QT'], w=[pk(hh)])

                        def pe_BC(bi):
                            (k0, nk, vb, mj) = blks[bi]
                            lastb = bi == nb_ - 1
                            for hh in range(2):
                                hs = slice(hh * 64, (hh + 1) * 64)
                                sk = ('sp', bi % 2, hh); spx = spt[(bi % 2) * 2 + hh]
                                op('pe', lambda e, hh=hh, spx=spx: e.matmul(PS[2 + hh][0:nk, 0:QW], lhsT=linc_bf[0:nk, 0:nk], rhs=spx[0:nk, 0:QW], start=True, stop=False),
                                   r=[sk, 'linc'], w=[pk(2 + hh)], signal=False)
                                op('pe', lambda e, hh=hh, hs=hs: e.matmul(PS[2 + hh][0:nk, 0:QW], lhsT=NKT[hs, k0:k0 + nk], rhs=QT[hs, q0:q0 + QW], start=False, stop=True),
                                   r=['NKT', 'QT'], w=[pk(2 + hh)])
                                if not lastb:
                                    op('pe', lambda e, hh=hh, spx=spx: e.matmul(PS[4 + hh][:, 0:QW], lhsT=ones_bf[0:nk, :], rhs=spx[0:nk, 0:QW], start=True, stop=True),
                                       r=[sk, 'ones'], w=[pk(4 + hh)])

                        def dve_t(bi):
                            (k0, nk, vb, mj) = blks[bi]
                            for hh in range(2):
                                tx = tt[(bi % 2) * 2 + hh]
                                op('dve', lambda e, hh=hh, tx=tx: e.tensor_tensor(out=tx[0:nk, 0:QW], in0=PS[2 + hh][0:nk, 0:QW], in1=carry[hh][0:nk, 0:QW], op=ALU.add),
                                   r=[pk(2 + hh), ('carry', hh)], w=[('t', bi % 2, hh)])

                        def dve_cu(bi):
                            if bi == nb_ - 1:
                                return
                            for hh in range(2):
                                op('dve', lambda e, hh=hh: e.tensor_tensor(out=carry[hh][:, 0:QW], in0=PS[4 + hh][:, 0:QW], in1=carry[hh][:, 0:QW], op=ALU.add),
                                   r=[pk(4 + hh), ('carry', hh)], w=[('carry', hh)])

                        def act_esp(bi):
                            (k0, nk, vb, mj) = blks[bi]
                            for hh in range(2):
                                op('act', lambda e, hh=hh: e.activation(out=et[hh][0:nk, 0:QW], in_=PS[hh][0:nk, 0:QW], func=AF.Exp), r=[pk(hh)], w=[('e', hh)])
                            for hh in range(2):
                                sk = ('sp', bi % 2, hh); spx = spt[(bi % 2) * 2 + hh]
                                op('act', lambda e, hh=hh, spx=spx: e.activation(out=spx[0:nk, 0:QW], in_=et[hh][0:nk, 0:QW], func=AF.Ln, bias=1.0, scale=1.0),
                                   r=[('e', hh)], w=[sk])
                            if mj is not None:
                                for hh in range(2):
                                    sk = ('sp', bi % 2, hh); spx = spt[(bi % 2) * 2 + hh]
                                    op('pool', lambda e, spx=spx: e.tensor_tensor(out=spx[0:nk, 0:QW], in0=spx[0:nk, 0:QW], in1=masks[0:nk, mj, 0:QW], op=ALU.mult),
                                       r=[sk, 'masks'], w=[sk])

                        def act_a(bi):
                            (k0, nk, vb, mj) = blks[bi]
                            for hh in range(2):
                                tx = tt[(bi % 2) * 2 + hh]; ax = at[(bi % 2) * 2 + hh]
                                op('act', lambda e, tx=tx, ax=ax: e.activation(out=ax[0:nk, 0:QW], in_=tx[0:nk, 0:QW], func=AF.Exp, scale=-1.0),
                                   r=[('t', bi % 2, hh)], w=[('a', bi % 2, hh)])
                            if mj is not None:
                                for hh in range(2):
                                    ax = at[(bi % 2) * 2 + hh]
                                    op('pool', lambda e, ax=ax: e.tensor_tensor(out=ax[0:nk, 0:QW], in0=ax[0:nk, 0:QW], in1=masks[0:nk, mj, 0:QW], op=ALU.mult),
                                       r=[('a', bi % 2, hh), 'masks'], w=[('a', bi % 2, hh)])

                        def pe_O(bi):
                            (k0, nk, vb, mj) = blks[bi]
                            for hh in range(2):
                                ax = at[(bi % 2) * 2 + hh]
                                first = (bi == 0 and hh == 0); last_ = (bi == nb_ - 1 and hh == 1)
                                op('pe', lambda e, hh=hh, ax=ax: e.matmul(O[:, 0:QW], lhsT=Vp[0:nk, vb, hh, :], rhs=ax[0:nk, 0:QW], start=first, stop=last_),
                                   r=[('a', bi % 2, hh), 'Vp'], w=[pk(6)], signal=last_)
                        pe_A(0)
                        for s_ in range(nb_ + 2):
                            if 1 <= s_ <= nb_:
                                pe_BC(s_ - 1)
                                dve_t(s_ - 1)
                            if s_ < nb_:
                                act_esp(s_)
                            if 1 <= s_ <= nb_:
                                act_a(s_ - 1)
                                dve_cu(s_ - 1)
                            if 2 <= s_ <= nb_ + 1:
                                pe_O(s_ - 2)
                            if s_ + 1 < nb_:
                                pe_A(s_ + 1)
                        os_ = ost[ocnt % 2]; okey = ('ost', ocnt % 2); dk = 'ost%d' % (ocnt % 2); ocnt += 1
                        op('act', lambda e, os_=os_: e.activation(out=os_[:, 0:QW], in_=PS[6][:, 0:QW], func=AF.Copy), r=[pk(6)], w=[okey])
                        QS = min(QW, T)
                        op('sp', lambda e, os_=os_: e.dma_start(out=OSTs[g][p][:, q0:q0 + QS], in_=os_[:, 0:QS]), r=[okey], dma=dk)
            em.barrier()

    def gdn_phase(l):
        with contextlib.ExitStack() as es:
            def sb(name, shape, dt=F32):
                return es.enter_context(_sbuf_tensor(name, shape, dt))
            convw = sb("g_convw", [128, 12, 4])
            neg = sb("g_neg", [64, 64]); mstrict = sb("g_mstrict", [64, 256]); i4 = sb("g_i4", [64, 256])
            sel = sb("g_sel", [4, 4, 128]); reset = sb("g_reset", [4, 512])
            alog = sb("g_alog", [4, 1]); dtb = sb("g_dtb", [4, 1]); negA = sb("g_negA", [4, 1]); gng = sb("g_gng", [128, 1])
            S = sb("g_S", [128, 4, 128])
            raw = sb("g_raw", [128, 12, 515])
            qkv = sb("g_qkv", [128, 12, 512])
            ytmpS = [sb("g_ytmp%d" % i, [128, 512]) for i in range(2)]
            sqbS = [sb("g_sqb%d" % i, [128, 512], BF16) for i in range(2)]
            lntS = [sb("g_lnt%d" % i, [128, 512]) for i in range(2)]
            rows_b = sb("g_rb", [4, 512]); rows_a = sb("g_ra", [4, 512])
            r_beta = sb("g_rbeta", [4, 512]); r_g = sb("g_rg", [4, 512]); r_gc = sb("g_rgc", [4, 512]); r_ngc = sb("g_rngc", [4, 512])
            r_gl = sb("g_rgl", [4, 512]); r_eg = sb("g_reg", [4, 512]); r_kes = sb("g_rkes", [4, 512]); r_cd = sb("g_rcd", [4, 512])
            r_beg = sb("g_rbeg", [4, 512]); r_tmp = sb("g_rtmp", [4, 512])
            LL = sb("g_LL", [2, 4, 512]); RR = sb("g_RR", [2, 4, 512])
            vbT = sb("g_vbT", [128, 4, 512]); kbT = sb("g_kbT", [128, 4, 512], BF16); kbgT = sb("g_kbgT", [128, 4, 512], BF16)
            qdT = sb("g_qdT", [128, 4, 512], BF16); keT = sb("g_keT", [128, 4, 512]); cdc = sb("g_cdc", [128, 4, 8])
            qk16 = sb("g_qk16", [128, 8, 512], BF16); Sb = sb("g_Sb", [128, 4, 128], BF16)
            zT = sb("g_zT", [128, 4, 512]); OT = sb("g_OT", [128, 4, 512]); ogs = sb("g_ogs", [128, 4, 512], BF16)
            gmS = [sb("g_gm%d" % i, [64, 256]) for i in range(2)]; BmS = [sb("g_B%d" % i, [64, 256]) for i in range(2)]
            BTmS = [sb("g_BT%d" % i, [64, 256]) for i in range(2)]
            PmS = [[sb("g_P%d_%d" % (j, i), [64, 256]) for i in range(2)] for j in range(2)]
            PTmS = [[sb("g_PT%d_%d" % (j, i), [64, 256]) for i in range(2)] for j in range(2)]
            XmS = [[sb("g_X%d_%d" % (j, i), [64, 256]) for i in range(2)] for j in range(2)]
            QKmO = [sb("g_QKm%d" % i, [64, 256], BF16) for i in range(4)]; XO = [sb("g_XO%d" % i, [64, 256], BF16) for i in range(4)]
            vbm = sb("g_vb", [64, 4, 128]); kem = sb("g_ke", [64, 4, 128], BF16); rhs2 = sb("g_rhs2", [64, 4, 128], BF16); vnew = sb("g_vnew", [64, 4, 128], BF16)

            for wi_ in range(4):
                op('sp', lambda e, wi_=wi_: e.dma_start(out=convw[:, :, wi_], in_=conv_w[l, wi_].rearrange("(c p) -> p c", p=128), allow_slow_non_contiguous=True),
                   w=['convw'], dma='gc0')
            for (t, s_, kname) in ((neg, c_neg, 'neg'), (mstrict, c_mstrict, 'mstrict'), (i4, c_i4, 'i4'), (sel, c_sel, 'sel'), (reset, c_reset, 'reset')):
                op('sp', lambda e, t=t, s_=s_: e.dma_start(out=t[:], in_=s_), w=[kname], dma='gc0')
            op('sp', lambda e: e.dma_start(out=alog[:], in_=a_log[l].rearrange("(h o) -> h o", o=1)), w=['alog'], dma='gc0')
            op('sp', lambda e: e.dma_start(out=dtb[:], in_=dt_bias[l].rearrange("(h o) -> h o", o=1)), w=['dtb'], dma='gc0')
            op('sp', lambda e: e.dma_start(out=gng[:], in_=gdn_gain[l].rearrange("(h o) -> h o", o=1)), w=['gng'], dma='gc0')
            op('act', lambda e: e.activation(out=negA[:], in_=alog[:], func=AF.Exp), r=['alog'], w=['negA'])
            op('dve', lambda e: e.tensor_scalar(out=negA[:], in0=negA[:], scalar1=-1.0, scalar2=None, op0=ALU.mult), r=['negA'], w=['negA'])
            op('dve', lambda e: e.memset(LL[:], 1.0), w=['LL'])
            op('dve', lambda e: e.memset(RR[:], 1.0), w=['RR'])
            pbc = [0]

            def nb():
                pbc[0] += 1
                return pbc[0] % 8

            for g, T in groups:
                TT = min(512, T)
                nch = TT // 64
                if g == 'p':
                    op('dve', lambda e: e.memset(S[:], 0.0), w=['S0', 'S1', 'S2', 'S3'])
                else:
                    op('sp', lambda e: e.dma_start(out=S[:], in_=state_gdn[l].rearrange("h k v -> k h v")), w=['S0', 'S1', 'S2', 'S3'], dma='gS')
                op('pool', lambda e: e.tensor_copy(out=Sb[:], in_=S[:]), r=['S0', 'S1', 'S2', 'S3'], w=['Sb'])
                def partA(ti_):
                    t0_ = ti_ * TT
                    if ti_ == 0:
                        if g == 'p':
                            op('dve', lambda e: e.memset(raw[:, :, 0:3], 0.0), w=['raw'])
                        else:
                            for wi_ in range(3):
                                op('sp', lambda e, wi_=wi_: e.dma_start(out=raw[:, :, wi_], in_=state_conv[l, wi_].rearrange("(c p) -> p c", p=128),
                                                                        allow_slow_non_contiguous=True), w=['raw'], dma='graw')
                        for c in range(12):
                            op('sp', lambda e, c=c: e.dma_start(out=raw[:, c, 3:3 + TT], in_=RAWs[g][c][:, 0:TT]), w=['raw'], dma='graw')
                    else:
                        for c in range(12):
                            op('sp', lambda e, c=c: e.dma_start(out=raw[:, c, 0:3 + TT], in_=RAWs[g][c][:, t0_ - 3:t0_ + TT]), w=['raw'], dma='graw')
                    yield
                    for c in range(12):
                        op('dve', lambda e, c=c: e.tensor_scalar(out=qkv[:, c, 0:TT], in0=raw[:, c, 0:TT], scalar1=convw[:, c, 0:1], scalar2=None, op0=ALU.mult),
                           r=['raw', 'convw'], w=[('qkv', c)])
                        for i in range(1, 4):
                            op('dve', lambda e, c=c, i=i: e.scalar_tensor_tensor(out=qkv[:, c, 0:TT], in0=raw[:, c, i:i + TT], scalar=convw[:, c, i:i + 1],
                                                                               in1=qkv[:, c, 0:TT], op0=ALU.mult, op1=ALU.add),
                               r=['raw', 'convw', ('qkv', c)], w=[('qkv', c)])
                        yield

                def silu_all():
                    for c3 in range(0, 12, 4):
                        op('act', lambda e, c3=c3: e.activation(out=qkv[:, c3:c3 + 4, 0:TT], in_=qkv[:, c3:c3 + 4, 0:TT], func=AF.Silu),
                           r=[('qkv', c) for c in range(c3, c3 + 4)], w=[('qkv', c) for c in range(c3, c3 + 4)])

                for ti in range(T // TT):
                    t0 = ti * TT
                    if ti == 0:
                        for _ in partA(0):
                            pass
                        silu_all()
                    op('sp', lambda e: e.dma_start(out=rows_b[:, 0:TT], in_=BAs[g][0][:, t0:t0 + TT]), w=['rows_b'], dma='grow')
                    op('sp', lambda e: e.dma_start(out=rows_a[:, 0:TT], in_=BAs[g][1][:, t0:t0 + TT]), w=['rows_a'], dma='grow')
                    for h in range(4):
                        op('sp', lambda e, h=h: e.dma_start(out=zT[:, h, 0:TT], in_=ZTs[g][h][:, t0:t0 + TT]), w=['zT'], dma='gz')
                    for c in range(8):
                        b = nb()
                        sqb = sqbS[c % 2]; lnt = lntS[c % 2]; sqk = ('sqb', c % 2); lnk = ('lnt', c % 2)
                        op('dve', lambda e, c=c, sqb=sqb: e.tensor_tensor(out=sqb[:, 0:TT], in0=qkv[:, c, 0:TT], in1=qkv[:, c, 0:TT], op=ALU.mult),
                           r=[('qkv', c)], w=[sqk])
                        op('pe', lambda e, b=b, sqb=sqb: e.matmul(PS[b][:, 0:TT], lhsT=ones_bf[:], rhs=sqb[:, 0:TT], start=True, stop=True), r=[sqk, 'ones'], w=[pk(b)])
                        op('act', lambda e, b=b, lnt=lnt: e.activation(out=lnt[:, 0:TT], in_=PS[b][:, 0:TT], func=AF.Ln, bias=EPS, scale=1.0), r=[pk(b)], w=[lnk])
                        bias = float(-0.5 * np.log(128.0)) if c < 4 else 0.0
                        op('act', lambda e, bias=bias, lnt=lnt: e.activation(out=lnt[:, 0:TT], in_=lnt[:, 0:TT], func=AF.Exp, scale=-0.5, bias=bias), r=[lnk], w=[lnk])
                        op('dve', lambda e, c=c, lnt=lnt: e.tensor_tensor(out=qkv[:, c, 0:TT], in0=qkv[:, c, 0:TT], in1=lnt[:, 0:TT], op=ALU.mult),
                           r=[('qkv', c), lnk], w=[('qkv', c)])
                        op('pool', lambda e, c=c: e.tensor_copy(out=qk16[:, c, 0:TT], in_=qkv[:, c, 0:TT]), r=[('qkv', c)], w=[('qk16', c)])
                    op('act', lambda e: e.activation(out=r_beta[:, 0:TT], in_=rows_b[:, 0:TT], func=AF.Exp, scale=-1.0), r=['rows_b'], w=['r_beta'])
                    op('dve', lambda e: e.tensor_scalar(out=r_beta[:, 0:TT], in0=r_beta[:, 0:TT], scalar1=1.0, scalar2=None, op0=ALU.add), r=['r_beta'], w=['r_beta'])
                    op('dve', lambda e: e.reciprocal(out=r_beta[:, 0:TT], in_=r_beta[:, 0:TT]), r=['r_beta'], w=['r_beta'])
                    op('act', lambda e: e.activation(out=r_tmp[:, 0:TT], in_=rows_a[:, 0:TT], func=AF.Exp, bias=dtb[:, 0:1], scale=1.0), r=['rows_a', 'dtb'], w=['r_tmp'])
                    op('act', lambda e: e.activation(out=r_tmp[:, 0:TT], in_=r_tmp[:, 0:TT], func=AF.Ln, bias=1.0, scale=1.0), r=['r_tmp'], w=['r_tmp'])
                    op('dve', lambda e: e.tensor_scalar(out=r_g[:, 0:TT], in0=r_tmp[:, 0:TT], scalar1=negA[:, 0:1], scalar2=None, op0=ALU.mult),
                       r=['r_tmp', 'negA'], w=['r_g'])
                    op('dve', lambda e: e.tensor_tensor_scan(out=r_gc[:, 0:TT], data0=reset[:, 0:TT], data1=r_g[:, 0:TT], initial=0.0, op0=ALU.mult, op1=ALU.add),
                       r=['r_g', 'reset'], w=['r_gc'])
                    op('dve', lambda e: e.tensor_scalar(out=r_ngc[:, 0:TT], in0=r_gc[:, 0:TT], scalar1=-1.0, scalar2=None, op0=ALU.mult), r=['r_gc'], w=['r_ngc'])
                    gc3 = r_gc[:, 0:TT].rearrange("h (n c) -> h n c", c=64)
                    op('dve', lambda e: e.tensor_copy(out=r_gl[:, 0:TT].rearrange("h (n c) -> h n c", c=64), in_=gc3[:, :, 63:64].to_broadcast([4, nch, 64])),
                       r=['r_gc'], w=['r_gl'])
                    op('act', lambda e: e.activation(out=r_eg[:, 0:TT], in_=r_gc[:, 0:TT], func=AF.Exp), r=['r_gc'], w=['r_eg'])
                    op('dve', lambda e: e.tensor_tensor(out=r_tmp[:, 0:TT], in0=r_gl[:, 0:TT], in1=r_gc[:, 0:TT], op=ALU.subtract), r=['r_gl', 'r_gc', 'r_tmp'], w=['r_tmp'])
                    op('act', lambda e: e.activation(out=r_kes[:, 0:TT], in_=r_tmp[:, 0:TT], func=AF.Exp), r=['r_tmp'], w=['r_kes'])
                    op('act', lambda e: e.activation(out=r_cd[:, 0:TT], in_=r_gl[:, 0:TT], func=AF.Exp), r=['r_gl'], w=['r_cd'])
                    op('dve', lambda e: e.tensor_tensor(out=r_beg[:, 0:TT], in0=r_beta[:, 0:TT], in1=r_eg[:, 0:TT], op=ALU.mult), r=['r_beta', 'r_eg'], w=['r_beg'])
                    for h in range(4):
                        op('sp', lambda e, h=h: e.dma_start(out=LL[1:2, h, 0:TT], in_=r_ngc[h:h + 1, 0:TT]), r=['r_ngc'], w=['LL'], dma='gLL')
                        op('sp', lambda e, h=h: e.dma_start(out=RR[0:1, h, 0:TT], in_=r_gc[h:h + 1, 0:TT]), r=['r_gc'], w=['RR'], dma='gLL')
                    for h in range(4):
                        def bc(rows, rkey):
                            b = nb()
                            op('pe', lambda e: e.matmul(PS[b][:, 0:TT], lhsT=sel[:, h, :], rhs=rows[:, 0:TT], start=True, stop=True), r=[rkey, 'sel'], w=[pk(b)])
                            return b
                        b = bc(r_beta, 'r_beta')
                        op('dve', lambda e: e.tensor_tensor(out=vbT[:, h, 0:TT], in0=qkv[:, 8 + h, 0:TT], in1=PS[b][:, 0:TT], op=ALU.mult),
                           r=[pk(b), ('qkv', 8 + h)], w=['vbT'])
                        op('dve', lambda e: e.tensor_tensor(out=kbT[:, h, 0:TT], in0=qkv[:, 4 + h, 0:TT], in1=PS[b][:, 0:TT], op=ALU.mult),
                           r=[pk(b), ('qkv', 4 + h)], w=['kbT'])
                        b = bc(r_beg, 'r_beg')
                        op('dve', lambda e: e.tensor_tensor(out=kbgT[:, h, 0:TT], in0=qkv[:, 4 + h, 0:TT], in1=PS[b][:, 0:TT], op=ALU.mult),
                           r=[pk(b), ('qkv', 4 + h)], w=['kbgT'])
                        b = bc(r_eg, 'r_eg')
                        op('dve', lambda e: e.tensor_tensor(out=qdT[:, h, 0:TT], in0=qkv[:, h, 0:TT], in1=PS[b][:, 0:TT], op=ALU.mult),
                           r=[pk(b), ('qkv', h)], w=['qdT'])
                        b = bc(r_kes, 'r_kes')
                        op('dve', lambda e: e.tensor_tensor(out=keT[:, h, 0:TT], in0=qkv[:, 4 + h, 0:TT], in1=PS[b][:, 0:TT], op=ALU.mult),
                           r=[pk(b), ('qkv', 4 + h)], w=['keT'])
                        b = bc(r_cd, 'r_cd')
                        op('act', lambda e: e.activation(out=cdc[:, h, 0:nch], in_=PS[b][:, 0:TT].rearrange("p (n c) -> p n c", c=64)[:, :, 0], func=AF.Copy),
                           r=[pk(b)], w=['cdc'])
                    qk_all = [('qkv', c) for c in range(12)]
                    qk16_all = [('qk16', c) for c in range(8)]
                    Sk = ['S0', 'S1', 'S2', 'S3']

                    def prep(n):
                        cs = slice(n * 64, (n + 1) * 64)
                        st = n % 2; o4 = n % 4
                        c0 = 0
                        PB = {0: 3 * st, 3: 3 * st, 1: 3 * st + 1, 4: 3 * st + 1, 2: 3 * st + 2, 5: 3 * st + 2}
                        gm, Bm, BTm, Pm, PTm, Xm = gmS[st], BmS[st], BTmS[st], PmS[st], PTmS[st], XmS[st]
                        QKm = QKmO[o4]

                        def K(name, *a):
                            return (name, st) + a

                        def pp(b):
                            return pk(PB[b])
                        for h in range(4):
                            hc = slice(c0 + h * 64, c0 + (h + 1) * 64)
                            op('pe', lambda e, h=h, hc=hc: e.matmul(PS[PB[0]][0:64, hc], lhsT=LL[0:2, h, cs], rhs=RR[0:2, h, cs], start=True, stop=False),
                               r=['LL', 'RR'], w=[pp(0)], signal=False)
                            op('pe', lambda e, h=h, hc=hc: e.matmul(PS[PB[0]][0:64, hc], lhsT=ident[0:64, 0:64], rhs=neg[:, :], start=False, stop=True),
                               r=['ident', 'neg'], w=[pp(0)], signal=(h == 3))
                        for h in range(4):
                            hc = slice(c0 + h * 64, c0 + (h + 1) * 64)
                            op('pe', lambda e, h=h, hc=hc: e.matmul(PS[PB[1]][0:64, hc], lhsT=qk16[:, 4 + h, cs], rhs=kbT[:, h, cs], start=True, stop=True),
                               r=qk16_all + ['kbT'], w=[pp(1)], signal=(h == 3))
                        for h in range(4):
                            hc = slice(c0 + h * 64, c0 + (h + 1) * 64)
                            op('pe', lambda e, h=h, hc=hc: e.matmul(PS[PB[2]][0:64, hc], lhsT=qk16[:, 4 + h, cs], rhs=qk16[:, h, cs], start=True, stop=True),
                               r=qk16_all, w=[pp(2)], signal=(h == 3))
                        yield
                        op('act', lambda e: e.activation(out=gm[:, :], in_=PS[PB[0]][0:64, c0:c0 + 256], func=AF.Exp), r=[pp(0)], w=[K('gm')])
                        yield
                        op('dve', lambda e: e.scalar_tensor_tensor(out=Bm[:, :], in0=PS[PB[1]][0:64, c0:c0 + 256], scalar=-1.0, in1=gm[:, :], op0=ALU.mult, op1=ALU.mult),
                           r=[pp(1), K('gm')], w=[K('Bm')])
                        op('dve', lambda e: e.tensor_tensor(out=QKm[:, :], in0=PS[PB[2]][0:64, c0:c0 + 256], in1=gm[:, :], op=ALU.mult), r=[pp(2), K('gm')], w=[('QKm', o4)])
                        yield
                        op('dve', lambda e: e.tensor_tensor(out=Bm[:, :], in0=Bm[:, :], in1=mstrict[:, :], op=ALU.mult), r=[K('Bm'), 'mstrict'], w=[K('Bm')])
                        yield
                        for h in range(4):
                            hc = slice(h * 64, (h + 1) * 64)
                            pc = slice(c0 + h * 64, c0 + (h + 1) * 64)
                            op('pe', lambda e, hc=hc, pc=pc: e.transpose(PS[PB[3]][0:64, pc], Bm[:, hc], ident[0:64, 0:64]), r=[K('Bm'), 'ident'], w=[pp(3)], signal=(h == 3))
                        op('dve', lambda e: e.tensor_tensor(out=Xm[0][:, :], in0=Bm[:, :], in1=i4[:, :], op=ALU.add), r=[K('Bm'), 'i4'], w=[K('X', 0)])
                        yield
                        op('act', lambda e: e.activation(out=BTm[:, :], in_=PS[PB[3]][0:64, c0:c0 + 256], func=AF.Copy), r=[pp(3)], w=[K('BTm')])
                        yield
                        Pp, Ppk, PTp, PTpk = Bm, K('Bm'), BTm, K('BTm')
                        xi = 0
                        for lev in range(1, 6):
                            pi = lev % 2
                            if lev < 5:
                                for h in range(4):
                                    hc = slice(h * 64, (h + 1) * 64)
                                    pc = slice(c0 + h * 64, c0 + (h + 1) * 64)
                                    op('pe', lambda e, hc=hc, pc=pc, Pp=Pp, PTp=PTp: e.matmul(PS[PB[4]][0:64, pc], lhsT=PTp[:, hc], rhs=Pp[:, hc], start=True, stop=True),
                                       r=[Ppk, PTpk], w=[pp(4)], signal=(h == 3))
                            for h in range(4):
                                hc = slice(h * 64, (h + 1) * 64)
                                pc = slice(c0 + h * 64, c0 + (h + 1) * 64)
                                op('pe', lambda e, hc=hc, pc=pc, Pp=Pp, PTp=PTp: e.matmul(PS[PB[3]][0:64, pc], lhsT=Pp[:, hc], rhs=PTp[:, hc], start=True, stop=True),
                                   r=[Ppk, PTpk], w=[pp(3)], signal=(h == 3))
                            yield
                            if lev < 5:
                                op('dve', lambda e, pi=pi: e.tensor_copy(out=Pm[pi][:, :], in_=PS[PB[4]][0:64, c0:c0 + 256]), r=[pp(4)], w=[K('P', pi)])
                            op('act', lambda e, pi=pi: e.activation(out=PTm[pi][:, :], in_=PS[PB[3]][0:64, c0:c0 + 256], func=AF.Copy), r=[pp(3)], w=[K('PT', pi)])
                            yield
                            for h in range(4):
                                hc = slice(h * 64, (h + 1) * 64)
                                pc = slice(c0 + h * 64, c0 + (h + 1) * 64)
                                op('pe', lambda e, hc=hc, pc=pc, pi=pi, xi=xi: e.matmul(PS[PB[5]][0:64, pc], lhsT=PTm[pi][:, hc], rhs=Xm[xi][:, hc], start=True, stop=True),
                                   r=[K('PT', pi), K('X', xi)], w=[pp(5)], signal=(h == 3))
                            yield
                            if lev < 5:
                                op('dve', lambda e, xi=xi: e.tensor_tensor(out=Xm[1 - xi][:, :], in0=PS[PB[5]][0:64, c0:c0 + 256], in1=Xm[xi][:, :], op=ALU.add),
                                   r=[pp(5), K('X', xi)], w=[K('X', 1 - xi)])
                            else:
                                op('dve', lambda e, xi=xi: e.tensor_tensor(out=XO[o4][:, :], in0=PS[PB[5]][0:64, c0:c0 + 256], in1=Xm[xi][:, :], op=ALU.add),
                                   r=[pp(5), K('X', xi)], w=[('XO', o4)])
                            yield
                            xi = 1 - xi
                            Pp, Ppk, PTp, PTpk = Pm[pi], K('P', pi), PTm[pi], K('PT', pi)

                    def seq(n):
                        cs = slice(n * 64, (n + 1) * 64)
                        o4 = n % 4
                        X = XO[o4]; Xk = ('XO', o4); QKm = QKmO[o4]
                        for h in range(4):
                            op('pe', lambda e, h=h: e.transpose(PS[6][0:64, h * 128:(h + 1) * 128], vbT[:, h, cs], ident[:]), r=['vbT', 'ident'], w=[pk(6)], signal=(h == 3))
                        for h in range(4):
                            op('pe', lambda e, h=h: e.transpose(PS[7][0:64, h * 128:(h + 1) * 128], keT[:, h, cs], ident[:]), r=['keT', 'ident'], w=[pk(7)], signal=(h == 3))
                        yield
                        op('act', lambda e: e.activation(out=vbm[:].rearrange("p h d -> p (h d)"), in_=PS[6][0:64, :], func=AF.Copy), r=[pk(6)], w=['vbm'])
                        op('act', lambda e: e.activation(out=kem[:].rearrange("p h d -> p (h d)"), in_=PS[7][0:64, :], func=AF.Copy), r=[pk(7)], w=['kem'])
                        yield
                        for h in range(4):
                            op('pe', lambda e, h=h: e.matmul(PS[6][0:64, h * 128:(h + 1) * 128], lhsT=kbgT[:, h, cs], rhs=Sb[:, h, :], start=True, stop=True),
                               r=['kbgT', 'Sb'], w=[pk(6)], signal=(h == 3))
                        yield
                        op('dve', lambda e: e.tensor_tensor(out=rhs2[:].rearrange("p h d -> p (h d)"), in0=vbm[:].rearrange("p h d -> p (h d)"), in1=PS[6][0:64, :], op=ALU.subtract),
                           r=[pk(6), 'vbm'], w=['rhs2'])
                        yield
                        for h in range(4):
                            hc = slice(h * 64, (h + 1) * 64)
                            op('pe', lambda e, h=h, hc=hc: e.matmul(PS[7][0:64, h * 128:(h + 1) * 128], lhsT=X[:, hc], rhs=rhs2[:, h, :], start=True, stop=True),
                               r=[Xk, 'rhs2'], w=[pk(7)], signal=(h == 3))
                        yield
                        op('act', lambda e: e.activation(out=vnew[:].rearrange("p h d -> p (h d)"), in_=PS[7][0:64, :], func=AF.Copy), r=[pk(7)], w=['vnew'])
                        yield
                        for h in range(4):
                            hc = slice(h * 64, (h + 1) * 64)
                            op('pe', lambda e, h=h, hc=hc: e.matmul(PS[6][:, hc], lhsT=Sb[:, h, :], rhs=qdT[:, h, cs], start=True, stop=False),
                               r=['Sb', 'qdT'], w=[pk(6)], signal=False)
                            op('pe', lambda e, h=h, hc=hc: e.matmul(PS[6][:, hc], lhsT=vnew[:, h, :], rhs=QKm[:, hc], start=False, stop=True),
                               r=['vnew', ('QKm', o4)], w=[pk(6)], signal=(h == 3))
                        for h in range(4):
                            op('pe', lambda e, h=h: e.matmul(PS[7][:, h * 128:(h + 1) * 128], lhsT=kem[:, h, :], rhs=vnew[:, h, :], start=True, stop=True),
                               r=['kem', 'vnew'], w=[pk(7)], signal=(h == 3))
                        yield
                        op('act', lambda e: e.activation(out=OT[:, :, cs], in_=PS[6][:, 0:256].rearrange("p (h c) -> p h c", c=64), func=AF.Copy), r=[pk(6)], w=['OT'])
                        for h in range(4):
                            op('dve', lambda e, h=h: e.scalar_tensor_tensor(out=S[:, h, :], in0=S[:, h, :], scalar=cdc[:, h, n:n + 1], in1=PS[7][:, h * 128:(h + 1) * 128],
                                                                          op0=ALU.mult, op1=ALU.add),
                               r=[pk(7), 'cdc', Sk[h]], w=[Sk[h]])
                        op('pool', lambda e: e.tensor_copy(out=Sb[:], in_=S[:]), r=Sk, w=['Sb'])
                        yield

                    def chain(gens):
                        for g_ in gens:
                            yield from g_

                    def interleave(gens):
                        gens = [g_ for g_ in gens if g_ is not None]
                        while gens:
                            for g_ in list(gens):
                                try:
                                    next(g_)
                                except StopIteration:
                                    gens.remove(g_)
                    npair = (nch + 1) // 2
                    genA = partA(ti + 1) if ti + 1 < T // TT else None
                    interleave([prep(0), prep(1) if nch > 1 else None, genA])
                    for kp in range(npair):
                        nxt = []
                        if kp + 1 < npair:
                            nxt = [prep(2 * kp + 2), prep(2 * kp + 3)]
                        seqs = [seq(2 * kp)] + ([seq(2 * kp + 1)] if 2 * kp + 1 < nch else [])
                        interleave(nxt + [chain(seqs), genA])
                    if genA is not None:
                        for _ in genA:
                            pass
                        silu_all()
                    for h in range(4):
                        b = nb()
                        sqb = sqbS[h % 2]; lnt = lntS[h % 2]; ytmp = ytmpS[h % 2]
                        sqk = ('sqb', h % 2); lnk = ('lnt', h % 2); ytk = ('ytmp', h % 2)
                        op('dve', lambda e, h=h, sqb=sqb: e.tensor_tensor(out=sqb[:, 0:TT], in0=OT[:, h, 0:TT], in1=OT[:, h, 0:TT], op=ALU.mult), r=['OT'], w=[sqk])
                        op('pe', lambda e, b=b, sqb=sqb: e.matmul(PS[b][:, 0:TT], lhsT=ones_bf[:], rhs=sqb[:, 0:TT], start=True, stop=True), r=[sqk, 'ones'], w=[pk(b)])
                        op('act', lambda e, b=b, lnt=lnt: e.activation(out=lnt[:, 0:TT], in_=PS[b][:, 0:TT], func=AF.Ln, bias=EPS, scale=1.0 / 128), r=[pk(b)], w=[lnk])
                        op('act', lambda e, lnt=lnt: e.activation(out=lnt[:, 0:TT], in_=lnt[:, 0:TT], func=AF.Exp, scale=-0.5), r=[lnk], w=[lnk])
                        op('dve', lambda e, h=h, lnt=lnt, ytmp=ytmp: e.tensor_tensor(out=ytmp[:, 0:TT], in0=OT[:, h, 0:TT], in1=lnt[:, 0:TT], op=ALU.mult), r=['OT', lnk], w=[ytk])
                        op('act', lambda e, h=h, lnt=lnt: e.activation(out=lnt[:, 0:TT], in_=zT[:, h, 0:TT], func=AF.Silu), r=['zT', lnk], w=[lnk])
                        op('dve', lambda e, h=h, lnt=lnt, ytmp=ytmp: e.scalar_tensor_tensor(out=ogs[:, h, 0:TT], in0=ytmp[:, 0:TT], scalar=gng[:, 0:1], in1=lnt[:, 0:TT], op0=ALU.mult, op1=ALU.mult),
                           r=[ytk, lnk, 'gng'], w=[('ogs', h)])
                        op('sp', lambda e, h=h: e.dma_start(out=OGTs[g][h][:, t0:t0 + TT], in_=ogs[:, h, 0:TT]), r=[('ogs', h)], dma='gog%d' % h)
                out_dma_keys.add(('dma', 'gSo'))
                op('sp', lambda e: e.dma_start(out=g_out[g][l].rearrange("h k v -> k h v"), in_=S[:]), r=['S0', 'S1', 'S2', 'S3'], dma='gSo')
            em.barrier()

    def m2_phase(l):
        with contextlib.ExitStack() as es:
            Wg = load_weight_bf16(es, "Wg", w_in[l], KD, 2048, c0=3592)
            Wsb = load_weight_bf16(es, "Wsb", w_bsb[l], 4, D)
            Wgd = load_weight_bf16(es, "Wgd", w_bgdn[l], 4, D)
            Wo = load_weight_bf16(es, "Wo", w_out[l], KD, D)
            g2 = load_gain_bc(es, "g2", norm_g[l, 2])
            g3 = load_gain_bc(es, "g3", norm_g[l, 3])
            scr = alloc_norm_scr(es)
            ss2 = es.enter_context(_sbuf_tensor("f_ss2", [128, 1], F32))
            lnv2 = es.enter_context(_sbuf_tensor("f_lnv2", [128, 1], F32))
            rstd2 = es.enter_context(_sbuf_tensor("f_rstd2", [128, 1], F32))
            scr2 = (scr[0], ss2, lnv2, rstd2)
            xts = [es.enter_context(_sbuf_tensor("m_x%d" % i, [128, 4, D], F32)) for i in range(2)]
            xnT = es.enter_context(_sbuf_tensor("m_xnT", [128, KD, 512], BF16))
            osb = [es.enter_context(_sbuf_tensor("m_osb%d" % i, [128, 4, 512], BF16)) for i in range(2)]
            ogd = [es.enter_context(_sbuf_tensor("m_ogd%d" % i, [128, 4, 512], BF16)) for i in range(2)]
            sg0 = [es.enter_context(_sbuf_tensor("m_sg0%d" % i, [128, 512], F32)) for i in range(2)]
            sg1 = [es.enter_context(_sbuf_tensor("m_sg1%d" % i, [128, 512], F32)) for i in range(2)]
            tmp = [es.enter_context(_sbuf_tensor("m_tmp%d" % i, [128, 512], F32)) for i in range(2)]
            mT = es.enter_context(_sbuf_tensor("m_mT", [128, KD, 512], BF16))
            ycat = [es.enter_context(_sbuf_tensor("m_y%d" % i, [128, D], F32)) for i in range(2)]
            tiles = []
            for g, T in groups:
                TT = min(512, T)
                for ti in range(T // TT):
                    tiles.append((g, ti * TT, TT))

            def load(i):
                g, t0, TT = tiles[i]
                pt = min(128, TT); nsub = TT // pt
                xt = xts[i % 2]
                op('sp', lambda e: e.dma_start(out=xt[0:pt, 0:nsub, :], in_=RES[g][t0:t0 + TT, :].rearrange("(s p) d -> p s d", p=pt)),
                   r=[('res', g, t0)], w=[('mx', i % 2)], dma='mx%d' % (i % 2))
                for k in range(4):
                    op('sp', lambda e, k=k: e.dma_start(out=osb[i % 2][:, k, 0:TT], in_=OSTs[g][k][:, t0:t0 + TT]), w=[('osb', i % 2)], dma='mo%d' % (i % 2))
                    op('sp', lambda e, k=k: e.dma_start(out=ogd[i % 2][:, k, 0:TT], in_=OGTs[g][k][:, t0:t0 + TT]), w=[('ogd', i % 2)], dma='mo%d' % (i % 2))
            load(0)
            cnt = 0
            for i, (g, t0, TT) in enumerate(tiles):
                if i + 1 < len(tiles):
                    load(i + 1)
                pt = min(128, TT); nsub = TT // pt
                xt = xts[i % 2]; xkey = ('mx', i % 2)
                norm_to_T(xt, xkey, pt, nsub, g2, 'g2', xnT, 'xnT', scr, 'm')
                xk = xnT_keys('xnT', nsub)
                for m in range(KD):
                    j = m % 2
                    for (gi, b, sgt, sgk) in ((0, 0 + j, sg0[j], ('sg0', j)), (1, 2 + j, sg1[j], ('sg1', j))):
                        for k in range(KD):
                            op('pe', lambda e, k=k, b=b, gi=gi, m=m: e.matmul(PS[b][:, 0:TT], lhsT=Wg[:, k, gi * D + m * 128:gi * D + (m + 1) * 128], rhs=xnT[:, k, 0:TT],
                                                                             start=(k == 0), stop=(k == KD - 1)),
                               r=xk + ['Wg'], w=[pk(b)], signal=(k == KD - 1))
                        op('act', lambda e, b=b, sgt=sgt: e.activation(out=sgt[:, 0:TT], in_=PS[b][:, 0:TT], func=AF.Sigmoid), r=[pk(b)], w=[sgk])
                    for (b, Wb, wk, src, srck) in ((4, Wsb, 'Wsb', osb[i % 2], ('osb', i % 2)), (5, Wgd, 'Wgd', ogd[i % 2], ('ogd', i % 2))):
                        for k in range(4):
                            op('pe', lambda e, k=k, b=b, Wb=Wb, src=src, m=m: e.matmul(PS[b][:, 0:TT], lhsT=Wb[:, k, m * 128:(m + 1) * 128], rhs=src[:, k, 0:TT],
                                                                                      start=(k == 0), stop=(k == 3)),
                               r=[wk, srck], w=[pk(b)], signal=(k == 3))
                    op('dve', lambda e, j=j: e.tensor_tensor(out=tmp[j][:, 0:TT], in0=PS[4][:, 0:TT], in1=sg0[j][:, 0:TT], op=ALU.mult),
                       r=[pk(4), ('sg0', j)], w=[('tmp', j)])
                    op('dve', lambda e, j=j: e.tensor_tensor(out=sg1[j][:, 0:TT], in0=PS[5][:, 0:TT], in1=sg1[j][:, 0:TT], op=ALU.mult),
                       r=[pk(5), ('sg1', j)], w=[('sg1', j)])
                    op('dve', lambda e, j=j, m=m: e.tensor_tensor(out=mT[:, m, 0:TT], in0=tmp[j][:, 0:TT], in1=sg1[j][:, 0:TT], op=ALU.add),
                       r=[('tmp', j), ('sg1', j)], w=[('mT', m)])
                mk = [('mT', m) for m in range(KD)]
                for s in range(nsub):
                    yc = ycat[cnt % 2]; ykey = ('ycat', cnt % 2); cnt += 1
                    for half in range(2):
                        b = 6 + half
                        for k in range(KD):
                            op('pe', lambda e, k=k, b=b, s=s, half=half: e.matmul(PS[b][0:pt, :], lhsT=mT[:, k, s * pt:(s + 1) * pt], rhs=Wo[:, k, half * 512:(half + 1) * 512],
                                                                                 start=(k == 0), stop=(k == KD - 1)),
                               r=mk + ['Wo'], w=[pk(b)], signal=(k == KD - 1))
                        op('act', lambda e, b=b, half=half, yc=yc: e.activation(out=yc[0:pt, half * 512:(half + 1) * 512], in_=PS[b][0:pt, :], func=AF.Copy),
                           r=[pk(b)], w=[ykey])
                    post_norm_residual(yc, ykey, pt, xt, xkey, s, g3, 'g3', scr2, 1.0)
                op('sp', lambda e: e.dma_start(out=RES[g][t0:t0 + TT, :].rearrange("(s p) d -> p s d", p=pt), in_=xt[0:pt, 0:nsub, :]),
                   r=[xkey], w=[('res', g, t0)], dma='mo_%d' % (i % 2))
            em.barrier()

    nph = [0]
    stop_after = cfg.get('stop_after', 10 ** 9)

    def runp(fn, *a):
        if nph[0] < stop_after:
            fn(*a)
        nph[0] += 1
    for l in range(DEPTH):
        src = x_in if l == 0 else RES
        runp(ffn_phase, l, 0, src, RES, False)
        runp(m1_phase, l)
        runp(att_phase, l)
        runp(gdn_phase, l)
        runp(m2_phase, l)
        runp(ffn_phase, l, 1, RES, RES, l == DEPTH - 1)
    em.barrier()
    return nc, em


def make_consts():
    c = {}
    c['c_ident'] = np.eye(128, dtype=np.float32)
    j = np.arange(128)
    c['c_linc'] = (j[:, None] >= j[None, :]).astype(np.float32)
    r = np.arange(128)[:, None, None]; jj = np.arange(4)[None, :, None]; cc = np.arange(512)[None, None, :]
    c['c_masks'] = ((cc - r - 128 * jj) > 0).astype(np.float32)
    j64 = np.arange(64)
    c['c_neg'] = np.where(j64[:, None] <= j64[None, :], 0.0, NEGBIG).astype(np.float32)
    ms = (j64[:, None] < j64[None, :]).astype(np.float32)
    c['c_mstrict'] = np.tile(ms, (1, 4))
    c['c_i4'] = np.tile(np.eye(64, dtype=np.float32), (1, 4))
    sel = np.zeros((4, 4, 128), np.float32)
    for h in range(4):
        sel[h, h, :] = 1.0
    c['c_sel'] = sel
    rs = np.ones((4, 512), np.float32); rs[:, ::64] = 0.0
    c['c_reset'] = rs
    return c


def run(cfg, inputs, n_cores=8):
    nc, em = build(cfg)
    T_P, T_S, PAST, DEPTH = cfg['T_P'], cfg['T_S'], cfg['PAST'], cfg['DEPTH']
    f = lambda a: np.ascontiguousarray(np.asarray(a, dtype=np.float32))
    consts = make_consts()
    NB = inputs['x_prompt'].shape[0]
    shared = {
        'norm_g': f(inputs['norm_gains']), 'w1u': f(inputs['w_ffn1_up']), 'w1d': f(inputs['w_ffn1_down']),
        'w2u': f(inputs['w_ffn2_up']), 'w2d': f(inputs['w_ffn2_down']), 'w_in': f(inputs['w_in']),
        'conv_w': f(inputs['conv_w']), 'a_log': f(inputs['gdn_a_log']), 'dt_bias': f(inputs['gdn_dt_bias']),
        'gdn_gain': f(inputs['gdn_norm_gain']), 'w_bsb': f(inputs['w_branch_sb']), 'w_bgdn': f(inputs['w_branch_gdn']),
        'w_out': f(inputs['w_out']),
    }
    shared.update(consts)
    in_maps = []
    for c in range(n_cores):
        m = dict(shared)
        m['x_p'] = f(inputs['x_prompt'][c % NB])
        m['x_s'] = f(inputs['x_sample'][c])
        m['cache_k'] = f(np.asarray(inputs['cache_sb_k'])[:, c].reshape(DEPTH, PAST, 512))
        m['cache_v'] = f(np.asarray(inputs['cache_sb_v'])[:, c].reshape(DEPTH, PAST, 512))
        m['state_gdn'] = f(np.asarray(inputs['state_gdn'])[:, c])
        m['state_conv'] = f(np.asarray(inputs['state_conv'])[:, c])
        in_maps.append(m)
    res = run_bass_kernel_spmd(nc, in_maps, core_ids=list(range(n_cores)))
    R = res.results
    NS = n_cores
    y_p = np.stack([R[b]['y_p'] for b in range(NB)])
    y_s = np.stack([R[c]['y_s'] for c in range(NS)])
    k_p = np.stack([R[b]['k_p'] for b in range(NB)], axis=1).reshape(DEPTH, NB, T_P, 8, 64)
    v_p = np.stack([R[b]['v_p'] for b in range(NB)], axis=1).reshape(DEPTH, NB, T_P, 8, 64)
    g_p = np.stack([R[b]['g_p'] for b in range(NB)], axis=1)
    c_p = np.stack([R[b]['c_p'] for b in range(NB)], axis=1)
    k_s = np.stack([R[c]['k_s'] for c in range(NS)], axis=1).reshape(DEPTH, NS, T_S, 8, 64)
    v_s = np.stack([R[c]['v_s'] for c in range(NS)], axis=1).reshape(DEPTH, NS, T_S, 8, 64)
    g_s = np.stack([R[c]['g_s'] for c in range(NS)], axis=1)
    c_s = np.stack([R[c]['c_s'] for c in range(NS)], axis=1)
    outs = (y_p, y_s, k_p, v_p, g_p, c_p, k_s, v_s, g_s, c_s)
    return tuple(np.ascontiguousarray(o, dtype=np.float32) for o in outs)


def kernel(**inputs):
    cfg = dict(T_P=8192, T_S=64, PAST=2048, DEPTH=2, FT=384)
    return run(cfg, inputs, n_cores=8)
```
